# Optimizing a Trainium2 kernel written in Bass

```python
import jax, jax.numpy as jnp
from jax import lax
import numpy as np

D_MODEL = 1024
BATCH = 32
SEQ = 2048
DEPTH = 1

HG_HEADS = 4
HG_DIM = 128
HG_WIDTH = HG_HEADS * HG_DIM
HG_CHUNK = 64
MB_HEADS = 8
MB_HEAD_DIM = 64
MB_WIDTH = MB_HEADS * MB_HEAD_DIM
MB_BLOCK = 256
MB_TOPK = 3
MB_QCHUNK = 4
ROPE_THETA = 10000.0
D_FF = 2816
CONV_WIDTH = 3
NORM_EPS = 1e-6
IN_WIDTHS = (HG_WIDTH, HG_WIDTH, HG_WIDTH, HG_WIDTH, MB_WIDTH, MB_WIDTH, MB_WIDTH, D_MODEL, D_MODEL)
D_IN = sum(IN_WIDTHS)
IN_SPLITS = tuple(int(s) for s in np.cumsum(IN_WIDTHS)[:-1])

kernel_name = "hybrid_hgrn2_moba_convffn_block"


def rms_norm(x, g):
    xf = x.astype(jnp.float32)
    y = xf * lax.rsqrt(jnp.mean(xf * xf, axis=-1, keepdims=True) + NORM_EPS)
    return (y * g.astype(jnp.float32)).astype(x.dtype)


def rope(x, pos):
    d = x.shape[-1]
    half = d // 2
    inv = 1.0 / (ROPE_THETA ** (jnp.arange(half, dtype=jnp.float32) * 2.0 / d))
    ang = pos.astype(jnp.float32)[:, None] * inv[None, :]
    cos = jnp.cos(ang)[:, None, :]
    sin = jnp.sin(ang)[:, None, :]
    xf = x.astype(jnp.float32)
    x1, x2 = xf[..., :half], xf[..., half:]
    return jnp.concatenate([x1 * cos - x2 * sin, x2 * cos + x1 * sin], axis=-1).astype(x.dtype)


def _to_chunks(t):
    b, s, h, e = t.shape
    return t.reshape(b, s // HG_CHUNK, HG_CHUNK, h, e).transpose(1, 0, 3, 2, 4)


def hgrn2_chunkwise(q, k, v, logf):
    b, s, h, dk = q.shape
    dv = v.shape[-1]
    causal = jnp.tril(jnp.ones((HG_CHUNK, HG_CHUNK), dtype=bool))

    def step(state, inp):
        qc, kc, vc, lf = inp
        cum = jnp.cumsum(lf, axis=2)
        o_inter = jnp.einsum('bhck,bhkv->bhcv', qc * jnp.exp(cum), state)
        diff = cum[:, :, :, None, :] - cum[:, :, None, :, :]
        decay = jnp.exp(jnp.where(causal[None, None, :, :, None], diff, -jnp.inf))
        attn = jnp.einsum('bhtk,bhtsk,bhsk->bhts', qc, decay, kc)
        o = o_inter + jnp.einsum('bhts,bhsv->bhtv', attn, vc)
        last = cum[:, :, -1:, :]
        new_state = jnp.exp(last[:, :, 0, :])[..., None] * state + jnp.einsum(
            'bhsk,bhsv->bhkv', kc * jnp.exp(last - cum), vc)
        return new_state, o

    state0 = jnp.zeros((b, h, dk, dv), jnp.float32)
    _, o = lax.scan(step, state0, (_to_chunks(q), _to_chunks(k), _to_chunks(v), _to_chunks(logf)))
    return o.transpose(1, 0, 3, 2, 4).reshape(b, s, h, dv)


def moba_attention(q, k, v):
    b, s, h, d = q.shape
    nb = -(-s // MB_BLOCK)
    pad = nb * MB_BLOCK - s
    scale = 1.0 / float(np.sqrt(d))
    qt = q.transpose(0, 2, 1, 3)
    kb = jnp.pad(k.transpose(0, 2, 1, 3), ((0, 0), (0, 0), (0, pad), (0, 0))).reshape(b, h, nb, MB_BLOCK, d)
    vb = jnp.pad(v.transpose(0, 2, 1, 3), ((0, 0), (0, 0), (0, pad), (0, 0))).reshape(b, h, nb, MB_BLOCK, d)
    kmean = jnp.mean(kb.astype(jnp.float32), axis=3)

    pos = jnp.arange(s)
    qblk = pos // MB_BLOCK
    gate = jnp.einsum('bhsd,bhnd->bhsn', qt.astype(jnp.float32), kmean)
    fully_past = jnp.arange(nb)[None, :] < qblk[:, None]
    gate = jnp.where(fully_past[None, None], gate, -jnp.inf)
    n_sel = max(1, min(MB_TOPK, nb - 1))
    _, idx = lax.top_k(gate, n_sel)
    valid = jnp.arange(n_sel)[None, :] < qblk[:, None]

    nq = s // MB_QCHUNK
    q_ch = qt.reshape(b, h, nq, MB_QCHUNK, d).transpose(2, 0, 1, 3, 4)
    i_ch = idx.reshape(b, h, nq, MB_QCHUNK, n_sel).transpose(2, 0, 1, 3, 4)
    v_ch = valid.reshape(nq, MB_QCHUNK, n_sel)
    p_ch = pos.reshape(nq, MB_QCHUNK)
    bi = jnp.arange(b)[:, None, None, None]
    hi = jnp.arange(h)[None, :, None, None]

    def attend(inp):
        qc, ic, vc, pc = inp
        kg = kb[bi, hi, ic]
        vg = vb[bi, hi, ic]
        own = pc[0] // MB_BLOCK
        k_own = lax.dynamic_index_in_dim(kb, own, axis=2, keepdims=False)
        v_own = lax.dynamic_index_in_dim(vb, own, axis=2, keepdims=False)
        s_sel = jnp.einsum('bhqd,bhqrkd->bhqrk', qc, kg).astype(jnp.float32) * scale
        s_sel = jnp.where(vc[None, None, :, :, None], s_sel, -jnp.inf)
        s_own = jnp.einsum('bhqd,bhkd->bhqk', qc, k_own).astype(jnp.float32) * scale
        key_pos = own * MB_BLOCK + jnp.arange(MB_BLOCK)
        s_own = jnp.where((key_pos[None, :] <= pc[:, None])[None, None], s_own, -jnp.inf)
        logits = jnp.concatenate([s_sel.reshape(b, h, MB_QCHUNK, n_sel * MB_BLOCK), s_own], axis=-1)
        p = jax.nn.softmax(logits, axis=-1)
        p_sel = p[..., :n_sel * MB_BLOCK].reshape(b, h, MB_QCHUNK, n_sel, MB_BLOCK).astype(vg.dtype)
        p_own = p[..., n_sel * MB_BLOCK:].astype(v_own.dtype)
        return (jnp.einsum('bhqrk,bhqrkd->bhqd', p_sel, vg)
                + jnp.einsum('bhqk,bhkd->bhqd', p_own, v_own))

    o = lax.map(attend, (q_ch, i_ch, v_ch, p_ch))
    return o.transpose(1, 0, 3, 2, 4).reshape(b, s, h, d)


def causal_dwconv(u, w, bias):
    s = u.shape[1]
    up = jnp.pad(u, ((0, 0), (CONV_WIDTH - 1, 0), (0, 0)))
    out = bias
    for j in range(CONV_WIDTH):
        out = out + up[:, j:j + s] * w[j]
    return out


def setup_inputs(seed: int = 0) -> dict:
    key = jax.random.key(seed)
    ks = jax.random.split(key, 16)
    f32 = jnp.float32

    def nrm(k, shape, scale):
        return jax.random.normal(k, shape, f32) * scale

    return {
        "x": nrm(ks[0], (BATCH, SEQ, D_MODEL), 1.0),
        "norm1_g": 1.0 + nrm(ks[1], (DEPTH, D_MODEL), 0.02),
        "w_in": nrm(ks[2], (DEPTH, D_MODEL, D_IN), D_MODEL ** -0.5),
        "hg_lb_logits": nrm(ks[3], (DEPTH + 1, HG_WIDTH), 1.0),
        "hg_onorm_g": 1.0 + nrm(ks[4], (DEPTH, HG_HEADS, HG_DIM), 0.02),
        "q_norm_g": 1.0 + nrm(ks[5], (DEPTH, MB_HEAD_DIM), 0.02),
        "k_norm_g": 1.0 + nrm(ks[6], (DEPTH, MB_HEAD_DIM), 0.02),
        "w_a": nrm(ks[7], (DEPTH, HG_WIDTH, D_MODEL), HG_WIDTH ** -0.5),
        "w_b": nrm(ks[8], (DEPTH, MB_WIDTH, D_MODEL), MB_WIDTH ** -0.5),
        "w_out": nrm(ks[9], (DEPTH, D_MODEL, D_MODEL), D_MODEL ** -0.5),
        "norm2_g": 1.0 + nrm(ks[10], (DEPTH, D_MODEL), 0.02),
        "w_up": nrm(ks[11], (DEPTH, D_MODEL, 2 * D_FF), D_MODEL ** -0.5),
        "conv_w": nrm(ks[12], (DEPTH, CONV_WIDTH, D_FF), CONV_WIDTH ** -0.5),
        "conv_b": nrm(ks[13], (DEPTH, D_FF), 0.01),
        "w_down": nrm(ks[14], (DEPTH, D_FF, D_MODEL), D_FF ** -0.5),
    }


def reference(x, norm1_g, w_in, hg_lb_logits, hg_onorm_g, q_norm_g, k_norm_g, w_a, w_b, w_out,
              norm2_g, w_up, conv_w, conv_b, w_down):
    b, s, _ = x.shape
    pos = jnp.arange(s)
    lower_bounds = jnp.cumsum(jax.nn.softmax(hg_lb_logits.astype(jnp.float32), axis=0), axis=0)
    for l in range(DEPTH):
        h = rms_norm(x, norm1_g[l])
        proj = h @ w_in[l]
        hq, hf, hi, hg, mq, mk, mv, ga, gb = jnp.split(proj, IN_SPLITS, axis=-1)

        lb = lower_bounds[l]
        f = lb + (1.0 - lb) * jax.nn.sigmoid(hf.astype(jnp.float32))
        logf = jnp.log(f)
        k_in = 1.0 - f
        q_a = jax.nn.silu(hq.astype(jnp.float32))
        hd = (b, s, HG_HEADS, HG_DIM)
        o_a = hgrn2_chunkwise(q_a.reshape(hd), k_in.reshape(hd),
                              hi.astype(jnp.float32).reshape(hd), logf.reshape(hd))
        o_a = rms_norm(o_a, hg_onorm_g[l]).reshape(b, s, HG_WIDTH).astype(x.dtype) * jax.nn.silu(hg)

        md = (b, s, MB_HEADS, MB_HEAD_DIM)
        q_b = rope(rms_norm(mq.reshape(md), q_norm_g[l]), pos)
        k_b = rope(rms_norm(mk.reshape(md), k_norm_g[l]), pos)
        o_b = moba_attention(q_b, k_b, mv.reshape(md)).reshape(b, s, MB_WIDTH)

        mix = jax.nn.sigmoid(ga) * (o_a @ w_a[l]) + jax.nn.sigmoid(gb) * (o_b @ w_b[l])
        x = x + mix @ w_out[l]

        h2 = rms_norm(x, norm2_g[l])
        u, v = jnp.split(h2 @ w_up[l], 2, axis=-1)
        u = causal_dwconv(u, conv_w[l], conv_b[l])
        x = x + (jax.nn.gelu(u, approximate=False) * v) @ w_down[l]
    return x
```

```python
from contextlib import ExitStack
import numpy as np
import concourse.bass as bass
import concourse.mybir as mybir
from concourse.bass_utils import run_bass_kernel_spmd

F32 = mybir.dt.float32
BF16 = mybir.dt.bfloat16
AF = mybir.ActivationFunctionType
ALU = mybir.AluOpType
AX = mybir.AxisListType

NCORES = 8
S = 2048
D = 1024
CH = 512
NCH = S // CH
DFF = 2816
NKF = DFF // 128
EPS = 1e-6
BIG = 30000.0
NBLK = 32
RING = 3

(B_HQ, B_HF, B_HI, B_HG, B_MK, B_MQ, B_MV, B_GAB0, B_WAB0, B_GAB1, B_GAB2, B_WAB1, B_GAB3,
 B_WO0, B_WO1) = range(15)
B_UP0 = 15
B_DN0 = 26


class Q:
    def __init__(self, name, issuer, sem, inc, kind):
        self.name, self.issuer, self.sem, self.inc, self.kind = name, issuer, sem, inc, kind
        self.nsig = 0
        self.last = None


class Sched:
    def __init__(self, nc):
        self.nc = nc
        self.ins = []
        self.queues = []

    def queue(self, name, issuer, sem, inc, kind):
        q = Q(name, issuer, sem, inc, kind)
        self.queues.append(q)
        return q

    def add(self, q, fn, reads=(), writes=(), sig=True):
        self.ins.append((q, fn, tuple(reads), tuple(writes), sig or q.kind == 'dma'))

    def finalize(self, final_issuer):
        ins = self.ins
        n = len(ins)
        sigval = [0] * n
        nextsig = [None] * n
        for i, (q, fn, r, w, sig) in enumerate(ins):
            if sig:
                q.nsig += 1
                sigval[i] = q.nsig
        lastsig = {}
        for i in range(n - 1, -1, -1):
            q = ins[i][0]
            if ins[i][4]:
                lastsig[q] = i
            nextsig[i] = lastsig.get(q)
        writers, readers = {}, {}
        clocks = {}
        iclk = [None] * n
        nwaits = 0
        for i, (q, fn, rds, wrs, sig) in enumerate(ins):
            deps = set()
            for k in rds:
                for qq, j in writers.get(k, {}).items():
                    if qq is q and q.kind == 'pe':
                        continue
                    deps.add(j)
            for k in wrs:
                for qq, j in writers.get(k, {}).items():
                    if qq is q and q.kind != 'dma':
                        continue
                    deps.add(j)
                for qq, j in readers.get(k, {}).items():
                    if qq is q and q.kind != 'dma':
                        continue
                    deps.add(j)
            if q.kind == 'dma' and q.last is not None:
                deps.add(q.last)
            clk = clocks.setdefault(id(q.issuer), {})
            for j in sorted(deps):
                js = nextsig[j]
                assert js is not None and js < i, f"dep signal after waiter: ins {i} dep {j} sig {js}"
                qj = ins[js][0]
                val = sigval[js] * qj.inc
                if clk.get(qj, 0) >= val:
                    continue
                q.issuer.wait_ge(qj.sem, val)
                nwaits += 1
                for qq, v in iclk[js].items():
                    if clk.get(qq, 0) < v:
                        clk[qq] = v
            r = fn()
            if sig:
                r.then_inc(q.sem, q.inc)
                c2 = dict(clk)
                c2[q] = sigval[i] * q.inc
                iclk[i] = c2
            for k in rds:
                readers.setdefault(k, {})[q] = i
            for k in wrs:
                writers.setdefault(k, {})[q] = i
            if q.kind == 'dma':
                q.last = i
        for q in self.queues:
            if q.nsig:
                final_issuer.wait_ge(q.sem, q.nsig * q.inc)
        return n, nwaits


def host_consts():
    c = {}
    c["c_ident"] = np.eye(128, dtype=np.float32)
    s = np.arange(128)[:, None]
    t = np.arange(128)[None, :]
    same = (s // 64) == (t // 64)
    c["c_U"] = ((s > t) & same).astype(np.float32)
    c["c_bd"] = ((s <= t) & same).astype(np.float32)
    c["c_tri"] = (s <= t).astype(np.float32)
    c["c_cind"] = np.stack([(np.arange(128) < 64), (np.arange(128) >= 64)], 1).astype(np.float32)
    half = 32
    inv = 1.0 / (10000.0 ** (np.arange(half, dtype=np.float32) * 2.0 / 64))
    pos = np.arange(S, dtype=np.float32)
    ang = pos[:, None] * inv[None, :]
    cs = np.stack([np.cos(ang), np.sin(ang)], 1).astype(np.float32)
    c["c_cs"] = np.ascontiguousarray(cs.reshape(16, 128, 2, 32).transpose(1, 0, 2, 3))
    kind = (np.arange(S)[None, :] // 256 == np.arange(8)[:, None]).astype(np.float32)
    c["c_kind"] = kind
    return c


def build(NSEQ, dump_names=()):
    nc = bass.Bass("TRN2", target_bir_lowering=False)
    es = ExitStack()

    def din(name, shape, dt=F32):
        return nc.dram_tensor(name, list(shape), dt, kind="ExternalInput").ap()

    x = din("x", [NSEQ, S, D])
    norm1_g = din("norm1_g", [1, D]); norm2_g = din("norm2_g", [1, D])
    w_in = din("w_in", [D, 5632]); hg_lb = din("hg_lb_logits", [2, 512])
    hg_on = din("hg_onorm_g", [1, 512]); qng = din("q_norm_g", [1, 64]); kng = din("k_norm_g", [1, 64])
    w_a = din("w_a", [512, D]); w_b = din("w_b", [512, D]); w_out = din("w_out", [D, D])
    w_up = din("w_up", [D, 2 * DFF]); conv_w = din("conv_w", [3, DFF]); conv_b = din("conv_b", [1, DFF])
    w_down = din("w_down", [DFF, D])
    c_ident = din("c_ident", [128, 128]); c_U = din("c_U", [128, 128]); c_bd = din("c_bd", [128, 128])
    c_tri = din("c_tri", [128, 128]); c_cind = din("c_cind", [128, 2]); c_cs = din("c_cs", [128, 16, 2, 32])
    c_kind = din("c_kind", [8, S])
    out = nc.dram_tensor("out", [NSEQ, S, D], F32, kind="ExternalOutput").ap()
    wbf = nc.dram_tensor("wbf", [NBLK, 128, 4096], BF16).ap()
    dumps = {}

    def sb(name, shape, dt):
        return es.enter_context(nc.sbuf_tensor(name, list(shape), dt))

    def ps(name, shape, dt):
        return es.enter_context(nc.psum_tensor(name, list(shape), dt))

    def sem(name):
        return es.enter_context(nc.semaphore(name))

    sch = Sched(nc)
    PE = sch.queue("pe", nc.tensor, sem("s_pe"), 1, 'pe')
    ACT = sch.queue("act", nc.scalar, sem("s_act"), 1, 'cmp')
    DVE = sch.queue("dve", nc.vector, sem("s_dve"), 1, 'cmp')
    POOL = sch.queue("pool", nc.gpsimd, sem("s_pool"), 1, 'cmp')
    WQ = [sch.queue(f"wq{i}", nc.sync, sem(f"s_wq{i}"), 16, 'dma') for i in range(RING)]
    XQ = [sch.queue(f"xq{i}", nc.sync, sem(f"s_xq{i}"), 16, 'dma') for i in range(2)]
    OQ = sch.queue("oq", nc.gpsimd, sem("s_oq"), 16, 'dma')
    CQ = [sch.queue(f"cq{i}", nc.gpsimd, sem(f"s_cq{i}"), 16, 'dma') for i in range(4)]
    KQ = [sch.queue(f"kq{i}", nc.sync, sem(f"s_kq{i}"), 16, 'dma') for i in range(4)]
    DQ = sch.queue("dq", nc.sync, sem("s_dq"), 16, 'dma')

    wring = sb("wring", [128, RING, 4096], BF16)
    kTe = sb("kTe", [128, 8, S], BF16)
    vext = sb("vext", [128, 16, 8, 65], BF16)
    xmid = sb("xmid", [128, 4, D], F32)
    hT = sb("hT", [128, 8, CH], BF16)
    g1b = sb("g1b", [128, D], F32); g2b = sb("g2b", [128, D], F32)
    fS = sb("fS", [128, 18, 512], F32)
    bB = sb("bB", [128, 24, 512], BF16)
    Sst = sb("Sst", [128, 512], F32)
    qtil = sb("qtil", [128, 2, 512], BF16)
    attn = sb("attn", [128, 2, 512], BF16)
    sdb = sb("sdb", [128, 2, 512], BF16)
    oabf = sb("oabf", [128, 2, 512], BF16)
    oaT = sb("oaT", [128, 4, CH], BF16); obT = sb("obT", [128, 4, CH], BF16)
    ropeo = sb("ropeo", [128, 2, 512], BF16)
    bias = sb("bias", [128, 2, 8, 72], BF16)
    pt = sb("pt", [128, 3, 512], BF16)
    ident = sb("ident", [128, 128], BF16)
    Um = sb("Um", [128, 128], F32); bd = sb("bd", [128, 128], F32); tri = sb("tri", [128, 128], BF16)
    cind = sb("cind", [128, 2], F32)
    omlb = sb("omlb", [128, 512], F32); gob = sb("gob", [128, 512], F32)
    cs = sb("cs", [128, 16, 2, 32], F32)
    gq8 = sb("gq8", [128, 64], F32); gkb = sb("gkb", [128, 64], F32)
    cw = sb("cw", [128, 3, NKF], F32); cb = sb("cb", [128, NKF], F32)
    halo = sb("halo", [128, 2, NKF, 2], F32)
    elast = sb("elast", [128, 4, 4, 2], F32)
    st = sb("st", [128, 128], F32)
    kmT = sb("kmT", [128, 8, 8], BF16)
    kmf = sb("kmf", [128, 8], F32)
    gsm = sb("gsm", [128, 64], F32)
    cmpb = sb("cmpb", [128, 8 * 7 * 7], F32)
    cnt = sb("cnt", [128, 56], F32)
    rden = sb("rden", [128, 2, 4], F32)
    Pf = [ps(f"P{i}", [128, 512], F32) for i in range(6)]
    Tb = [ps(f"T{i}", [128, 1024], BF16) for i in range(2)]

    K = lambda *a: tuple(a)

    def fslot(i, n=1):
        return fS[:, i, :] if n == 1 else fS[:, i:i + n, :]

    class Rot:
        def __init__(self, n):
            self.n, self.i = n, 0

        def nxt(self):
            v = self.i % self.n
            self.i += 1
            return v

    rP = Rot(4)
    rP3 = Rot(3)
    rT = Rot(2)
    rxt = Rot(2)
    rpt = Rot(3)

    def dump(name, ap, keys, shape, dt=F32):
        if name not in dump_names or name in dumps:
            return
        t = nc.dram_tensor("dbg_" + name, list(shape), dt, kind="ExternalOutput").ap()
        dumps[name] = t
        sch.add(DQ, lambda: nc.sync.dma_start(out=t, in_=ap), reads=keys)

    cqi = [0]

    def cast(dst, src, b):
        q = CQ[cqi[0] % 4]
        cqi[0] += 1
        sch.add(q, lambda: nc.gpsimd.dma_start(out=dst, in_=src), writes=[K("wbf", b)])

    def blkv(b, kc0, nkc, n0, nn):
        v = wbf[b].rearrange("p (kc n) -> p kc n", n=512)
        return v[:, kc0:kc0 + nkc, n0:n0 + nn]

    def rows(w, r0, nkc, c0, ncol):
        return w[r0:r0 + nkc * 128, c0:c0 + ncol].rearrange("(kc p) n -> p kc n", p=128)

    for g, b in enumerate([B_HQ, B_HF, B_HI, B_HG, B_MQ, B_MK, B_MV]):
        cast(blkv(b, 0, 8, 0, 512), rows(w_in, 0, 8, g * 512, 512), b)
    for qd, b in enumerate([B_GAB0, B_GAB1, B_GAB2, B_GAB3]):
        cast(blkv(b, 0, 8, 0, 256), rows(w_in, 0, 8, 3584 + qd * 256, 256), b)
        cast(blkv(b, 0, 8, 256, 256), rows(w_in, 0, 8, 4608 + qd * 256, 256), b)
    for hf, b in enumerate([B_WAB0, B_WAB1]):
        cast(blkv(b, 0, 4, 0, 512), rows(w_a, 0, 4, hf * 512, 512), b)
        cast(blkv(b, 4, 4, 0, 512), rows(w_b, 0, 4, hf * 512, 512), b)
    for hf, b in enumerate([B_WO0, B_WO1]):
        cast(blkv(b, 0, 8, 0, 512), rows(w_out, 0, 8, hf * 512, 512), b)
    for u in range(11):
        cast(blkv(B_UP0 + u, 0, 8, 0, 256), rows(w_up, 0, 8, u * 256, 256), B_UP0 + u)
        cast(blkv(B_UP0 + u, 0, 8, 256, 256), rows(w_up, 0, 8, DFF + u * 256, 256), B_UP0 + u)
    DN_P = [(0, 8), (8, 8), (16, 6)]
    for nh in range(2):
        for pi, (k0, nk) in enumerate(DN_P):
            cast(blkv(B_DN0 + nh * 3 + pi, 0, nk, 0, 512), rows(w_down, k0 * 128, nk, nh * 512, 512), B_DN0 + nh * 3 + pi)

    kqi = [0]

    def kload(dst, src, key, eng=None):
        q = KQ[kqi[0] % 4]
        kqi[0] += 1
        sch.add(q, lambda: nc.sync.dma_start(out=dst, in_=src), writes=[key])

    def kcast(dst, src, key):
        q = CQ[cqi[0] % 4]
        cqi[0] += 1
        sch.add(q, lambda: nc.gpsimd.dma_start(out=dst, in_=src), writes=[key])

    kcast(ident[:], c_ident, K("ident"))
    kcast(tri[:], c_tri, K("tri"))
    kload(Um[:], c_U, K("Um")); kload(bd[:], c_bd, K("bd")); kload(cind[:], c_cind, K("cind"))
    kload(cs[:], c_cs, K("cs"))
    kload(g1b[:], norm1_g.partition_broadcast(128), K("g1b"))
    kload(g2b[:], norm2_g.partition_broadcast(128), K("g2b"))
    kload(gob[:], hg_on.partition_broadcast(128), K("gob"))
    kload(gq8[:], qng.partition_broadcast(128), K("gq8"))
    kload(gkb[:], kng.partition_broadcast(128), K("gkb"))
    kload(fS[:, 0, :], hg_lb[0:1, :].partition_broadcast(128), K("fS", 0))
    kload(fS[:, 1, :], hg_lb[1:2, :].partition_broadcast(128), K("fS", 1))
    for j in range(3):
        sch.add(KQ[j % 4], (lambda j=j: nc.sync.dma_start(
            out=cw[:, j, :], in_=conv_w[j:j + 1, :].rearrange("o (kc p) -> p (o kc)", p=128),
            allow_slow_non_contiguous=True)), writes=[K("cw")])
    sch.add(KQ[3], lambda: nc.sync.dma_start(
        out=cb[:], in_=conv_b.rearrange("o (kc p) -> p (o kc)", p=128), allow_slow_non_contiguous=True),
        writes=[K("cb")])
    for h in range(8):
        kcast(kTe[64:72, h, :], c_kind, K("kTe_ind"))
    sch.add(DVE, lambda: nc.vector.tensor_tensor(out=fS[:, 0, :], in0=fS[:, 0, :], in1=fS[:, 1, :], op=ALU.subtract),
            reads=[K("fS", 0), K("fS", 1)], writes=[K("fS", 0)])
    sch.add(ACT, lambda: nc.scalar.activation(out=omlb[:], in_=fS[:, 0, :], func=AF.Sigmoid, scale=-1.0),
            reads=[K("fS", 0)], writes=[K("omlb")])
    sch.add(ACT, lambda: nc.scalar.mul(out=gq8[:], in_=gq8[:], mul=0.125), reads=[K("gq8")], writes=[K("gq8")])
    sch.add(POOL, lambda: nc.gpsimd.memset(vext[:, :, :, 64:65], 1.0), writes=[K("vext_one")])
    sch.add(POOL, lambda: nc.gpsimd.memset(bias[:, :, :, 0:64], 0.0), writes=[K("bias", 0), K("bias", 1)])

    wstate = {"next": 0}
    total_blocks = NSEQ * NCH * NBLK

    def wload_upto(gb):
        while wstate["next"] <= gb and wstate["next"] < total_blocks:
            g = wstate["next"]
            slot = g % RING
            b = g % NBLK
            sch.add(WQ[slot], (lambda slot=slot, b=b: nc.sync.dma_start(out=wring[:, slot, :], in_=wbf[b])),
                    reads=[K("wbf", b)], writes=[K("w", slot)])
            wstate["next"] += 1

    def wblk(gc, b, ahead=2):
        gb = gc * NBLK + b
        wload_upto(gb + ahead)
        slot = gb % RING
        return wring[:, slot, :].rearrange("p (kc n) -> p kc n", n=512), K("w", slot)

    def mm(out_ap, lhsT, rhs, start, stop, reads, writes, sig, **kw):
        sch.add(PE, lambda: nc.tensor.matmul(out_ap, lhsT=lhsT, rhs=rhs, start=start, stop=stop, **kw),
                reads=reads, writes=writes, sig=sig)

    def tp(out_ap, in_ap, reads, writes, sig):
        sch.add(PE, lambda: nc.tensor.transpose(out_ap, in_ap, ident[:]), reads=list(reads) + [K("ident")],
                writes=writes, sig=sig)

    hbuf = sb("hbuf", [128, 2, D], BF16)
    junk2 = sb("junk2", [128, 128], BF16)

    def norm_A(src_ap, src_keys, gb_t, gkey, tt, stc):
        junk = bB[:, 22:24, :].rearrange("p a b -> p (a b)")
        sch.add(ACT, lambda: nc.scalar.activation(out=junk, in_=src_ap, func=AF.Square, scale=1.0 / 32.0,
                                                  accum_out=st[:, stc:stc + 1]),
                reads=src_keys, writes=[K("bB", 22), K("bB", 23), K("st", stc)])
        sch.add(DVE, lambda: nc.vector.tensor_scalar(out=st[:, stc + 1:stc + 2], in0=st[:, stc:stc + 1], scalar1=EPS,
                                                     scalar2=None, op0=ALU.add),
                reads=[K("st", stc)], writes=[K("st", stc + 1)])
        sch.add(ACT, lambda: nc.scalar.activation(out=st[:, stc + 2:stc + 3], in_=st[:, stc + 1:stc + 2], func=AF.Ln),
                reads=[K("st", stc + 1)], writes=[K("st", stc + 2)])
        sch.add(ACT, lambda: nc.scalar.activation(out=st[:, stc + 3:stc + 4], in_=st[:, stc + 2:stc + 3], func=AF.Exp,
                                                  scale=-0.5),
                reads=[K("st", stc + 2)], writes=[K("st", stc + 3)])
        hi_ = tt % 2
        sch.add(DVE, lambda: nc.vector.scalar_tensor_tensor(out=hbuf[:, hi_, :], in0=src_ap,
                                                            scalar=st[:, stc + 3:stc + 4],
                                                            in1=gb_t[:], op0=ALU.mult, op1=ALU.mult),
                reads=list(src_keys) + [K("st", stc + 3), gkey], writes=[K("hbuf", hi_)])

    def norm_B(tt):
        hi_ = tt % 2
        ti = rT.nxt()
        for kc in range(8):
            tp(Tb[ti][:, kc * 128:(kc + 1) * 128], hbuf[:, hi_, kc * 128:(kc + 1) * 128], [K("hbuf", hi_)],
               [K("T", ti)], kc == 7)
        sch.add(DVE, lambda: nc.vector.tensor_copy(out=hT[:, :, tt * 128:(tt + 1) * 128],
                                                   in_=Tb[ti][:, :].rearrange("p (kc t) -> p kc t", t=128)),
                reads=[K("T", ti)], writes=[K("hT", tt)])

    def norm_tile(src_ap, src_keys, gb_t, gkey, tt, stc):
        norm_A(src_ap, src_keys, gb_t, gkey, tt, stc)
        norm_B(tt)

    def x_norm_A(s_, c_, tt):
        xi = rxt.nxt()
        xs = 12 + 2 * xi
        xap = fS[:, xs:xs + 2, :].rearrange("p a b -> p (a b)")
        xk = [K("fS", xs), K("fS", xs + 1)]
        r0 = c_ * CH + tt * 128
        sch.add(XQ[xi], lambda: nc.sync.dma_start(out=xap, in_=x[s_, r0:r0 + 128, :]), writes=xk)
        norm_A(xap, xk, g1b, K("g1b"), tt, 4 * tt)

    prefetched = set()

    HTK = [K("hT", t) for t in range(4)]

    def chunk(s, c):
        gc = s * NCH + c
        first = (gc == 0)
        for tt in range(4):
            if (gc, tt) in prefetched:
                continue
            x_norm_A(s, c, tt)
            norm_B(tt)
        if first:
            dump("hT", hT[:], HTK, [128, 8, CH], BF16)
        wq_, kq_ = wblk(gc, B_HQ)
        wf_, kf_ = wblk(gc, B_HF, 1)
        if c == 0:
            sch.add(DVE, lambda: nc.vector.memset(Sst[:], 0.0), writes=[K("Sst")])
        KQT = [K("bB", i) for i in range(8)]

        def h_stageA(tt):
            r = tt % 2
            qs, ks, ls = 0 + r, 2 + r, 4 + r
            pq = rP3.nxt()
            for kc in range(8):
                mm(Pf[pq][:, :], hT[:, kc, tt * 128:(tt + 1) * 128], wq_[:, kc, :], kc == 0, kc == 7,
                   [K("hT", tt), kq_], [K("P", pq)], kc == 7)
            sch.add(ACT, lambda: nc.scalar.activation(out=fS[:, qs, :], in_=Pf[pq][:, :], func=AF.Silu),
                    reads=[K("P", pq)], writes=[K("fS", qs)])
            pf = rP3.nxt()
            for kc in range(8):
                mm(Pf[pf][:, :], hT[:, kc, tt * 128:(tt + 1) * 128], wf_[:, kc, :], kc == 0, kc == 7,
                   [K("hT", tt), kf_], [K("P", pf)], kc == 7)
            sch.add(ACT, lambda: nc.scalar.activation(out=fS[:, ks, :], in_=Pf[pf][:, :], func=AF.Sigmoid, scale=-1.0),
                    reads=[K("P", pf)], writes=[K("fS", ks)])
            sch.add(DVE, lambda: nc.vector.tensor_tensor(out=fS[:, ks, :], in0=fS[:, ks, :], in1=omlb[:], op=ALU.mult),
                    reads=[K("fS", ks), K("omlb")], writes=[K("fS", ks)])
            sch.add(ACT, lambda: nc.scalar.activation(out=fS[:, ls, :], in_=fS[:, ks, :], func=AF.Ln, scale=-1.0,
                                                      bias=1.0),
                    reads=[K("fS", ks)], writes=[K("fS", ls)])

        def h_stageB(tt):
            r = tt % 2
            qs, ks, ls, es_, ns = 0 + r, 2 + r, 4 + r, 6 + r, 8 + r
            pa = rP3.nxt()
            mm(Pf[pa][:, :], Um[:], fS[:, ls, :], True, True, [K("Um"), K("fS", ls)], [K("P", pa)], True)
            sch.add(ACT, lambda: nc.scalar.activation(out=fS[:, es_, :], in_=Pf[pa][:, :], func=AF.Exp),
                    reads=[K("P", pa)], writes=[K("fS", es_)])
            sch.add(ACT, lambda: nc.scalar.activation(out=fS[:, ns, :], in_=Pf[pa][:, :], func=AF.Exp, scale=-1.0),
                    reads=[K("P", pa)], writes=[K("fS", ns)])
            pl = rP3.nxt()
            for h in range(4):
                mm(Pf[pl][:, h * 2:h * 2 + 2], fS[:, ls, h * 128:(h + 1) * 128], cind[:], True, True,
                   [K("fS", ls), K("cind")], [K("P", pl)], h == 3)
            sch.add(ACT, lambda: nc.scalar.activation(
                out=elast[:, tt, :, :].rearrange("p h j -> p (h j)"), in_=Pf[pl][:, 0:8], func=AF.Exp),
                reads=[K("P", pl)], writes=[K("elast", tt)])
            sch.add(DVE, lambda: nc.vector.tensor_tensor(out=bB[:, 12 + tt, :], in0=fS[:, ks, :], in1=fS[:, es_, :],
                                                         op=ALU.mult),
                    reads=[K("fS", ks), K("fS", es_)], writes=[K("bB", 12 + tt)])
            sch.add(DVE, lambda: nc.vector.tensor_tensor(out=qtil[:, r, :], in0=fS[:, qs, :], in1=fS[:, ns, :],
                                                         op=ALU.mult),
                    reads=[K("fS", qs), K("fS", ns)], writes=[K("qtil", r)])
            if first and tt == 0:
                dump("khat0", bB[:, 12, :], [K("bB", 12)], [128, 512], BF16)
                dump("qtil0", qtil[:, 0, :], [K("qtil", 0)], [128, 512], BF16)

        def h_stageB2(tt):
            r = tt % 2
            ti = rT.nxt()
            for h in range(4):
                tp(Tb[ti][:, h * 128:(h + 1) * 128], bB[:, 12 + tt, h * 128:(h + 1) * 128], [K("bB", 12 + tt)],
                   [K("T", ti)], False)
            for h in range(4):
                tp(Tb[ti][:, (4 + h) * 128:(5 + h) * 128], qtil[:, r, h * 128:(h + 1) * 128], [K("qtil", r)],
                   [K("T", ti)], h == 3)
            sch.add(ACT, lambda: nc.scalar.copy(out=bB[:, 0:8, tt * 128:(tt + 1) * 128],
                                                in_=Tb[ti][:, :].rearrange("p (a t) -> p a t", t=128)),
                    reads=[K("T", ti)], writes=[K("kqT", tt)] + KQT)

        hi_state = {}

        def h_hi(tt):
            if "w" not in hi_state:
                hi_state["w"] = wblk(gc, B_HI)
            wi_, ki_ = hi_state["w"]
            pv = rP3.nxt()
            for kc in range(8):
                mm(Pf[pv][:, :], hT[:, kc, tt * 128:(tt + 1) * 128], wi_[:, kc, :], kc == 0, kc == 7,
                   [K("hT", tt), ki_], [K("P", pv)], kc == 7)
            sch.add(ACT, lambda: nc.scalar.copy(out=bB[:, 8 + tt, :], in_=Pf[pv][:, :]),
                    reads=[K("P", pv)], writes=[K("bB", 8 + tt)])

        h_stageA(0); h_stageA(1); h_stageB(0); h_stageA(2); h_stageB(1); h_stageB2(0); h_stageA(3); h_stageB(2)
        h_stageB2(1); h_hi(0); h_stageB(3); h_hi(1); h_stageB2(2); h_hi(2); h_hi(3); h_stageB2(3)
        wg_, kg_ = wblk(gc, B_HG)
        for tt in range(4):
            ph = rP3.nxt()
            for kc in range(8):
                mm(Pf[ph][:, :], hT[:, kc, tt * 128:(tt + 1) * 128], wg_[:, kc, :], kc == 0, kc == 7,
                   [K("hT", tt), kg_], [K("P", ph)], kc == 7)
            sch.add(ACT, lambda ph=ph, tt=tt: nc.scalar.activation(out=fS[:, tt, :], in_=Pf[ph][:, :], func=AF.Silu),
                    reads=[K("P", ph)], writes=[K("fS", tt)])
            sch.add(POOL, lambda tt=tt: nc.gpsimd.tensor_tensor(out=fS[:, tt, :], in0=fS[:, tt, :], in1=gob[:],
                                                                op=ALU.mult),
                    reads=[K("fS", tt), K("gob")], writes=[K("fS", tt)])

        def h_post_tr(tt):
            r = tt % 2
            ti = rT.nxt()
            for kc in range(4):
                tp(Tb[ti][:, kc * 128:(kc + 1) * 128], oabf[:, r, kc * 128:(kc + 1) * 128], [K("oabf", r)],
                   [K("T", ti)], kc == 3)
            sch.add(ACT, lambda: nc.scalar.copy(out=oaT[:, :, tt * 128:(tt + 1) * 128],
                                                in_=Tb[ti][:, 0:512].rearrange("p (a t) -> p a t", t=128)),
                    reads=[K("T", ti)], writes=[K("oaT", tt)])

        def h_attn(tt):
            r = tt % 2
            po = 4 + r
            pat = rP3.nxt()
            for h in range(4):
                mm(Pf[pat][:, h * 128:(h + 1) * 128], bB[:, h, tt * 128:(tt + 1) * 128],
                   bB[:, 4 + h, tt * 128:(tt + 1) * 128], True, True, KQT, [K("P", pat)], h == 3)
            sch.add(DVE, lambda: nc.vector.scalar_tensor_tensor(
                out=attn[:, r, :].rearrange("p (h t) -> p h t", h=4),
                in0=Pf[pat][:, :].rearrange("p (h t) -> p h t", h=4), scalar=1e30,
                in1=bd[:].unsqueeze(1).broadcast_to([128, 4, 128]), op0=ALU.min, op1=ALU.mult),
                reads=[K("P", pat), K("bd")], writes=[K("attn", r)])
            for h in range(4):
                mm(Pf[po][:, h * 128:(h + 1) * 128], attn[:, r, h * 128:(h + 1) * 128],
                   bB[:, 8 + tt, h * 128:(h + 1) * 128], h == 0, False, [K("attn", r), K("bB", 8 + tt)],
                   [K("P", po)], False, skip_group_check=True)

        def h_rec(tt):
            r = tt % 2
            po = 4 + r
            for j in range(2):
                n = 2 * tt + j
                sd = n % 2
                sch.add(DVE, lambda j=j: nc.vector.tensor_tensor(
                    out=fS[:, 16, :].rearrange("p (h v) -> p h v", h=4),
                    in0=Sst[:].rearrange("p (h v) -> p h v", h=4),
                    in1=elast[:, tt, :, j:j + 1].broadcast_to([128, 4, 128]), op=ALU.mult),
                    reads=[K("Sst"), K("elast", tt)], writes=[K("fS", 16)])
                sch.add(ACT, lambda sd=sd: nc.scalar.copy(out=sdb[:, sd, :], in_=fS[:, 16, :]),
                        reads=[K("fS", 16)], writes=[K("sdb", sd)])
                for h in range(4):
                    mm(Pf[3][:, h * 128:(h + 1) * 128], bB[j * 64:(j + 1) * 64, 12 + tt, h * 128:(h + 1) * 128],
                       bB[j * 64:(j + 1) * 64, 8 + tt, h * 128:(h + 1) * 128], True, True,
                       [K("bB", 12 + tt), K("bB", 8 + tt)], [K("P", 3)], h == 3)
                for h in range(4):
                    t0 = tt * 128 + j * 64
                    mm(Pf[po][j * 64:(j + 1) * 64, h * 128:(h + 1) * 128], bB[:, 4 + h, t0:t0 + 64],
                       sdb[:, sd, h * 128:(h + 1) * 128], False, (j == 1 and h == 3), KQT + [K("sdb", sd)],
                       [K("P", po)], (j == 1 and h == 3), skip_group_check=True)
                sch.add(DVE, lambda: nc.vector.tensor_tensor(out=Sst[:], in0=fS[:, 16, :], in1=Pf[3][:, :], op=ALU.add),
                        reads=[K("fS", 16), K("P", 3)], writes=[K("Sst")])

        def h_post(tt):
            r = tt % 2
            po = 4 + r
            so = 16 + 16 * r
            for h in range(4):
                sch.add(ACT, lambda h=h: nc.scalar.activation(
                    out=junk2[:], in_=Pf[po][:, h * 128:(h + 1) * 128], func=AF.Square,
                    scale=float(1.0 / np.sqrt(128.0)), accum_out=st[:, so + h:so + h + 1]),
                    reads=[K("P", po)], writes=[K("junk2"), K("st", so + h)])
            sch.add(DVE, lambda: nc.vector.tensor_scalar(out=st[:, so + 4:so + 8], in0=st[:, so:so + 4], scalar1=EPS,
                                                         scalar2=None, op0=ALU.add),
                    reads=[K("st", so + h) for h in range(4)], writes=[K("st", so + 4)])
            sch.add(ACT, lambda: nc.scalar.activation(out=st[:, so + 8:so + 12], in_=st[:, so + 4:so + 8], func=AF.Ln),
                    reads=[K("st", so + 4)], writes=[K("st", so + 8)])
            sch.add(ACT, lambda: nc.scalar.activation(out=st[:, so + 12:so + 16], in_=st[:, so + 8:so + 12],
                                                      func=AF.Exp, scale=-0.5),
                    reads=[K("st", so + 8)], writes=[K("st", so + 12)])
            for h in range(4):
                sch.add(DVE, lambda h=h: nc.vector.scalar_tensor_tensor(
                    out=oabf[:, r, h * 128:(h + 1) * 128], in0=Pf[po][:, h * 128:(h + 1) * 128],
                    scalar=st[:, so + 12 + h:so + 13 + h], in1=fS[:, tt, h * 128:(h + 1) * 128], op0=ALU.mult,
                    op1=ALU.mult),
                    reads=[K("P", po), K("st", so + 12), K("fS", tt)], writes=[K("oabf", r)])
            if first and tt == 0:
                dump("oa0", oabf[:, 0, :], [K("oabf", 0)], [128, 512], BF16)

        QTE = [K("bB", 16 + i) for i in range(8)]
        qTe = bB[:, 16:24, :]
        wmk, kmk = wblk(gc, B_MK)
        wstore = {"k": (wmk, kmk)}

        def m_stageA(kind, tt, idx):
            is_q = kind == "q"
            if is_q and "q" not in wstore:
                wstore["q"] = wblk(gc, B_MQ)
            wv, wk = wstore[kind]
            par = idx % 3
            sq, mn = [4, 5, 10][par], [6, 7, 11][par]
            sc = 64 + 16 * par
            pm = rP3.nxt()
            for kc in range(8):
                mm(Pf[pm][:, :], hT[:, kc, tt * 128:(tt + 1) * 128], wv[:, kc, :], kc == 0, kc == 7,
                   [K("hT", tt), wk], [K("P", pm)], kc == 7)
            sch.add(ACT, lambda: nc.scalar.activation(out=fS[:, sq, :], in_=Pf[pm][:, :], func=AF.Square, scale=0.125),
                    reads=[K("P", pm)], writes=[K("fS", sq)])
            sch.add(DVE, lambda: nc.vector.tensor_reduce(out=st[:, sc:sc + 8],
                                                         in_=fS[:, sq, :].rearrange("p (h d) -> p h d", h=8),
                                                         axis=AX.X, op=ALU.add),
                    reads=[K("fS", sq)], writes=[K("st", sc)])
            sch.add(DVE, lambda: nc.vector.tensor_scalar(out=st[:, sc:sc + 8], in0=st[:, sc:sc + 8], scalar1=EPS,
                                                         scalar2=None, op0=ALU.add),
                    reads=[K("st", sc)], writes=[K("st", sc)])
            sch.add(ACT, lambda: nc.scalar.activation(out=st[:, sc + 8:sc + 16], in_=st[:, sc:sc + 8], func=AF.Ln),
                    reads=[K("st", sc)], writes=[K("st", sc + 8)])
            sch.add(ACT, lambda: nc.scalar.activation(out=st[:, sc:sc + 8], in_=st[:, sc + 8:sc + 16], func=AF.Exp,
                                                      scale=-0.5),
                    reads=[K("st", sc + 8)], writes=[K("st", sc)])
            sch.add(DVE, lambda: nc.vector.tensor_tensor(
                out=fS[:, mn, :].rearrange("p (h d) -> p h d", h=8),
                in0=Pf[pm][:, :].rearrange("p (h d) -> p h d", h=8),
                in1=st[:, sc:sc + 8].unsqueeze(2).broadcast_to([128, 8, 64]), op=ALU.mult),
                reads=[K("P", pm), K("st", sc)], writes=[K("fS", mn)])
            gvec, gkey = (gq8, K("gq8")) if is_q else (gkb, K("gkb"))
            m3 = fS[:, mn, :].rearrange("p (h d) -> p h d", h=8)
            sch.add(DVE, lambda: nc.vector.tensor_tensor(out=m3, in0=m3,
                                                         in1=gvec[:].unsqueeze(1).broadcast_to([128, 8, 64]),
                                                         op=ALU.mult),
                    reads=[K("fS", mn), gkey], writes=[K("fS", mn)])

        def m_stageB(kind, tt, idx):
            is_q = kind == "q"
            par = idx % 3
            mn, tB = [6, 7, 11][par], [8, 9, 17][par]
            tile_i = c * 4 + tt
            m4 = fS[:, mn, :].rearrange("p (h a d) -> p h a d", h=8, a=2)
            tB4 = fS[:, tB, :].rearrange("p (h a d) -> p h a d", h=8, a=2)
            cosb = cs[:, tile_i, 0, :]
            sinb3 = cs[:, tile_i, 1, :].unsqueeze(1).broadcast_to([128, 8, 32])
            sch.add(POOL, lambda: nc.gpsimd.tensor_tensor(out=tB4[:, :, 0, :], in0=m4[:, :, 1, :], in1=sinb3,
                                                          op=ALU.mult),
                    reads=[K("fS", mn), K("cs")], writes=[K("fS", tB)])
            sch.add(POOL, lambda: nc.gpsimd.tensor_tensor(out=tB4[:, :, 1, :], in0=m4[:, :, 0, :], in1=sinb3,
                                                          op=ALU.mult),
                    reads=[K("fS", mn), K("cs")], writes=[K("fS", tB)])
            sch.add(DVE, lambda: nc.vector.tensor_tensor(
                out=m4, in0=m4, in1=cosb.unsqueeze(1).unsqueeze(1).broadcast_to([128, 8, 2, 32]), op=ALU.mult),
                reads=[K("fS", mn), K("cs"), K("fS", tB)], writes=[K("fS", mn)])
            ro = idx % 2
            ro4 = ropeo[:, ro, :].rearrange("p (h a d) -> p h a d", h=8, a=2)
            sch.add(POOL, lambda: nc.gpsimd.tensor_tensor(out=ro4[:, :, 0, :], in0=m4[:, :, 0, :], in1=tB4[:, :, 0, :],
                                                          op=ALU.subtract),
                    reads=[K("fS", mn), K("fS", tB)], writes=[K("ropeo", ro)])
            sch.add(POOL, lambda: nc.gpsimd.tensor_tensor(out=ro4[:, :, 1, :], in0=m4[:, :, 1, :], in1=tB4[:, :, 1, :],
                                                          op=ALU.add),
                    reads=[K("fS", mn), K("fS", tB)], writes=[K("ropeo", ro)])
            ti = rT.nxt()
            for h in range(8):
                tp(Tb[ti][0:64, h * 128:(h + 1) * 128], ropeo[:, ro, h * 64:(h + 1) * 64], [K("ropeo", ro)],
                   [K("T", ti)], h == 7)
            src = Tb[ti][0:64, :].rearrange("p (h t) -> p h t", h=8)
            if is_q:
                sch.add(DVE, lambda: nc.vector.tensor_copy(out=qTe[0:64, :, tt * 128:(tt + 1) * 128], in_=src),
                        reads=[K("T", ti)], writes=QTE + [K("qTe", tt)])
            else:
                p0 = c * CH + tt * 128
                sch.add(DVE, lambda: nc.vector.tensor_copy(out=kTe[0:64, :, p0:p0 + 128], in_=src),
                        reads=[K("T", ti)], writes=[K("kTe", tile_i)])
                if tt % 2 == 1:
                    blk = 2 * c + tt // 2
                    sch.add(DVE, lambda: nc.vector.tensor_reduce(out=kmf[0:64, :],
                                                                 in_=kTe[0:64, :, blk * 256:(blk + 1) * 256],
                                                                 axis=AX.X, op=ALU.add),
                            reads=[K("kTe", 2 * blk), K("kTe", 2 * blk + 1)], writes=[K("kmf")])
                    sch.add(DVE, lambda: nc.vector.tensor_scalar(out=kmT[0:64, :, blk:blk + 1],
                                                                 in0=kmf[0:64, :].unsqueeze(2), scalar1=1.0 / 256.0,
                                                                 scalar2=None, op0=ALU.mult),
                            reads=[K("kmf")], writes=[K("kmT")])

        def m_stageC(tt):
            qb = 2 * c + tt // 2
            bi = tt % 2
            if qb >= 4:
                pg = rP3.nxt()
                for h in range(8):
                    mm(Pf[pg][:, h * 8:h * 8 + qb], qTe[0:64, h, tt * 128:(tt + 1) * 128], kmT[0:64, h, 0:qb], True, True,
                       [K("qTe", tt), K("kmT")], [K("P", pg)], h == 7)
                sch.add(ACT, lambda: nc.scalar.copy(out=gsm[:], in_=Pf[pg][:, 0:64]),
                        reads=[K("P", pg)], writes=[K("gsm")])
                g3 = gsm[:].rearrange("p (h j) -> p h j", h=8)[:, :, 0:qb]
                c4 = cmpb[:, 0:8 * qb * qb].rearrange("p (h j k) -> p h j k", h=8, j=qb)
                sch.add(DVE, lambda: nc.vector.tensor_tensor(
                    out=c4, in0=g3.unsqueeze(2).broadcast_to([128, 8, qb, qb]),
                    in1=g3.unsqueeze(3).broadcast_to([128, 8, qb, qb]), op=ALU.is_gt),
                    reads=[K("gsm")], writes=[K("cmpb")])
                cn3 = cnt[:, 0:8 * qb].rearrange("p (h j) -> p h j", h=8)
                sch.add(DVE, lambda: nc.vector.tensor_reduce(out=cn3, in_=c4, axis=AX.X, op=ALU.add),
                        reads=[K("cmpb")], writes=[K("cnt")])
                sch.add(DVE, lambda: nc.vector.tensor_scalar(
                    out=bias[:, bi, :, 64:64 + qb], in0=cn3, scalar1=2.5, scalar2=-BIG, op0=ALU.is_gt, op1=ALU.mult),
                    reads=[K("cnt")], writes=[K("bias", bi)])
                sch.add(POOL, lambda: nc.gpsimd.memset(bias[:, bi, :, 64 + qb:65 + qb], 0.0), writes=[K("bias", bi)])
            else:
                sch.add(POOL, lambda: nc.gpsimd.memset(bias[:, bi, :, 64:65 + qb], 0.0), writes=[K("bias", bi)])
            if qb < 7:
                sch.add(POOL, lambda: nc.gpsimd.memset(bias[:, bi, :, 65 + qb:72], -BIG), writes=[K("bias", bi)])
            for hh in range(2):
                pb = rP3.nxt()
                for h4 in range(4):
                    h = hh * 4 + h4
                    mm(Pf[pb][0:72, h4 * 128:(h4 + 1) * 128], bias[:, bi, h, :], ident[:], True, True,
                       [K("bias", bi), K("ident")], [K("P", pb)], h4 == 3)
                sch.add(ACT, lambda pb=pb, hh=hh: nc.scalar.copy(
                    out=qTe[64:72, hh * 4:hh * 4 + 4, tt * 128:(tt + 1) * 128],
                    in_=Pf[pb][64:72, :].rearrange("p (h t) -> p h t", h=4)),
                    reads=[K("P", pb)], writes=QTE + [K("qTeb", tt)])

        def m_v(tt):
            if "v" not in wstore:
                wstore["v"] = wblk(gc, B_MV)
            wmv, kmv = wstore["v"]
            tile_i = c * 4 + tt
            pv = rP3.nxt()
            for kc in range(8):
                mm(Pf[pv][:, :], hT[:, kc, tt * 128:(tt + 1) * 128], wmv[:, kc, :], kc == 0, kc == 7,
                   [K("hT", tt), kmv], [K("P", pv)], kc == 7)
            sch.add(ACT, lambda: nc.scalar.copy(
                out=vext[:, tile_i, :, 0:64], in_=Pf[pv][:, :].rearrange("p (h d) -> p h d", h=8)),
                reads=[K("P", pv)], writes=[K("vext", tile_i)])

        items2 = [("k", t) for t in range(4)] + [("q", t) for t in range(4)]
        calls = [lambda: m_stageA("k", 0, 0), lambda: m_stageA("k", 1, 1)]
        for i in range(2, 8):
            calls.append(lambda i=i: m_stageA(items2[i][0], items2[i][1], i))
            calls.append(lambda i=i: m_stageB(items2[i - 2][0], items2[i - 2][1], i - 2))
        calls += [lambda: m_v(0), lambda: m_stageB("q", 2, 6), lambda: m_stageC(0), lambda: m_v(1),
                  lambda: m_stageB("q", 3, 7), lambda: m_stageC(1), lambda: m_v(2), lambda: m_stageC(2),
                  lambda: m_v(3), lambda: m_stageC(3)]

        def pop(n):
            for _ in range(n):
                if calls:
                    calls.pop(0)()

        h_attn(0)
        for tt in range(4):
            h_rec(tt)
            pop(2)
            if tt < 3:
                h_attn(tt + 1)
            pop(1)
            h_post(tt)
            pop(1)
            if tt >= 1:
                h_post_tr(tt - 1)
        h_post_tr(3)
        pop(len(calls))
        if first:
            dump("kTe", kTe[0:72, :, 0:512], [K("kTe", i) for i in range(4)] + [K("kTe_ind")], [72, 8, 512], BF16)
            dump("qTe", qTe[0:72, :, :], QTE, [72, 8, 512], BF16)
        nkt = 4 * c + 4
        OB = [K("bB", 8 + i) for i in range(4)]
        obv = bB[:, 8:12, :].rearrange("p t (h d) -> p t h d", h=8)
        items = [(h, kt) for h in range(8) for kt in range(nkt)]
        pend = None

        def att_qk(h, kt):
            n0 = max(0, kt * 128 - c * CH)
            pst = rP.nxt()
            pi = rpt.nxt()
            mm(Pf[pst][:, n0:512], kTe[0:72, h, kt * 128:(kt + 1) * 128], qTe[0:72, h, n0:512], True, True,
               [K("kTe", kt), K("kTe_ind")] + QTE, [K("P", pst)], True)
            sch.add(ACT, lambda pst=pst, pi=pi, n0=n0: nc.scalar.activation(out=pt[:, pi, n0:512],
                                                                          in_=Pf[pst][:, n0:512], func=AF.Exp),
                    reads=[K("P", pst)], writes=[K("pt", pi)])
            if kt * 128 >= c * CH:
                sch.add(POOL, lambda pi=pi, n0=n0: nc.gpsimd.tensor_tensor(out=pt[:, pi, n0:n0 + 128],
                                                                           in0=pt[:, pi, n0:n0 + 128], in1=tri[:],
                                                                           op=ALU.mult),
                        reads=[K("pt", pi), K("tri")], writes=[K("pt", pi)])
            return (h, kt, n0, pi)

        def att_pv(h, kt, n0, pi):
            pob = 4 + (h % 2)
            for sub in range(n0 // 128, 4):
                last = (kt == nkt - 1 and sub == 3)
                mm(Pf[pob][:, sub * 65:(sub + 1) * 65], pt[:, pi, sub * 128:(sub + 1) * 128], vext[:, kt, h, :],
                   (kt == 0 and sub == 0), last, [K("pt", pi), K("vext", kt), K("vext_one")], [K("P", pob)],
                   sub == 3, skip_group_check=True)
            if kt == nkt - 1:
                rd = h % 2
                po3 = Pf[pob][:, 0:260].rearrange("p (s e) -> p s e", e=65)
                sch.add(DVE, lambda po3=po3, rd=rd: nc.vector.reciprocal(out=rden[:, rd, :].unsqueeze(2),
                                                                         in_=po3[:, :, 64:65]),
                        reads=[K("P", pob)], writes=[K("rden", rd)])
                sch.add(DVE, lambda po3=po3, rd=rd, h=h: nc.vector.tensor_tensor(
                    out=obv[:, :, h, :], in0=po3[:, :, 0:64],
                    in1=rden[:, rd, :].unsqueeze(2).broadcast_to([128, 4, 64]), op=ALU.mult),
                    reads=[K("P", pob), K("rden", rd)], writes=OB)

        for (h, kt) in items:
            cur = att_qk(h, kt)
            if pend is not None:
                att_pv(*pend)
            pend = cur
        att_pv(*pend)
        if first:
            dump("ob", bB[:, 8:12, :], OB, [128, 4, 512], BF16)
        for tt in range(4):
            ti = rT.nxt()
            for kc in range(4):
                tp(Tb[ti][:, kc * 128:(kc + 1) * 128], bB[:, 8 + tt, kc * 128:(kc + 1) * 128], [K("bB", 8 + tt)],
                   [K("T", ti)], kc == 3)
            sch.add(ACT, lambda ti=ti, tt=tt: nc.scalar.copy(out=obT[:, :, tt * 128:(tt + 1) * 128],
                                                             in_=Tb[ti][:, 0:512].rearrange("p (a t) -> p a t", t=128)),
                    reads=[K("T", ti)], writes=[K("obT", tt)])
        OAT = [K("oaT", t) for t in range(4)]
        OBT = [K("obT", t) for t in range(4)]
        MIX = [K("bB", i) for i in range(8)]
        gab_ids = [B_GAB0, B_GAB1, B_GAB2, B_GAB3]
        for qd in range(4):
            wg2, kg2 = wblk(gc, gab_ids[qd], 1)
            wab, kab = wblk(gc, B_WAB0 if qd < 2 else B_WAB1, 1)
            for e in range(2):
                i = 2 * qd + e
                col = (i % 4) * 128
                res = []
                for br in range(2):
                    pgt = rP.nxt()
                    for kc in range(8):
                        mm(Pf[pgt][:, :], wg2[:, kc, br * 256 + e * 128: br * 256 + (e + 1) * 128], hT[:, kc, :],
                           kc == 0, kc == 7, HTK + [kg2], [K("P", pgt)], kc == 7)
                    pab = rP.nxt()
                    src = oaT if br == 0 else obT
                    srk = OAT if br == 0 else OBT
                    for kc in range(4):
                        mm(Pf[pab][:, :], wab[:, br * 4 + kc, col:col + 128], src[:, kc, :], kc == 0, kc == 3,
                           srk + [kab], [K("P", pab)], kc == 3)
                    ss_, ms_ = 0 + br, 2 + br
                    sch.add(ACT, lambda pgt=pgt, ss_=ss_: nc.scalar.activation(out=fS[:, ss_, :], in_=Pf[pgt][:, :],
                                                                             func=AF.Sigmoid),
                            reads=[K("P", pgt)], writes=[K("fS", ss_)])
                    sch.add(DVE, lambda pab=pab, ss_=ss_, ms_=ms_: nc.vector.tensor_tensor(
                        out=fS[:, ms_, :], in0=fS[:, ss_, :], in1=Pf[pab][:, :], op=ALU.mult),
                        reads=[K("fS", ss_), K("P", pab)], writes=[K("fS", ms_)])
                sch.add(POOL, lambda i=i: nc.gpsimd.tensor_tensor(out=bB[:, i, :], in0=fS[:, 2, :], in1=fS[:, 3, :],
                                                                  op=ALU.add),
                        reads=[K("fS", 2), K("fS", 3)], writes=[K("bB", i)])
        if first:
            dump("mixT", bB[:, 0:8, :], MIX, [128, 8, 512], BF16)
        wo0, ko0 = wblk(gc, B_WO0)
        wo1, ko1 = wblk(gc, B_WO1, 1)
        for tt in range(4):
            xi = rxt.nxt()
            xs = 12 + 2 * xi
            xap = fS[:, xs:xs + 2, :].rearrange("p a b -> p (a b)")
            xk = [K("fS", xs), K("fS", xs + 1)]
            r0 = c * CH + tt * 128
            sch.add(XQ[xi], lambda xap=xap, r0=r0: nc.sync.dma_start(out=xap, in_=x[s, r0:r0 + 128, :]), writes=xk)
            for nh, (wo, ko) in enumerate([(wo0, ko0), (wo1, ko1)]):
                pw = rP.nxt()
                for kc in range(8):
                    mm(Pf[pw][:, :], bB[:, kc, tt * 128:(tt + 1) * 128], wo[:, kc, :], kc == 0, kc == 7,
                       MIX + [ko], [K("P", pw)], kc == 7)
                sch.add(DVE, lambda pw=pw, nh=nh, tt=tt, xap=xap: nc.vector.tensor_tensor(
                    out=xmid[:, tt, nh * 512:(nh + 1) * 512], in0=xap[:, nh * 512:(nh + 1) * 512], in1=Pf[pw][:, :],
                    op=ALU.add), reads=xk + [K("P", pw)], writes=[K("xmid", tt, nh)])
        if first:
            dump("xmid", xmid[:], [K("xmid", t, n) for t in range(4) for n in range(2)], [128, 4, D])
        for tt in range(4):
            norm_tile(xmid[:, tt, :], [K("xmid", tt, 0), K("xmid", tt, 1)], g2b, K("g2b"), tt, 4 * tt)
        par = c % 2
        if c == 0:
            sch.add(POOL, lambda: nc.gpsimd.memset(halo[:, 0, :, :], 0.0), writes=[K("halo", 0)])
        GT = [K("bB", i) for i in range(NKF)]
        ngc = gc + 1
        has_next = ngc < NSEQ * NCH
        for u in range(11):
            wu, ku = wblk(gc, B_UP0 + u)
            if has_next and u in (7, 9):
                ptt = 0 if u == 7 else 1
                x_norm_A(ngc // NCH, ngc % NCH, ptt)
            for e in range(2):
                i = 2 * u + e
                rb = i % 2
                ub = 0 + 2 * rb
                ac = 4 + rb
                gl = 6 + rb
                ubuf = fS[:, ub:ub + 2, :].rearrange("p a b -> p (a b)")
                UBK = [K("fS", ub), K("fS", ub + 1)]
                pu = rP.nxt()
                for kc in range(8):
                    mm(Pf[pu][:, :], wu[:, kc, e * 128:(e + 1) * 128], hT[:, kc, :], kc == 0, kc == 7, HTK + [ku],
                       [K("P", pu)], kc == 7)
                pv2 = rP.nxt()
                for kc in range(8):
                    mm(Pf[pv2][:, :], wu[:, kc, 256 + e * 128:256 + (e + 1) * 128], hT[:, kc, :], kc == 0, kc == 7,
                       HTK + [ku], [K("P", pv2)], kc == 7)
                sch.add(ACT, lambda pu=pu, ubuf=ubuf: nc.scalar.copy(out=ubuf[:, 2:514], in_=Pf[pu][:, :]),
                        reads=[K("P", pu)], writes=UBK)
                sch.add(POOL, lambda ubuf=ubuf, i=i: nc.gpsimd.tensor_copy(out=ubuf[:, 0:2], in_=halo[:, par, i, :]),
                        reads=[K("halo", par)], writes=UBK)
                sch.add(POOL, lambda ubuf=ubuf, i=i: nc.gpsimd.tensor_copy(out=halo[:, 1 - par, i, :],
                                                                           in_=ubuf[:, 512:514]),
                        reads=UBK, writes=[K("halo", 1 - par)])
                sch.add(DVE, lambda ubuf=ubuf, i=i, ac=ac: nc.vector.tensor_scalar(
                    out=fS[:, ac, :], in0=ubuf[:, 2:514], scalar1=cw[:, 2, i:i + 1], scalar2=cb[:, i:i + 1],
                    op0=ALU.mult, op1=ALU.add), reads=UBK + [K("cw"), K("cb")], writes=[K("fS", ac)])
                sch.add(DVE, lambda ubuf=ubuf, i=i, ac=ac: nc.vector.scalar_tensor_tensor(
                    out=fS[:, ac, :], in0=ubuf[:, 1:513], scalar=cw[:, 1, i:i + 1], in1=fS[:, ac, :], op0=ALU.mult,
                    op1=ALU.add), reads=UBK + [K("cw"), K("fS", ac)], writes=[K("fS", ac)])
                sch.add(DVE, lambda ubuf=ubuf, i=i, ac=ac: nc.vector.scalar_tensor_tensor(
                    out=fS[:, ac, :], in0=ubuf[:, 0:512], scalar=cw[:, 0, i:i + 1], in1=fS[:, ac, :], op0=ALU.mult,
                    op1=ALU.add), reads=UBK + [K("cw"), K("fS", ac)], writes=[K("fS", ac)])
                sch.add(ACT, lambda ac=ac, gl=gl: nc.scalar.activation(out=fS[:, gl, :], in_=fS[:, ac, :], func=AF.Gelu),
                        reads=[K("fS", ac)], writes=[K("fS", gl)])
                sch.add(DVE, lambda gl=gl, pv2=pv2, i=i: nc.vector.tensor_tensor(out=bB[:, i, :], in0=fS[:, gl, :],
                                                                                 in1=Pf[pv2][:, :], op=ALU.mult),
                        reads=[K("fS", gl), K("P", pv2)], writes=[K("bB", i)])
        if first:
            dump("gT", bB[:, 0:NKF, :], GT, [128, NKF, 512], BF16)
        if has_next:
            for ptt in (0, 1):
                norm_B(ptt)
                prefetched.add((ngc, ptt))
        for nh in range(2):
            banks = [0, 1, 2, 3] if nh == 0 else [4, 5, 0, 1]
            for pi_, (k0, nk) in enumerate(DN_P):
                wd, kd = wblk(gc, B_DN0 + nh * 3 + pi_)
                for tt in range(4):
                    for kk in range(nk):
                        kc = k0 + kk
                        mm(Pf[banks[tt]][:, :], bB[:, kc, tt * 128:(tt + 1) * 128], wd[:, kk, :], kc == 0,
                           kc == NKF - 1, [K("bB", kc), kd], [K("P", banks[tt])], kk == nk - 1)
            for tt in range(4):
                sch.add(DVE, lambda nh=nh, tt=tt, b=banks[tt]: nc.vector.tensor_tensor(
                    out=xmid[:, tt, nh * 512:(nh + 1) * 512], in0=xmid[:, tt, nh * 512:(nh + 1) * 512],
                    in1=Pf[b][:, :], op=ALU.add),
                    reads=[K("xmid", tt, nh), K("P", banks[tt])], writes=[K("xmid", tt, nh)])
        sch.add(OQ, lambda: nc.gpsimd.dma_start(
            out=out[s, c * CH:(c + 1) * CH, :].rearrange("(t p) d -> p t d", p=128), in_=xmid[:]),
            reads=[K("xmid", t, n) for t in range(4) for n in range(2)])

    for s in range(NSEQ):
        for c in range(NCH):
            chunk(s, c)
    stats = sch.finalize(nc.sync)
    es.close()
    return nc, dumps, stats


_CACHE = {}


def kernel(**inputs):
    nseq = 32 // NCORES
    if "nc" not in _CACHE:
        _CACHE["nc"] = build(nseq)[0]
    nc = _CACHE["nc"]
    consts = host_consts()
    x = np.ascontiguousarray(np.asarray(inputs["x"], dtype=np.float32))
    shared = {
        "norm1_g": np.asarray(inputs["norm1_g"], np.float32).reshape(1, D),
        "norm2_g": np.asarray(inputs["norm2_g"], np.float32).reshape(1, D),
        "w_in": np.asarray(inputs["w_in"], np.float32).reshape(D, 5632),
        "hg_lb_logits": np.asarray(inputs["hg_lb_logits"], np.float32).reshape(2, 512),
        "hg_onorm_g": np.asarray(inputs["hg_onorm_g"], np.float32).reshape(1, 512),
        "q_norm_g": np.asarray(inputs["q_norm_g"], np.float32).reshape(1, 64),
        "k_norm_g": np.asarray(inputs["k_norm_g"], np.float32).reshape(1, 64),
        "w_a": np.asarray(inputs["w_a"], np.float32).reshape(512, D),
        "w_b": np.asarray(inputs["w_b"], np.float32).reshape(512, D),
        "w_out": np.asarray(inputs["w_out"], np.float32).reshape(D, D),
        "w_up": np.asarray(inputs["w_up"], np.float32).reshape(D, 2 * DFF),
        "conv_w": np.asarray(inputs["conv_w"], np.float32).reshape(3, DFF),
        "conv_b": np.asarray(inputs["conv_b"], np.float32).reshape(1, DFF),
        "w_down": np.asarray(inputs["w_down"], np.float32).reshape(DFF, D),
    }
    shared.update(consts)
    in_maps = []
    for i in range(NCORES):
        m = dict(shared)
        m["x"] = x[i * nseq:(i + 1) * nseq]
        in_maps.append(m)
    res = run_bass_kernel_spmd(nc, in_maps, core_ids=list(range(NCORES)))
    return np.concatenate([np.asarray(r["out"]) for r in res.results], axis=0).astype(np.float32)
```

```python
from contextlib import ExitStack
import numpy as np
import concourse.bass as bass
import concourse.mybir as mybir
from concourse.bass_utils import run_bass_kernel_spmd

F32 = mybir.dt.float32
BF16 = mybir.dt.bfloat16
AF = mybir.ActivationFunctionType
ALU = mybir.AluOpType
AX = mybir.AxisListType

NCORES = 8
S = 2048
D = 1024
CH = 512
NCH = S // CH
DFF = 2816
NKF = DFF // 128
EPS = 1e-6
BIG = 30000.0
NBLK = 32
RING = 3

(B_HQ, B_HF, B_HI, B_HG, B_MK, B_MQ, B_MV, B_GAB0, B_WAB0, B_GAB1, B_GAB2, B_WAB1, B_GAB3,
 B_WO0, B_WO1) = range(15)
B_UP0 = 15
B_DN0 = 26


class Q:
    def __init__(self, name, issuer, sem, inc, kind):
        self.name, self.issuer, self.sem, self.inc, self.kind = name, issuer, sem, inc, kind
        self.nsig = 0
        self.last = None


class Sched:
    def __init__(self, nc):
        self.nc = nc
        self.ins = []
        self.queues = []

    def queue(self, name, issuer, sem, inc, kind):
        q = Q(name, issuer, sem, inc, kind)
        self.queues.append(q)
        return q

    def add(self, q, fn, reads=(), writes=(), sig=True):
        self.ins.append((q, fn, tuple(reads), tuple(writes), sig or q.kind == 'dma'))

    def finalize(self, final_issuer):
        ins = self.ins
        n = len(ins)
        sigval = [0] * n
        nextsig = [None] * n
        for i, (q, fn, r, w, sig) in enumerate(ins):
            if sig:
                q.nsig += 1
                sigval[i] = q.nsig
        lastsig = {}
        for i in range(n - 1, -1, -1):
            q = ins[i][0]
            if ins[i][4]:
                lastsig[q] = i
            nextsig[i] = lastsig.get(q)
        writers, readers = {}, {}
        clocks = {}
        iclk = [None] * n
        nwaits = 0
        for i, (q, fn, rds, wrs, sig) in enumerate(ins):
            deps = set()
            for k in rds:
                for qq, j in writers.get(k, {}).items():
                    if qq is q and q.kind == 'pe':
                        continue
                    deps.add(j)
            for k in wrs:
                for qq, j in writers.get(k, {}).items():
                    if qq is q and q.kind != 'dma':
                        continue
                    deps.add(j)
                for qq, j in readers.get(k, {}).items():
                    if qq is q and q.kind != 'dma':
                        continue
                    deps.add(j)
            if q.kind == 'dma' and q.last is not None:
                deps.add(q.last)
            clk = clocks.setdefault(id(q.issuer), {})
            for j in sorted(deps):
                js = nextsig[j]
                assert js is not None and js < i, f"dep signal after waiter: ins {i} dep {j} sig {js}"
                qj = ins[js][0]
                val = sigval[js] * qj.inc
                if clk.get(qj, 0) >= val:
                    continue
                q.issuer.wait_ge(qj.sem, val)
                nwaits += 1
                for qq, v in iclk[js].items():
                    if clk.get(qq, 0) < v:
                        clk[qq] = v
            r = fn()
            if sig:
                r.then_inc(q.sem, q.inc)
                c2 = dict(clk)
                c2[q] = sigval[i] * q.inc
                iclk[i] = c2
            for k in rds:
                readers.setdefault(k, {})[q] = i
            for k in wrs:
                writers.setdefault(k, {})[q] = i
            if q.kind == 'dma':
                q.last = i
        for q in self.queues:
            if q.nsig:
                final_issuer.wait_ge(q.sem, q.nsig * q.inc)
        return n, nwaits


def host_consts():
    c = {}
    c["c_ident"] = np.eye(128, dtype=np.float32)
    s = np.arange(128)[:, None]
    t = np.arange(128)[None, :]
    same = (s // 64) == (t // 64)
    c["c_U"] = ((s > t) & same).astype(np.float32)
    c["c_bd"] = ((s <= t) & same).astype(np.float32)
    c["c_tri"] = (s <= t).astype(np.float32)
    c["c_cind"] = np.stack([(np.arange(128) < 64), (np.arange(128) >= 64)], 1).astype(np.float32)
    half = 32
    inv = 1.0 / (10000.0 ** (np.arange(half, dtype=np.float32) * 2.0 / 64))
    pos = np.arange(S, dtype=np.float32)
    ang = pos[:, None] * inv[None, :]
    cs = np.stack([np.cos(ang), np.sin(ang)], 1).astype(np.float32)
    c["c_cs"] = np.ascontiguousarray(cs.reshape(16, 128, 2, 32).transpose(1, 0, 2, 3))
    kind = (np.arange(S)[None, :] // 256 == np.arange(8)[:, None]).astype(np.float32)
    c["c_kind"] = kind
    return c


def build(NSEQ, dump_names=()):
    nc = bass.Bass("TRN2", target_bir_lowering=False)
    es = ExitStack()

    def din(name, shape, dt=F32):
        return nc.dram_tensor(name, list(shape), dt, kind="ExternalInput").ap()

    x = din("x", [NSEQ, S, D])
    norm1_g = din("norm1_g", [1, D]); norm2_g = din("norm2_g", [1, D])
    w_in = din("w_in", [D, 5632]); hg_lb = din("hg_lb_logits", [2, 512])
    hg_on = din("hg_onorm_g", [1, 512]); qng = din("q_norm_g", [1, 64]); kng = din("k_norm_g", [1, 64])
    w_a = din("w_a", [512, D]); w_b = din("w_b", [512, D]); w_out = din("w_out", [D, D])
    w_up = din("w_up", [D, 2 * DFF]); conv_w = din("conv_w", [3, DFF]); conv_b = din("conv_b", [1, DFF])
    w_down = din("w_down", [DFF, D])
    c_ident = din("c_ident", [128, 128]); c_U = din("c_U", [128, 128]); c_bd = din("c_bd", [128, 128])
    c_tri = din("c_tri", [128, 128]); c_cind = din("c_cind", [128, 2]); c_cs = din("c_cs", [128, 16, 2, 32])
    c_kind = din("c_kind", [8, S])
    out = nc.dram_tensor("out", [NSEQ, S, D], F32, kind="ExternalOutput").ap()
    wbf = nc.dram_tensor("wbf", [NBLK, 128, 4096], BF16).ap()
    dumps = {}

    def sb(name, shape, dt):
        return es.enter_context(nc.sbuf_tensor(name, list(shape), dt))

    def ps(name, shape, dt):
        return es.enter_context(nc.psum_tensor(name, list(shape), dt))

    def sem(name):
        return es.enter_context(nc.semaphore(name))

    sch = Sched(nc)
    PE = sch.queue("pe", nc.tensor, sem("s_pe"), 1, 'pe')
    ACT = sch.queue("act", nc.scalar, sem("s_act"), 1, 'cmp')
    DVE = sch.queue("dve", nc.vector, sem("s_dve"), 1, 'cmp')
    POOL = sch.queue("pool", nc.gpsimd, sem("s_pool"), 1, 'cmp')
    WQ = [sch.queue(f"wq{i}", nc.sync, sem(f"s_wq{i}"), 16, 'dma') for i in range(RING)]
    XQ = [sch.queue(f"xq{i}", nc.sync, sem(f"s_xq{i}"), 16, 'dma') for i in range(2)]
    OQ = sch.queue("oq", nc.gpsimd, sem("s_oq"), 16, 'dma')
    CQ = [sch.queue(f"cq{i}", nc.gpsimd, sem(f"s_cq{i}"), 16, 'dma') for i in range(4)]
    KQ = [sch.queue(f"kq{i}", nc.sync, sem(f"s_kq{i}"), 16, 'dma') for i in range(4)]
    DQ = sch.queue("dq", nc.sync, sem("s_dq"), 16, 'dma')

    wring = sb("wring", [128, RING, 4096], BF16)
    kTe = sb("kTe", [128, 8, S], BF16)
    vext = sb("vext", [128, 16, 8, 65], BF16)
    xmid = sb("xmid", [128, 4, D], F32)
    hT = sb("hT", [128, 8, CH], BF16)
    g1b = sb("g1b", [128, D], F32); g2b = sb("g2b", [128, D], F32)
    fS = sb("fS", [128, 18, 512], F32)
    bB = sb("bB", [128, 24, 512], BF16)
    Sst = sb("Sst", [128, 512], F32)
    qtil = sb("qtil", [128, 2, 512], BF16)
    attn = sb("attn", [128, 2, 512], BF16)
    sdb = sb("sdb", [128, 2, 512], BF16)
    oabf = sb("oabf", [128, 2, 512], BF16)
    oaT = sb("oaT", [128, 4, CH], BF16); obT = sb("obT", [128, 4, CH], BF16)
    ropeo = sb("ropeo", [128, 2, 512], BF16)
    bias = sb("bias", [128, 2, 8, 72], BF16)
    pt = sb("pt", [128, 3, 512], BF16)
    ident = sb("ident", [128, 128], BF16)
    Um = sb("Um", [128, 128], F32); bd = sb("bd", [128, 128], F32); tri = sb("tri", [128, 128], BF16)
    cind = sb("cind", [128, 2], F32)
    omlb = sb("omlb", [128, 512], F32); gob = sb("gob", [128, 512], F32)
    cs = sb("cs", [128, 16, 2, 32], F32)
    gq8 = sb("gq8", [128, 64], F32); gkb = sb("gkb", [128, 64], F32)
    cw = sb("cw", [128, 3, NKF], F32); cb = sb("cb", [128, NKF], F32)
    halo = sb("halo", [128, 2, NKF, 2], F32)
    elast = sb("elast", [128, 4, 4, 2], F32)
    st = sb("st", [128, 128], F32)
    kmT = sb("kmT", [128, 8, 8], BF16)
    kmf = sb("kmf", [128, 8], F32)
    epsb = sb("epsb", [128, 1], F32)
    gsm = sb("gsm", [128, 64], F32)
    cmpb = sb("cmpb", [128, 8 * 7 * 7], F32)
    cnt = sb("cnt", [128, 56], F32)
    rden = sb("rden", [128, 2, 4], F32)
    Pf = [ps(f"P{i}", [128, 512], F32) for i in range(6)]
    Tb = [ps(f"T{i}", [128, 1024], BF16) for i in range(2)]

    K = lambda *a: tuple(a)

    def fslot(i, n=1):
        return fS[:, i, :] if n == 1 else fS[:, i:i + n, :]

    class Rot:
        def __init__(self, n):
            self.n, self.i = n, 0

        def nxt(self):
            v = self.i % self.n
            self.i += 1
            return v

    rP = Rot(4)
    rP3 = Rot(3)
    rT = Rot(2)
    rxt = Rot(2)
    rpt = Rot(3)

    def dump(name, ap, keys, shape, dt=F32):
        if name not in dump_names or name in dumps:
            return
        t = nc.dram_tensor("dbg_" + name, list(shape), dt, kind="ExternalOutput").ap()
        dumps[name] = t
        sch.add(DQ, lambda: nc.sync.dma_start(out=t, in_=ap), reads=keys)

    cqi = [0]

    def cast(dst, src, b):
        q = CQ[cqi[0] % 4]
        cqi[0] += 1
        sch.add(q, lambda: nc.gpsimd.dma_start(out=dst, in_=src), writes=[K("wbf", b)])

    kqi = [0]

    def kload(dst, src, key, eng=None):
        q = KQ[kqi[0] % 4]
        kqi[0] += 1
        sch.add(q, lambda: nc.sync.dma_start(out=dst, in_=src), writes=[key])

    def kcast(dst, src, key):
        q = CQ[cqi[0] % 4]
        cqi[0] += 1
        sch.add(q, lambda: nc.gpsimd.dma_start(out=dst, in_=src), writes=[key])

    kcast(ident[:], c_ident, K("ident"))
    kcast(tri[:], c_tri, K("tri"))
    kload(Um[:], c_U, K("Um")); kload(bd[:], c_bd, K("bd")); kload(cind[:], c_cind, K("cind"))
    kload(cs[:], c_cs, K("cs"))
    kload(g1b[:], norm1_g.partition_broadcast(128), K("g1b"))
    kload(g2b[:], norm2_g.partition_broadcast(128), K("g2b"))
    kload(gob[:], hg_on.partition_broadcast(128), K("gob"))
    kload(gq8[:], qng.partition_broadcast(128), K("gq8"))
    kload(gkb[:], kng.partition_broadcast(128), K("gkb"))
    kload(fS[:, 0, :], hg_lb[0:1, :].partition_broadcast(128), K("fS", 0))
    kload(fS[:, 1, :], hg_lb[1:2, :].partition_broadcast(128), K("fS", 1))
    for j in range(3):
        sch.add(KQ[j % 4], (lambda j=j: nc.sync.dma_start(
            out=cw[:, j, :], in_=conv_w[j:j + 1, :].rearrange("o (kc p) -> p (o kc)", p=128),
            allow_slow_non_contiguous=True)), writes=[K("cw")])
    sch.add(KQ[3], lambda: nc.sync.dma_start(
        out=cb[:], in_=conv_b.rearrange("o (kc p) -> p (o kc)", p=128), allow_slow_non_contiguous=True),
        writes=[K("cb")])
    for h in range(8):
        kcast(kTe[64:72, h, :], c_kind, K("kTe_ind"))
    sch.add(DVE, lambda: nc.vector.tensor_tensor(out=fS[:, 0, :], in0=fS[:, 0, :], in1=fS[:, 1, :], op=ALU.subtract),
            reads=[K("fS", 0), K("fS", 1)], writes=[K("fS", 0)])
    sch.add(ACT, lambda: nc.scalar.activation(out=omlb[:], in_=fS[:, 0, :], func=AF.Sigmoid, scale=-1.0),
            reads=[K("fS", 0)], writes=[K("omlb")])
    sch.add(ACT, lambda: nc.scalar.mul(out=gq8[:], in_=gq8[:], mul=0.125), reads=[K("gq8")], writes=[K("gq8")])
    sch.add(POOL, lambda: nc.gpsimd.memset(vext[:, :, :, 64:65], 1.0), writes=[K("vext_one")])
    sch.add(POOL, lambda: nc.gpsimd.memset(epsb[:], EPS), writes=[K("epsb")])
    sch.add(POOL, lambda: nc.gpsimd.memset(bias[:, :, :, 0:64], 0.0), writes=[K("bias", 0), K("bias", 1)])

    def blkv(b, kc0, nkc, n0, nn):
        v = wbf[b].rearrange("p (kc n) -> p kc n", n=512)
        return v[:, kc0:kc0 + nkc, n0:n0 + nn]

    def rows(w, r0, nkc, c0, ncol):
        return w[r0:r0 + nkc * 128, c0:c0 + ncol].rearrange("(kc p) n -> p kc n", p=128)

    for g, b in enumerate([B_HQ, B_HF, B_HI, B_HG, B_MQ, B_MK, B_MV]):
        cast(blkv(b, 0, 8, 0, 512), rows(w_in, 0, 8, g * 512, 512), b)
    for qd, b in enumerate([B_GAB0, B_GAB1, B_GAB2, B_GAB3]):
        cast(blkv(b, 0, 8, 0, 256), rows(w_in, 0, 8, 3584 + qd * 256, 256), b)
        cast(blkv(b, 0, 8, 256, 256), rows(w_in, 0, 8, 4608 + qd * 256, 256), b)
    for hf, b in enumerate([B_WAB0, B_WAB1]):
        cast(blkv(b, 0, 4, 0, 512), rows(w_a, 0, 4, hf * 512, 512), b)
        cast(blkv(b, 4, 4, 0, 512), rows(w_b, 0, 4, hf * 512, 512), b)
    for hf, b in enumerate([B_WO0, B_WO1]):
        cast(blkv(b, 0, 8, 0, 512), rows(w_out, 0, 8, hf * 512, 512), b)
    for u in range(11):
        cast(blkv(B_UP0 + u, 0, 8, 0, 256), rows(w_up, 0, 8, u * 256, 256), B_UP0 + u)
        cast(blkv(B_UP0 + u, 0, 8, 256, 256), rows(w_up, 0, 8, DFF + u * 256, 256), B_UP0 + u)
    DN_P = [(0, 8), (8, 8), (16, 6)]
    for nh in range(2):
        for pi, (k0, nk) in enumerate(DN_P):
            cast(blkv(B_DN0 + nh * 3 + pi, 0, nk, 0, 512), rows(w_down, k0 * 128, nk, nh * 512, 512), B_DN0 + nh * 3 + pi)

    wstate = {"next": 0}
    total_blocks = NSEQ * NCH * NBLK

    def wload_upto(gb):
        while wstate["next"] <= gb and wstate["next"] < total_blocks:
            g = wstate["next"]
            slot = g % RING
            b = g % NBLK
            sch.add(WQ[slot], (lambda slot=slot, b=b: nc.sync.dma_start(out=wring[:, slot, :], in_=wbf[b])),
                    reads=[K("wbf", b)], writes=[K("w", slot)])
            wstate["next"] += 1

    def wblk(gc, b, ahead=2):
        gb = gc * NBLK + b
        wload_upto(gb + ahead)
        slot = gb % RING
        return wring[:, slot, :].rearrange("p (kc n) -> p kc n", n=512), K("w", slot)

    def mm(out_ap, lhsT, rhs, start, stop, reads, writes, sig, **kw):
        sch.add(PE, lambda: nc.tensor.matmul(out_ap, lhsT=lhsT, rhs=rhs, start=start, stop=stop, **kw),
                reads=reads, writes=writes, sig=sig)

    def tp(out_ap, in_ap, reads, writes, sig):
        sch.add(PE, lambda: nc.tensor.transpose(out_ap, in_ap, ident[:]), reads=list(reads) + [K("ident")],
                writes=writes, sig=sig)

    hbuf = sb("hbuf", [128, 2, D], BF16)

    def norm_A(src_ap, src_keys, gb_t, gkey, tt, stc):
        junk = bB[:, 22:24, :].rearrange("p a b -> p (a b)")
        sch.add(ACT, lambda: nc.scalar.activation(out=junk, in_=src_ap, func=AF.Square, scale=1.0 / 32.0,
                                                  accum_out=st[:, stc:stc + 1]),
                reads=src_keys, writes=[K("bB", 22), K("bB", 23), K("st", stc)])
        sch.add(ACT, lambda: nc.scalar.activation(out=st[:, stc + 2:stc + 3], in_=st[:, stc:stc + 1], func=AF.Ln,
                                                  bias=epsb[:, 0:1]),
                reads=[K("st", stc), K("epsb")], writes=[K("st", stc + 2)])
        sch.add(ACT, lambda: nc.scalar.activation(out=st[:, stc + 3:stc + 4], in_=st[:, stc + 2:stc + 3], func=AF.Exp,
                                                  scale=-0.5),
                reads=[K("st", stc + 2)], writes=[K("st", stc + 3)])
        hi_ = tt % 2
        sch.add(DVE, lambda: nc.vector.scalar_tensor_tensor(out=hbuf[:, hi_, :], in0=src_ap,
                                                            scalar=st[:, stc + 3:stc + 4],
                                                            in1=gb_t[:], op0=ALU.mult, op1=ALU.mult),
                reads=list(src_keys) + [K("st", stc + 3), gkey], writes=[K("hbuf", hi_)])

    def norm_B(tt):
        hi_ = tt % 2
        ti = rT.nxt()
        for kc in range(8):
            tp(Tb[ti][:, kc * 128:(kc + 1) * 128], hbuf[:, hi_, kc * 128:(kc + 1) * 128], [K("hbuf", hi_)],
               [K("T", ti)], kc == 7)
        sch.add(DVE, lambda: nc.vector.tensor_copy(out=hT[:, :, tt * 128:(tt + 1) * 128],
                                                   in_=Tb[ti][:, :].rearrange("p (kc t) -> p kc t", t=128)),
                reads=[K("T", ti)], writes=[K("hT", tt)])

    def norm_tile(src_ap, src_keys, gb_t, gkey, tt, stc):
        norm_A(src_ap, src_keys, gb_t, gkey, tt, stc)
        norm_B(tt)

    def x_norm_A(s_, c_, tt):
        xi = rxt.nxt()
        xs = 12 + 2 * xi
        xap = fS[:, xs:xs + 2, :].rearrange("p a b -> p (a b)")
        xk = [K("fS", xs), K("fS", xs + 1)]
        r0 = c_ * CH + tt * 128
        sch.add(XQ[xi], lambda: nc.sync.dma_start(out=xap, in_=x[s_, r0:r0 + 128, :]), writes=xk)
        norm_A(xap, xk, g1b, K("g1b"), tt, 4 * tt)

    prefetched = set()

    HTK = [K("hT", t) for t in range(4)]

    def chunk(s, c):
        gc = s * NCH + c
        first = (gc == 0)
        for tt in range(4):
            if (gc, tt) in prefetched:
                continue
            x_norm_A(s, c, tt)
            norm_B(tt)
        if first:
            dump("hT", hT[:], HTK, [128, 8, CH], BF16)
        wq_, kq_ = wblk(gc, B_HQ)
        wf_, kf_ = wblk(gc, B_HF, 1)
        if c == 0:
            sch.add(DVE, lambda: nc.vector.memset(Sst[:], 0.0), writes=[K("Sst")])
        KQT = [K("bB", i) for i in range(8)]

        def h_stageA(tt):
            r = tt % 2
            qs, ks, ls = 0 + r, 2 + r, 4 + r
            pq = rP3.nxt()
            for kc in range(8):
                mm(Pf[pq][:, :], hT[:, kc, tt * 128:(tt + 1) * 128], wq_[:, kc, :], kc == 0, kc == 7,
                   [K("hT", tt), kq_], [K("P", pq)], kc == 7)
            sch.add(ACT, lambda: nc.scalar.activation(out=fS[:, qs, :], in_=Pf[pq][:, :], func=AF.Silu),
                    reads=[K("P", pq)], writes=[K("fS", qs)])
            pf = rP3.nxt()
            for kc in range(8):
                mm(Pf[pf][:, :], hT[:, kc, tt * 128:(tt + 1) * 128], wf_[:, kc, :], kc == 0, kc == 7,
                   [K("hT", tt), kf_], [K("P", pf)], kc == 7)
            sch.add(ACT, lambda: nc.scalar.activation(out=fS[:, ks, :], in_=Pf[pf][:, :], func=AF.Sigmoid, scale=-1.0),
                    reads=[K("P", pf)], writes=[K("fS", ks)])
            sch.add(DVE, lambda: nc.vector.tensor_tensor(out=fS[:, ks, :], in0=fS[:, ks, :], in1=omlb[:], op=ALU.mult),
                    reads=[K("fS", ks), K("omlb")], writes=[K("fS", ks)])
            sch.add(ACT, lambda: nc.scalar.activation(out=fS[:, ls, :], in_=fS[:, ks, :], func=AF.Ln, scale=-1.0,
                                                      bias=1.0),
                    reads=[K("fS", ks)], writes=[K("fS", ls)])

        def h_stageB(tt):
            r = tt % 2
            qs, ks, ls, es_, ns = 0 + r, 2 + r, 4 + r, 6 + r, 8 + r
            pa = rP3.nxt()
            mm(Pf[pa][:, :], Um[:], fS[:, ls, :], True, True, [K("Um"), K("fS", ls)], [K("P", pa)], True)
            sch.add(ACT, lambda: nc.scalar.activation(out=fS[:, es_, :], in_=Pf[pa][:, :], func=AF.Exp),
                    reads=[K("P", pa)], writes=[K("fS", es_)])
            sch.add(ACT, lambda: nc.scalar.activation(out=fS[:, ns, :], in_=Pf[pa][:, :], func=AF.Exp, scale=-1.0),
                    reads=[K("P", pa)], writes=[K("fS", ns)])
            pl = rP3.nxt()
            for h in range(4):
                mm(Pf[pl][:, h * 2:h * 2 + 2], fS[:, ls, h * 128:(h + 1) * 128], cind[:], True, True,
                   [K("fS", ls), K("cind")], [K("P", pl)], h == 3)
            sch.add(ACT, lambda: nc.scalar.activation(
                out=elast[:, tt, :, :].rearrange("p h j -> p (h j)"), in_=Pf[pl][:, 0:8], func=AF.Exp),
                reads=[K("P", pl)], writes=[K("elast", tt)])
            sch.add(DVE, lambda: nc.vector.tensor_tensor(out=bB[:, 12 + tt, :], in0=fS[:, ks, :], in1=fS[:, es_, :],
                                                         op=ALU.mult),
                    reads=[K("fS", ks), K("fS", es_)], writes=[K("bB", 12 + tt)])
            sch.add(DVE, lambda: nc.vector.tensor_tensor(out=qtil[:, r, :], in0=fS[:, qs, :], in1=fS[:, ns, :],
                                                         op=ALU.mult),
                    reads=[K("fS", qs), K("fS", ns)], writes=[K("qtil", r)])
            if first and tt == 0:
                dump("khat0", bB[:, 12, :], [K("bB", 12)], [128, 512], BF16)
                dump("qtil0", qtil[:, 0, :], [K("qtil", 0)], [128, 512], BF16)

        def h_stageB2(tt):
            r = tt % 2
            ti = rT.nxt()
            for h in range(4):
                tp(Tb[ti][:, h * 128:(h + 1) * 128], bB[:, 12 + tt, h * 128:(h + 1) * 128], [K("bB", 12 + tt)],
                   [K("T", ti)], False)
            for h in range(4):
                tp(Tb[ti][:, (4 + h) * 128:(5 + h) * 128], qtil[:, r, h * 128:(h + 1) * 128], [K("qtil", r)],
                   [K("T", ti)], h == 3)
            sch.add(ACT, lambda: nc.scalar.copy(out=bB[:, 0:8, tt * 128:(tt + 1) * 128],
                                                in_=Tb[ti][:, :].rearrange("p (a t) -> p a t", t=128)),
                    reads=[K("T", ti)], writes=[K("kqT", tt)] + KQT)

        hi_state = {}

        def h_hi(tt):
            if "w" not in hi_state:
                hi_state["w"] = wblk(gc, B_HI)
            wi_, ki_ = hi_state["w"]
            pv = rP3.nxt()
            for kc in range(8):
                mm(Pf[pv][:, :], hT[:, kc, tt * 128:(tt + 1) * 128], wi_[:, kc, :], kc == 0, kc == 7,
                   [K("hT", tt), ki_], [K("P", pv)], kc == 7)
            sch.add(ACT, lambda: nc.scalar.copy(out=bB[:, 8 + tt, :], in_=Pf[pv][:, :]),
                    reads=[K("P", pv)], writes=[K("bB", 8 + tt)])

        h_stageA(0); h_stageA(1); h_stageB(0); h_stageA(2); h_stageB(1); h_stageB2(0); h_stageA(3); h_stageB(2)
        h_stageB2(1); h_hi(0); h_stageB(3); h_hi(1); h_stageB2(2); h_hi(2); h_hi(3); h_stageB2(3)
        wg_, kg_ = wblk(gc, B_HG)
        for tt in range(4):
            ph = rP3.nxt()
            for kc in range(8):
                mm(Pf[ph][:, :], hT[:, kc, tt * 128:(tt + 1) * 128], wg_[:, kc, :], kc == 0, kc == 7,
                   [K("hT", tt), kg_], [K("P", ph)], kc == 7)
            sch.add(ACT, lambda ph=ph, tt=tt: nc.scalar.activation(out=fS[:, tt, :], in_=Pf[ph][:, :], func=AF.Silu),
                    reads=[K("P", ph)], writes=[K("fS", tt)])
            sch.add(POOL, lambda tt=tt: nc.gpsimd.tensor_tensor(out=fS[:, tt, :], in0=fS[:, tt, :], in1=gob[:],
                                                                op=ALU.mult),
                    reads=[K("fS", tt), K("gob")], writes=[K("fS", tt)])

        def h_post_tr(tt):
            r = tt % 2
            ti = rT.nxt()
            for kc in range(4):
                tp(Tb[ti][:, kc * 128:(kc + 1) * 128], oabf[:, r, kc * 128:(kc + 1) * 128], [K("oabf", r)],
                   [K("T", ti)], kc == 3)
            sch.add(ACT, lambda: nc.scalar.copy(out=oaT[:, :, tt * 128:(tt + 1) * 128],
                                                in_=Tb[ti][:, 0:512].rearrange("p (a t) -> p a t", t=128)),
                    reads=[K("T", ti)], writes=[K("oaT", tt)])

        def h_attn(tt):
            r = tt % 2
            po = 4 + r
            pat = rP3.nxt()
            for h in range(4):
                mm(Pf[pat][:, h * 128:(h + 1) * 128], bB[:, h, tt * 128:(tt + 1) * 128],
                   bB[:, 4 + h, tt * 128:(tt + 1) * 128], True, True, KQT, [K("P", pat)], h == 3)
            sch.add(DVE, lambda: nc.vector.scalar_tensor_tensor(
                out=attn[:, r, :].rearrange("p (h t) -> p h t", h=4),
                in0=Pf[pat][:, :].rearrange("p (h t) -> p h t", h=4), scalar=1e30,
                in1=bd[:].unsqueeze(1).broadcast_to([128, 4, 128]), op0=ALU.min, op1=ALU.mult),
                reads=[K("P", pat), K("bd")], writes=[K("attn", r)])
            for h in range(4):
                mm(Pf[po][:, h * 128:(h + 1) * 128], attn[:, r, h * 128:(h + 1) * 128],
                   bB[:, 8 + tt, h * 128:(h + 1) * 128], h == 0, False, [K("attn", r), K("bB", 8 + tt)],
                   [K("P", po)], False, skip_group_check=True)

        def h_rec(tt):
            r = tt % 2
            po = 4 + r
            for j in range(2):
                n = 2 * tt + j
                sd = n % 2
                sch.add(DVE, lambda j=j: nc.vector.tensor_tensor(
                    out=fS[:, 16, :].rearrange("p (h v) -> p h v", h=4),
                    in0=Sst[:].rearrange("p (h v) -> p h v", h=4),
                    in1=elast[:, tt, :, j:j + 1].broadcast_to([128, 4, 128]), op=ALU.mult),
                    reads=[K("Sst"), K("elast", tt)], writes=[K("fS", 16)])
                sch.add(ACT, lambda sd=sd: nc.scalar.copy(out=sdb[:, sd, :], in_=fS[:, 16, :]),
                        reads=[K("fS", 16)], writes=[K("sdb", sd)])
                for h in range(4):
                    mm(Pf[3][:, h * 128:(h + 1) * 128], bB[j * 64:(j + 1) * 64, 12 + tt, h * 128:(h + 1) * 128],
                       bB[j * 64:(j + 1) * 64, 8 + tt, h * 128:(h + 1) * 128], True, True,
                       [K("bB", 12 + tt), K("bB", 8 + tt)], [K("P", 3)], h == 3)
                for h in range(4):
                    t0 = tt * 128 + j * 64
                    mm(Pf[po][j * 64:(j + 1) * 64, h * 128:(h + 1) * 128], bB[:, 4 + h, t0:t0 + 64],
                       sdb[:, sd, h * 128:(h + 1) * 128], False, (j == 1 and h == 3), KQT + [K("sdb", sd)],
                       [K("P", po)], (j == 1 and h == 3), skip_group_check=True)
                sch.add(DVE, lambda: nc.vector.tensor_tensor(out=Sst[:], in0=fS[:, 16, :], in1=Pf[3][:, :], op=ALU.add),
                        reads=[K("fS", 16), K("P", 3)], writes=[K("Sst")])

        def h_post(tt):
            r = tt % 2
            po = 4 + r
            so = 16 + 16 * r
            for h in range(4):
                sch.add(ACT, lambda h=h: nc.scalar.activation(
                    out=bB[:, 22, h * 128:(h + 1) * 128], in_=Pf[po][:, h * 128:(h + 1) * 128], func=AF.Square,
                    scale=float(1.0 / np.sqrt(128.0)), accum_out=st[:, so + h:so + h + 1]),
                    reads=[K("P", po)], writes=[K("bB", 22), K("st", so + h)])
            sch.add(ACT, lambda: nc.scalar.activation(out=st[:, so + 8:so + 12], in_=st[:, so:so + 4], func=AF.Ln,
                                                      bias=epsb[:, 0:1]),
                    reads=[K("st", so + h) for h in range(4)] + [K("epsb")], writes=[K("st", so + 8)])
            sch.add(ACT, lambda: nc.scalar.activation(out=st[:, so + 12:so + 16], in_=st[:, so + 8:so + 12],
                                                      func=AF.Exp, scale=-0.5),
                    reads=[K("st", so + 8)], writes=[K("st", so + 12)])
            for h in range(4):
                sch.add(DVE, lambda h=h: nc.vector.scalar_tensor_tensor(
                    out=oabf[:, r, h * 128:(h + 1) * 128], in0=Pf[po][:, h * 128:(h + 1) * 128],
                    scalar=st[:, so + 12 + h:so + 13 + h], in1=fS[:, tt, h * 128:(h + 1) * 128], op0=ALU.mult,
                    op1=ALU.mult),
                    reads=[K("P", po), K("st", so + 12), K("fS", tt)], writes=[K("oabf", r)])
            if first and tt == 0:
                dump("oa0", oabf[:, 0, :], [K("oabf", 0)], [128, 512], BF16)

        h_attn(0)
        for tt in range(4):
            h_rec(tt)
            if tt < 3:
                h_attn(tt + 1)
            h_post(tt)
            if tt >= 1:
                h_post_tr(tt - 1)
        QTE = [K("bB", 16 + i) for i in range(8)]
        qTe = bB[:, 16:24, :]
        wmk, kmk = wblk(gc, B_MK)
        wstore = {"k": (wmk, kmk)}

        def m_stageA(kind, tt, idx):
            is_q = kind == "q"
            if is_q and "q" not in wstore:
                wstore["q"] = wblk(gc, B_MQ)
            wv, wk = wstore[kind]
            par = idx % 3
            sq, mn = [4, 5, 10][par], [6, 7, 11][par]
            sc = 64 + 16 * par
            pm = rP.nxt()
            for kc in range(8):
                mm(Pf[pm][:, :], hT[:, kc, tt * 128:(tt + 1) * 128], wv[:, kc, :], kc == 0, kc == 7,
                   [K("hT", tt), wk], [K("P", pm)], kc == 7)
            sch.add(ACT, lambda: nc.scalar.activation(out=fS[:, sq, :], in_=Pf[pm][:, :], func=AF.Square, scale=0.125),
                    reads=[K("P", pm)], writes=[K("fS", sq)])
            sch.add(DVE, lambda: nc.vector.tensor_reduce(out=st[:, sc:sc + 8],
                                                         in_=fS[:, sq, :].rearrange("p (h d) -> p h d", h=8),
                                                         axis=AX.X, op=ALU.add),
                    reads=[K("fS", sq)], writes=[K("st", sc)])
            sch.add(ACT, lambda: nc.scalar.activation(out=st[:, sc + 8:sc + 16], in_=st[:, sc:sc + 8], func=AF.Ln,
                                                      bias=epsb[:, 0:1]),
                    reads=[K("st", sc), K("epsb")], writes=[K("st", sc + 8)])
            sch.add(ACT, lambda: nc.scalar.activation(out=st[:, sc:sc + 8], in_=st[:, sc + 8:sc + 16], func=AF.Exp,
                                                      scale=-0.5),
                    reads=[K("st", sc + 8)], writes=[K("st", sc)])
            sch.add(DVE, lambda: nc.vector.tensor_tensor(
                out=fS[:, mn, :].rearrange("p (h d) -> p h d", h=8),
                in0=Pf[pm][:, :].rearrange("p (h d) -> p h d", h=8),
                in1=st[:, sc:sc + 8].unsqueeze(2).broadcast_to([128, 8, 64]), op=ALU.mult),
                reads=[K("P", pm), K("st", sc)], writes=[K("fS", mn)])
            gvec, gkey = (gq8, K("gq8")) if is_q else (gkb, K("gkb"))
            m3 = fS[:, mn, :].rearrange("p (h d) -> p h d", h=8)
            sch.add(DVE, lambda: nc.vector.tensor_tensor(out=m3, in0=m3,
                                                         in1=gvec[:].unsqueeze(1).broadcast_to([128, 8, 64]),
                                                         op=ALU.mult),
                    reads=[K("fS", mn), gkey], writes=[K("fS", mn)])

        def m_stageB(kind, tt, idx):
            is_q = kind == "q"
            par = idx % 3
            mn, tB = [6, 7, 11][par], [8, 9, 17][par]
            tile_i = c * 4 + tt
            m4 = fS[:, mn, :].rearrange("p (h a d) -> p h a d", h=8, a=2)
            tB4 = fS[:, tB, :].rearrange("p (h a d) -> p h a d", h=8, a=2)
            cosb = cs[:, tile_i, 0, :]
            sinb3 = cs[:, tile_i, 1, :].unsqueeze(1).broadcast_to([128, 8, 32])
            sch.add(POOL, lambda: nc.gpsimd.tensor_tensor(out=tB4[:, :, 0, :], in0=m4[:, :, 1, :], in1=sinb3,
                                                          op=ALU.mult),
                    reads=[K("fS", mn), K("cs")], writes=[K("fS", tB)])
            sch.add(POOL, lambda: nc.gpsimd.tensor_tensor(out=tB4[:, :, 1, :], in0=m4[:, :, 0, :], in1=sinb3,
                                                          op=ALU.mult),
                    reads=[K("fS", mn), K("cs")], writes=[K("fS", tB)])
            sch.add(DVE, lambda: nc.vector.tensor_tensor(
                out=m4, in0=m4, in1=cosb.unsqueeze(1).unsqueeze(1).broadcast_to([128, 8, 2, 32]), op=ALU.mult),
                reads=[K("fS", mn), K("cs"), K("fS", tB)], writes=[K("fS", mn)])
            ro = idx % 2
            ro4 = ropeo[:, ro, :].rearrange("p (h a d) -> p h a d", h=8, a=2)
            sch.add(POOL, lambda: nc.gpsimd.tensor_tensor(out=ro4[:, :, 0, :], in0=m4[:, :, 0, :], in1=tB4[:, :, 0, :],
                                                          op=ALU.subtract),
                    reads=[K("fS", mn), K("fS", tB)], writes=[K("ropeo", ro)])
            sch.add(POOL, lambda: nc.gpsimd.tensor_tensor(out=ro4[:, :, 1, :], in0=m4[:, :, 1, :], in1=tB4[:, :, 1, :],
                                                          op=ALU.add),
                    reads=[K("fS", mn), K("fS", tB)], writes=[K("ropeo", ro)])
            ti = rT.nxt()
            for h in range(8):
                tp(Tb[ti][0:64, h * 128:(h + 1) * 128], ropeo[:, ro, h * 64:(h + 1) * 64], [K("ropeo", ro)],
                   [K("T", ti)], h == 7)
            src = Tb[ti][0:64, :].rearrange("p (h t) -> p h t", h=8)
            if is_q:
                sch.add(DVE, lambda: nc.vector.tensor_copy(out=qTe[0:64, :, tt * 128:(tt + 1) * 128], in_=src),
                        reads=[K("T", ti)], writes=QTE + [K("qTe", tt)])
            else:
                p0 = c * CH + tt * 128
                sch.add(DVE, lambda: nc.vector.tensor_copy(out=kTe[0:64, :, p0:p0 + 128], in_=src),
                        reads=[K("T", ti)], writes=[K("kTe", tile_i)])
                if tt % 2 == 1:
                    blk = 2 * c + tt // 2
                    sch.add(DVE, lambda: nc.vector.tensor_reduce(out=kmf[0:64, :],
                                                                 in_=kTe[0:64, :, blk * 256:(blk + 1) * 256],
                                                                 axis=AX.X, op=ALU.add),
                            reads=[K("kTe", 2 * blk), K("kTe", 2 * blk + 1)], writes=[K("kmf")])
                    sch.add(DVE, lambda: nc.vector.tensor_scalar(out=kmT[0:64, :, blk:blk + 1],
                                                                 in0=kmf[0:64, :].unsqueeze(2), scalar1=1.0 / 256.0,
                                                                 scalar2=None, op0=ALU.mult),
                            reads=[K("kmf")], writes=[K("kmT")])

        def m_stageC(tt):
            qb = 2 * c + tt // 2
            bi = tt % 2
            if qb >= 4:
                pg = rP.nxt()
                for h in range(8):
                    mm(Pf[pg][:, h * 8:h * 8 + qb], qTe[0:64, h, tt * 128:(tt + 1) * 128], kmT[0:64, h, 0:qb], True, True,
                       [K("qTe", tt), K("kmT")], [K("P", pg)], h == 7)
                sch.add(ACT, lambda: nc.scalar.copy(out=gsm[:], in_=Pf[pg][:, 0:64]),
                        reads=[K("P", pg)], writes=[K("gsm")])
                g3 = gsm[:].rearrange("p (h j) -> p h j", h=8)[:, :, 0:qb]
                c4 = cmpb[:, 0:8 * qb * qb].rearrange("p (h j k) -> p h j k", h=8, j=qb)
                sch.add(DVE, lambda: nc.vector.tensor_tensor(
                    out=c4, in0=g3.unsqueeze(2).broadcast_to([128, 8, qb, qb]),
                    in1=g3.unsqueeze(3).broadcast_to([128, 8, qb, qb]), op=ALU.is_gt),
                    reads=[K("gsm")], writes=[K("cmpb")])
                cn3 = cnt[:, 0:8 * qb].rearrange("p (h j) -> p h j", h=8)
                sch.add(DVE, lambda: nc.vector.tensor_reduce(out=cn3, in_=c4, axis=AX.X, op=ALU.add),
                        reads=[K("cmpb")], writes=[K("cnt")])
                sch.add(DVE, lambda: nc.vector.tensor_scalar(
                    out=bias[:, bi, :, 64:64 + qb], in0=cn3, scalar1=2.5, scalar2=-BIG, op0=ALU.is_gt, op1=ALU.mult),
                    reads=[K("cnt")], writes=[K("bias", bi)])
                sch.add(POOL, lambda: nc.gpsimd.memset(bias[:, bi, :, 64 + qb:65 + qb], 0.0), writes=[K("bias", bi)])
            else:
                sch.add(POOL, lambda: nc.gpsimd.memset(bias[:, bi, :, 64:65 + qb], 0.0), writes=[K("bias", bi)])
            if qb < 7:
                sch.add(POOL, lambda: nc.gpsimd.memset(bias[:, bi, :, 65 + qb:72], -BIG), writes=[K("bias", bi)])
            for hh in range(2):
                pb = rP.nxt()
                for h4 in range(4):
                    h = hh * 4 + h4
                    mm(Pf[pb][0:72, h4 * 128:(h4 + 1) * 128], bias[:, bi, h, :], ident[:], True, True,
                       [K("bias", bi), K("ident")], [K("P", pb)], h4 == 3)
                sch.add(ACT, lambda pb=pb, hh=hh: nc.scalar.copy(
                    out=qTe[64:72, hh * 4:hh * 4 + 4, tt * 128:(tt + 1) * 128],
                    in_=Pf[pb][64:72, :].rearrange("p (h t) -> p h t", h=4)),
                    reads=[K("P", pb)], writes=QTE + [K("qTeb", tt)])

        def m_v(tt):
            if "v" not in wstore:
                wstore["v"] = wblk(gc, B_MV)
            wmv, kmv = wstore["v"]
            tile_i = c * 4 + tt
            pv = rP.nxt()
            for kc in range(8):
                mm(Pf[pv][:, :], hT[:, kc, tt * 128:(tt + 1) * 128], wmv[:, kc, :], kc == 0, kc == 7,
                   [K("hT", tt), kmv], [K("P", pv)], kc == 7)
            sch.add(ACT, lambda: nc.scalar.copy(
                out=vext[:, tile_i, :, 0:64], in_=Pf[pv][:, :].rearrange("p (h d) -> p h d", h=8)),
                reads=[K("P", pv)], writes=[K("vext", tile_i)])

        items2 = [("k", t) for t in range(4)] + [("q", t) for t in range(4)]
        m_stageA("k", 0, 0)
        h_post_tr(3)
        m_stageA("k", 1, 1)
        for i in range(2, 8):
            m_stageA(items2[i][0], items2[i][1], i)
            m_stageB(items2[i - 2][0], items2[i - 2][1], i - 2)
        m_v(0)
        m_stageB("q", 2, 6)
        m_stageC(0)
        m_v(1)
        m_stageB("q", 3, 7)
        m_stageC(1)
        m_v(2)
        m_stageC(2)
        m_v(3)
        m_stageC(3)
        if first:
            dump("kTe", kTe[0:72, :, 0:512], [K("kTe", i) for i in range(4)] + [K("kTe_ind")], [72, 8, 512], BF16)
            dump("qTe", qTe[0:72, :, :], QTE, [72, 8, 512], BF16)
        nkt = 4 * c + 4
        OB = [K("bB", 8 + i) for i in range(4)]
        obv = bB[:, 8:12, :].rearrange("p t (h d) -> p t h d", h=8)
        items = [(h, kt) for h in range(8) for kt in range(nkt)]
        pend = None

        def att_qk(h, kt):
            n0 = max(0, kt * 128 - c * CH)
            pst = rP.nxt()
            pi = rpt.nxt()
            mm(Pf[pst][:, n0:512], kTe[0:72, h, kt * 128:(kt + 1) * 128], qTe[0:72, h, n0:512], True, True,
               [K("kTe", kt), K("kTe_ind")] + QTE, [K("P", pst)], True)
            sch.add(ACT, lambda pst=pst, pi=pi, n0=n0: nc.scalar.activation(out=pt[:, pi, n0:512],
                                                                          in_=Pf[pst][:, n0:512], func=AF.Exp),
                    reads=[K("P", pst)], writes=[K("pt", pi)])
            if kt * 128 >= c * CH:
                sch.add(POOL, lambda pi=pi, n0=n0: nc.gpsimd.tensor_tensor(out=pt[:, pi, n0:n0 + 128],
                                                                           in0=pt[:, pi, n0:n0 + 128], in1=tri[:],
                                                                           op=ALU.mult),
                        reads=[K("pt", pi), K("tri")], writes=[K("pt", pi)])
            return (h, kt, n0, pi)

        def att_pv(h, kt, n0, pi):
            pob = 4 + (h % 2)
            for sub in range(n0 // 128, 4):
                last = (kt == nkt - 1 and sub == 3)
                mm(Pf[pob][:, sub * 65:(sub + 1) * 65], pt[:, pi, sub * 128:(sub + 1) * 128], vext[:, kt, h, :],
                   (kt == 0 and sub == 0), last, [K("pt", pi), K("vext", kt), K("vext_one")], [K("P", pob)],
                   sub == 3, skip_group_check=True)
            if kt == nkt - 1:
                rd = h % 2
                po3 = Pf[pob][:, 0:260].rearrange("p (s e) -> p s e", e=65)
                sch.add(DVE, lambda po3=po3, rd=rd: nc.vector.reciprocal(out=rden[:, rd, :].unsqueeze(2),
                                                                         in_=po3[:, :, 64:65]),
                        reads=[K("P", pob)], writes=[K("rden", rd)])
                sch.add(DVE, lambda po3=po3, rd=rd, h=h: nc.vector.tensor_tensor(
                    out=obv[:, :, h, :], in0=po3[:, :, 0:64],
                    in1=rden[:, rd, :].unsqueeze(2).broadcast_to([128, 4, 64]), op=ALU.mult),
                    reads=[K("P", pob), K("rden", rd)], writes=OB)

        for (h, kt) in items:
            cur = att_qk(h, kt)
            if pend is not None:
                att_pv(*pend)
            pend = cur
        att_pv(*pend)
        if first:
            dump("ob", bB[:, 8:12, :], OB, [128, 4, 512], BF16)
        for tt in range(4):
            ti = rT.nxt()
            for kc in range(4):
                tp(Tb[ti][:, kc * 128:(kc + 1) * 128], bB[:, 8 + tt, kc * 128:(kc + 1) * 128], [K("bB", 8 + tt)],
                   [K("T", ti)], kc == 3)
            sch.add(ACT, lambda ti=ti, tt=tt: nc.scalar.copy(out=obT[:, :, tt * 128:(tt + 1) * 128],
                                                             in_=Tb[ti][:, 0:512].rearrange("p (a t) -> p a t", t=128)),
                    reads=[K("T", ti)], writes=[K("obT", tt)])
        OAT = [K("oaT", t) for t in range(4)]
        OBT = [K("obT", t) for t in range(4)]
        MIX = [K("bB", i) for i in range(8)]
        gab_ids = [B_GAB0, B_GAB1, B_GAB2, B_GAB3]
        for qd in range(4):
            wg2, kg2 = wblk(gc, gab_ids[qd], 1)
            wab, kab = wblk(gc, B_WAB0 if qd < 2 else B_WAB1, 1)
            for e in range(2):
                i = 2 * qd + e
                col = (i % 4) * 128
                res = []
                for br in range(2):
                    pgt = rP.nxt()
                    for kc in range(8):
                        mm(Pf[pgt][:, :], wg2[:, kc, br * 256 + e * 128: br * 256 + (e + 1) * 128], hT[:, kc, :],
                           kc == 0, kc == 7, HTK + [kg2], [K("P", pgt)], kc == 7)
                    pab = rP.nxt()
                    src = oaT if br == 0 else obT
                    srk = OAT if br == 0 else OBT
                    for kc in range(4):
                        mm(Pf[pab][:, :], wab[:, br * 4 + kc, col:col + 128], src[:, kc, :], kc == 0, kc == 3,
                           srk + [kab], [K("P", pab)], kc == 3)
                    ss_, ms_ = 0 + br, 2 + br
                    sch.add(ACT, lambda pgt=pgt, ss_=ss_: nc.scalar.activation(out=fS[:, ss_, :], in_=Pf[pgt][:, :],
                                                                             func=AF.Sigmoid),
                            reads=[K("P", pgt)], writes=[K("fS", ss_)])
                    sch.add(DVE, lambda pab=pab, ss_=ss_, ms_=ms_: nc.vector.tensor_tensor(
                        out=fS[:, ms_, :], in0=fS[:, ss_, :], in1=Pf[pab][:, :], op=ALU.mult),
                        reads=[K("fS", ss_), K("P", pab)], writes=[K("fS", ms_)])
                sch.add(POOL, lambda i=i: nc.gpsimd.tensor_tensor(out=bB[:, i, :], in0=fS[:, 2, :], in1=fS[:, 3, :],
                                                                  op=ALU.add),
                        reads=[K("fS", 2), K("fS", 3)], writes=[K("bB", i)])
        if first:
            dump("mixT", bB[:, 0:8, :], MIX, [128, 8, 512], BF16)
        wo0, ko0 = wblk(gc, B_WO0)
        wo1, ko1 = wblk(gc, B_WO1, 1)
        for tt in range(4):
            xi = rxt.nxt()
            xs = 12 + 2 * xi
            xap = fS[:, xs:xs + 2, :].rearrange("p a b -> p (a b)")
            xk = [K("fS", xs), K("fS", xs + 1)]
            r0 = c * CH + tt * 128
            sch.add(XQ[xi], lambda xap=xap, r0=r0: nc.sync.dma_start(out=xap, in_=x[s, r0:r0 + 128, :]), writes=xk)
            for nh, (wo, ko) in enumerate([(wo0, ko0), (wo1, ko1)]):
                pw = rP.nxt()
                for kc in range(8):
                    mm(Pf[pw][:, :], bB[:, kc, tt * 128:(tt + 1) * 128], wo[:, kc, :], kc == 0, kc == 7,
                       MIX + [ko], [K("P", pw)], kc == 7)
                sch.add(DVE, lambda pw=pw, nh=nh, tt=tt, xap=xap: nc.vector.tensor_tensor(
                    out=xmid[:, tt, nh * 512:(nh + 1) * 512], in0=xap[:, nh * 512:(nh + 1) * 512], in1=Pf[pw][:, :],
                    op=ALU.add), reads=xk + [K("P", pw)], writes=[K("xmid", tt, nh)])
        if first:
            dump("xmid", xmid[:], [K("xmid", t, n) for t in range(4) for n in range(2)], [128, 4, D])
        for tt in range(4):
            norm_tile(xmid[:, tt, :], [K("xmid", tt, 0), K("xmid", tt, 1)], g2b, K("g2b"), tt, 4 * tt)
        par = c % 2
        if c == 0:
            sch.add(POOL, lambda: nc.gpsimd.memset(halo[:, 0, :, :], 0.0), writes=[K("halo", 0)])
        GT = [K("bB", i) for i in range(NKF)]
        ngc = gc + 1
        has_next = ngc < NSEQ * NCH
        for u in range(11):
            wu, ku = wblk(gc, B_UP0 + u)
            if has_next and u in (7, 9):
                ptt = 0 if u == 7 else 1
                x_norm_A(ngc // NCH, ngc % NCH, ptt)
            for e in range(2):
                i = 2 * u + e
                rb = i % 2
                ub = 0 + 2 * rb
                ac = 4 + rb
                gl = 6 + rb
                ubuf = fS[:, ub:ub + 2, :].rearrange("p a b -> p (a b)")
                UBK = [K("fS", ub), K("fS", ub + 1)]
                pu = rP.nxt()
                for kc in range(8):
                    mm(Pf[pu][:, :], wu[:, kc, e * 128:(e + 1) * 128], hT[:, kc, :], kc == 0, kc == 7, HTK + [ku],
                       [K("P", pu)], kc == 7)
                pv2 = rP.nxt()
                for kc in range(8):
                    mm(Pf[pv2][:, :], wu[:, kc, 256 + e * 128:256 + (e + 1) * 128], hT[:, kc, :], kc == 0, kc == 7,
                       HTK + [ku], [K("P", pv2)], kc == 7)
                sch.add(ACT, lambda pu=pu, ubuf=ubuf: nc.scalar.copy(out=ubuf[:, 2:514], in_=Pf[pu][:, :]),
                        reads=[K("P", pu)], writes=UBK)
                sch.add(POOL, lambda ubuf=ubuf, i=i: nc.gpsimd.tensor_copy(out=ubuf[:, 0:2], in_=halo[:, par, i, :]),
                        reads=[K("halo", par)], writes=UBK)
                sch.add(POOL, lambda ubuf=ubuf, i=i: nc.gpsimd.tensor_copy(out=halo[:, 1 - par, i, :],
                                                                           in_=ubuf[:, 512:514]),
                        reads=UBK, writes=[K("halo", 1 - par)])
                sch.add(DVE, lambda ubuf=ubuf, i=i, ac=ac: nc.vector.tensor_scalar(
                    out=fS[:, ac, :], in0=ubuf[:, 2:514], scalar1=cw[:, 2, i:i + 1], scalar2=cb[:, i:i + 1],
                    op0=ALU.mult, op1=ALU.add), reads=UBK + [K("cw"), K("cb")], writes=[K("fS", ac)])
                sch.add(DVE, lambda ubuf=ubuf, i=i, ac=ac: nc.vector.scalar_tensor_tensor(
                    out=fS[:, ac, :], in0=ubuf[:, 1:513], scalar=cw[:, 1, i:i + 1], in1=fS[:, ac, :], op0=ALU.mult,
                    op1=ALU.add), reads=UBK + [K("cw"), K("fS", ac)], writes=[K("fS", ac)])
                sch.add(DVE, lambda ubuf=ubuf, i=i, ac=ac: nc.vector.scalar_tensor_tensor(
                    out=fS[:, ac, :], in0=ubuf[:, 0:512], scalar=cw[:, 0, i:i + 1], in1=fS[:, ac, :], op0=ALU.mult,
                    op1=ALU.add), reads=UBK + [K("cw"), K("fS", ac)], writes=[K("fS", ac)])
                sch.add(ACT, lambda ac=ac, gl=gl: nc.scalar.activation(out=fS[:, gl, :], in_=fS[:, ac, :], func=AF.Gelu),
                        reads=[K("fS", ac)], writes=[K("fS", gl)])
                sch.add(DVE, lambda gl=gl, pv2=pv2, i=i: nc.vector.tensor_tensor(out=bB[:, i, :], in0=fS[:, gl, :],
                                                                                 in1=Pf[pv2][:, :], op=ALU.mult),
                        reads=[K("fS", gl), K("P", pv2)], writes=[K("bB", i)])
        if first:
            dump("gT", bB[:, 0:NKF, :], GT, [128, NKF, 512], BF16)
        if has_next:
            for ptt in (0, 1):
                norm_B(ptt)
                prefetched.add((ngc, ptt))
        for nh in range(2):
            banks = [0, 1, 2, 3] if nh == 0 else [4, 5, 0, 1]
            for pi_, (k0, nk) in enumerate(DN_P):
                wd, kd = wblk(gc, B_DN0 + nh * 3 + pi_)
                for tt in range(4):
                    for kk in range(nk):
                        kc = k0 + kk
                        mm(Pf[banks[tt]][:, :], bB[:, kc, tt * 128:(tt + 1) * 128], wd[:, kk, :], kc == 0,
                           kc == NKF - 1, [K("bB", kc), kd], [K("P", banks[tt])], kk == nk - 1)
            for tt in range(4):
                sch.add(DVE, lambda nh=nh, tt=tt, b=banks[tt]: nc.vector.tensor_tensor(
                    out=xmid[:, tt, nh * 512:(nh + 1) * 512], in0=xmid[:, tt, nh * 512:(nh + 1) * 512],
                    in1=Pf[b][:, :], op=ALU.add),
                    reads=[K("xmid", tt, nh), K("P", banks[tt])], writes=[K("xmid", tt, nh)])
        sch.add(OQ, lambda: nc.gpsimd.dma_start(
            out=out[s, c * CH:(c + 1) * CH, :].rearrange("(t p) d -> p t d", p=128), in_=xmid[:]),
            reads=[K("xmid", t, n) for t in range(4) for n in range(2)])

    for s in range(NSEQ):
        for c in range(NCH):
            chunk(s, c)
    stats = sch.finalize(nc.sync)
    es.close()
    return nc, dumps, stats


_CACHE = {}


def kernel(**inputs):
    nseq = 32 // NCORES
    if "nc" not in _CACHE:
        _CACHE["nc"] = build(nseq)[0]
    nc = _CACHE["nc"]
    consts = host_consts()
    x = np.ascontiguousarray(np.asarray(inputs["x"], dtype=np.float32))
    shared = {
        "norm1_g": np.asarray(inputs["norm1_g"], np.float32).reshape(1, D),
        "norm2_g": np.asarray(inputs["norm2_g"], np.float32).reshape(1, D),
        "w_in": np.asarray(inputs["w_in"], np.float32).reshape(D, 5632),
        "hg_lb_logits": np.asarray(inputs["hg_lb_logits"], np.float32).reshape(2, 512),
        "hg_onorm_g": np.asarray(inputs["hg_onorm_g"], np.float32).reshape(1, 512),
        "q_norm_g": np.asarray(inputs["q_norm_g"], np.float32).reshape(1, 64),
        "k_norm_g": np.asarray(inputs["k_norm_g"], np.float32).reshape(1, 64),
        "w_a": np.asarray(inputs["w_a"], np.float32).reshape(512, D),
        "w_b": np.asarray(inputs["w_b"], np.float32).reshape(512, D),
        "w_out": np.asarray(inputs["w_out"], np.float32).reshape(D, D),
        "w_up": np.asarray(inputs["w_up"], np.float32).reshape(D, 2 * DFF),
        "conv_w": np.asarray(inputs["conv_w"], np.float32).reshape(3, DFF),
        "conv_b": np.asarray(inputs["conv_b"], np.float32).reshape(1, DFF),
        "w_down": np.asarray(inputs["w_down"], np.float32).reshape(DFF, D),
    }
    shared.update(consts)
    in_maps = []
    for i in range(NCORES):
        m = dict(shared)
        m["x"] = x[i * nseq:(i + 1) * nseq]
        in_maps.append(m)
    res = run_bass_kernel_spmd(nc, in_maps, core_ids=list(range(NCORES)))
    return np.concatenate([np.asarray(r["out"]) for r in res.results], axis=0).astype(np.float32)
```

```python
from contextlib import ExitStack
import numpy as np
import concourse.bass as bass
import concourse.mybir as mybir
from concourse.bass_utils import run_bass_kernel_spmd

F32 = mybir.dt.float32
BF16 = mybir.dt.bfloat16
AF = mybir.ActivationFunctionType
ALU = mybir.AluOpType
AX = mybir.AxisListType

NCORES = 8
S = 2048
D = 1024
CH = 512
NCH = S // CH
DFF = 2816
NKF = DFF // 128
EPS = 1e-6
BIG = 30000.0
NBLK = 32
RING = 3

(B_HQ, B_HF, B_HI, B_HG, B_MK, B_MQ, B_MV, B_GAB0, B_WAB0, B_GAB1, B_GAB2, B_WAB1, B_GAB3,
 B_WO0, B_WO1) = range(15)
B_UP0 = 15
B_DN0 = 26


class Q:
    def __init__(self, name, issuer, sem, inc, kind):
        self.name, self.issuer, self.sem, self.inc, self.kind = name, issuer, sem, inc, kind
        self.nsig = 0
        self.last = None


class Sched:
    def __init__(self, nc):
        self.nc = nc
        self.ins = []
        self.queues = []

    def queue(self, name, issuer, sem, inc, kind):
        q = Q(name, issuer, sem, inc, kind)
        self.queues.append(q)
        return q

    def add(self, q, fn, reads=(), writes=(), sig=True):
        self.ins.append((q, fn, tuple(reads), tuple(writes), sig or q.kind == 'dma'))

    def finalize(self, final_issuer):
        ins = self.ins
        n = len(ins)
        sigval = [0] * n
        nextsig = [None] * n
        for i, (q, fn, r, w, sig) in enumerate(ins):
            if sig:
                q.nsig += 1
                sigval[i] = q.nsig
        lastsig = {}
        for i in range(n - 1, -1, -1):
            q = ins[i][0]
            if ins[i][4]:
                lastsig[q] = i
            nextsig[i] = lastsig.get(q)
        writers, readers = {}, {}
        clocks = {}
        iclk = [None] * n
        nwaits = 0
        for i, (q, fn, rds, wrs, sig) in enumerate(ins):
            deps = set()
            for k in rds:
                for qq, j in writers.get(k, {}).items():
                    if qq is q and q.kind == 'pe':
                        continue
                    deps.add(j)
            for k in wrs:
                for qq, j in writers.get(k, {}).items():
                    if qq is q and q.kind != 'dma':
                        continue
                    deps.add(j)
                for qq, j in readers.get(k, {}).items():
                    if qq is q and q.kind != 'dma':
                        continue
                    deps.add(j)
            if q.kind == 'dma' and q.last is not None:
                deps.add(q.last)
            clk = clocks.setdefault(id(q.issuer), {})
            for j in sorted(deps):
                js = nextsig[j]
                assert js is not None and js < i, f"dep signal after waiter: ins {i} dep {j} sig {js}"
                qj = ins[js][0]
                val = sigval[js] * qj.inc
                if clk.get(qj, 0) >= val:
                    continue
                q.issuer.wait_ge(qj.sem, val)
                nwaits += 1
                for qq, v in iclk[js].items():
                    if clk.get(qq, 0) < v:
                        clk[qq] = v
            r = fn()
            if sig:
                r.then_inc(q.sem, q.inc)
                c2 = dict(clk)
                c2[q] = sigval[i] * q.inc
                iclk[i] = c2
            for k in rds:
                readers.setdefault(k, {})[q] = i
            for k in wrs:
                writers.setdefault(k, {})[q] = i
            if q.kind == 'dma':
                q.last = i
        for q in self.queues:
            if q.nsig:
                final_issuer.wait_ge(q.sem, q.nsig * q.inc)
        return n, nwaits


def host_consts():
    c = {}
    c["c_ident"] = np.eye(128, dtype=np.float32)
    s = np.arange(128)[:, None]
    t = np.arange(128)[None, :]
    same = (s // 64) == (t // 64)
    c["c_U"] = ((s > t) & same).astype(np.float32)
    c["c_bd"] = ((s <= t) & same).astype(np.float32)
    c["c_tri"] = (s <= t).astype(np.float32)
    c["c_cind"] = np.stack([(np.arange(128) < 64), (np.arange(128) >= 64)], 1).astype(np.float32)
    half = 32
    inv = 1.0 / (10000.0 ** (np.arange(half, dtype=np.float32) * 2.0 / 64))
    pos = np.arange(S, dtype=np.float32)
    ang = pos[:, None] * inv[None, :]
    cs = np.stack([np.cos(ang), np.sin(ang)], 1).astype(np.float32)
    c["c_cs"] = np.ascontiguousarray(cs.reshape(16, 128, 2, 32).transpose(1, 0, 2, 3))
    kind = (np.arange(S)[None, :] // 256 == np.arange(8)[:, None]).astype(np.float32)
    c["c_kind"] = kind
    return c


def build(NSEQ, dump_names=()):
    nc = bass.Bass("TRN2", target_bir_lowering=False)
    es = ExitStack()

    def din(name, shape, dt=F32):
        return nc.dram_tensor(name, list(shape), dt, kind="ExternalInput").ap()

    x = din("x", [NSEQ, S, D])
    norm1_g = din("norm1_g", [1, D]); norm2_g = din("norm2_g", [1, D])
    w_in = din("w_in", [D, 5632]); hg_lb = din("hg_lb_logits", [2, 512])
    hg_on = din("hg_onorm_g", [1, 512]); qng = din("q_norm_g", [1, 64]); kng = din("k_norm_g", [1, 64])
    w_a = din("w_a", [512, D]); w_b = din("w_b", [512, D]); w_out = din("w_out", [D, D])
    w_up = din("w_up", [D, 2 * DFF]); conv_w = din("conv_w", [3, DFF]); conv_b = din("conv_b", [1, DFF])
    w_down = din("w_down", [DFF, D])
    c_ident = din("c_ident", [128, 128]); c_U = din("c_U", [128, 128]); c_bd = din("c_bd", [128, 128])
    c_tri = din("c_tri", [128, 128]); c_cind = din("c_cind", [128, 2]); c_cs = din("c_cs", [128, 16, 2, 32])
    c_kind = din("c_kind", [8, S])
    out = nc.dram_tensor("out", [NSEQ, S, D], F32, kind="ExternalOutput").ap()
    wbf = nc.dram_tensor("wbf", [NBLK, 128, 4096], BF16).ap()
    dumps = {}

    def sb(name, shape, dt):
        return es.enter_context(nc.sbuf_tensor(name, list(shape), dt))

    def ps(name, shape, dt):
        return es.enter_context(nc.psum_tensor(name, list(shape), dt))

    def sem(name):
        return es.enter_context(nc.semaphore(name))

    sch = Sched(nc)
    PE = sch.queue("pe", nc.tensor, sem("s_pe"), 1, 'pe')
    ACT = sch.queue("act", nc.scalar, sem("s_act"), 1, 'cmp')
    DVE = sch.queue("dve", nc.vector, sem("s_dve"), 1, 'cmp')
    POOL = sch.queue("pool", nc.gpsimd, sem("s_pool"), 1, 'cmp')
    WQ = [sch.queue(f"wq{i}", nc.sync, sem(f"s_wq{i}"), 16, 'dma') for i in range(RING)]
    XQ = [sch.queue(f"xq{i}", nc.sync, sem(f"s_xq{i}"), 16, 'dma') for i in range(2)]
    OQ = sch.queue("oq", nc.gpsimd, sem("s_oq"), 16, 'dma')
    CQ = [sch.queue(f"cq{i}", nc.gpsimd, sem(f"s_cq{i}"), 16, 'dma') for i in range(4)]
    KQ = [sch.queue(f"kq{i}", nc.sync, sem(f"s_kq{i}"), 16, 'dma') for i in range(4)]
    DQ = sch.queue("dq", nc.sync, sem("s_dq"), 16, 'dma')

    wring = sb("wring", [128, RING, 4096], BF16)
    kTe = sb("kTe", [128, 8, S], BF16)
    vext = sb("vext", [128, 16, 8, 65], BF16)
    xmid = sb("xmid", [128, 4, D], F32)
    hT = sb("hT", [128, 8, CH], BF16)
    g1b = sb("g1b", [128, D], F32); g2b = sb("g2b", [128, D], F32)
    fS = sb("fS", [128, 18, 512], F32)
    bB = sb("bB", [128, 24, 512], BF16)
    Sst = sb("Sst", [128, 512], F32)
    qtil = sb("qtil", [128, 2, 512], BF16)
    attn = sb("attn", [128, 2, 512], BF16)
    sdb = sb("sdb", [128, 2, 512], BF16)
    oabf = sb("oabf", [128, 2, 512], BF16)
    oaT = sb("oaT", [128, 4, CH], BF16); obT = sb("obT", [128, 4, CH], BF16)
    ropeo = sb("ropeo", [128, 2, 512], BF16)
    bias = sb("bias", [128, 2, 8, 72], BF16)
    pt = sb("pt", [128, 3, 512], BF16)
    ident = sb("ident", [128, 128], BF16)
    Um = sb("Um", [128, 128], F32); bd = sb("bd", [128, 128], F32); tri = sb("tri", [128, 128], BF16)
    cind = sb("cind", [128, 2], F32)
    omlb = sb("omlb", [128, 512], F32); gob = sb("gob", [128, 512], F32)
    cs = sb("cs", [128, 16, 2, 32], F32)
    gq8 = sb("gq8", [128, 64], F32); gkb = sb("gkb", [128, 64], F32)
    cw = sb("cw", [128, 3, NKF], F32); cb = sb("cb", [128, NKF], F32)
    halo = sb("halo", [128, 2, NKF, 2], F32)
    elast = sb("elast", [128, 4, 4, 2], F32)
    st = sb("st", [128, 128], F32)
    kmT = sb("kmT", [128, 8, 8], BF16)
    kmf = sb("kmf", [128, 8], F32)
    epsb = sb("epsb", [128, 1], F32)
    gsm = sb("gsm", [128, 64], F32)
    cmpb = sb("cmpb", [128, 8 * 7 * 7], F32)
    cnt = sb("cnt", [128, 56], F32)
    rden = sb("rden", [128, 2, 4], F32)
    Pf = [ps(f"P{i}", [128, 512], F32) for i in range(6)]
    Tb = [ps(f"T{i}", [128, 1024], BF16) for i in range(2)]

    K = lambda *a: tuple(a)

    def fslot(i, n=1):
        return fS[:, i, :] if n == 1 else fS[:, i:i + n, :]

    class Rot:
        def __init__(self, n):
            self.n, self.i = n, 0

        def nxt(self):
            v = self.i % self.n
            self.i += 1
            return v

    rP = Rot(4)
    rP6 = Rot(6)
    rP3 = Rot(3)
    rT = Rot(2)
    rxt = Rot(2)
    rpt = Rot(3)

    def dump(name, ap, keys, shape, dt=F32):
        if name not in dump_names or name in dumps:
            return
        t = nc.dram_tensor("dbg_" + name, list(shape), dt, kind="ExternalOutput").ap()
        dumps[name] = t
        sch.add(DQ, lambda: nc.sync.dma_start(out=t, in_=ap), reads=keys)

    cqi = [0]

    def cast(dst, src, b):
        q = CQ[cqi[0] % 4]
        cqi[0] += 1
        sch.add(q, lambda: nc.gpsimd.dma_start(out=dst, in_=src), writes=[K("wbf", b)])

    kqi = [0]

    def kload(dst, src, key, eng=None):
        q = KQ[kqi[0] % 4]
        kqi[0] += 1
        sch.add(q, lambda: nc.sync.dma_start(out=dst, in_=src), writes=[key])

    def kcast(dst, src, key):
        q = CQ[cqi[0] % 4]
        cqi[0] += 1
        sch.add(q, lambda: nc.gpsimd.dma_start(out=dst, in_=src), writes=[key])

    kcast(ident[:], c_ident, K("ident"))
    kcast(tri[:], c_tri, K("tri"))
    kload(Um[:], c_U, K("Um")); kload(bd[:], c_bd, K("bd")); kload(cind[:], c_cind, K("cind"))
    kload(cs[:], c_cs, K("cs"))
    kload(g1b[:], norm1_g.partition_broadcast(128), K("g1b"))
    kload(g2b[:], norm2_g.partition_broadcast(128), K("g2b"))
    kload(gob[:], hg_on.partition_broadcast(128), K("gob"))
    kload(gq8[:], qng.partition_broadcast(128), K("gq8"))
    kload(gkb[:], kng.partition_broadcast(128), K("gkb"))
    kload(fS[:, 0, :], hg_lb[0:1, :].partition_broadcast(128), K("fS", 0))
    kload(fS[:, 1, :], hg_lb[1:2, :].partition_broadcast(128), K("fS", 1))
    for j in range(3):
        sch.add(KQ[j % 4], (lambda j=j: nc.sync.dma_start(
            out=cw[:, j, :], in_=conv_w[j:j + 1, :].rearrange("o (kc p) -> p (o kc)", p=128),
            allow_slow_non_contiguous=True)), writes=[K("cw")])
    sch.add(KQ[3], lambda: nc.sync.dma_start(
        out=cb[:], in_=conv_b.rearrange("o (kc p) -> p (o kc)", p=128), allow_slow_non_contiguous=True),
        writes=[K("cb")])
    for h in range(8):
        kcast(kTe[64:72, h, :], c_kind, K("kTe_ind"))
    sch.add(DVE, lambda: nc.vector.tensor_tensor(out=fS[:, 0, :], in0=fS[:, 0, :], in1=fS[:, 1, :], op=ALU.subtract),
            reads=[K("fS", 0), K("fS", 1)], writes=[K("fS", 0)])
    sch.add(ACT, lambda: nc.scalar.activation(out=omlb[:], in_=fS[:, 0, :], func=AF.Sigmoid, scale=-1.0),
            reads=[K("fS", 0)], writes=[K("omlb")])
    sch.add(ACT, lambda: nc.scalar.mul(out=gq8[:], in_=gq8[:], mul=0.125), reads=[K("gq8")], writes=[K("gq8")])
    sch.add(POOL, lambda: nc.gpsimd.memset(vext[:, :, :, 64:65], 1.0), writes=[K("vext_one")])
    sch.add(POOL, lambda: nc.gpsimd.memset(epsb[:], EPS), writes=[K("epsb")])
    sch.add(POOL, lambda: nc.gpsimd.memset(bias[:, :, :, 0:64], 0.0), writes=[K("bias", 0), K("bias", 1)])

    def blkv(b, kc0, nkc, n0, nn):
        v = wbf[b].rearrange("p (kc n) -> p kc n", n=512)
        return v[:, kc0:kc0 + nkc, n0:n0 + nn]

    def rows(w, r0, nkc, c0, ncol):
        return w[r0:r0 + nkc * 128, c0:c0 + ncol].rearrange("(kc p) n -> p kc n", p=128)

    for g, b in enumerate([B_HQ, B_HF, B_HI, B_HG, B_MQ, B_MK, B_MV]):
        cast(blkv(b, 0, 8, 0, 512), rows(w_in, 0, 8, g * 512, 512), b)
    for qd, b in enumerate([B_GAB0, B_GAB1, B_GAB2, B_GAB3]):
        cast(blkv(b, 0, 8, 0, 256), rows(w_in, 0, 8, 3584 + qd * 256, 256), b)
        cast(blkv(b, 0, 8, 256, 256), rows(w_in, 0, 8, 4608 + qd * 256, 256), b)
    for hf, b in enumerate([B_WAB0, B_WAB1]):
        cast(blkv(b, 0, 4, 0, 512), rows(w_a, 0, 4, hf * 512, 512), b)
        cast(blkv(b, 4, 4, 0, 512), rows(w_b, 0, 4, hf * 512, 512), b)
    for hf, b in enumerate([B_WO0, B_WO1]):
        cast(blkv(b, 0, 8, 0, 512), rows(w_out, 0, 8, hf * 512, 512), b)
    for u in range(11):
        cast(blkv(B_UP0 + u, 0, 8, 0, 256), rows(w_up, 0, 8, u * 256, 256), B_UP0 + u)
        cast(blkv(B_UP0 + u, 0, 8, 256, 256), rows(w_up, 0, 8, DFF + u * 256, 256), B_UP0 + u)
    DN_P = [(0, 8), (8, 8), (16, 6)]
    for nh in range(2):
        for pi, (k0, nk) in enumerate(DN_P):
            cast(blkv(B_DN0 + nh * 3 + pi, 0, nk, 0, 512), rows(w_down, k0 * 128, nk, nh * 512, 512), B_DN0 + nh * 3 + pi)

    wstate = {"next": 0}
    total_blocks = NSEQ * NCH * NBLK

    def wload_upto(gb):
        while wstate["next"] <= gb and wstate["next"] < total_blocks:
            g = wstate["next"]
            slot = g % RING
            b = g % NBLK
            sch.add(WQ[slot], (lambda slot=slot, b=b: nc.sync.dma_start(out=wring[:, slot, :], in_=wbf[b])),
                    reads=[K("wbf", b)], writes=[K("w", slot)])
            wstate["next"] += 1

    def wblk(gc, b, ahead=2):
        gb = gc * NBLK + b
        wload_upto(gb + ahead)
        slot = gb % RING
        return wring[:, slot, :].rearrange("p (kc n) -> p kc n", n=512), K("w", slot)

    def mm(out_ap, lhsT, rhs, start, stop, reads, writes, sig, **kw):
        sch.add(PE, lambda: nc.tensor.matmul(out_ap, lhsT=lhsT, rhs=rhs, start=start, stop=stop, **kw),
                reads=reads, writes=writes, sig=sig)

    def tp(out_ap, in_ap, reads, writes, sig):
        sch.add(PE, lambda: nc.tensor.transpose(out_ap, in_ap, ident[:]), reads=list(reads) + [K("ident")],
                writes=writes, sig=sig)

    hbuf = sb("hbuf", [128, 2, D], BF16)

    def norm_A(src_ap, src_keys, gb_t, gkey, tt, stc):
        junk = bB[:, 22:24, :].rearrange("p a b -> p (a b)")
        sch.add(ACT, lambda: nc.scalar.activation(out=junk, in_=src_ap, func=AF.Square, scale=1.0 / 32.0,
                                                  accum_out=st[:, stc:stc + 1]),
                reads=src_keys, writes=[K("bB", 22), K("bB", 23), K("st", stc)])
        sch.add(ACT, lambda: nc.scalar.activation(out=st[:, stc + 2:stc + 3], in_=st[:, stc:stc + 1], func=AF.Ln,
                                                  bias=epsb[:, 0:1]),
                reads=[K("st", stc), K("epsb")], writes=[K("st", stc + 2)])
        sch.add(ACT, lambda: nc.scalar.activation(out=st[:, stc + 3:stc + 4], in_=st[:, stc + 2:stc + 3], func=AF.Exp,
                                                  scale=-0.5),
                reads=[K("st", stc + 2)], writes=[K("st", stc + 3)])
        hi_ = tt % 2
        sch.add(DVE, lambda: nc.vector.scalar_tensor_tensor(out=hbuf[:, hi_, :], in0=src_ap,
                                                            scalar=st[:, stc + 3:stc + 4],
                                                            in1=gb_t[:], op0=ALU.mult, op1=ALU.mult),
                reads=list(src_keys) + [K("st", stc + 3), gkey], writes=[K("hbuf", hi_)])

    def norm_B(tt):
        hi_ = tt % 2
        ti = rT.nxt()
        for kc in range(8):
            tp(Tb[ti][:, kc * 128:(kc + 1) * 128], hbuf[:, hi_, kc * 128:(kc + 1) * 128], [K("hbuf", hi_)],
               [K("T", ti)], kc == 7)
        sch.add(DVE, lambda: nc.vector.tensor_copy(out=hT[:, :, tt * 128:(tt + 1) * 128],
                                                   in_=Tb[ti][:, :].rearrange("p (kc t) -> p kc t", t=128)),
                reads=[K("T", ti)], writes=[K("hT", tt)])

    def norm_tile(src_ap, src_keys, gb_t, gkey, tt, stc):
        norm_A(src_ap, src_keys, gb_t, gkey, tt, stc)
        norm_B(tt)

    def x_norm_A(s_, c_, tt):
        xi = rxt.nxt()
        xs = 12 + 2 * xi
        xap = fS[:, xs:xs + 2, :].rearrange("p a b -> p (a b)")
        xk = [K("fS", xs), K("fS", xs + 1)]
        r0 = c_ * CH + tt * 128
        sch.add(XQ[xi], lambda: nc.sync.dma_start(out=xap, in_=x[s_, r0:r0 + 128, :]), writes=xk)
        norm_A(xap, xk, g1b, K("g1b"), tt, 4 * tt)

    prefetched = set()

    HTK = [K("hT", t) for t in range(4)]

    def chunk(s, c):
        gc = s * NCH + c
        first = (gc == 0)
        for tt in range(4):
            if (gc, tt) in prefetched:
                continue
            x_norm_A(s, c, tt)
            norm_B(tt)
        if first:
            dump("hT", hT[:], HTK, [128, 8, CH], BF16)
        wq_, kq_ = wblk(gc, B_HQ)
        wf_, kf_ = wblk(gc, B_HF, 1)
        if c == 0:
            sch.add(DVE, lambda: nc.vector.memset(Sst[:], 0.0), writes=[K("Sst")])
        KQT = [K("bB", i) for i in range(8)]

        def h_stageA(tt):
            r = tt % 2
            qs, ks, ls = 0 + r, 2 + r, 4 + r
            pq = rP6.nxt()
            for kc in range(8):
                mm(Pf[pq][:, :], hT[:, kc, tt * 128:(tt + 1) * 128], wq_[:, kc, :], kc == 0, kc == 7,
                   [K("hT", tt), kq_], [K("P", pq)], kc == 7)
            sch.add(ACT, lambda: nc.scalar.activation(out=fS[:, qs, :], in_=Pf[pq][:, :], func=AF.Silu),
                    reads=[K("P", pq)], writes=[K("fS", qs)])
            pf = rP6.nxt()
            for kc in range(8):
                mm(Pf[pf][:, :], hT[:, kc, tt * 128:(tt + 1) * 128], wf_[:, kc, :], kc == 0, kc == 7,
                   [K("hT", tt), kf_], [K("P", pf)], kc == 7)
            sch.add(ACT, lambda: nc.scalar.activation(out=fS[:, ks, :], in_=Pf[pf][:, :], func=AF.Sigmoid, scale=-1.0),
                    reads=[K("P", pf)], writes=[K("fS", ks)])
            sch.add(DVE, lambda: nc.vector.tensor_tensor(out=fS[:, ks, :], in0=fS[:, ks, :], in1=omlb[:], op=ALU.mult),
                    reads=[K("fS", ks), K("omlb")], writes=[K("fS", ks)])
            sch.add(ACT, lambda: nc.scalar.activation(out=fS[:, ls, :], in_=fS[:, ks, :], func=AF.Ln, scale=-1.0,
                                                      bias=1.0),
                    reads=[K("fS", ks)], writes=[K("fS", ls)])

        def h_stageB(tt):
            r = tt % 2
            qs, ks, ls, es_, ns = 0 + r, 2 + r, 4 + r, 6 + r, 8 + r
            pa = rP6.nxt()
            mm(Pf[pa][:, :], Um[:], fS[:, ls, :], True, True, [K("Um"), K("fS", ls)], [K("P", pa)], True)
            sch.add(ACT, lambda: nc.scalar.activation(out=fS[:, es_, :], in_=Pf[pa][:, :], func=AF.Exp),
                    reads=[K("P", pa)], writes=[K("fS", es_)])
            sch.add(ACT, lambda: nc.scalar.activation(out=fS[:, ns, :], in_=Pf[pa][:, :], func=AF.Exp, scale=-1.0),
                    reads=[K("P", pa)], writes=[K("fS", ns)])
            pl = rP6.nxt()
            for h in range(4):
                mm(Pf[pl][:, h * 2:h * 2 + 2], fS[:, ls, h * 128:(h + 1) * 128], cind[:], True, True,
                   [K("fS", ls), K("cind")], [K("P", pl)], h == 3)
            sch.add(ACT, lambda: nc.scalar.activation(
                out=elast[:, tt, :, :].rearrange("p h j -> p (h j)"), in_=Pf[pl][:, 0:8], func=AF.Exp),
                reads=[K("P", pl)], writes=[K("elast", tt)])
            sch.add(DVE, lambda: nc.vector.tensor_tensor(out=bB[:, 12 + tt, :], in0=fS[:, ks, :], in1=fS[:, es_, :],
                                                         op=ALU.mult),
                    reads=[K("fS", ks), K("fS", es_)], writes=[K("bB", 12 + tt)])
            sch.add(DVE, lambda: nc.vector.tensor_tensor(out=qtil[:, r, :], in0=fS[:, qs, :], in1=fS[:, ns, :],
                                                         op=ALU.mult),
                    reads=[K("fS", qs), K("fS", ns)], writes=[K("qtil", r)])
            if first and tt == 0:
                dump("khat0", bB[:, 12, :], [K("bB", 12)], [128, 512], BF16)
                dump("qtil0", qtil[:, 0, :], [K("qtil", 0)], [128, 512], BF16)

        def h_stageB2(tt):
            r = tt % 2
            ti = rT.nxt()
            for h in range(4):
                tp(Tb[ti][:, h * 128:(h + 1) * 128], bB[:, 12 + tt, h * 128:(h + 1) * 128], [K("bB", 12 + tt)],
                   [K("T", ti)], False)
            for h in range(4):
                tp(Tb[ti][:, (4 + h) * 128:(5 + h) * 128], qtil[:, r, h * 128:(h + 1) * 128], [K("qtil", r)],
                   [K("T", ti)], h == 3)
            sch.add(ACT, lambda: nc.scalar.copy(out=bB[:, 0:8, tt * 128:(tt + 1) * 128],
                                                in_=Tb[ti][:, :].rearrange("p (a t) -> p a t", t=128)),
                    reads=[K("T", ti)], writes=[K("kqT", tt)] + KQT)

        hi_state = {}

        def h_hi(tt):
            if "w" not in hi_state:
                hi_state["w"] = wblk(gc, B_HI)
            wi_, ki_ = hi_state["w"]
            pv = rP6.nxt()
            for kc in range(8):
                mm(Pf[pv][:, :], hT[:, kc, tt * 128:(tt + 1) * 128], wi_[:, kc, :], kc == 0, kc == 7,
                   [K("hT", tt), ki_], [K("P", pv)], kc == 7)
            sch.add(ACT, lambda: nc.scalar.copy(out=bB[:, 8 + tt, :], in_=Pf[pv][:, :]),
                    reads=[K("P", pv)], writes=[K("bB", 8 + tt)])

        h_stageA(0); h_stageA(1); h_stageB(0); h_stageA(2); h_stageB(1); h_stageB2(0); h_stageA(3); h_stageB(2)
        h_stageB2(1); h_hi(0); h_stageB(3); h_hi(1); h_stageB2(2); h_hi(2); h_hi(3); h_stageB2(3)
        wg_, kg_ = wblk(gc, B_HG)
        for tt in range(4):
            ph = rP6.nxt()
            for kc in range(8):
                mm(Pf[ph][:, :], hT[:, kc, tt * 128:(tt + 1) * 128], wg_[:, kc, :], kc == 0, kc == 7,
                   [K("hT", tt), kg_], [K("P", ph)], kc == 7)
            sch.add(ACT, lambda ph=ph, tt=tt: nc.scalar.activation(out=fS[:, tt, :], in_=Pf[ph][:, :], func=AF.Silu),
                    reads=[K("P", ph)], writes=[K("fS", tt)])
            sch.add(POOL, lambda tt=tt: nc.gpsimd.tensor_tensor(out=fS[:, tt, :], in0=fS[:, tt, :], in1=gob[:],
                                                                op=ALU.mult),
                    reads=[K("fS", tt), K("gob")], writes=[K("fS", tt)])

        def h_post_tr(tt):
            r = tt % 2
            ti = rT.nxt()
            for kc in range(4):
                tp(Tb[ti][:, kc * 128:(kc + 1) * 128], oabf[:, r, kc * 128:(kc + 1) * 128], [K("oabf", r)],
                   [K("T", ti)], kc == 3)
            sch.add(ACT, lambda: nc.scalar.copy(out=oaT[:, :, tt * 128:(tt + 1) * 128],
                                                in_=Tb[ti][:, 0:512].rearrange("p (a t) -> p a t", t=128)),
                    reads=[K("T", ti)], writes=[K("oaT", tt)])

        def h_attn(tt):
            r = tt % 2
            po = 4 + r
            pat = rP3.nxt()
            for h in range(4):
                mm(Pf[pat][:, h * 128:(h + 1) * 128], bB[:, h, tt * 128:(tt + 1) * 128],
                   bB[:, 4 + h, tt * 128:(tt + 1) * 128], True, True, KQT, [K("P", pat)], h == 3)
            sch.add(DVE, lambda: nc.vector.scalar_tensor_tensor(
                out=attn[:, r, :].rearrange("p (h t) -> p h t", h=4),
                in0=Pf[pat][:, :].rearrange("p (h t) -> p h t", h=4), scalar=1e30,
                in1=bd[:].unsqueeze(1).broadcast_to([128, 4, 128]), op0=ALU.min, op1=ALU.mult),
                reads=[K("P", pat), K("bd")], writes=[K("attn", r)])
            for h in range(4):
                mm(Pf[po][:, h * 128:(h + 1) * 128], attn[:, r, h * 128:(h + 1) * 128],
                   bB[:, 8 + tt, h * 128:(h + 1) * 128], h == 0, False, [K("attn", r), K("bB", 8 + tt)],
                   [K("P", po)], False, skip_group_check=True)

        def h_rec(tt):
            r = tt % 2
            po = 4 + r
            for j in range(2):
                n = 2 * tt + j
                sd = n % 2
                sch.add(DVE, lambda j=j: nc.vector.tensor_tensor(
                    out=fS[:, 16, :].rearrange("p (h v) -> p h v", h=4),
                    in0=Sst[:].rearrange("p (h v) -> p h v", h=4),
                    in1=elast[:, tt, :, j:j + 1].broadcast_to([128, 4, 128]), op=ALU.mult),
                    reads=[K("Sst"), K("elast", tt)], writes=[K("fS", 16)])
                sch.add(ACT, lambda sd=sd: nc.scalar.copy(out=sdb[:, sd, :], in_=fS[:, 16, :]),
                        reads=[K("fS", 16)], writes=[K("sdb", sd)])
                for h in range(4):
                    mm(Pf[3][:, h * 128:(h + 1) * 128], bB[j * 64:(j + 1) * 64, 12 + tt, h * 128:(h + 1) * 128],
                       bB[j * 64:(j + 1) * 64, 8 + tt, h * 128:(h + 1) * 128], True, True,
                       [K("bB", 12 + tt), K("bB", 8 + tt)], [K("P", 3)], h == 3)
                for h in range(4):
                    t0 = tt * 128 + j * 64
                    mm(Pf[po][j * 64:(j + 1) * 64, h * 128:(h + 1) * 128], bB[:, 4 + h, t0:t0 + 64],
                       sdb[:, sd, h * 128:(h + 1) * 128], False, (j == 1 and h == 3), KQT + [K("sdb", sd)],
                       [K("P", po)], (j == 1 and h == 3), skip_group_check=True)
                sch.add(DVE, lambda: nc.vector.tensor_tensor(out=Sst[:], in0=fS[:, 16, :], in1=Pf[3][:, :], op=ALU.add),
                        reads=[K("fS", 16), K("P", 3)], writes=[K("Sst")])

        def h_post(tt):
            r = tt % 2
            po = 4 + r
            so = 16 + 16 * r
            for h in range(4):
                sch.add(ACT, lambda h=h: nc.scalar.activation(
                    out=bB[:, 22, h * 128:(h + 1) * 128], in_=Pf[po][:, h * 128:(h + 1) * 128], func=AF.Square,
                    scale=float(1.0 / np.sqrt(128.0)), accum_out=st[:, so + h:so + h + 1]),
                    reads=[K("P", po)], writes=[K("bB", 22), K("st", so + h)])
            sch.add(ACT, lambda: nc.scalar.activation(out=st[:, so + 8:so + 12], in_=st[:, so:so + 4], func=AF.Ln,
                                                      bias=epsb[:, 0:1]),
                    reads=[K("st", so + h) for h in range(4)] + [K("epsb")], writes=[K("st", so + 8)])
            sch.add(ACT, lambda: nc.scalar.activation(out=st[:, so + 12:so + 16], in_=st[:, so + 8:so + 12],
                                                      func=AF.Exp, scale=-0.5),
                    reads=[K("st", so + 8)], writes=[K("st", so + 12)])
            for h in range(4):
                sch.add(DVE, lambda h=h: nc.vector.scalar_tensor_tensor(
                    out=oabf[:, r, h * 128:(h + 1) * 128], in0=Pf[po][:, h * 128:(h + 1) * 128],
                    scalar=st[:, so + 12 + h:so + 13 + h], in1=fS[:, tt, h * 128:(h + 1) * 128], op0=ALU.mult,
                    op1=ALU.mult),
                    reads=[K("P", po), K("st", so + 12), K("fS", tt)], writes=[K("oabf", r)])
            if first and tt == 0:
                dump("oa0", oabf[:, 0, :], [K("oabf", 0)], [128, 512], BF16)

        h_attn(0)
        for tt in range(4):
            h_rec(tt)
            if tt < 3:
                h_attn(tt + 1)
            h_post(tt)
            if tt >= 1:
                h_post_tr(tt - 1)
        QTE = [K("bB", 16 + i) for i in range(8)]
        qTe = bB[:, 16:24, :]
        wmk, kmk = wblk(gc, B_MK)
        wstore = {"k": (wmk, kmk)}

        def m_stageA(kind, tt, idx):
            is_q = kind == "q"
            if is_q and "q" not in wstore:
                wstore["q"] = wblk(gc, B_MQ)
            wv, wk = wstore[kind]
            par = idx % 3
            sq, mn = [4, 5, 10][par], [6, 7, 11][par]
            sc = 64 + 16 * par
            pm = rP.nxt()
            for kc in range(8):
                mm(Pf[pm][:, :], hT[:, kc, tt * 128:(tt + 1) * 128], wv[:, kc, :], kc == 0, kc == 7,
                   [K("hT", tt), wk], [K("P", pm)], kc == 7)
            sch.add(ACT, lambda: nc.scalar.activation(out=fS[:, sq, :], in_=Pf[pm][:, :], func=AF.Square, scale=0.125),
                    reads=[K("P", pm)], writes=[K("fS", sq)])
            sch.add(DVE, lambda: nc.vector.tensor_reduce(out=st[:, sc:sc + 8],
                                                         in_=fS[:, sq, :].rearrange("p (h d) -> p h d", h=8),
                                                         axis=AX.X, op=ALU.add),
                    reads=[K("fS", sq)], writes=[K("st", sc)])
            sch.add(ACT, lambda: nc.scalar.activation(out=st[:, sc + 8:sc + 16], in_=st[:, sc:sc + 8], func=AF.Ln,
                                                      bias=epsb[:, 0:1]),
                    reads=[K("st", sc), K("epsb")], writes=[K("st", sc + 8)])
            sch.add(ACT, lambda: nc.scalar.activation(out=st[:, sc:sc + 8], in_=st[:, sc + 8:sc + 16], func=AF.Exp,
                                                      scale=-0.5),
                    reads=[K("st", sc + 8)], writes=[K("st", sc)])
            sch.add(DVE, lambda: nc.vector.tensor_tensor(
                out=fS[:, mn, :].rearrange("p (h d) -> p h d", h=8),
                in0=Pf[pm][:, :].rearrange("p (h d) -> p h d", h=8),
                in1=st[:, sc:sc + 8].unsqueeze(2).broadcast_to([128, 8, 64]), op=ALU.mult),
                reads=[K("P", pm), K("st", sc)], writes=[K("fS", mn)])
            gvec, gkey = (gq8, K("gq8")) if is_q else (gkb, K("gkb"))
            m3 = fS[:, mn, :].rearrange("p (h d) -> p h d", h=8)
            sch.add(DVE, lambda: nc.vector.tensor_tensor(out=m3, in0=m3,
                                                         in1=gvec[:].unsqueeze(1).broadcast_to([128, 8, 64]),
                                                         op=ALU.mult),
                    reads=[K("fS", mn), gkey], writes=[K("fS", mn)])

        def m_stageB(kind, tt, idx):
            is_q = kind == "q"
            par = idx % 3
            mn, tB = [6, 7, 11][par], [8, 9, 17][par]
            tile_i = c * 4 + tt
            m4 = fS[:, mn, :].rearrange("p (h a d) -> p h a d", h=8, a=2)
            tB4 = fS[:, tB, :].rearrange("p (h a d) -> p h a d", h=8, a=2)
            cosb = cs[:, tile_i, 0, :]
            sinb3 = cs[:, tile_i, 1, :].unsqueeze(1).broadcast_to([128, 8, 32])
            sch.add(POOL, lambda: nc.gpsimd.tensor_tensor(out=tB4[:, :, 0, :], in0=m4[:, :, 1, :], in1=sinb3,
                                                          op=ALU.mult),
                    reads=[K("fS", mn), K("cs")], writes=[K("fS", tB)])
            sch.add(POOL, lambda: nc.gpsimd.tensor_tensor(out=tB4[:, :, 1, :], in0=m4[:, :, 0, :], in1=sinb3,
                                                          op=ALU.mult),
                    reads=[K("fS", mn), K("cs")], writes=[K("fS", tB)])
            sch.add(DVE, lambda: nc.vector.tensor_tensor(
                out=m4, in0=m4, in1=cosb.unsqueeze(1).unsqueeze(1).broadcast_to([128, 8, 2, 32]), op=ALU.mult),
                reads=[K("fS", mn), K("cs"), K("fS", tB)], writes=[K("fS", mn)])
            ro = idx % 2
            ro4 = ropeo[:, ro, :].rearrange("p (h a d) -> p h a d", h=8, a=2)
            sch.add(POOL, lambda: nc.gpsimd.tensor_tensor(out=ro4[:, :, 0, :], in0=m4[:, :, 0, :], in1=tB4[:, :, 0, :],
                                                          op=ALU.subtract),
                    reads=[K("fS", mn), K("fS", tB)], writes=[K("ropeo", ro)])
            sch.add(POOL, lambda: nc.gpsimd.tensor_tensor(out=ro4[:, :, 1, :], in0=m4[:, :, 1, :], in1=tB4[:, :, 1, :],
                                                          op=ALU.add),
                    reads=[K("fS", mn), K("fS", tB)], writes=[K("ropeo", ro)])

        def m_stageB2(kind, tt, idx):
            is_q = kind == "q"
            ro = idx % 2
            tile_i = c * 4 + tt
            ti = rT.nxt()
            for h in range(8):
                tp(Tb[ti][0:64, h * 128:(h + 1) * 128], ropeo[:, ro, h * 64:(h + 1) * 64], [K("ropeo", ro)],
                   [K("T", ti)], h == 7)
            src = Tb[ti][0:64, :].rearrange("p (h t) -> p h t", h=8)
            if is_q:
                sch.add(DVE, lambda: nc.vector.tensor_copy(out=qTe[0:64, :, tt * 128:(tt + 1) * 128], in_=src),
                        reads=[K("T", ti)], writes=QTE + [K("qTe", tt)])
            else:
                p0 = c * CH + tt * 128
                sch.add(DVE, lambda: nc.vector.tensor_copy(out=kTe[0:64, :, p0:p0 + 128], in_=src),
                        reads=[K("T", ti)], writes=[K("kTe", tile_i)])
                if tt % 2 == 1:
                    blk = 2 * c + tt // 2
                    sch.add(DVE, lambda: nc.vector.tensor_reduce(out=kmf[0:64, :],
                                                                 in_=kTe[0:64, :, blk * 256:(blk + 1) * 256],
                                                                 axis=AX.X, op=ALU.add),
                            reads=[K("kTe", 2 * blk), K("kTe", 2 * blk + 1)], writes=[K("kmf")])
                    sch.add(DVE, lambda: nc.vector.tensor_scalar(out=kmT[0:64, :, blk:blk + 1],
                                                                 in0=kmf[0:64, :].unsqueeze(2), scalar1=1.0 / 256.0,
                                                                 scalar2=None, op0=ALU.mult),
                            reads=[K("kmf")], writes=[K("kmT")])

        def m_stageC(tt):
            qb = 2 * c + tt // 2
            bi = tt % 2
            if qb >= 4:
                pg = rP.nxt()
                for h in range(8):
                    mm(Pf[pg][:, h * 8:h * 8 + qb], qTe[0:64, h, tt * 128:(tt + 1) * 128], kmT[0:64, h, 0:qb], True, True,
                       [K("qTe", tt), K("kmT")], [K("P", pg)], h == 7)
                sch.add(ACT, lambda: nc.scalar.copy(out=gsm[:], in_=Pf[pg][:, 0:64]),
                        reads=[K("P", pg)], writes=[K("gsm")])
                g3 = gsm[:].rearrange("p (h j) -> p h j", h=8)[:, :, 0:qb]
                c4 = cmpb[:, 0:8 * qb * qb].rearrange("p (h j k) -> p h j k", h=8, j=qb)
                sch.add(DVE, lambda: nc.vector.tensor_tensor(
                    out=c4, in0=g3.unsqueeze(2).broadcast_to([128, 8, qb, qb]),
                    in1=g3.unsqueeze(3).broadcast_to([128, 8, qb, qb]), op=ALU.is_gt),
                    reads=[K("gsm")], writes=[K("cmpb")])
                cn3 = cnt[:, 0:8 * qb].rearrange("p (h j) -> p h j", h=8)
                sch.add(DVE, lambda: nc.vector.tensor_reduce(out=cn3, in_=c4, axis=AX.X, op=ALU.add),
                        reads=[K("cmpb")], writes=[K("cnt")])
                sch.add(DVE, lambda: nc.vector.tensor_scalar(
                    out=bias[:, bi, :, 64:64 + qb], in0=cn3, scalar1=2.5, scalar2=-BIG, op0=ALU.is_gt, op1=ALU.mult),
                    reads=[K("cnt")], writes=[K("bias", bi)])
                sch.add(POOL, lambda: nc.gpsimd.memset(bias[:, bi, :, 64 + qb:65 + qb], 0.0), writes=[K("bias", bi)])
            else:
                sch.add(POOL, lambda: nc.gpsimd.memset(bias[:, bi, :, 64:65 + qb], 0.0), writes=[K("bias", bi)])
            if qb < 7:
                sch.add(POOL, lambda: nc.gpsimd.memset(bias[:, bi, :, 65 + qb:72], -BIG), writes=[K("bias", bi)])
            for hh in range(2):
                pb = rP.nxt()
                for h4 in range(4):
                    h = hh * 4 + h4
                    mm(Pf[pb][0:72, h4 * 128:(h4 + 1) * 128], bias[:, bi, h, :], ident[:], True, True,
                       [K("bias", bi), K("ident")], [K("P", pb)], h4 == 3)
                sch.add(ACT, lambda pb=pb, hh=hh: nc.scalar.copy(
                    out=qTe[64:72, hh * 4:hh * 4 + 4, tt * 128:(tt + 1) * 128],
                    in_=Pf[pb][64:72, :].rearrange("p (h t) -> p h t", h=4)),
                    reads=[K("P", pb)], writes=QTE + [K("qTeb", tt)])

        def m_v(tt):
            if "v" not in wstore:
                wstore["v"] = wblk(gc, B_MV)
            wmv, kmv = wstore["v"]
            tile_i = c * 4 + tt
            pv = rP.nxt()
            for kc in range(8):
                mm(Pf[pv][:, :], hT[:, kc, tt * 128:(tt + 1) * 128], wmv[:, kc, :], kc == 0, kc == 7,
                   [K("hT", tt), kmv], [K("P", pv)], kc == 7)
            sch.add(ACT, lambda: nc.scalar.copy(
                out=vext[:, tile_i, :, 0:64], in_=Pf[pv][:, :].rearrange("p (h d) -> p h d", h=8)),
                reads=[K("P", pv)], writes=[K("vext", tile_i)])

        items2 = [("k", t) for t in range(4)] + [("q", t) for t in range(4)]
        m_stageA("k", 0, 0)
        h_post_tr(3)
        m_stageA("k", 1, 1)
        m_stageB("k", 0, 0)
        for i in range(2, 8):
            m_stageA(items2[i][0], items2[i][1], i)
            m_stageB(items2[i - 1][0], items2[i - 1][1], i - 1)
            m_stageB2(items2[i - 2][0], items2[i - 2][1], i - 2)
        m_v(0)
        m_stageB("q", 3, 7)
        m_stageB2("q", 2, 6)
        m_stageC(0)
        m_v(1)
        m_stageB2("q", 3, 7)
        m_stageC(1)
        m_v(2)
        m_stageC(2)
        m_v(3)
        m_stageC(3)
        if first:
            dump("kTe", kTe[0:72, :, 0:512], [K("kTe", i) for i in range(4)] + [K("kTe_ind")], [72, 8, 512], BF16)
            dump("qTe", qTe[0:72, :, :], QTE, [72, 8, 512], BF16)
        nkt = 4 * c + 4
        OB = [K("bB", 8 + i) for i in range(4)]
        obv = bB[:, 8:12, :].rearrange("p t (h d) -> p t h d", h=8)
        items = [(h, kt) for h in range(8) for kt in range(nkt)]
        pend = None

        def att_qk(h, kt):
            n0 = max(0, kt * 128 - c * CH)
            pst = rP.nxt()
            pi = rpt.nxt()
            mm(Pf[pst][:, n0:512], kTe[0:72, h, kt * 128:(kt + 1) * 128], qTe[0:72, h, n0:512], True, True,
               [K("kTe", kt), K("kTe_ind")] + QTE, [K("P", pst)], True)
            sch.add(ACT, lambda pst=pst, pi=pi, n0=n0: nc.scalar.activation(out=pt[:, pi, n0:512],
                                                                          in_=Pf[pst][:, n0:512], func=AF.Exp),
                    reads=[K("P", pst)], writes=[K("pt", pi)])
            if kt * 128 >= c * CH:
                sch.add(POOL, lambda pi=pi, n0=n0: nc.gpsimd.tensor_tensor(out=pt[:, pi, n0:n0 + 128],
                                                                           in0=pt[:, pi, n0:n0 + 128], in1=tri[:],
                                                                           op=ALU.mult),
                        reads=[K("pt", pi), K("tri")], writes=[K("pt", pi)])
            return (h, kt, n0, pi)

        def att_pv(h, kt, n0, pi):
            pob = 4 + (h % 2)
            for sub in range(n0 // 128, 4):
                last = (kt == nkt - 1 and sub == 3)
                mm(Pf[pob][:, sub * 65:(sub + 1) * 65], pt[:, pi, sub * 128:(sub + 1) * 128], vext[:, kt, h, :],
                   (kt == 0 and sub == 0), last, [K("pt", pi), K("vext", kt), K("vext_one")], [K("P", pob)],
                   sub == 3, skip_group_check=True)
            if kt == nkt - 1:
                rd = h % 2
                po3 = Pf[pob][:, 0:260].rearrange("p (s e) -> p s e", e=65)
                sch.add(DVE, lambda po3=po3, rd=rd: nc.vector.reciprocal(out=rden[:, rd, :].unsqueeze(2),
                                                                         in_=po3[:, :, 64:65]),
                        reads=[K("P", pob)], writes=[K("rden", rd)])
                sch.add(DVE, lambda po3=po3, rd=rd, h=h: nc.vector.tensor_tensor(
                    out=obv[:, :, h, :], in0=po3[:, :, 0:64],
                    in1=rden[:, rd, :].unsqueeze(2).broadcast_to([128, 4, 64]), op=ALU.mult),
                    reads=[K("P", pob), K("rden", rd)], writes=OB)

        for (h, kt) in items:
            cur = att_qk(h, kt)
            if pend is not None:
                att_pv(*pend)
            pend = cur
        att_pv(*pend)
        if first:
            dump("ob", bB[:, 8:12, :], OB, [128, 4, 512], BF16)
        for tt in range(4):
            ti = rT.nxt()
            for kc in range(4):
                tp(Tb[ti][:, kc * 128:(kc + 1) * 128], bB[:, 8 + tt, kc * 128:(kc + 1) * 128], [K("bB", 8 + tt)],
                   [K("T", ti)], kc == 3)
            sch.add(ACT, lambda ti=ti, tt=tt: nc.scalar.copy(out=obT[:, :, tt * 128:(tt + 1) * 128],
                                                             in_=Tb[ti][:, 0:512].rearrange("p (a t) -> p a t", t=128)),
                    reads=[K("T", ti)], writes=[K("obT", tt)])
        OAT = [K("oaT", t) for t in range(4)]
        OBT = [K("obT", t) for t in range(4)]
        MIX = [K("bB", i) for i in range(8)]
        gab_ids = [B_GAB0, B_GAB1, B_GAB2, B_GAB3]
        for qd in range(4):
            wg2, kg2 = wblk(gc, gab_ids[qd], 1)
            wab, kab = wblk(gc, B_WAB0 if qd < 2 else B_WAB1, 1)
            for e in range(2):
                i = 2 * qd + e
                col = (i % 4) * 128
                res = []
                for br in range(2):
                    pgt = rP.nxt()
                    for kc in range(8):
                        mm(Pf[pgt][:, :], wg2[:, kc, br * 256 + e * 128: br * 256 + (e + 1) * 128], hT[:, kc, :],
                           kc == 0, kc == 7, HTK + [kg2], [K("P", pgt)], kc == 7)
                    pab = rP.nxt()
                    src = oaT if br == 0 else obT
                    srk = OAT if br == 0 else OBT
                    for kc in range(4):
                        mm(Pf[pab][:, :], wab[:, br * 4 + kc, col:col + 128], src[:, kc, :], kc == 0, kc == 3,
                           srk + [kab], [K("P", pab)], kc == 3)
                    ss_, ms_ = 0 + br, 2 + br
                    sch.add(ACT, lambda pgt=pgt, ss_=ss_: nc.scalar.activation(out=fS[:, ss_, :], in_=Pf[pgt][:, :],
                                                                             func=AF.Sigmoid),
                            reads=[K("P", pgt)], writes=[K("fS", ss_)])
                    sch.add(DVE, lambda pab=pab, ss_=ss_, ms_=ms_: nc.vector.tensor_tensor(
                        out=fS[:, ms_, :], in0=fS[:, ss_, :], in1=Pf[pab][:, :], op=ALU.mult),
                        reads=[K("fS", ss_), K("P", pab)], writes=[K("fS", ms_)])
                sch.add(POOL, lambda i=i: nc.gpsimd.tensor_tensor(out=bB[:, i, :], in0=fS[:, 2, :], in1=fS[:, 3, :],
                                                                  op=ALU.add),
                        reads=[K("fS", 2), K("fS", 3)], writes=[K("bB", i)])
        if first:
            dump("mixT", bB[:, 0:8, :], MIX, [128, 8, 512], BF16)
        wo0, ko0 = wblk(gc, B_WO0)
        wo1, ko1 = wblk(gc, B_WO1, 1)

        def w_out_tile(tt):
            xi = rxt.nxt()
            xs = 12 + 2 * xi
            xap = fS[:, xs:xs + 2, :].rearrange("p a b -> p (a b)")
            xk = [K("fS", xs), K("fS", xs + 1)]
            r0 = c * CH + tt * 128
            sch.add(XQ[xi], lambda: nc.sync.dma_start(out=xap, in_=x[s, r0:r0 + 128, :]), writes=xk)
            for nh, (wo, ko) in enumerate([(wo0, ko0), (wo1, ko1)]):
                pw = rP.nxt()
                for kc in range(8):
                    mm(Pf[pw][:, :], bB[:, kc, tt * 128:(tt + 1) * 128], wo[:, kc, :], kc == 0, kc == 7,
                       MIX + [ko], [K("P", pw)], kc == 7)
                sch.add(DVE, lambda pw=pw, nh=nh: nc.vector.tensor_tensor(
                    out=xmid[:, tt, nh * 512:(nh + 1) * 512], in0=xap[:, nh * 512:(nh + 1) * 512], in1=Pf[pw][:, :],
                    op=ALU.add), reads=xk + [K("P", pw)], writes=[K("xmid", tt, nh)])

        def n2A(tt):
            norm_A(xmid[:, tt, :], [K("xmid", tt, 0), K("xmid", tt, 1)], g2b, K("g2b"), tt, 4 * tt)

        w_out_tile(0); w_out_tile(1); n2A(0); w_out_tile(2); n2A(1); norm_B(0); w_out_tile(3)
        if first:
            dump("xmid", xmid[:], [K("xmid", t, n) for t in range(4) for n in range(2)], [128, 4, D])
        n2A(2); norm_B(1); n2A(3); norm_B(2); norm_B(3)
        par = c % 2
        if c == 0:
            sch.add(POOL, lambda: nc.gpsimd.memset(halo[:, 0, :, :], 0.0), writes=[K("halo", 0)])
        GT = [K("bB", i) for i in range(NKF)]
        ngc = gc + 1
        has_next = ngc < NSEQ * NCH
        for u in range(11):
            wu, ku = wblk(gc, B_UP0 + u)
            if has_next and u in (7, 9):
                ptt = 0 if u == 7 else 1
                x_norm_A(ngc // NCH, ngc % NCH, ptt)
            for e in range(2):
                i = 2 * u + e
                rb = i % 2
                ub = 0 + 2 * rb
                ac = 4 + rb
                gl = 6 + rb
                ubuf = fS[:, ub:ub + 2, :].rearrange("p a b -> p (a b)")
                UBK = [K("fS", ub), K("fS", ub + 1)]
                pu = rP.nxt()
                for kc in range(8):
                    mm(Pf[pu][:, :], wu[:, kc, e * 128:(e + 1) * 128], hT[:, kc, :], kc == 0, kc == 7, HTK + [ku],
                       [K("P", pu)], kc == 7)
                pv2 = rP.nxt()
                for kc in range(8):
                    mm(Pf[pv2][:, :], wu[:, kc, 256 + e * 128:256 + (e + 1) * 128], hT[:, kc, :], kc == 0, kc == 7,
                       HTK + [ku], [K("P", pv2)], kc == 7)
                sch.add(ACT, lambda pu=pu, ubuf=ubuf: nc.scalar.copy(out=ubuf[:, 2:514], in_=Pf[pu][:, :]),
                        reads=[K("P", pu)], writes=UBK)
                sch.add(POOL, lambda ubuf=ubuf, i=i: nc.gpsimd.tensor_copy(out=ubuf[:, 0:2], in_=halo[:, par, i, :]),
                        reads=[K("halo", par)], writes=UBK)
                sch.add(POOL, lambda ubuf=ubuf, i=i: nc.gpsimd.tensor_copy(out=halo[:, 1 - par, i, :],
                                                                           in_=ubuf[:, 512:514]),
                        reads=UBK, writes=[K("halo", 1 - par)])
                sch.add(DVE, lambda ubuf=ubuf, i=i, ac=ac: nc.vector.tensor_scalar(
                    out=fS[:, ac, :], in0=ubuf[:, 2:514], scalar1=cw[:, 2, i:i + 1], scalar2=cb[:, i:i + 1],
                    op0=ALU.mult, op1=ALU.add), reads=UBK + [K("cw"), K("cb")], writes=[K("fS", ac)])
                sch.add(DVE, lambda ubuf=ubuf, i=i, ac=ac: nc.vector.scalar_tensor_tensor(
                    out=fS[:, ac, :], in0=ubuf[:, 1:513], scalar=cw[:, 1, i:i + 1], in1=fS[:, ac, :], op0=ALU.mult,
                    op1=ALU.add), reads=UBK + [K("cw"), K("fS", ac)], writes=[K("fS", ac)])
                sch.add(DVE, lambda ubuf=ubuf, i=i, ac=ac: nc.vector.scalar_tensor_tensor(
                    out=fS[:, ac, :], in0=ubuf[:, 0:512], scalar=cw[:, 0, i:i + 1], in1=fS[:, ac, :], op0=ALU.mult,
                    op1=ALU.add), reads=UBK + [K("cw"), K("fS", ac)], writes=[K("fS", ac)])
                sch.add(ACT, lambda ac=ac, gl=gl: nc.scalar.activation(out=fS[:, gl, :], in_=fS[:, ac, :], func=AF.Gelu),
                        reads=[K("fS", ac)], writes=[K("fS", gl)])
                sch.add(DVE, lambda gl=gl, pv2=pv2, i=i: nc.vector.tensor_tensor(out=bB[:, i, :], in0=fS[:, gl, :],
                                                                                 in1=Pf[pv2][:, :], op=ALU.mult),
                        reads=[K("fS", gl), K("P", pv2)], writes=[K("bB", i)])
        if first:
            dump("gT", bB[:, 0:NKF, :], GT, [128, NKF, 512], BF16)
        if has_next:
            for ptt in (0, 1):
                norm_B(ptt)
                prefetched.add((ngc, ptt))
        for nh in range(2):
            banks = [0, 1, 2, 3] if nh == 0 else [4, 5, 0, 1]
            for pi_, (k0, nk) in enumerate(DN_P):
                wd, kd = wblk(gc, B_DN0 + nh * 3 + pi_)
                for tt in range(4):
                    for kk in range(nk):
                        kc = k0 + kk
                        mm(Pf[banks[tt]][:, :], bB[:, kc, tt * 128:(tt + 1) * 128], wd[:, kk, :], kc == 0,
                           kc == NKF - 1, [K("bB", kc), kd], [K("P", banks[tt])], kk == nk - 1)
            for tt in range(4):
                sch.add(DVE, lambda nh=nh, tt=tt, b=banks[tt]: nc.vector.tensor_tensor(
                    out=xmid[:, tt, nh * 512:(nh + 1) * 512], in0=xmid[:, tt, nh * 512:(nh + 1) * 512],
                    in1=Pf[b][:, :], op=ALU.add),
                    reads=[K("xmid", tt, nh), K("P", banks[tt])], writes=[K("xmid", tt, nh)])
        sch.add(OQ, lambda: nc.gpsimd.dma_start(
            out=out[s, c * CH:(c + 1) * CH, :].rearrange("(t p) d -> p t d", p=128), in_=xmid[:]),
            reads=[K("xmid", t, n) for t in range(4) for n in range(2)])

    for s in range(NSEQ):
        for c in range(NCH):
            chunk(s, c)
    stats = sch.finalize(nc.sync)
    es.close()
    return nc, dumps, stats


_CACHE = {}


def kernel(**inputs):
    nseq = 32 // NCORES
    if "nc" not in _CACHE:
        _CACHE["nc"] = build(nseq)[0]
    nc = _CACHE["nc"]
    consts = host_consts()
    x = np.ascontiguousarray(np.asarray(inputs["x"], dtype=np.float32))
    shared = {
        "norm1_g": np.asarray(inputs["norm1_g"], np.float32).reshape(1, D),
        "norm2_g": np.asarray(inputs["norm2_g"], np.float32).reshape(1, D),
        "w_in": np.asarray(inputs["w_in"], np.float32).reshape(D, 5632),
        "hg_lb_logits": np.asarray(inputs["hg_lb_logits"], np.float32).reshape(2, 512),
        "hg_onorm_g": np.asarray(inputs["hg_onorm_g"], np.float32).reshape(1, 512),
        "q_norm_g": np.asarray(inputs["q_norm_g"], np.float32).reshape(1, 64),
        "k_norm_g": np.asarray(inputs["k_norm_g"], np.float32).reshape(1, 64),
        "w_a": np.asarray(inputs["w_a"], np.float32).reshape(512, D),
        "w_b": np.asarray(inputs["w_b"], np.float32).reshape(512, D),
        "w_out": np.asarray(inputs["w_out"], np.float32).reshape(D, D),
        "w_up": np.asarray(inputs["w_up"], np.float32).reshape(D, 2 * DFF),
        "conv_w": np.asarray(inputs["conv_w"], np.float32).reshape(3, DFF),
        "conv_b": np.asarray(inputs["conv_b"], np.float32).reshape(1, DFF),
        "w_down": np.asarray(inputs["w_down"], np.float32).reshape(DFF, D),
    }
    shared.update(consts)
    in_maps = []
    for i in range(NCORES):
        m = dict(shared)
        m["x"] = x[i * nseq:(i + 1) * nseq]
        in_maps.append(m)
    res = run_bass_kernel_spmd(nc, in_maps, core_ids=list(range(NCORES)))
    return np.concatenate([np.asarray(r["out"]) for r in res.results], axis=0).astype(np.float32)
```

```python
from contextlib import ExitStack
import numpy as np
import concourse.bass as bass
import concourse.mybir as mybir
from concourse.bass_utils import run_bass_kernel_spmd

F32 = mybir.dt.float32
BF16 = mybir.dt.bfloat16
AF = mybir.ActivationFunctionType
ALU = mybir.AluOpType
AX = mybir.AxisListType

NCORES = 8
S = 2048
D = 1024
CH = 512
NCH = S // CH
DFF = 2816
NKF = DFF // 128
EPS = 1e-6
BIG = 30000.0
NBLK = 32
RING = 3

(B_HQ, B_HF, B_HI, B_HG, B_MK, B_MQ, B_MV, B_GAB0, B_WAB0, B_GAB1, B_GAB2, B_WAB1, B_GAB3,
 B_WO0, B_WO1) = range(15)
B_UP0 = 15
B_DN0 = 26


class Q:
    def __init__(self, name, issuer, sem, inc, kind):
        self.name, self.issuer, self.sem, self.inc, self.kind = name, issuer, sem, inc, kind
        self.nsig = 0
        self.last = None


class Sched:
    def __init__(self, nc):
        self.nc = nc
        self.ins = []
        self.queues = []

    def queue(self, name, issuer, sem, inc, kind):
        q = Q(name, issuer, sem, inc, kind)
        self.queues.append(q)
        return q

    def add(self, q, fn, reads=(), writes=(), sig=True):
        self.ins.append((q, fn, tuple(reads), tuple(writes), sig or q.kind == 'dma'))

    def finalize(self, final_issuer):
        ins = self.ins
        n = len(ins)
        sigval = [0] * n
        nextsig = [None] * n
        for i, (q, fn, r, w, sig) in enumerate(ins):
            if sig:
                q.nsig += 1
                sigval[i] = q.nsig
        lastsig = {}
        for i in range(n - 1, -1, -1):
            q = ins[i][0]
            if ins[i][4]:
                lastsig[q] = i
            nextsig[i] = lastsig.get(q)
        writers, readers = {}, {}
        clocks = {}
        iclk = [None] * n
        nwaits = 0
        for i, (q, fn, rds, wrs, sig) in enumerate(ins):
            deps = set()
            for k in rds:
                for qq, j in writers.get(k, {}).items():
                    if qq is q and q.kind == 'pe':
                        continue
                    deps.add(j)
            for k in wrs:
                for qq, j in writers.get(k, {}).items():
                    if qq is q and q.kind != 'dma':
                        continue
                    deps.add(j)
                for qq, j in readers.get(k, {}).items():
                    if qq is q and q.kind != 'dma':
                        continue
                    deps.add(j)
            if q.kind == 'dma' and q.last is not None:
                deps.add(q.last)
            clk = clocks.setdefault(id(q.issuer), {})
            for j in sorted(deps):
                js = nextsig[j]
                assert js is not None and js < i, f"dep signal after waiter: ins {i} dep {j} sig {js}"
                qj = ins[js][0]
                val = sigval[js] * qj.inc
                if clk.get(qj, 0) >= val:
                    continue
                q.issuer.wait_ge(qj.sem, val)
                nwaits += 1
                for qq, v in iclk[js].items():
                    if clk.get(qq, 0) < v:
                        clk[qq] = v
            r = fn()
            if sig:
                r.then_inc(q.sem, q.inc)
                c2 = dict(clk)
                c2[q] = sigval[i] * q.inc
                iclk[i] = c2
            for k in rds:
                readers.setdefault(k, {})[q] = i
            for k in wrs:
                writers.setdefault(k, {})[q] = i
            if q.kind == 'dma':
                q.last = i
        for q in self.queues:
            if q.nsig:
                final_issuer.wait_ge(q.sem, q.nsig * q.inc)
        return n, nwaits


def host_consts():
    c = {}
    c["c_ident"] = np.eye(128, dtype=np.float32)
    s = np.arange(128)[:, None]
    t = np.arange(128)[None, :]
    same = (s // 64) == (t // 64)
    c["c_U"] = ((s > t) & same).astype(np.float32)
    c["c_bd"] = ((s <= t) & same).astype(np.float32)
    c["c_tri"] = (s <= t).astype(np.float32)
    c["c_cind"] = np.stack([(np.arange(128) < 64), (np.arange(128) >= 64)], 1).astype(np.float32)
    half = 32
    inv = 1.0 / (10000.0 ** (np.arange(half, dtype=np.float32) * 2.0 / 64))
    pos = np.arange(S, dtype=np.float32)
    ang = pos[:, None] * inv[None, :]
    cs = np.stack([np.cos(ang), np.sin(ang)], 1).astype(np.float32)
    c["c_cs"] = np.ascontiguousarray(cs.reshape(16, 128, 2, 32).transpose(1, 0, 2, 3))
    kind = (np.arange(S)[None, :] // 256 == np.arange(8)[:, None]).astype(np.float32)
    c["c_kind"] = kind
    return c


def build(NSEQ, dump_names=()):
    nc = bass.Bass("TRN2", target_bir_lowering=False)
    es = ExitStack()

    def din(name, shape, dt=F32):
        return nc.dram_tensor(name, list(shape), dt, kind="ExternalInput").ap()

    x = din("x", [NSEQ, S, D])
    norm1_g = din("norm1_g", [1, D]); norm2_g = din("norm2_g", [1, D])
    w_in = din("w_in", [D, 5632]); hg_lb = din("hg_lb_logits", [2, 512])
    hg_on = din("hg_onorm_g", [1, 512]); qng = din("q_norm_g", [1, 64]); kng = din("k_norm_g", [1, 64])
    w_a = din("w_a", [512, D]); w_b = din("w_b", [512, D]); w_out = din("w_out", [D, D])
    w_up = din("w_up", [D, 2 * DFF]); conv_w = din("conv_w", [3, DFF]); conv_b = din("conv_b", [1, DFF])
    w_down = din("w_down", [DFF, D])
    c_ident = din("c_ident", [128, 128]); c_U = din("c_U", [128, 128]); c_bd = din("c_bd", [128, 128])
    c_tri = din("c_tri", [128, 128]); c_cind = din("c_cind", [128, 2]); c_cs = din("c_cs", [128, 16, 2, 32])
    c_kind = din("c_kind", [8, S])
    out = nc.dram_tensor("out", [NSEQ, S, D], F32, kind="ExternalOutput").ap()
    wbf = nc.dram_tensor("wbf", [NBLK, 128, 4096], BF16).ap()
    dumps = {}

    def sb(name, shape, dt):
        return es.enter_context(nc.sbuf_tensor(name, list(shape), dt))

    def ps(name, shape, dt):
        return es.enter_context(nc.psum_tensor(name, list(shape), dt))

    def sem(name):
        return es.enter_context(nc.semaphore(name))

    sch = Sched(nc)
    PE = sch.queue("pe", nc.tensor, sem("s_pe"), 1, 'pe')
    ACT = sch.queue("act", nc.scalar, sem("s_act"), 1, 'cmp')
    DVE = sch.queue("dve", nc.vector, sem("s_dve"), 1, 'cmp')
    POOL = sch.queue("pool", nc.gpsimd, sem("s_pool"), 1, 'cmp')
    WQ = [sch.queue(f"wq{i}", nc.sync, sem(f"s_wq{i}"), 16, 'dma') for i in range(RING)]
    XQ = [sch.queue(f"xq{i}", nc.sync, sem(f"s_xq{i}"), 16, 'dma') for i in range(2)]
    OQ = sch.queue("oq", nc.gpsimd, sem("s_oq"), 16, 'dma')
    CQ = [sch.queue(f"cq{i}", nc.gpsimd, sem(f"s_cq{i}"), 16, 'dma') for i in range(4)]
    KQ = [sch.queue(f"kq{i}", nc.sync, sem(f"s_kq{i}"), 16, 'dma') for i in range(4)]
    DQ = sch.queue("dq", nc.sync, sem("s_dq"), 16, 'dma')

    wring = sb("wring", [128, RING, 4096], BF16)
    kTe = sb("kTe", [128, 8, S], BF16)
    vext = sb("vext", [128, 16, 8, 65], BF16)
    xmid = sb("xmid", [128, 4, D], F32)
    hT = sb("hT", [128, 8, CH], BF16)
    g1b = sb("g1b", [128, D], F32); g2b = sb("g2b", [128, D], F32)
    fS = sb("fS", [128, 18, 512], F32)
    bB = sb("bB", [128, 24, 512], BF16)
    Sst = sb("Sst", [128, 512], F32)
    qtil = sb("qtil", [128, 2, 512], BF16)
    attn = sb("attn", [128, 2, 512], BF16)
    sdb = sb("sdb", [128, 2, 512], BF16)
    oabf = sb("oabf", [128, 2, 512], BF16)
    oaT = sb("oaT", [128, 4, CH], BF16); obT = sb("obT", [128, 4, CH], BF16)
    ropeo = sb("ropeo", [128, 2, 512], BF16)
    bias = sb("bias", [128, 2, 8, 72], BF16)
    pt = sb("pt", [128, 3, 512], BF16)
    ident = sb("ident", [128, 128], BF16)
    Um = sb("Um", [128, 128], F32); bd = sb("bd", [128, 128], F32); tri = sb("tri", [128, 128], BF16)
    cind = sb("cind", [128, 2], F32)
    omlb = sb("omlb", [128, 512], F32); gob = sb("gob", [128, 512], F32)
    cs = sb("cs", [128, 16, 2, 32], F32)
    gq8 = sb("gq8", [128, 64], F32); gkb = sb("gkb", [128, 64], F32)
    cw = sb("cw", [128, 3, NKF], F32); cb = sb("cb", [128, NKF], F32)
    halo = sb("halo", [128, 2, NKF, 2], F32)
    elast = sb("elast", [128, 4, 4, 2], F32)
    st = sb("st", [128, 128], F32)
    kmT = sb("kmT", [128, 8, 8], BF16)
    kmf = sb("kmf", [128, 8], F32)
    epsb = sb("epsb", [128, 1], F32)
    gsm = sb("gsm", [128, 64], F32)
    cmpb = sb("cmpb", [128, 8 * 7 * 7], F32)
    cnt = sb("cnt", [128, 56], F32)
    rden = sb("rden", [128, 2, 4], F32)
    Pf = [ps(f"P{i}", [128, 512], F32) for i in range(6)]
    Tb = [ps(f"T{i}", [128, 1024], BF16) for i in range(2)]

    K = lambda *a: tuple(a)

    def fslot(i, n=1):
        return fS[:, i, :] if n == 1 else fS[:, i:i + n, :]

    class Rot:
        def __init__(self, n):
            self.n, self.i = n, 0

        def nxt(self):
            v = self.i % self.n
            self.i += 1
            return v

    rP = Rot(4)
    rP6 = Rot(6)
    rP3 = Rot(3)
    rT = Rot(2)
    rxt = Rot(2)
    rpt = Rot(3)

    def dump(name, ap, keys, shape, dt=F32):
        if name not in dump_names or name in dumps:
            return
        t = nc.dram_tensor("dbg_" + name, list(shape), dt, kind="ExternalOutput").ap()
        dumps[name] = t
        sch.add(DQ, lambda: nc.sync.dma_start(out=t, in_=ap), reads=keys)

    cqi = [0]

    def cast(dst, src, b):
        q = CQ[cqi[0] % 4]
        cqi[0] += 1
        sch.add(q, lambda: nc.gpsimd.dma_start(out=dst, in_=src), writes=[K("wbf", b)])

    kqi = [0]

    def kload(dst, src, key, eng=None):
        q = KQ[kqi[0] % 4]
        kqi[0] += 1
        sch.add(q, lambda: nc.sync.dma_start(out=dst, in_=src), writes=[key])

    def kcast(dst, src, key):
        q = CQ[cqi[0] % 4]
        cqi[0] += 1
        sch.add(q, lambda: nc.gpsimd.dma_start(out=dst, in_=src), writes=[key])

    kcast(ident[:], c_ident, K("ident"))
    kcast(tri[:], c_tri, K("tri"))
    kload(Um[:], c_U, K("Um")); kload(bd[:], c_bd, K("bd")); kload(cind[:], c_cind, K("cind"))
    kload(cs[:], c_cs, K("cs"))
    kload(g1b[:], norm1_g.partition_broadcast(128), K("g1b"))
    kload(g2b[:], norm2_g.partition_broadcast(128), K("g2b"))
    kload(gob[:], hg_on.partition_broadcast(128), K("gob"))
    kload(gq8[:], qng.partition_broadcast(128), K("gq8"))
    kload(gkb[:], kng.partition_broadcast(128), K("gkb"))
    kload(fS[:, 0, :], hg_lb[0:1, :].partition_broadcast(128), K("fS", 0))
    kload(fS[:, 1, :], hg_lb[1:2, :].partition_broadcast(128), K("fS", 1))
    for j in range(3):
        sch.add(KQ[j % 4], (lambda j=j: nc.sync.dma_start(
            out=cw[:, j, :], in_=conv_w[j:j + 1, :].rearrange("o (kc p) -> p (o kc)", p=128),
            allow_slow_non_contiguous=True)), writes=[K("cw")])
    sch.add(KQ[3], lambda: nc.sync.dma_start(
        out=cb[:], in_=conv_b.rearrange("o (kc p) -> p (o kc)", p=128), allow_slow_non_contiguous=True),
        writes=[K("cb")])
    for h in range(8):
        kcast(kTe[64:72, h, :], c_kind, K("kTe_ind"))
    sch.add(DVE, lambda: nc.vector.tensor_tensor(out=fS[:, 0, :], in0=fS[:, 0, :], in1=fS[:, 1, :], op=ALU.subtract),
            reads=[K("fS", 0), K("fS", 1)], writes=[K("fS", 0)])
    sch.add(ACT, lambda: nc.scalar.activation(out=omlb[:], in_=fS[:, 0, :], func=AF.Sigmoid, scale=-1.0),
            reads=[K("fS", 0)], writes=[K("omlb")])
    sch.add(ACT, lambda: nc.scalar.mul(out=gq8[:], in_=gq8[:], mul=0.125), reads=[K("gq8")], writes=[K("gq8")])
    sch.add(POOL, lambda: nc.gpsimd.memset(vext[:, :, :, 64:65], 1.0), writes=[K("vext_one")])
    sch.add(POOL, lambda: nc.gpsimd.memset(epsb[:], EPS), writes=[K("epsb")])
    sch.add(POOL, lambda: nc.gpsimd.memset(bias[:, :, :, 0:64], 0.0), writes=[K("bias", 0), K("bias", 1)])

    def blkv(b, kc0, nkc, n0, nn):
        v = wbf[b].rearrange("p (kc n) -> p kc n", n=512)
        return v[:, kc0:kc0 + nkc, n0:n0 + nn]

    def rows(w, r0, nkc, c0, ncol):
        return w[r0:r0 + nkc * 128, c0:c0 + ncol].rearrange("(kc p) n -> p kc n", p=128)

    for g, b in enumerate([B_HQ, B_HF, B_HI, B_HG, B_MQ, B_MK, B_MV]):
        cast(blkv(b, 0, 8, 0, 512), rows(w_in, 0, 8, g * 512, 512), b)
    for qd, b in enumerate([B_GAB0, B_GAB1, B_GAB2, B_GAB3]):
        cast(blkv(b, 0, 8, 0, 256), rows(w_in, 0, 8, 3584 + qd * 256, 256), b)
        cast(blkv(b, 0, 8, 256, 256), rows(w_in, 0, 8, 4608 + qd * 256, 256), b)
    for hf, b in enumerate([B_WAB0, B_WAB1]):
        cast(blkv(b, 0, 4, 0, 512), rows(w_a, 0, 4, hf * 512, 512), b)
        cast(blkv(b, 4, 4, 0, 512), rows(w_b, 0, 4, hf * 512, 512), b)
    for hf, b in enumerate([B_WO0, B_WO1]):
        cast(blkv(b, 0, 8, 0, 512), rows(w_out, 0, 8, hf * 512, 512), b)
    for u in range(11):
        cast(blkv(B_UP0 + u, 0, 8, 0, 256), rows(w_up, 0, 8, u * 256, 256), B_UP0 + u)
        cast(blkv(B_UP0 + u, 0, 8, 256, 256), rows(w_up, 0, 8, DFF + u * 256, 256), B_UP0 + u)
    DN_P = [(0, 8), (8, 8), (16, 6)]
    for nh in range(2):
        for pi, (k0, nk) in enumerate(DN_P):
            cast(blkv(B_DN0 + nh * 3 + pi, 0, nk, 0, 512), rows(w_down, k0 * 128, nk, nh * 512, 512), B_DN0 + nh * 3 + pi)

    wstate = {"next": 0}
    total_blocks = NSEQ * NCH * NBLK

    def wload_upto(gb):
        while wstate["next"] <= gb and wstate["next"] < total_blocks:
            g = wstate["next"]
            slot = g % RING
            b = g % NBLK
            ne = 3072 if b in (B_DN0 + 2, B_DN0 + 5) else 4096
            sch.add(WQ[slot], (lambda slot=slot, b=b, ne=ne: nc.sync.dma_start(out=wring[:, slot, 0:ne],
                                                                              in_=wbf[b][:, 0:ne])),
                    reads=[K("wbf", b)], writes=[K("w", slot)])
            wstate["next"] += 1

    def wblk(gc, b, ahead=2):
        gb = gc * NBLK + b
        wload_upto(gb + ahead)
        slot = gb % RING
        return wring[:, slot, :].rearrange("p (kc n) -> p kc n", n=512), K("w", slot)

    def mm(out_ap, lhsT, rhs, start, stop, reads, writes, sig, **kw):
        sch.add(PE, lambda: nc.tensor.matmul(out_ap, lhsT=lhsT, rhs=rhs, start=start, stop=stop, **kw),
                reads=reads, writes=writes, sig=sig)

    def tp(out_ap, in_ap, reads, writes, sig):
        sch.add(PE, lambda: nc.tensor.transpose(out_ap, in_ap, ident[:]), reads=list(reads) + [K("ident")],
                writes=writes, sig=sig)

    hbuf = sb("hbuf", [128, 2, D], BF16)

    def norm_A(src_ap, src_keys, gb_t, gkey, tt, stc):
        junk = bB[:, 22:24, :].rearrange("p a b -> p (a b)")
        sch.add(ACT, lambda: nc.scalar.activation(out=junk, in_=src_ap, func=AF.Square, scale=1.0 / 32.0,
                                                  accum_out=st[:, stc:stc + 1]),
                reads=src_keys, writes=[K("bB", 22), K("bB", 23), K("st", stc)])
        sch.add(ACT, lambda: nc.scalar.activation(out=st[:, stc + 2:stc + 3], in_=st[:, stc:stc + 1], func=AF.Ln,
                                                  bias=epsb[:, 0:1]),
                reads=[K("st", stc), K("epsb")], writes=[K("st", stc + 2)])
        sch.add(ACT, lambda: nc.scalar.activation(out=st[:, stc + 3:stc + 4], in_=st[:, stc + 2:stc + 3], func=AF.Exp,
                                                  scale=-0.5),
                reads=[K("st", stc + 2)], writes=[K("st", stc + 3)])
        hi_ = tt % 2
        sch.add(DVE, lambda: nc.vector.scalar_tensor_tensor(out=hbuf[:, hi_, :], in0=src_ap,
                                                            scalar=st[:, stc + 3:stc + 4],
                                                            in1=gb_t[:], op0=ALU.mult, op1=ALU.mult),
                reads=list(src_keys) + [K("st", stc + 3), gkey], writes=[K("hbuf", hi_)])

    def norm_B(tt):
        hi_ = tt % 2
        ti = rT.nxt()
        for kc in range(8):
            tp(Tb[ti][:, kc * 128:(kc + 1) * 128], hbuf[:, hi_, kc * 128:(kc + 1) * 128], [K("hbuf", hi_)],
               [K("T", ti)], kc == 7)
        sch.add(DVE, lambda: nc.vector.tensor_copy(out=hT[:, :, tt * 128:(tt + 1) * 128],
                                                   in_=Tb[ti][:, :].rearrange("p (kc t) -> p kc t", t=128)),
                reads=[K("T", ti)], writes=[K("hT", tt)])

    def norm_tile(src_ap, src_keys, gb_t, gkey, tt, stc):
        norm_A(src_ap, src_keys, gb_t, gkey, tt, stc)
        norm_B(tt)

    def x_norm_A(s_, c_, tt):
        xi = rxt.nxt()
        xs = 12 + 2 * xi
        xap = fS[:, xs:xs + 2, :].rearrange("p a b -> p (a b)")
        xk = [K("fS", xs), K("fS", xs + 1)]
        r0 = c_ * CH + tt * 128
        sch.add(XQ[xi], lambda: nc.sync.dma_start(out=xap, in_=x[s_, r0:r0 + 128, :]), writes=xk)
        norm_A(xap, xk, g1b, K("g1b"), tt, 4 * tt)

    prefetched = set()

    HTK = [K("hT", t) for t in range(4)]

    def chunk(s, c):
        gc = s * NCH + c
        first = (gc == 0)
        for tt in range(4):
            if (gc, tt) in prefetched:
                continue
            x_norm_A(s, c, tt)
            norm_B(tt)
        if first:
            dump("hT", hT[:], HTK, [128, 8, CH], BF16)
        wq_, kq_ = wblk(gc, B_HQ)
        wf_, kf_ = wblk(gc, B_HF, 1)
        if c == 0:
            sch.add(DVE, lambda: nc.vector.memset(Sst[:], 0.0), writes=[K("Sst")])
        KQT = [K("bB", i) for i in range(8)]

        def h_stageA(tt):
            r = tt % 2
            qs, ks, ls = 0 + r, 2 + r, 4 + r
            pq = rP6.nxt()
            for kc in range(8):
                mm(Pf[pq][:, :], hT[:, kc, tt * 128:(tt + 1) * 128], wq_[:, kc, :], kc == 0, kc == 7,
                   [K("hT", tt), kq_], [K("P", pq)], kc == 7)
            sch.add(ACT, lambda: nc.scalar.activation(out=fS[:, qs, :], in_=Pf[pq][:, :], func=AF.Silu),
                    reads=[K("P", pq)], writes=[K("fS", qs)])
            pf = rP6.nxt()
            for kc in range(8):
                mm(Pf[pf][:, :], hT[:, kc, tt * 128:(tt + 1) * 128], wf_[:, kc, :], kc == 0, kc == 7,
                   [K("hT", tt), kf_], [K("P", pf)], kc == 7)
            sch.add(ACT, lambda: nc.scalar.activation(out=fS[:, ks, :], in_=Pf[pf][:, :], func=AF.Sigmoid, scale=-1.0),
                    reads=[K("P", pf)], writes=[K("fS", ks)])
            sch.add(DVE, lambda: nc.vector.tensor_tensor(out=fS[:, ks, :], in0=fS[:, ks, :], in1=omlb[:], op=ALU.mult),
                    reads=[K("fS", ks), K("omlb")], writes=[K("fS", ks)])
            sch.add(ACT, lambda: nc.scalar.activation(out=fS[:, ls, :], in_=fS[:, ks, :], func=AF.Ln, scale=-1.0,
                                                      bias=1.0),
                    reads=[K("fS", ks)], writes=[K("fS", ls)])

        def h_stageB(tt):
            r = tt % 2
            qs, ks, ls, es_, ns = 0 + r, 2 + r, 4 + r, 6 + r, 8 + r
            pa = rP6.nxt()
            mm(Pf[pa][:, :], Um[:], fS[:, ls, :], True, True, [K("Um"), K("fS", ls)], [K("P", pa)], True)
            sch.add(ACT, lambda: nc.scalar.activation(out=fS[:, es_, :], in_=Pf[pa][:, :], func=AF.Exp),
                    reads=[K("P", pa)], writes=[K("fS", es_)])
            sch.add(ACT, lambda: nc.scalar.activation(out=fS[:, ns, :], in_=Pf[pa][:, :], func=AF.Exp, scale=-1.0),
                    reads=[K("P", pa)], writes=[K("fS", ns)])
            pl = rP6.nxt()
            for h in range(4):
                mm(Pf[pl][:, h * 2:h * 2 + 2], fS[:, ls, h * 128:(h + 1) * 128], cind[:], True, True,
                   [K("fS", ls), K("cind")], [K("P", pl)], h == 3)
            sch.add(ACT, lambda: nc.scalar.activation(
                out=elast[:, tt, :, :].rearrange("p h j -> p (h j)"), in_=Pf[pl][:, 0:8], func=AF.Exp),
                reads=[K("P", pl)], writes=[K("elast", tt)])
            sch.add(DVE, lambda: nc.vector.tensor_tensor(out=bB[:, 12 + tt, :], in0=fS[:, ks, :], in1=fS[:, es_, :],
                                                         op=ALU.mult),
                    reads=[K("fS", ks), K("fS", es_)], writes=[K("bB", 12 + tt)])
            sch.add(DVE, lambda: nc.vector.tensor_tensor(out=qtil[:, r, :], in0=fS[:, qs, :], in1=fS[:, ns, :],
                                                         op=ALU.mult),
                    reads=[K("fS", qs), K("fS", ns)], writes=[K("qtil", r)])
            if first and tt == 0:
                dump("khat0", bB[:, 12, :], [K("bB", 12)], [128, 512], BF16)
                dump("qtil0", qtil[:, 0, :], [K("qtil", 0)], [128, 512], BF16)

        def h_stageB2(tt):
            r = tt % 2
            ti = rT.nxt()
            for h in range(4):
                tp(Tb[ti][:, h * 128:(h + 1) * 128], bB[:, 12 + tt, h * 128:(h + 1) * 128], [K("bB", 12 + tt)],
                   [K("T", ti)], False)
            for h in range(4):
                tp(Tb[ti][:, (4 + h) * 128:(5 + h) * 128], qtil[:, r, h * 128:(h + 1) * 128], [K("qtil", r)],
                   [K("T", ti)], h == 3)
            sch.add(ACT, lambda: nc.scalar.copy(out=bB[:, 0:8, tt * 128:(tt + 1) * 128],
                                                in_=Tb[ti][:, :].rearrange("p (a t) -> p a t", t=128)),
                    reads=[K("T", ti)], writes=[K("kqT", tt)] + KQT)

        hi_state = {}

        def h_hi(tt):
            if "w" not in hi_state:
                hi_state["w"] = wblk(gc, B_HI)
            wi_, ki_ = hi_state["w"]
            pv = rP6.nxt()
            for kc in range(8):
                mm(Pf[pv][:, :], hT[:, kc, tt * 128:(tt + 1) * 128], wi_[:, kc, :], kc == 0, kc == 7,
                   [K("hT", tt), ki_], [K("P", pv)], kc == 7)
            sch.add(ACT, lambda: nc.scalar.copy(out=bB[:, 8 + tt, :], in_=Pf[pv][:, :]),
                    reads=[K("P", pv)], writes=[K("bB", 8 + tt)])

        h_stageA(0); h_stageA(1); h_stageB(0); h_stageA(2); h_stageB(1); h_stageB2(0); h_stageA(3); h_stageB(2)
        h_stageB2(1); h_hi(0); h_stageB(3); h_hi(1); h_stageB2(2); h_hi(2); h_hi(3); h_stageB2(3)
        wg_, kg_ = wblk(gc, B_HG)
        for tt in range(4):
            ph = rP6.nxt()
            for kc in range(8):
                mm(Pf[ph][:, :], hT[:, kc, tt * 128:(tt + 1) * 128], wg_[:, kc, :], kc == 0, kc == 7,
                   [K("hT", tt), kg_], [K("P", ph)], kc == 7)
            sch.add(ACT, lambda ph=ph, tt=tt: nc.scalar.activation(out=fS[:, tt, :], in_=Pf[ph][:, :], func=AF.Silu),
                    reads=[K("P", ph)], writes=[K("fS", tt)])
            sch.add(POOL, lambda tt=tt: nc.gpsimd.tensor_tensor(out=fS[:, tt, :], in0=fS[:, tt, :], in1=gob[:],
                                                                op=ALU.mult),
                    reads=[K("fS", tt), K("gob")], writes=[K("fS", tt)])

        def h_post_tr(tt):
            r = tt % 2
            ti = rT.nxt()
            for kc in range(4):
                tp(Tb[ti][:, kc * 128:(kc + 1) * 128], oabf[:, r, kc * 128:(kc + 1) * 128], [K("oabf", r)],
                   [K("T", ti)], kc == 3)
            sch.add(ACT, lambda: nc.scalar.copy(out=oaT[:, :, tt * 128:(tt + 1) * 128],
                                                in_=Tb[ti][:, 0:512].rearrange("p (a t) -> p a t", t=128)),
                    reads=[K("T", ti)], writes=[K("oaT", tt)])

        def h_attn(tt):
            r = tt % 2
            po = 4 + r
            pat = rP3.nxt()
            for h in range(4):
                mm(Pf[pat][:, h * 128:(h + 1) * 128], bB[:, h, tt * 128:(tt + 1) * 128],
                   bB[:, 4 + h, tt * 128:(tt + 1) * 128], True, True, KQT, [K("P", pat)], h == 3)
            sch.add(DVE, lambda: nc.vector.scalar_tensor_tensor(
                out=attn[:, r, :].rearrange("p (h t) -> p h t", h=4),
                in0=Pf[pat][:, :].rearrange("p (h t) -> p h t", h=4), scalar=1e30,
                in1=bd[:].unsqueeze(1).broadcast_to([128, 4, 128]), op0=ALU.min, op1=ALU.mult),
                reads=[K("P", pat), K("bd")], writes=[K("attn", r)])
            for h in range(4):
                mm(Pf[po][:, h * 128:(h + 1) * 128], attn[:, r, h * 128:(h + 1) * 128],
                   bB[:, 8 + tt, h * 128:(h + 1) * 128], h == 0, False, [K("attn", r), K("bB", 8 + tt)],
                   [K("P", po)], False, skip_group_check=True)

        def h_rec(tt):
            r = tt % 2
            po = 4 + r
            for j in range(2):
                n = 2 * tt + j
                sd = n % 2
                sch.add(DVE, lambda j=j: nc.vector.tensor_tensor(
                    out=fS[:, 16, :].rearrange("p (h v) -> p h v", h=4),
                    in0=Sst[:].rearrange("p (h v) -> p h v", h=4),
                    in1=elast[:, tt, :, j:j + 1].broadcast_to([128, 4, 128]), op=ALU.mult),
                    reads=[K("Sst"), K("elast", tt)], writes=[K("fS", 16)])
                sch.add(ACT, lambda sd=sd: nc.scalar.copy(out=sdb[:, sd, :], in_=fS[:, 16, :]),
                        reads=[K("fS", 16)], writes=[K("sdb", sd)])
                for h in range(4):
                    mm(Pf[3][:, h * 128:(h + 1) * 128], bB[j * 64:(j + 1) * 64, 12 + tt, h * 128:(h + 1) * 128],
                       bB[j * 64:(j + 1) * 64, 8 + tt, h * 128:(h + 1) * 128], True, True,
                       [K("bB", 12 + tt), K("bB", 8 + tt)], [K("P", 3)], h == 3)
                for h in range(4):
                    t0 = tt * 128 + j * 64
                    mm(Pf[po][j * 64:(j + 1) * 64, h * 128:(h + 1) * 128], bB[:, 4 + h, t0:t0 + 64],
                       sdb[:, sd, h * 128:(h + 1) * 128], False, (j == 1 and h == 3), KQT + [K("sdb", sd)],
                       [K("P", po)], (j == 1 and h == 3), skip_group_check=True)
                sch.add(DVE, lambda: nc.vector.tensor_tensor(out=Sst[:], in0=fS[:, 16, :], in1=Pf[3][:, :], op=ALU.add),
                        reads=[K("fS", 16), K("P", 3)], writes=[K("Sst")])

        def h_post(tt):
            r = tt % 2
            po = 4 + r
            so = 16 + 16 * r
            for h in range(4):
                sch.add(ACT, lambda h=h: nc.scalar.activation(
                    out=bB[:, 22, h * 128:(h + 1) * 128], in_=Pf[po][:, h * 128:(h + 1) * 128], func=AF.Square,
                    scale=float(1.0 / np.sqrt(128.0)), accum_out=st[:, so + h:so + h + 1]),
                    reads=[K("P", po)], writes=[K("bB", 22), K("st", so + h)])
            sch.add(ACT, lambda: nc.scalar.activation(out=st[:, so + 8:so + 12], in_=st[:, so:so + 4], func=AF.Ln,
                                                      bias=epsb[:, 0:1]),
                    reads=[K("st", so + h) for h in range(4)] + [K("epsb")], writes=[K("st", so + 8)])
            sch.add(ACT, lambda: nc.scalar.activation(out=st[:, so + 12:so + 16], in_=st[:, so + 8:so + 12],
                                                      func=AF.Exp, scale=-0.5),
                    reads=[K("st", so + 8)], writes=[K("st", so + 12)])
            for h in range(4):
                sch.add(DVE, lambda h=h: nc.vector.scalar_tensor_tensor(
                    out=oabf[:, r, h * 128:(h + 1) * 128], in0=Pf[po][:, h * 128:(h + 1) * 128],
                    scalar=st[:, so + 12 + h:so + 13 + h], in1=fS[:, tt, h * 128:(h + 1) * 128], op0=ALU.mult,
                    op1=ALU.mult),
                    reads=[K("P", po), K("st", so + 12), K("fS", tt)], writes=[K("oabf", r)])
            if first and tt == 0:
                dump("oa0", oabf[:, 0, :], [K("oabf", 0)], [128, 512], BF16)

        h_attn(0)
        for tt in range(4):
            h_rec(tt)
            if tt < 3:
                h_attn(tt + 1)
            h_post(tt)
            if tt >= 1:
                h_post_tr(tt - 1)
        QTE = [K("bB", 16 + i) for i in range(8)]
        qTe = bB[:, 16:24, :]
        wmk, kmk = wblk(gc, B_MK)
        wstore = {"k": (wmk, kmk)}

        def m_stageA(kind, tt, idx):
            is_q = kind == "q"
            if is_q and "q" not in wstore:
                wstore["q"] = wblk(gc, B_MQ)
            wv, wk = wstore[kind]
            par = idx % 3
            sq, mn = [4, 5, 10][par], [6, 7, 11][par]
            sc = 64 + 16 * par
            pm = rP.nxt()
            for kc in range(8):
                mm(Pf[pm][:, :], hT[:, kc, tt * 128:(tt + 1) * 128], wv[:, kc, :], kc == 0, kc == 7,
                   [K("hT", tt), wk], [K("P", pm)], kc == 7)
            sch.add(ACT, lambda: nc.scalar.activation(out=fS[:, sq, :], in_=Pf[pm][:, :], func=AF.Square, scale=0.125),
                    reads=[K("P", pm)], writes=[K("fS", sq)])
            sch.add(DVE, lambda: nc.vector.tensor_reduce(out=st[:, sc:sc + 8],
                                                         in_=fS[:, sq, :].rearrange("p (h d) -> p h d", h=8),
                                                         axis=AX.X, op=ALU.add),
                    reads=[K("fS", sq)], writes=[K("st", sc)])
            sch.add(ACT, lambda: nc.scalar.activation(out=st[:, sc + 8:sc + 16], in_=st[:, sc:sc + 8], func=AF.Ln,
                                                      bias=epsb[:, 0:1]),
                    reads=[K("st", sc), K("epsb")], writes=[K("st", sc + 8)])
            sch.add(ACT, lambda: nc.scalar.activation(out=st[:, sc:sc + 8], in_=st[:, sc + 8:sc + 16], func=AF.Exp,
                                                      scale=-0.5),
                    reads=[K("st", sc + 8)], writes=[K("st", sc)])
            sch.add(DVE, lambda: nc.vector.tensor_tensor(
                out=fS[:, mn, :].rearrange("p (h d) -> p h d", h=8),
                in0=Pf[pm][:, :].rearrange("p (h d) -> p h d", h=8),
                in1=st[:, sc:sc + 8].unsqueeze(2).broadcast_to([128, 8, 64]), op=ALU.mult),
                reads=[K("P", pm), K("st", sc)], writes=[K("fS", mn)])
            gvec, gkey = (gq8, K("gq8")) if is_q else (gkb, K("gkb"))
            m3 = fS[:, mn, :].rearrange("p (h d) -> p h d", h=8)
            sch.add(DVE, lambda: nc.vector.tensor_tensor(out=m3, in0=m3,
                                                         in1=gvec[:].unsqueeze(1).broadcast_to([128, 8, 64]),
                                                         op=ALU.mult),
                    reads=[K("fS", mn), gkey], writes=[K("fS", mn)])

        def m_stageB(kind, tt, idx):
            is_q = kind == "q"
            par = idx % 3
            mn, tB = [6, 7, 11][par], [8, 9, 17][par]
            tile_i = c * 4 + tt
            m4 = fS[:, mn, :].rearrange("p (h a d) -> p h a d", h=8, a=2)
            tB4 = fS[:, tB, :].rearrange("p (h a d) -> p h a d", h=8, a=2)
            cosb = cs[:, tile_i, 0, :]
            sinb3 = cs[:, tile_i, 1, :].unsqueeze(1).broadcast_to([128, 8, 32])
            sch.add(POOL, lambda: nc.gpsimd.tensor_tensor(out=tB4[:, :, 0, :], in0=m4[:, :, 1, :], in1=sinb3,
                                                          op=ALU.mult),
                    reads=[K("fS", mn), K("cs")], writes=[K("fS", tB)])
            sch.add(POOL, lambda: nc.gpsimd.tensor_tensor(out=tB4[:, :, 1, :], in0=m4[:, :, 0, :], in1=sinb3,
                                                          op=ALU.mult),
                    reads=[K("fS", mn), K("cs")], writes=[K("fS", tB)])
            sch.add(DVE, lambda: nc.vector.tensor_tensor(
                out=m4, in0=m4, in1=cosb.unsqueeze(1).unsqueeze(1).broadcast_to([128, 8, 2, 32]), op=ALU.mult),
                reads=[K("fS", mn), K("cs"), K("fS", tB)], writes=[K("fS", mn)])
            ro = idx % 2
            ro4 = ropeo[:, ro, :].rearrange("p (h a d) -> p h a d", h=8, a=2)
            sch.add(POOL, lambda: nc.gpsimd.tensor_tensor(out=ro4[:, :, 0, :], in0=m4[:, :, 0, :], in1=tB4[:, :, 0, :],
                                                          op=ALU.subtract),
                    reads=[K("fS", mn), K("fS", tB)], writes=[K("ropeo", ro)])
            sch.add(POOL, lambda: nc.gpsimd.tensor_tensor(out=ro4[:, :, 1, :], in0=m4[:, :, 1, :], in1=tB4[:, :, 1, :],
                                                          op=ALU.add),
                    reads=[K("fS", mn), K("fS", tB)], writes=[K("ropeo", ro)])

        def m_stageB2(kind, tt, idx):
            is_q = kind == "q"
            ro = idx % 2
            tile_i = c * 4 + tt
            ti = rT.nxt()
            for h in range(8):
                tp(Tb[ti][0:64, h * 128:(h + 1) * 128], ropeo[:, ro, h * 64:(h + 1) * 64], [K("ropeo", ro)],
                   [K("T", ti)], h == 7)
            src = Tb[ti][0:64, :].rearrange("p (h t) -> p h t", h=8)
            if is_q:
                sch.add(DVE, lambda: nc.vector.tensor_copy(out=qTe[0:64, :, tt * 128:(tt + 1) * 128], in_=src),
                        reads=[K("T", ti)], writes=QTE + [K("qTe", tt)])
            else:
                p0 = c * CH + tt * 128
                sch.add(DVE, lambda: nc.vector.tensor_copy(out=kTe[0:64, :, p0:p0 + 128], in_=src),
                        reads=[K("T", ti)], writes=[K("kTe", tile_i)])
                if tt % 2 == 1:
                    blk = 2 * c + tt // 2
                    sch.add(DVE, lambda: nc.vector.tensor_reduce(out=kmf[0:64, :],
                                                                 in_=kTe[0:64, :, blk * 256:(blk + 1) * 256],
                                                                 axis=AX.X, op=ALU.add),
                            reads=[K("kTe", 2 * blk), K("kTe", 2 * blk + 1)], writes=[K("kmf")])
                    sch.add(DVE, lambda: nc.vector.tensor_scalar(out=kmT[0:64, :, blk:blk + 1],
                                                                 in0=kmf[0:64, :].unsqueeze(2), scalar1=1.0 / 256.0,
                                                                 scalar2=None, op0=ALU.mult),
                            reads=[K("kmf")], writes=[K("kmT")])

        def m_stageC(tt):
            qb = 2 * c + tt // 2
            bi = tt % 2
            if qb >= 4:
                pg = rP.nxt()
                for h in range(8):
                    mm(Pf[pg][:, h * 8:h * 8 + qb], qTe[0:64, h, tt * 128:(tt + 1) * 128], kmT[0:64, h, 0:qb], True, True,
                       [K("qTe", tt), K("kmT")], [K("P", pg)], h == 7)
                sch.add(ACT, lambda: nc.scalar.copy(
                    out=gsm[:].rearrange("p (h j) -> p h j", h=8)[:, :, 0:qb],
                    in_=Pf[pg][:, 0:64].rearrange("p (h j) -> p h j", h=8)[:, :, 0:qb]),
                        reads=[K("P", pg)], writes=[K("gsm")])
                g3 = gsm[:].rearrange("p (h j) -> p h j", h=8)[:, :, 0:qb]
                c4 = cmpb[:, 0:8 * qb * qb].rearrange("p (h j k) -> p h j k", h=8, j=qb)
                sch.add(DVE, lambda: nc.vector.tensor_tensor(
                    out=c4, in0=g3.unsqueeze(2).broadcast_to([128, 8, qb, qb]),
                    in1=g3.unsqueeze(3).broadcast_to([128, 8, qb, qb]), op=ALU.is_gt),
                    reads=[K("gsm")], writes=[K("cmpb")])
                cn3 = cnt[:, 0:8 * qb].rearrange("p (h j) -> p h j", h=8)
                sch.add(DVE, lambda: nc.vector.tensor_reduce(out=cn3, in_=c4, axis=AX.X, op=ALU.add),
                        reads=[K("cmpb")], writes=[K("cnt")])
                sch.add(DVE, lambda: nc.vector.tensor_scalar(
                    out=bias[:, bi, :, 64:64 + qb], in0=cn3, scalar1=2.5, scalar2=-BIG, op0=ALU.is_gt, op1=ALU.mult),
                    reads=[K("cnt")], writes=[K("bias", bi)])
                sch.add(POOL, lambda: nc.gpsimd.memset(bias[:, bi, :, 64 + qb:65 + qb], 0.0), writes=[K("bias", bi)])
            else:
                sch.add(POOL, lambda: nc.gpsimd.memset(bias[:, bi, :, 64:65 + qb], 0.0), writes=[K("bias", bi)])
            if qb < 7:
                sch.add(POOL, lambda: nc.gpsimd.memset(bias[:, bi, :, 65 + qb:72], -BIG), writes=[K("bias", bi)])
            for hh in range(2):
                pb = rP.nxt()
                for h4 in range(4):
                    h = hh * 4 + h4
                    mm(Pf[pb][0:72, h4 * 128:(h4 + 1) * 128], bias[:, bi, h, :], ident[:], True, True,
                       [K("bias", bi), K("ident")], [K("P", pb)], h4 == 3)
                sch.add(ACT, lambda pb=pb, hh=hh: nc.scalar.copy(
                    out=qTe[64:72, hh * 4:hh * 4 + 4, tt * 128:(tt + 1) * 128],
                    in_=Pf[pb][64:72, :].rearrange("p (h t) -> p h t", h=4)),
                    reads=[K("P", pb)], writes=QTE + [K("qTeb", tt)])

        def m_v(tt):
            if "v" not in wstore:
                wstore["v"] = wblk(gc, B_MV)
            wmv, kmv = wstore["v"]
            tile_i = c * 4 + tt
            pv = rP.nxt()
            for kc in range(8):
                mm(Pf[pv][:, :], hT[:, kc, tt * 128:(tt + 1) * 128], wmv[:, kc, :], kc == 0, kc == 7,
                   [K("hT", tt), kmv], [K("P", pv)], kc == 7)
            sch.add(ACT, lambda: nc.scalar.copy(
                out=vext[:, tile_i, :, 0:64], in_=Pf[pv][:, :].rearrange("p (h d) -> p h d", h=8)),
                reads=[K("P", pv)], writes=[K("vext", tile_i)])

        items2 = [("k", t) for t in range(4)] + [("q", t) for t in range(4)]
        m_stageA("k", 0, 0)
        h_post_tr(3)
        m_stageA("k", 1, 1)
        m_stageB("k", 0, 0)
        for i in range(2, 8):
            m_stageA(items2[i][0], items2[i][1], i)
            m_stageB(items2[i - 1][0], items2[i - 1][1], i - 1)
            m_stageB2(items2[i - 2][0], items2[i - 2][1], i - 2)
        m_v(0)
        m_stageB("q", 3, 7)
        m_stageB2("q", 2, 6)
        m_stageC(0)
        m_v(1)
        m_stageB2("q", 3, 7)
        m_stageC(1)
        m_v(2)
        m_stageC(2)
        m_v(3)
        m_stageC(3)
        if first:
            dump("kTe", kTe[0:72, :, 0:512], [K("kTe", i) for i in range(4)] + [K("kTe_ind")], [72, 8, 512], BF16)
            dump("qTe", qTe[0:72, :, :], QTE, [72, 8, 512], BF16)
        nkt = 4 * c + 4
        OB = [K("bB", 8 + i) for i in range(4)]
        obv = bB[:, 8:12, :].rearrange("p t (h d) -> p t h d", h=8)
        items = [(h, kt) for h in range(8) for kt in range(nkt)]
        pend = None

        def att_qk(h, kt):
            n0 = max(0, kt * 128 - c * CH)
            pst = rP.nxt()
            pi = rpt.nxt()
            mm(Pf[pst][:, n0:512], kTe[0:72, h, kt * 128:(kt + 1) * 128], qTe[0:72, h, n0:512], True, True,
               [K("kTe", kt), K("kTe_ind")] + QTE, [K("P", pst)], True)
            sch.add(ACT, lambda pst=pst, pi=pi, n0=n0: nc.scalar.activation(out=pt[:, pi, n0:512],
                                                                          in_=Pf[pst][:, n0:512], func=AF.Exp),
                    reads=[K("P", pst)], writes=[K("pt", pi)])
            if kt * 128 >= c * CH:
                sch.add(DVE, lambda pi=pi, n0=n0: nc.vector.tensor_tensor(out=pt[:, pi, n0:n0 + 128],
                                                                          in0=pt[:, pi, n0:n0 + 128], in1=tri[:],
                                                                          op=ALU.mult),
                        reads=[K("pt", pi), K("tri")], writes=[K("pt", pi)])
            return (h, kt, n0, pi)

        def att_pv(h, kt, n0, pi):
            pob = 4 + (h % 2)
            for sub in range(n0 // 128, 4):
                last = (kt == nkt - 1 and sub == 3)
                mm(Pf[pob][:, sub * 65:(sub + 1) * 65], pt[:, pi, sub * 128:(sub + 1) * 128], vext[:, kt, h, :],
                   (kt == 0 and sub == 0), last, [K("pt", pi), K("vext", kt), K("vext_one")], [K("P", pob)],
                   sub == 3, skip_group_check=True)
            if kt == nkt - 1:
                rd = h % 2
                po3 = Pf[pob][:, 0:260].rearrange("p (s e) -> p s e", e=65)
                sch.add(DVE, lambda po3=po3, rd=rd: nc.vector.reciprocal(out=rden[:, rd, :].unsqueeze(2),
                                                                         in_=po3[:, :, 64:65]),
                        reads=[K("P", pob)], writes=[K("rden", rd)])
                sch.add(DVE, lambda po3=po3, rd=rd, h=h: nc.vector.tensor_tensor(
                    out=obv[:, :, h, :], in0=po3[:, :, 0:64],
                    in1=rden[:, rd, :].unsqueeze(2).broadcast_to([128, 4, 64]), op=ALU.mult),
                    reads=[K("P", pob), K("rden", rd)], writes=OB)

        pendq = []
        for (h, kt) in items:
            pendq.append(att_qk(h, kt))
            if len(pendq) > 2:
                att_pv(*pendq.pop(0))
        while pendq:
            att_pv(*pendq.pop(0))
        if first:
            dump("ob", bB[:, 8:12, :], OB, [128, 4, 512], BF16)
        for tt in range(4):
            ti = rT.nxt()
            for kc in range(4):
                tp(Tb[ti][:, kc * 128:(kc + 1) * 128], bB[:, 8 + tt, kc * 128:(kc + 1) * 128], [K("bB", 8 + tt)],
                   [K("T", ti)], kc == 3)
            sch.add(ACT, lambda ti=ti, tt=tt: nc.scalar.copy(out=obT[:, :, tt * 128:(tt + 1) * 128],
                                                             in_=Tb[ti][:, 0:512].rearrange("p (a t) -> p a t", t=128)),
                    reads=[K("T", ti)], writes=[K("obT", tt)])
        OAT = [K("oaT", t) for t in range(4)]
        OBT = [K("obT", t) for t in range(4)]
        MIX = [K("bB", i) for i in range(8)]
        gab_ids = [B_GAB0, B_GAB1, B_GAB2, B_GAB3]
        for qd in range(4):
            wg2, kg2 = wblk(gc, gab_ids[qd], 1)
            wab, kab = wblk(gc, B_WAB0 if qd < 2 else B_WAB1, 1)
            for e in range(2):
                i = 2 * qd + e
                col = (i % 4) * 128
                res = []
                for br in range(2):
                    pgt = rP.nxt()
                    for kc in range(8):
                        mm(Pf[pgt][:, :], wg2[:, kc, br * 256 + e * 128: br * 256 + (e + 1) * 128], hT[:, kc, :],
                           kc == 0, kc == 7, HTK + [kg2], [K("P", pgt)], kc == 7)
                    pab = rP.nxt()
                    src = oaT if br == 0 else obT
                    srk = OAT if br == 0 else OBT
                    for kc in range(4):
                        mm(Pf[pab][:, :], wab[:, br * 4 + kc, col:col + 128], src[:, kc, :], kc == 0, kc == 3,
                           srk + [kab], [K("P", pab)], kc == 3)
                    ss_, ms_ = 0 + br, 2 + br
                    sch.add(ACT, lambda pgt=pgt, ss_=ss_: nc.scalar.activation(out=fS[:, ss_, :], in_=Pf[pgt][:, :],
                                                                             func=AF.Sigmoid),
                            reads=[K("P", pgt)], writes=[K("fS", ss_)])
                    sch.add(DVE, lambda pab=pab, ss_=ss_, ms_=ms_: nc.vector.tensor_tensor(
                        out=fS[:, ms_, :], in0=fS[:, ss_, :], in1=Pf[pab][:, :], op=ALU.mult),
                        reads=[K("fS", ss_), K("P", pab)], writes=[K("fS", ms_)])
                sch.add(POOL, lambda i=i: nc.gpsimd.tensor_tensor(out=bB[:, i, :], in0=fS[:, 2, :], in1=fS[:, 3, :],
                                                                  op=ALU.add),
                        reads=[K("fS", 2), K("fS", 3)], writes=[K("bB", i)])
        if first:
            dump("mixT", bB[:, 0:8, :], MIX, [128, 8, 512], BF16)
        wo0, ko0 = wblk(gc, B_WO0)
        wo1, ko1 = wblk(gc, B_WO1, 1)

        def w_out_tile(tt):
            xi = rxt.nxt()
            xs = 12 + 2 * xi
            xap = fS[:, xs:xs + 2, :].rearrange("p a b -> p (a b)")
            xk = [K("fS", xs), K("fS", xs + 1)]
            r0 = c * CH + tt * 128
            sch.add(XQ[xi], lambda: nc.sync.dma_start(out=xap, in_=x[s, r0:r0 + 128, :]), writes=xk)
            for nh, (wo, ko) in enumerate([(wo0, ko0), (wo1, ko1)]):
                pw = rP.nxt()
                for kc in range(8):
                    mm(Pf[pw][:, :], bB[:, kc, tt * 128:(tt + 1) * 128], wo[:, kc, :], kc == 0, kc == 7,
                       MIX + [ko], [K("P", pw)], kc == 7)
                sch.add(DVE, lambda pw=pw, nh=nh: nc.vector.tensor_tensor(
                    out=xmid[:, tt, nh * 512:(nh + 1) * 512], in0=xap[:, nh * 512:(nh + 1) * 512], in1=Pf[pw][:, :],
                    op=ALU.add), reads=xk + [K("P", pw)], writes=[K("xmid", tt, nh)])

        def n2A(tt):
            norm_A(xmid[:, tt, :], [K("xmid", tt, 0), K("xmid", tt, 1)], g2b, K("g2b"), tt, 4 * tt)

        w_out_tile(0); w_out_tile(1); n2A(0); w_out_tile(2); n2A(1); norm_B(0); w_out_tile(3)
        if first:
            dump("xmid", xmid[:], [K("xmid", t, n) for t in range(4) for n in range(2)], [128, 4, D])
        n2A(2); norm_B(1); n2A(3); norm_B(2); norm_B(3)
        par = c % 2
        if c == 0:
            sch.add(POOL, lambda: nc.gpsimd.memset(halo[:, 0, :, :], 0.0), writes=[K("halo", 0)])
        GT = [K("bB", i) for i in range(NKF)]
        ngc = gc + 1
        has_next = ngc < NSEQ * NCH
        for u in range(11):
            wu, ku = wblk(gc, B_UP0 + u)
            if has_next and u in (7, 9):
                ptt = 0 if u == 7 else 1
                x_norm_A(ngc // NCH, ngc % NCH, ptt)
            for e in range(2):
                i = 2 * u + e
                rb = i % 2
                ub = 0 + 2 * rb
                ac = 4 + rb
                gl = 6 + rb
                ubuf = fS[:, ub:ub + 2, :].rearrange("p a b -> p (a b)")
                UBK = [K("fS", ub), K("fS", ub + 1)]
                pu = rP.nxt()
                for kc in range(8):
                    mm(Pf[pu][:, :], wu[:, kc, e * 128:(e + 1) * 128], hT[:, kc, :], kc == 0, kc == 7, HTK + [ku],
                       [K("P", pu)], kc == 7)
                pv2 = rP.nxt()
                for kc in range(8):
                    mm(Pf[pv2][:, :], wu[:, kc, 256 + e * 128:256 + (e + 1) * 128], hT[:, kc, :], kc == 0, kc == 7,
                       HTK + [ku], [K("P", pv2)], kc == 7)
                sch.add(ACT, lambda pu=pu, ubuf=ubuf: nc.scalar.copy(out=ubuf[:, 2:514], in_=Pf[pu][:, :]),
                        reads=[K("P", pu)], writes=UBK)
                sch.add(POOL, lambda ubuf=ubuf, i=i: nc.gpsimd.tensor_copy(out=ubuf[:, 0:2], in_=halo[:, par, i, :]),
                        reads=[K("halo", par)], writes=UBK)
                sch.add(POOL, lambda ubuf=ubuf, i=i: nc.gpsimd.tensor_copy(out=halo[:, 1 - par, i, :],
                                                                           in_=ubuf[:, 512:514]),
                        reads=UBK, writes=[K("halo", 1 - par)])
                sch.add(DVE, lambda ubuf=ubuf, i=i, ac=ac: nc.vector.tensor_scalar(
                    out=fS[:, ac, :], in0=ubuf[:, 2:514], scalar1=cw[:, 2, i:i + 1], scalar2=cb[:, i:i + 1],
                    op0=ALU.mult, op1=ALU.add), reads=UBK + [K("cw"), K("cb")], writes=[K("fS", ac)])
                sch.add(DVE, lambda ubuf=ubuf, i=i, ac=ac: nc.vector.scalar_tensor_tensor(
                    out=fS[:, ac, :], in0=ubuf[:, 1:513], scalar=cw[:, 1, i:i + 1], in1=fS[:, ac, :], op0=ALU.mult,
                    op1=ALU.add), reads=UBK + [K("cw"), K("fS", ac)], writes=[K("fS", ac)])
                sch.add(DVE, lambda ubuf=ubuf, i=i, ac=ac: nc.vector.scalar_tensor_tensor(
                    out=fS[:, ac, :], in0=ubuf[:, 0:512], scalar=cw[:, 0, i:i + 1], in1=fS[:, ac, :], op0=ALU.mult,
                    op1=ALU.add), reads=UBK + [K("cw"), K("fS", ac)], writes=[K("fS", ac)])
                sch.add(ACT, lambda ac=ac, gl=gl: nc.scalar.activation(out=fS[:, gl, :], in_=fS[:, ac, :], func=AF.Gelu),
                        reads=[K("fS", ac)], writes=[K("fS", gl)])
                sch.add(DVE, lambda gl=gl, pv2=pv2, i=i: nc.vector.tensor_tensor(out=bB[:, i, :], in0=fS[:, gl, :],
                                                                                 in1=Pf[pv2][:, :], op=ALU.mult),
                        reads=[K("fS", gl), K("P", pv2)], writes=[K("bB", i)])
        if first:
            dump("gT", bB[:, 0:NKF, :], GT, [128, NKF, 512], BF16)
        if has_next:
            for ptt in (0, 1):
                norm_B(ptt)
                prefetched.add((ngc, ptt))
        for nh in range(2):
            banks = [0, 1, 2, 3] if nh == 0 else [4, 5, 0, 1]
            for pi_, (k0, nk) in enumerate(DN_P):
                wd, kd = wblk(gc, B_DN0 + nh * 3 + pi_)
                for tt in range(4):
                    for kk in range(nk):
                        kc = k0 + kk
                        mm(Pf[banks[tt]][:, :], bB[:, kc, tt * 128:(tt + 1) * 128], wd[:, kk, :], kc == 0,
                           kc == NKF - 1, [K("bB", kc), kd], [K("P", banks[tt])], kk == nk - 1)
            for tt in range(4):
                sch.add(DVE, lambda nh=nh, tt=tt, b=banks[tt]: nc.vector.tensor_tensor(
                    out=xmid[:, tt, nh * 512:(nh + 1) * 512], in0=xmid[:, tt, nh * 512:(nh + 1) * 512],
                    in1=Pf[b][:, :], op=ALU.add),
                    reads=[K("xmid", tt, nh), K("P", banks[tt])], writes=[K("xmid", tt, nh)])
        sch.add(OQ, lambda: nc.gpsimd.dma_start(
            out=out[s, c * CH:(c + 1) * CH, :].rearrange("(t p) d -> p t d", p=128), in_=xmid[:]),
            reads=[K("xmid", t, n) for t in range(4) for n in range(2)])

    for s in range(NSEQ):
        for c in range(NCH):
            chunk(s, c)
    stats = sch.finalize(nc.sync)
    es.close()
    return nc, dumps, stats


_CACHE = {}


def kernel(**inputs):
    nseq = 32 // NCORES
    if "nc" not in _CACHE:
        _CACHE["nc"] = build(nseq)[0]
    nc = _CACHE["nc"]
    consts = host_consts()
    x = np.ascontiguousarray(np.asarray(inputs["x"], dtype=np.float32))
    shared = {
        "norm1_g": np.asarray(inputs["norm1_g"], np.float32).reshape(1, D),
        "norm2_g": np.asarray(inputs["norm2_g"], np.float32).reshape(1, D),
        "w_in": np.asarray(inputs["w_in"], np.float32).reshape(D, 5632),
        "hg_lb_logits": np.asarray(inputs["hg_lb_logits"], np.float32).reshape(2, 512),
        "hg_onorm_g": np.asarray(inputs["hg_onorm_g"], np.float32).reshape(1, 512),
        "q_norm_g": np.asarray(inputs["q_norm_g"], np.float32).reshape(1, 64),
        "k_norm_g": np.asarray(inputs["k_norm_g"], np.float32).reshape(1, 64),
        "w_a": np.asarray(inputs["w_a"], np.float32).reshape(512, D),
        "w_b": np.asarray(inputs["w_b"], np.float32).reshape(512, D),
        "w_out": np.asarray(inputs["w_out"], np.float32).reshape(D, D),
        "w_up": np.asarray(inputs["w_up"], np.float32).reshape(D, 2 * DFF),
        "conv_w": np.asarray(inputs["conv_w"], np.float32).reshape(3, DFF),
        "conv_b": np.asarray(inputs["conv_b"], np.float32).reshape(1, DFF),
        "w_down": np.asarray(inputs["w_down"], np.float32).reshape(DFF, D),
    }
    shared.update(consts)
    in_maps = []
    for i in range(NCORES):
        m = dict(shared)
        m["x"] = x[i * nseq:(i + 1) * nseq]
        in_maps.append(m)
    res = run_bass_kernel_spmd(nc, in_maps, core_ids=list(range(NCORES)))
    return np.concatenate([np.asarray(r["out"]) for r in res.results], axis=0).astype(np.float32)
```

```python
from contextlib import ExitStack
import numpy as np
import concourse.bass as bass
import concourse.mybir as mybir
from concourse.bass_utils import run_bass_kernel_spmd

F32 = mybir.dt.float32
BF16 = mybir.dt.bfloat16
AF = mybir.ActivationFunctionType
ALU = mybir.AluOpType
AX = mybir.AxisListType

NCORES = 8
S = 2048
D = 1024
CH = 512
NCH = S // CH
DFF = 2816
NKF = DFF // 128
EPS = 1e-6
BIG = 30000.0
NBLK = 32
RING = 3

(B_HQ, B_HF, B_HI, B_HG, B_MK, B_MQ, B_MV, B_GAB0, B_WAB0, B_GAB1, B_GAB2, B_WAB1, B_GAB3,
 B_WO0, B_WO1) = range(15)
B_UP0 = 15
B_DN0 = 26


class Q:
    def __init__(self, name, issuer, sem, inc, kind):
        self.name, self.issuer, self.sem, self.inc, self.kind = name, issuer, sem, inc, kind
        self.nsig = 0
        self.last = None


class Sched:
    def __init__(self, nc):
        self.nc = nc
        self.ins = []
        self.queues = []

    def queue(self, name, issuer, sem, inc, kind):
        q = Q(name, issuer, sem, inc, kind)
        self.queues.append(q)
        return q

    def add(self, q, fn, reads=(), writes=(), sig=True):
        self.ins.append((q, fn, tuple(reads), tuple(writes), sig or q.kind == 'dma'))

    def finalize(self, final_issuer):
        ins = self.ins
        n = len(ins)
        sigval = [0] * n
        nextsig = [None] * n
        for i, (q, fn, r, w, sig) in enumerate(ins):
            if sig:
                q.nsig += 1
                sigval[i] = q.nsig
        lastsig = {}
        for i in range(n - 1, -1, -1):
            q = ins[i][0]
            if ins[i][4]:
                lastsig[q] = i
            nextsig[i] = lastsig.get(q)
        writers, readers = {}, {}
        clocks = {}
        iclk = [None] * n
        nwaits = 0
        for i, (q, fn, rds, wrs, sig) in enumerate(ins):
            deps = set()
            for k in rds:
                for qq, j in writers.get(k, {}).items():
                    if qq is q and q.kind == 'pe':
                        continue
                    deps.add(j)
            for k in wrs:
                for qq, j in writers.get(k, {}).items():
                    if qq is q and q.kind != 'dma':
                        continue
                    deps.add(j)
                for qq, j in readers.get(k, {}).items():
                    if qq is q and q.kind != 'dma':
                        continue
                    deps.add(j)
            if q.kind == 'dma' and q.last is not None:
                deps.add(q.last)
            clk = clocks.setdefault(id(q.issuer), {})
            for j in sorted(deps):
                js = nextsig[j]
                assert js is not None and js < i, f"dep signal after waiter: ins {i} dep {j} sig {js}"
                qj = ins[js][0]
                val = sigval[js] * qj.inc
                if clk.get(qj, 0) >= val:
                    continue
                q.issuer.wait_ge(qj.sem, val)
                nwaits += 1
                for qq, v in iclk[js].items():
                    if clk.get(qq, 0) < v:
                        clk[qq] = v
            r = fn()
            if sig:
                r.then_inc(q.sem, q.inc)
                c2 = dict(clk)
                c2[q] = sigval[i] * q.inc
                iclk[i] = c2
            for k in rds:
                readers.setdefault(k, {})[q] = i
            for k in wrs:
                writers.setdefault(k, {})[q] = i
            if q.kind == 'dma':
                q.last = i
        for q in self.queues:
            if q.nsig:
                final_issuer.wait_ge(q.sem, q.nsig * q.inc)
        return n, nwaits


def host_consts():
    c = {}
    c["c_ident"] = np.eye(128, dtype=np.float32)
    s = np.arange(128)[:, None]
    t = np.arange(128)[None, :]
    same = (s // 64) == (t // 64)
    c["c_U"] = ((s > t) & same).astype(np.float32)
    c["c_bd"] = ((s <= t) & same).astype(np.float32)
    c["c_tri"] = (s <= t).astype(np.float32)
    c["c_cind"] = np.stack([(np.arange(128) < 64), (np.arange(128) >= 64)], 1).astype(np.float32)
    half = 32
    inv = 1.0 / (10000.0 ** (np.arange(half, dtype=np.float32) * 2.0 / 64))
    pos = np.arange(S, dtype=np.float32)
    ang = pos[:, None] * inv[None, :]
    cs = np.stack([np.cos(ang), np.sin(ang)], 1).astype(np.float32)
    c["c_cs"] = np.ascontiguousarray(cs.reshape(16, 128, 2, 32).transpose(1, 0, 2, 3))
    kind = (np.arange(S)[None, :] // 256 == np.arange(8)[:, None]).astype(np.float32)
    c["c_kind"] = kind
    return c


def build(NSEQ, dump_names=()):
    nc = bass.Bass("TRN2", target_bir_lowering=False)
    es = ExitStack()

    def din(name, shape, dt=F32):
        return nc.dram_tensor(name, list(shape), dt, kind="ExternalInput").ap()

    x = din("x", [NSEQ, S, D])
    norm1_g = din("norm1_g", [1, D]); norm2_g = din("norm2_g", [1, D])
    w_in = din("w_in", [D, 5632]); hg_lb = din("hg_lb_logits", [2, 512])
    hg_on = din("hg_onorm_g", [1, 512]); qng = din("q_norm_g", [1, 64]); kng = din("k_norm_g", [1, 64])
    w_a = din("w_a", [512, D]); w_b = din("w_b", [512, D]); w_out = din("w_out", [D, D])
    w_up = din("w_up", [D, 2 * DFF]); conv_w = din("conv_w", [3, DFF]); conv_b = din("conv_b", [1, DFF])
    w_down = din("w_down", [DFF, D])
    c_ident = din("c_ident", [128, 128]); c_U = din("c_U", [128, 128]); c_bd = din("c_bd", [128, 128])
    c_tri = din("c_tri", [128, 128]); c_cind = din("c_cind", [128, 2]); c_cs = din("c_cs", [128, 16, 2, 32])
    c_kind = din("c_kind", [8, S])
    out = nc.dram_tensor("out", [NSEQ, S, D], F32, kind="ExternalOutput").ap()
    wbf = nc.dram_tensor("wbf", [NBLK, 128, 4096], BF16).ap()
    dumps = {}

    def sb(name, shape, dt):
        return es.enter_context(nc.sbuf_tensor(name, list(shape), dt))

    def ps(name, shape, dt):
        return es.enter_context(nc.psum_tensor(name, list(shape), dt))

    def sem(name):
        return es.enter_context(nc.semaphore(name))

    sch = Sched(nc)
    PE = sch.queue("pe", nc.tensor, sem("s_pe"), 1, 'pe')
    ACT = sch.queue("act", nc.scalar, sem("s_act"), 1, 'cmp')
    DVE = sch.queue("dve", nc.vector, sem("s_dve"), 1, 'cmp')
    POOL = sch.queue("pool", nc.gpsimd, sem("s_pool"), 1, 'cmp')
    WQ = [sch.queue(f"wq{i}", nc.sync, sem(f"s_wq{i}"), 16, 'dma') for i in range(RING)]
    XQ = [sch.queue(f"xq{i}", nc.sync, sem(f"s_xq{i}"), 16, 'dma') for i in range(2)]
    OQ = sch.queue("oq", nc.gpsimd, sem("s_oq"), 16, 'dma')
    CQ = [sch.queue(f"cq{i}", nc.gpsimd, sem(f"s_cq{i}"), 16, 'dma') for i in range(4)]
    KQ = [sch.queue(f"kq{i}", nc.sync, sem(f"s_kq{i}"), 16, 'dma') for i in range(4)]
    DQ = sch.queue("dq", nc.sync, sem("s_dq"), 16, 'dma')

    wring = sb("wring", [128, RING, 4096], BF16)
    kTe = sb("kTe", [128, 8, S], BF16)
    vext = sb("vext", [128, 16, 8, 65], BF16)
    xmid = sb("xmid", [128, 4, D], F32)
    hT = sb("hT", [128, 8, CH], BF16)
    g1b = sb("g1b", [128, D], F32); g2b = sb("g2b", [128, D], F32)
    fS = sb("fS", [128, 18, 512], F32)
    bB = sb("bB", [128, 24, 512], BF16)
    Sst = sb("Sst", [128, 512], F32)
    qtil = sb("qtil", [128, 2, 512], BF16)
    attn = sb("attn", [128, 2, 512], BF16)
    sdb = sb("sdb", [128, 2, 512], BF16)
    oabf = sb("oabf", [128, 2, 512], BF16)
    oaT = sb("oaT", [128, 4, CH], BF16); obT = sb("obT", [128, 4, CH], BF16)
    ropeo = sb("ropeo", [128, 2, 512], BF16)
    bias = sb("bias", [128, 2, 8, 72], BF16)
    pt = sb("pt", [128, 3, 512], BF16)
    ident = sb("ident", [128, 128], BF16)
    Um = sb("Um", [128, 128], F32); bd = sb("bd", [128, 128], F32); tri = sb("tri", [128, 128], BF16)
    cind = sb("cind", [128, 2], F32)
    omlb = sb("omlb", [128, 512], F32); gob = sb("gob", [128, 512], F32)
    cs = sb("cs", [128, 16, 2, 32], F32)
    gq8 = sb("gq8", [128, 64], F32); gkb = sb("gkb", [128, 64], F32)
    cw = sb("cw", [128, 3, NKF], F32); cb = sb("cb", [128, NKF], F32)
    halo = sb("halo", [128, 2, NKF, 2], F32)
    elast = sb("elast", [128, 4, 4, 2], F32)
    st = sb("st", [128, 128], F32)
    kmT = sb("kmT", [128, 8, 8], BF16)
    kmf = sb("kmf", [128, 8], F32)
    epsb = sb("epsb", [128, 1], F32)
    gsm = sb("gsm", [128, 64], F32)
    cmpb = sb("cmpb", [128, 8 * 7 * 7], F32)
    cnt = sb("cnt", [128, 56], F32)
    rden = sb("rden", [128, 2, 4], F32)
    Pf = [ps(f"P{i}", [128, 512], F32) for i in range(6)]
    Tb = [ps(f"T{i}", [128, 1024], BF16) for i in range(2)]

    K = lambda *a: tuple(a)

    def fslot(i, n=1):
        return fS[:, i, :] if n == 1 else fS[:, i:i + n, :]

    class Rot:
        def __init__(self, n):
            self.n, self.i = n, 0

        def nxt(self):
            v = self.i % self.n
            self.i += 1
            return v

    rP = Rot(4)
    rP6 = Rot(6)
    rP3 = Rot(3)
    rT = Rot(2)
    rxt = Rot(2)
    rpt = Rot(3)

    def dump(name, ap, keys, shape, dt=F32):
        if name not in dump_names or name in dumps:
            return
        t = nc.dram_tensor("dbg_" + name, list(shape), dt, kind="ExternalOutput").ap()
        dumps[name] = t
        sch.add(DQ, lambda: nc.sync.dma_start(out=t, in_=ap), reads=keys)

    cqi = [0]

    def cast(dst, src, b):
        q = CQ[cqi[0] % 4]
        cqi[0] += 1
        sch.add(q, lambda: nc.gpsimd.dma_start(out=dst, in_=src), writes=[K("wbf", b)])

    kqi = [0]

    def kload(dst, src, key, eng=None):
        q = KQ[kqi[0] % 4]
        kqi[0] += 1
        sch.add(q, lambda: nc.sync.dma_start(out=dst, in_=src), writes=[key])

    def kcast(dst, src, key):
        q = CQ[cqi[0] % 4]
        cqi[0] += 1
        sch.add(q, lambda: nc.gpsimd.dma_start(out=dst, in_=src), writes=[key])

    kcast(ident[:], c_ident, K("ident"))
    kcast(tri[:], c_tri, K("tri"))
    kload(Um[:], c_U, K("Um")); kload(bd[:], c_bd, K("bd")); kload(cind[:], c_cind, K("cind"))
    kload(cs[:], c_cs, K("cs"))
    kload(g1b[:], norm1_g.partition_broadcast(128), K("g1b"))
    kload(g2b[:], norm2_g.partition_broadcast(128), K("g2b"))
    kload(gob[:], hg_on.partition_broadcast(128), K("gob"))
    kload(gq8[:], qng.partition_broadcast(128), K("gq8"))
    kload(gkb[:], kng.partition_broadcast(128), K("gkb"))
    kload(fS[:, 0, :], hg_lb[0:1, :].partition_broadcast(128), K("fS", 0))
    kload(fS[:, 1, :], hg_lb[1:2, :].partition_broadcast(128), K("fS", 1))
    for j in range(3):
        sch.add(KQ[j % 4], (lambda j=j: nc.sync.dma_start(
            out=cw[:, j, :], in_=conv_w[j:j + 1, :].rearrange("o (kc p) -> p (o kc)", p=128),
            allow_slow_non_contiguous=True)), writes=[K("cw")])
    sch.add(KQ[3], lambda: nc.sync.dma_start(
        out=cb[:], in_=conv_b.rearrange("o (kc p) -> p (o kc)", p=128), allow_slow_non_contiguous=True),
        writes=[K("cb")])
    for h in range(8):
        kcast(kTe[64:72, h, :], c_kind, K("kTe_ind"))
    sch.add(DVE, lambda: nc.vector.tensor_tensor(out=fS[:, 0, :], in0=fS[:, 0, :], in1=fS[:, 1, :], op=ALU.subtract),
            reads=[K("fS", 0), K("fS", 1)], writes=[K("fS", 0)])
    sch.add(ACT, lambda: nc.scalar.activation(out=omlb[:], in_=fS[:, 0, :], func=AF.Sigmoid, scale=-1.0),
            reads=[K("fS", 0)], writes=[K("omlb")])
    sch.add(ACT, lambda: nc.scalar.mul(out=gq8[:], in_=gq8[:], mul=0.125), reads=[K("gq8")], writes=[K("gq8")])
    sch.add(POOL, lambda: nc.gpsimd.memset(vext[:, :, :, 64:65], 1.0), writes=[K("vext_one")])
    sch.add(POOL, lambda: nc.gpsimd.memset(epsb[:], EPS), writes=[K("epsb")])
    sch.add(POOL, lambda: nc.gpsimd.memset(bias[:, :, :, 0:64], 0.0), writes=[K("bias", 0), K("bias", 1)])

    def blkv(b, kc0, nkc, n0, nn):
        v = wbf[b].rearrange("p (kc n) -> p kc n", n=512)
        return v[:, kc0:kc0 + nkc, n0:n0 + nn]

    def rows(w, r0, nkc, c0, ncol):
        return w[r0:r0 + nkc * 128, c0:c0 + ncol].rearrange("(kc p) n -> p kc n", p=128)

    for g, b in enumerate([B_HQ, B_HF, B_HI, B_HG, B_MQ, B_MK, B_MV]):
        cast(blkv(b, 0, 8, 0, 512), rows(w_in, 0, 8, g * 512, 512), b)
    for qd, b in enumerate([B_GAB0, B_GAB1, B_GAB2, B_GAB3]):
        cast(blkv(b, 0, 8, 0, 256), rows(w_in, 0, 8, 3584 + qd * 256, 256), b)
        cast(blkv(b, 0, 8, 256, 256), rows(w_in, 0, 8, 4608 + qd * 256, 256), b)
    for hf, b in enumerate([B_WAB0, B_WAB1]):
        cast(blkv(b, 0, 4, 0, 512), rows(w_a, 0, 4, hf * 512, 512), b)
        cast(blkv(b, 4, 4, 0, 512), rows(w_b, 0, 4, hf * 512, 512), b)
    for hf, b in enumerate([B_WO0, B_WO1]):
        cast(blkv(b, 0, 8, 0, 512), rows(w_out, 0, 8, hf * 512, 512), b)
    for u in range(11):
        cast(blkv(B_UP0 + u, 0, 8, 0, 256), rows(w_up, 0, 8, u * 256, 256), B_UP0 + u)
        cast(blkv(B_UP0 + u, 0, 8, 256, 256), rows(w_up, 0, 8, DFF + u * 256, 256), B_UP0 + u)
    DN_P = [(0, 8), (8, 8), (16, 6)]
    for nh in range(2):
        for pi, (k0, nk) in enumerate(DN_P):
            cast(blkv(B_DN0 + nh * 3 + pi, 0, nk, 0, 512), rows(w_down, k0 * 128, nk, nh * 512, 512), B_DN0 + nh * 3 + pi)

    wstate = {"next": 0}
    total_blocks = NSEQ * NCH * NBLK

    def wload_upto(gb):
        while wstate["next"] <= gb and wstate["next"] < total_blocks:
            g = wstate["next"]
            slot = g % RING
            b = g % NBLK
            ne = 3072 if b in (B_DN0 + 2, B_DN0 + 5) else 4096
            sch.add(WQ[slot], (lambda slot=slot, b=b, ne=ne: nc.sync.dma_start(out=wring[:, slot, 0:ne],
                                                                              in_=wbf[b][:, 0:ne])),
                    reads=[K("wbf", b)], writes=[K("w", slot)])
            wstate["next"] += 1

    def wblk(gc, b, ahead=2):
        gb = gc * NBLK + b
        wload_upto(gb + ahead)
        slot = gb % RING
        return wring[:, slot, :].rearrange("p (kc n) -> p kc n", n=512), K("w", slot)

    def mm(out_ap, lhsT, rhs, start, stop, reads, writes, sig, **kw):
        sch.add(PE, lambda: nc.tensor.matmul(out_ap, lhsT=lhsT, rhs=rhs, start=start, stop=stop, **kw),
                reads=reads, writes=writes, sig=sig)

    def tp(out_ap, in_ap, reads, writes, sig):
        sch.add(PE, lambda: nc.tensor.transpose(out_ap, in_ap, ident[:]), reads=list(reads) + [K("ident")],
                writes=writes, sig=sig)

    hbuf = sb("hbuf", [128, 2, D], BF16)

    def norm_A(src_ap, src_keys, gb_t, gkey, tt, stc):
        junk = bB[:, 22:24, :].rearrange("p a b -> p (a b)")
        sch.add(ACT, lambda: nc.scalar.activation(out=junk, in_=src_ap, func=AF.Square, scale=1.0 / 32.0,
                                                  accum_out=st[:, stc:stc + 1]),
                reads=src_keys, writes=[K("bB", 22), K("bB", 23), K("st", stc)])
        sch.add(ACT, lambda: nc.scalar.activation(out=st[:, stc + 2:stc + 3], in_=st[:, stc:stc + 1], func=AF.Ln,
                                                  bias=epsb[:, 0:1]),
                reads=[K("st", stc), K("epsb")], writes=[K("st", stc + 2)])
        sch.add(ACT, lambda: nc.scalar.activation(out=st[:, stc + 3:stc + 4], in_=st[:, stc + 2:stc + 3], func=AF.Exp,
                                                  scale=-0.5),
                reads=[K("st", stc + 2)], writes=[K("st", stc + 3)])
        hi_ = tt % 2
        sch.add(DVE, lambda: nc.vector.scalar_tensor_tensor(out=hbuf[:, hi_, :], in0=src_ap,
                                                            scalar=st[:, stc + 3:stc + 4],
                                                            in1=gb_t[:], op0=ALU.mult, op1=ALU.mult),
                reads=list(src_keys) + [K("st", stc + 3), gkey], writes=[K("hbuf", hi_)])

    def norm_B(tt):
        hi_ = tt % 2
        ti = rT.nxt()
        for kc in range(8):
            tp(Tb[ti][:, kc * 128:(kc + 1) * 128], hbuf[:, hi_, kc * 128:(kc + 1) * 128], [K("hbuf", hi_)],
               [K("T", ti)], kc == 7)
        sch.add(DVE, lambda: nc.vector.tensor_copy(out=hT[:, :, tt * 128:(tt + 1) * 128],
                                                   in_=Tb[ti][:, :].rearrange("p (kc t) -> p kc t", t=128)),
                reads=[K("T", ti)], writes=[K("hT", tt)])

    def norm_tile(src_ap, src_keys, gb_t, gkey, tt, stc):
        norm_A(src_ap, src_keys, gb_t, gkey, tt, stc)
        norm_B(tt)

    def x_norm_A(s_, c_, tt):
        xi = rxt.nxt()
        xs = 12 + 2 * xi
        xap = fS[:, xs:xs + 2, :].rearrange("p a b -> p (a b)")
        xk = [K("fS", xs), K("fS", xs + 1)]
        r0 = c_ * CH + tt * 128
        sch.add(XQ[xi], lambda: nc.sync.dma_start(out=xap, in_=x[s_, r0:r0 + 128, :]), writes=xk)
        norm_A(xap, xk, g1b, K("g1b"), tt, 4 * tt)

    prefetched = set()

    HTK = [K("hT", t) for t in range(4)]

    def chunk(s, c):
        gc = s * NCH + c
        first = (gc == 0)
        for tt in range(4):
            if (gc, tt) in prefetched:
                continue
            x_norm_A(s, c, tt)
            norm_B(tt)
        if first:
            dump("hT", hT[:], HTK, [128, 8, CH], BF16)
        wq_, kq_ = wblk(gc, B_HQ)
        wf_, kf_ = wblk(gc, B_HF, 1)
        if c == 0:
            sch.add(DVE, lambda: nc.vector.memset(Sst[:], 0.0), writes=[K("Sst")])
        KQT = [K("bB", i) for i in range(8)]

        def h_stageA(tt):
            r = tt % 2
            qs, ks, ls = 0 + r, 2 + r, 4 + r
            pq = rP6.nxt()
            for kc in range(8):
                mm(Pf[pq][:, :], hT[:, kc, tt * 128:(tt + 1) * 128], wq_[:, kc, :], kc == 0, kc == 7,
                   [K("hT", tt), kq_], [K("P", pq)], kc == 7)
            sch.add(ACT, lambda: nc.scalar.activation(out=fS[:, qs, :], in_=Pf[pq][:, :], func=AF.Silu),
                    reads=[K("P", pq)], writes=[K("fS", qs)])
            pf = rP6.nxt()
            for kc in range(8):
                mm(Pf[pf][:, :], hT[:, kc, tt * 128:(tt + 1) * 128], wf_[:, kc, :], kc == 0, kc == 7,
                   [K("hT", tt), kf_], [K("P", pf)], kc == 7)
            sch.add(ACT, lambda: nc.scalar.activation(out=fS[:, ks, :], in_=Pf[pf][:, :], func=AF.Sigmoid, scale=-1.0),
                    reads=[K("P", pf)], writes=[K("fS", ks)])
            sch.add(DVE, lambda: nc.vector.tensor_tensor(out=fS[:, ks, :], in0=fS[:, ks, :], in1=omlb[:], op=ALU.mult),
                    reads=[K("fS", ks), K("omlb")], writes=[K("fS", ks)])
            sch.add(ACT, lambda: nc.scalar.activation(out=fS[:, ls, :], in_=fS[:, ks, :], func=AF.Ln, scale=-1.0,
                                                      bias=1.0),
                    reads=[K("fS", ks)], writes=[K("fS", ls)])

        def h_stageB(tt):
            r = tt % 2
            qs, ks, ls, es_, ns = 0 + r, 2 + r, 4 + r, 6 + r, 8 + r
            pa = rP6.nxt()
            mm(Pf[pa][:, :], Um[:], fS[:, ls, :], True, True, [K("Um"), K("fS", ls)], [K("P", pa)], True)
            sch.add(ACT, lambda: nc.scalar.activation(out=fS[:, es_, :], in_=Pf[pa][:, :], func=AF.Exp),
                    reads=[K("P", pa)], writes=[K("fS", es_)])
            sch.add(ACT, lambda: nc.scalar.activation(out=fS[:, ns, :], in_=Pf[pa][:, :], func=AF.Exp, scale=-1.0),
                    reads=[K("P", pa)], writes=[K("fS", ns)])
            pl = rP6.nxt()
            for h in range(4):
                mm(Pf[pl][:, h * 2:h * 2 + 2], fS[:, ls, h * 128:(h + 1) * 128], cind[:], True, True,
                   [K("fS", ls), K("cind")], [K("P", pl)], h == 3)
            sch.add(ACT, lambda: nc.scalar.activation(
                out=elast[:, tt, :, :].rearrange("p h j -> p (h j)"), in_=Pf[pl][:, 0:8], func=AF.Exp),
                reads=[K("P", pl)], writes=[K("elast", tt)])
            sch.add(DVE, lambda: nc.vector.tensor_tensor(out=bB[:, 12 + tt, :], in0=fS[:, ks, :], in1=fS[:, es_, :],
                                                         op=ALU.mult),
                    reads=[K("fS", ks), K("fS", es_)], writes=[K("bB", 12 + tt)])
            sch.add(DVE, lambda: nc.vector.tensor_tensor(out=qtil[:, r, :], in0=fS[:, qs, :], in1=fS[:, ns, :],
                                                         op=ALU.mult),
                    reads=[K("fS", qs), K("fS", ns)], writes=[K("qtil", r)])
            if first and tt == 0:
                dump("khat0", bB[:, 12, :], [K("bB", 12)], [128, 512], BF16)
                dump("qtil0", qtil[:, 0, :], [K("qtil", 0)], [128, 512], BF16)

        def h_stageB2(tt):
            r = tt % 2
            ti = rT.nxt()
            for h in range(4):
                tp(Tb[ti][:, h * 128:(h + 1) * 128], bB[:, 12 + tt, h * 128:(h + 1) * 128], [K("bB", 12 + tt)],
                   [K("T", ti)], False)
            for h in range(4):
                tp(Tb[ti][:, (4 + h) * 128:(5 + h) * 128], qtil[:, r, h * 128:(h + 1) * 128], [K("qtil", r)],
                   [K("T", ti)], h == 3)
            sch.add(ACT, lambda: nc.scalar.copy(out=bB[:, 0:8, tt * 128:(tt + 1) * 128],
                                                in_=Tb[ti][:, :].rearrange("p (a t) -> p a t", t=128)),
                    reads=[K("T", ti)], writes=[K("kqT", tt)] + KQT)

        hi_state = {}

        def h_hi(tt):
            if "w" not in hi_state:
                hi_state["w"] = wblk(gc, B_HI)
            wi_, ki_ = hi_state["w"]
            pv = rP6.nxt()
            for kc in range(8):
                mm(Pf[pv][:, :], hT[:, kc, tt * 128:(tt + 1) * 128], wi_[:, kc, :], kc == 0, kc == 7,
                   [K("hT", tt), ki_], [K("P", pv)], kc == 7)
            sch.add(ACT, lambda: nc.scalar.copy(out=bB[:, 8 + tt, :], in_=Pf[pv][:, :]),
                    reads=[K("P", pv)], writes=[K("bB", 8 + tt)])

        h_stageA(0); h_stageA(1); h_stageB(0); h_stageA(2); h_stageB(1); h_stageB2(0); h_stageA(3); h_stageB(2)
        h_stageB2(1); h_hi(0); h_stageB(3); h_hi(1); h_stageB2(2); h_hi(2); h_hi(3); h_stageB2(3)
        wg_, kg_ = wblk(gc, B_HG)
        for tt in range(4):
            ph = rP6.nxt()
            for kc in range(8):
                mm(Pf[ph][:, :], hT[:, kc, tt * 128:(tt + 1) * 128], wg_[:, kc, :], kc == 0, kc == 7,
                   [K("hT", tt), kg_], [K("P", ph)], kc == 7)
            sch.add(ACT, lambda ph=ph, tt=tt: nc.scalar.activation(out=fS[:, tt, :], in_=Pf[ph][:, :], func=AF.Silu),
                    reads=[K("P", ph)], writes=[K("fS", tt)])
            sch.add(POOL, lambda tt=tt: nc.gpsimd.tensor_tensor(out=fS[:, tt, :], in0=fS[:, tt, :], in1=gob[:],
                                                                op=ALU.mult),
                    reads=[K("fS", tt), K("gob")], writes=[K("fS", tt)])

        def h_post_tr(tt):
            r = tt % 2
            ti = rT.nxt()
            for kc in range(4):
                tp(Tb[ti][:, kc * 128:(kc + 1) * 128], oabf[:, r, kc * 128:(kc + 1) * 128], [K("oabf", r)],
                   [K("T", ti)], kc == 3)
            sch.add(ACT, lambda: nc.scalar.copy(out=oaT[:, :, tt * 128:(tt + 1) * 128],
                                                in_=Tb[ti][:, 0:512].rearrange("p (a t) -> p a t", t=128)),
                    reads=[K("T", ti)], writes=[K("oaT", tt)])

        def h_attn(tt):
            r = tt % 2
            po = 4 + r
            pat = rP3.nxt()
            for h in range(4):
                mm(Pf[pat][:, h * 128:(h + 1) * 128], bB[:, h, tt * 128:(tt + 1) * 128],
                   bB[:, 4 + h, tt * 128:(tt + 1) * 128], True, True, KQT, [K("P", pat)], h == 3)
            sch.add(DVE, lambda: nc.vector.scalar_tensor_tensor(
                out=attn[:, r, :].rearrange("p (h t) -> p h t", h=4),
                in0=Pf[pat][:, :].rearrange("p (h t) -> p h t", h=4), scalar=1e30,
                in1=bd[:].unsqueeze(1).broadcast_to([128, 4, 128]), op0=ALU.min, op1=ALU.mult),
                reads=[K("P", pat), K("bd")], writes=[K("attn", r)])
            for h in range(4):
                mm(Pf[po][:, h * 128:(h + 1) * 128], attn[:, r, h * 128:(h + 1) * 128],
                   bB[:, 8 + tt, h * 128:(h + 1) * 128], h == 0, False, [K("attn", r), K("bB", 8 + tt)],
                   [K("P", po)], False, skip_group_check=True)

        def h_rec(tt):
            r = tt % 2
            po = 4 + r
            for j in range(2):
                n = 2 * tt + j
                sd = n % 2
                sch.add(DVE, lambda j=j: nc.vector.tensor_tensor(
                    out=fS[:, 16, :].rearrange("p (h v) -> p h v", h=4),
                    in0=Sst[:].rearrange("p (h v) -> p h v", h=4),
                    in1=elast[:, tt, :, j:j + 1].broadcast_to([128, 4, 128]), op=ALU.mult),
                    reads=[K("Sst"), K("elast", tt)], writes=[K("fS", 16)])
                sch.add(ACT, lambda sd=sd: nc.scalar.copy(out=sdb[:, sd, :], in_=fS[:, 16, :]),
                        reads=[K("fS", 16)], writes=[K("sdb", sd)])
                for h in range(4):
                    mm(Pf[3][:, h * 128:(h + 1) * 128], bB[j * 64:(j + 1) * 64, 12 + tt, h * 128:(h + 1) * 128],
                       bB[j * 64:(j + 1) * 64, 8 + tt, h * 128:(h + 1) * 128], True, True,
                       [K("bB", 12 + tt), K("bB", 8 + tt)], [K("P", 3)], h == 3)
                for h in range(4):
                    t0 = tt * 128 + j * 64
                    mm(Pf[po][j * 64:(j + 1) * 64, h * 128:(h + 1) * 128], bB[:, 4 + h, t0:t0 + 64],
                       sdb[:, sd, h * 128:(h + 1) * 128], False, (j == 1 and h == 3), KQT + [K("sdb", sd)],
                       [K("P", po)], (j == 1 and h == 3), skip_group_check=True)
                sch.add(DVE, lambda: nc.vector.tensor_tensor(out=Sst[:], in0=fS[:, 16, :], in1=Pf[3][:, :], op=ALU.add),
                        reads=[K("fS", 16), K("P", 3)], writes=[K("Sst")])

        def h_post(tt):
            r = tt % 2
            po = 4 + r
            so = 16 + 16 * r
            for h in range(4):
                sch.add(ACT, lambda h=h: nc.scalar.activation(
                    out=bB[:, 22, h * 128:(h + 1) * 128], in_=Pf[po][:, h * 128:(h + 1) * 128], func=AF.Square,
                    scale=float(1.0 / np.sqrt(128.0)), accum_out=st[:, so + h:so + h + 1]),
                    reads=[K("P", po)], writes=[K("bB", 22), K("st", so + h)])
            sch.add(ACT, lambda: nc.scalar.activation(out=st[:, so + 8:so + 12], in_=st[:, so:so + 4], func=AF.Ln,
                                                      bias=epsb[:, 0:1]),
                    reads=[K("st", so + h) for h in range(4)] + [K("epsb")], writes=[K("st", so + 8)])
            sch.add(ACT, lambda: nc.scalar.activation(out=st[:, so + 12:so + 16], in_=st[:, so + 8:so + 12],
                                                      func=AF.Exp, scale=-0.5),
                    reads=[K("st", so + 8)], writes=[K("st", so + 12)])
            for h in range(4):
                sch.add(DVE, lambda h=h: nc.vector.scalar_tensor_tensor(
                    out=oabf[:, r, h * 128:(h + 1) * 128], in0=Pf[po][:, h * 128:(h + 1) * 128],
                    scalar=st[:, so + 12 + h:so + 13 + h], in1=fS[:, tt, h * 128:(h + 1) * 128], op0=ALU.mult,
                    op1=ALU.mult),
                    reads=[K("P", po), K("st", so + 12), K("fS", tt)], writes=[K("oabf", r)])
            if first and tt == 0:
                dump("oa0", oabf[:, 0, :], [K("oabf", 0)], [128, 512], BF16)

        h_attn(0)
        for tt in range(4):
            h_rec(tt)
            if tt < 3:
                h_attn(tt + 1)
            h_post(tt)
            if tt >= 1:
                h_post_tr(tt - 1)
        QTE = [K("bB", 16 + i) for i in range(8)]
        qTe = bB[:, 16:24, :]
        wmk, kmk = wblk(gc, B_MK)
        wstore = {"k": (wmk, kmk)}

        def m_stageA(kind, tt, idx):
            is_q = kind == "q"
            if is_q and "q" not in wstore:
                wstore["q"] = wblk(gc, B_MQ)
            wv, wk = wstore[kind]
            par = idx % 3
            sq, mn = [4, 5, 10][par], [6, 7, 11][par]
            sc = 64 + 16 * par
            pm = rP.nxt()
            for kc in range(8):
                mm(Pf[pm][:, :], hT[:, kc, tt * 128:(tt + 1) * 128], wv[:, kc, :], kc == 0, kc == 7,
                   [K("hT", tt), wk], [K("P", pm)], kc == 7)
            sch.add(ACT, lambda: nc.scalar.activation(out=fS[:, sq, :], in_=Pf[pm][:, :], func=AF.Square, scale=0.125),
                    reads=[K("P", pm)], writes=[K("fS", sq)])
            sch.add(DVE, lambda: nc.vector.tensor_reduce(out=st[:, sc:sc + 8],
                                                         in_=fS[:, sq, :].rearrange("p (h d) -> p h d", h=8),
                                                         axis=AX.X, op=ALU.add),
                    reads=[K("fS", sq)], writes=[K("st", sc)])
            sch.add(ACT, lambda: nc.scalar.activation(out=st[:, sc + 8:sc + 16], in_=st[:, sc:sc + 8], func=AF.Ln,
                                                      bias=epsb[:, 0:1]),
                    reads=[K("st", sc), K("epsb")], writes=[K("st", sc + 8)])
            sch.add(ACT, lambda: nc.scalar.activation(out=st[:, sc:sc + 8], in_=st[:, sc + 8:sc + 16], func=AF.Exp,
                                                      scale=-0.5),
                    reads=[K("st", sc + 8)], writes=[K("st", sc)])
            sch.add(DVE, lambda: nc.vector.tensor_tensor(
                out=fS[:, mn, :].rearrange("p (h d) -> p h d", h=8),
                in0=Pf[pm][:, :].rearrange("p (h d) -> p h d", h=8),
                in1=st[:, sc:sc + 8].unsqueeze(2).broadcast_to([128, 8, 64]), op=ALU.mult),
                reads=[K("P", pm), K("st", sc)], writes=[K("fS", mn)])
            gvec, gkey = (gq8, K("gq8")) if is_q else (gkb, K("gkb"))
            m3 = fS[:, mn, :].rearrange("p (h d) -> p h d", h=8)
            sch.add(DVE, lambda: nc.vector.tensor_tensor(out=m3, in0=m3,
                                                         in1=gvec[:].unsqueeze(1).broadcast_to([128, 8, 64]),
                                                         op=ALU.mult),
                    reads=[K("fS", mn), gkey], writes=[K("fS", mn)])

        def m_stageB(kind, tt, idx):
            is_q = kind == "q"
            par = idx % 3
            mn, tB = [6, 7, 11][par], [8, 9, 17][par]
            tile_i = c * 4 + tt
            m4 = fS[:, mn, :].rearrange("p (h a d) -> p h a d", h=8, a=2)
            tB4 = fS[:, tB, :].rearrange("p (h a d) -> p h a d", h=8, a=2)
            cosb = cs[:, tile_i, 0, :]
            sinb3 = cs[:, tile_i, 1, :].unsqueeze(1).broadcast_to([128, 8, 32])
            sch.add(POOL, lambda: nc.gpsimd.tensor_tensor(out=tB4[:, :, 0, :], in0=m4[:, :, 1, :], in1=sinb3,
                                                          op=ALU.mult),
                    reads=[K("fS", mn), K("cs")], writes=[K("fS", tB)])
            sch.add(POOL, lambda: nc.gpsimd.tensor_tensor(out=tB4[:, :, 1, :], in0=m4[:, :, 0, :], in1=sinb3,
                                                          op=ALU.mult),
                    reads=[K("fS", mn), K("cs")], writes=[K("fS", tB)])
            sch.add(DVE, lambda: nc.vector.tensor_tensor(
                out=m4, in0=m4, in1=cosb.unsqueeze(1).unsqueeze(1).broadcast_to([128, 8, 2, 32]), op=ALU.mult),
                reads=[K("fS", mn), K("cs"), K("fS", tB)], writes=[K("fS", mn)])
            ro = idx % 2
            ro4 = ropeo[:, ro, :].rearrange("p (h a d) -> p h a d", h=8, a=2)
            sch.add(POOL, lambda: nc.gpsimd.tensor_tensor(out=ro4[:, :, 0, :], in0=m4[:, :, 0, :], in1=tB4[:, :, 0, :],
                                                          op=ALU.subtract),
                    reads=[K("fS", mn), K("fS", tB)], writes=[K("ropeo", ro)])
            sch.add(POOL, lambda: nc.gpsimd.tensor_tensor(out=ro4[:, :, 1, :], in0=m4[:, :, 1, :], in1=tB4[:, :, 1, :],
                                                          op=ALU.add),
                    reads=[K("fS", mn), K("fS", tB)], writes=[K("ropeo", ro)])

        def m_stageB2(kind, tt, idx):
            is_q = kind == "q"
            ro = idx % 2
            tile_i = c * 4 + tt
            ti = rT.nxt()
            for h in range(8):
                tp(Tb[ti][0:64, h * 128:(h + 1) * 128], ropeo[:, ro, h * 64:(h + 1) * 64], [K("ropeo", ro)],
                   [K("T", ti)], h == 7)
            src = Tb[ti][0:64, :].rearrange("p (h t) -> p h t", h=8)
            if is_q:
                sch.add(DVE, lambda: nc.vector.tensor_copy(out=qTe[0:64, :, tt * 128:(tt + 1) * 128], in_=src),
                        reads=[K("T", ti)], writes=QTE + [K("qTe", tt)])
            else:
                p0 = c * CH + tt * 128
                sch.add(DVE, lambda: nc.vector.tensor_copy(out=kTe[0:64, :, p0:p0 + 128], in_=src),
                        reads=[K("T", ti)], writes=[K("kTe", tile_i)])
                if tt % 2 == 1:
                    blk = 2 * c + tt // 2
                    sch.add(DVE, lambda: nc.vector.tensor_reduce(out=kmf[0:64, :],
                                                                 in_=kTe[0:64, :, blk * 256:(blk + 1) * 256],
                                                                 axis=AX.X, op=ALU.add),
                            reads=[K("kTe", 2 * blk), K("kTe", 2 * blk + 1)], writes=[K("kmf")])
                    sch.add(DVE, lambda: nc.vector.tensor_scalar(out=kmT[0:64, :, blk:blk + 1],
                                                                 in0=kmf[0:64, :].unsqueeze(2), scalar1=1.0 / 256.0,
                                                                 scalar2=None, op0=ALU.mult),
                            reads=[K("kmf")], writes=[K("kmT")])

        def m_stageC(tt):
            qb = 2 * c + tt // 2
            bi = tt % 2
            if qb >= 4:
                pg = rP.nxt()
                for h in range(8):
                    mm(Pf[pg][:, h * 8:h * 8 + qb], qTe[0:64, h, tt * 128:(tt + 1) * 128], kmT[0:64, h, 0:qb], True, True,
                       [K("qTe", tt), K("kmT")], [K("P", pg)], h == 7)
                sch.add(ACT, lambda: nc.scalar.copy(
                    out=gsm[:].rearrange("p (h j) -> p h j", h=8)[:, :, 0:qb],
                    in_=Pf[pg][:, 0:64].rearrange("p (h j) -> p h j", h=8)[:, :, 0:qb]),
                        reads=[K("P", pg)], writes=[K("gsm")])
                g3 = gsm[:].rearrange("p (h j) -> p h j", h=8)[:, :, 0:qb]
                c4 = cmpb[:, 0:8 * qb * qb].rearrange("p (h j k) -> p h j k", h=8, j=qb)
                sch.add(DVE, lambda: nc.vector.tensor_tensor(
                    out=c4, in0=g3.unsqueeze(2).broadcast_to([128, 8, qb, qb]),
                    in1=g3.unsqueeze(3).broadcast_to([128, 8, qb, qb]), op=ALU.is_gt),
                    reads=[K("gsm")], writes=[K("cmpb")])
                cn3 = cnt[:, 0:8 * qb].rearrange("p (h j) -> p h j", h=8)
                sch.add(DVE, lambda: nc.vector.tensor_reduce(out=cn3, in_=c4, axis=AX.X, op=ALU.add),
                        reads=[K("cmpb")], writes=[K("cnt")])
                sch.add(DVE, lambda: nc.vector.tensor_scalar(
                    out=bias[:, bi, :, 64:64 + qb], in0=cn3, scalar1=2.5, scalar2=-BIG, op0=ALU.is_gt, op1=ALU.mult),
                    reads=[K("cnt")], writes=[K("bias", bi)])
                sch.add(POOL, lambda: nc.gpsimd.memset(bias[:, bi, :, 64 + qb:65 + qb], 0.0), writes=[K("bias", bi)])
            else:
                sch.add(POOL, lambda: nc.gpsimd.memset(bias[:, bi, :, 64:65 + qb], 0.0), writes=[K("bias", bi)])
            if qb < 7:
                sch.add(POOL, lambda: nc.gpsimd.memset(bias[:, bi, :, 65 + qb:72], -BIG), writes=[K("bias", bi)])
            for hh in range(2):
                pb = rP.nxt()
                for h4 in range(4):
                    h = hh * 4 + h4
                    mm(Pf[pb][0:72, h4 * 128:(h4 + 1) * 128], bias[:, bi, h, :], ident[:], True, True,
                       [K("bias", bi), K("ident")], [K("P", pb)], h4 == 3)
                sch.add(ACT, lambda pb=pb, hh=hh: nc.scalar.copy(
                    out=qTe[64:72, hh * 4:hh * 4 + 4, tt * 128:(tt + 1) * 128],
                    in_=Pf[pb][64:72, :].rearrange("p (h t) -> p h t", h=4)),
                    reads=[K("P", pb)], writes=QTE + [K("qTeb", tt)])

        def m_v(tt):
            if "v" not in wstore:
                wstore["v"] = wblk(gc, B_MV)
            wmv, kmv = wstore["v"]
            tile_i = c * 4 + tt
            pv = rP.nxt()
            for kc in range(8):
                mm(Pf[pv][:, :], hT[:, kc, tt * 128:(tt + 1) * 128], wmv[:, kc, :], kc == 0, kc == 7,
                   [K("hT", tt), kmv], [K("P", pv)], kc == 7)
            sch.add(ACT, lambda: nc.scalar.copy(
                out=vext[:, tile_i, :, 0:64], in_=Pf[pv][:, :].rearrange("p (h d) -> p h d", h=8)),
                reads=[K("P", pv)], writes=[K("vext", tile_i)])

        items2 = [("k", t) for t in range(4)] + [("q", t) for t in range(4)]
        m_stageA("k", 0, 0)
        h_post_tr(3)
        m_stageA("k", 1, 1)
        m_stageB("k", 0, 0)
        for i in range(2, 8):
            m_stageB(items2[i - 1][0], items2[i - 1][1], i - 1)
            m_stageA(items2[i][0], items2[i][1], i)
            m_stageB2(items2[i - 2][0], items2[i - 2][1], i - 2)
        m_stageB("q", 3, 7)
        m_v(0)
        m_stageB2("q", 2, 6)
        m_stageC(0)
        m_v(1)
        m_stageB2("q", 3, 7)
        m_stageC(1)
        m_v(2)
        m_stageC(2)
        m_v(3)
        m_stageC(3)
        if first:
            dump("kTe", kTe[0:72, :, 0:512], [K("kTe", i) for i in range(4)] + [K("kTe_ind")], [72, 8, 512], BF16)
            dump("qTe", qTe[0:72, :, :], QTE, [72, 8, 512], BF16)
        nkt = 4 * c + 4
        OB = [K("bB", 8 + i) for i in range(4)]
        obv = bB[:, 8:12, :].rearrange("p t (h d) -> p t h d", h=8)
        items = [(h, kt) for h in range(8) for kt in range(nkt)]
        pend = None

        def att_qk(h, kt):
            n0 = max(0, kt * 128 - c * CH)
            pst = rP.nxt()
            pi = rpt.nxt()
            mm(Pf[pst][:, n0:512], kTe[0:72, h, kt * 128:(kt + 1) * 128], qTe[0:72, h, n0:512], True, True,
               [K("kTe", kt), K("kTe_ind")] + QTE, [K("P", pst)], True)
            sch.add(ACT, lambda pst=pst, pi=pi, n0=n0: nc.scalar.activation(out=pt[:, pi, n0:512],
                                                                          in_=Pf[pst][:, n0:512], func=AF.Exp),
                    reads=[K("P", pst)], writes=[K("pt", pi)])
            if kt * 128 >= c * CH:
                sch.add(DVE, lambda pi=pi, n0=n0: nc.vector.tensor_tensor(out=pt[:, pi, n0:n0 + 128],
                                                                          in0=pt[:, pi, n0:n0 + 128], in1=tri[:],
                                                                          op=ALU.mult),
                        reads=[K("pt", pi), K("tri")], writes=[K("pt", pi)])
            return (h, kt, n0, pi)

        def att_pv(h, kt, n0, pi):
            pob = 4 + (h % 2)
            for sub in range(n0 // 128, 4):
                last = (kt == nkt - 1 and sub == 3)
                mm(Pf[pob][:, sub * 65:(sub + 1) * 65], pt[:, pi, sub * 128:(sub + 1) * 128], vext[:, kt, h, :],
                   (kt == 0 and sub == 0), last, [K("pt", pi), K("vext", kt), K("vext_one")], [K("P", pob)],
                   sub == 3, skip_group_check=True)
            if kt == nkt - 1:
                rd = h % 2
                po3 = Pf[pob][:, 0:260].rearrange("p (s e) -> p s e", e=65)
                sch.add(DVE, lambda po3=po3, rd=rd: nc.vector.reciprocal(out=rden[:, rd, :].unsqueeze(2),
                                                                         in_=po3[:, :, 64:65]),
                        reads=[K("P", pob)], writes=[K("rden", rd)])
                sch.add(DVE, lambda po3=po3, rd=rd, h=h: nc.vector.tensor_tensor(
                    out=obv[:, :, h, :], in0=po3[:, :, 0:64],
                    in1=rden[:, rd, :].unsqueeze(2).broadcast_to([128, 4, 64]), op=ALU.mult),
                    reads=[K("P", pob), K("rden", rd)], writes=OB)

        pendq = []
        for (h, kt) in items:
            pendq.append(att_qk(h, kt))
            if len(pendq) > 2:
                att_pv(*pendq.pop(0))
        while pendq:
            att_pv(*pendq.pop(0))
        if first:
            dump("ob", bB[:, 8:12, :], OB, [128, 4, 512], BF16)
        for tt in range(4):
            ti = rT.nxt()
            for kc in range(4):
                tp(Tb[ti][:, kc * 128:(kc + 1) * 128], bB[:, 8 + tt, kc * 128:(kc + 1) * 128], [K("bB", 8 + tt)],
                   [K("T", ti)], kc == 3)
            sch.add(ACT, lambda ti=ti, tt=tt: nc.scalar.copy(out=obT[:, :, tt * 128:(tt + 1) * 128],
                                                             in_=Tb[ti][:, 0:512].rearrange("p (a t) -> p a t", t=128)),
                    reads=[K("T", ti)], writes=[K("obT", tt)])
        OAT = [K("oaT", t) for t in range(4)]
        OBT = [K("obT", t) for t in range(4)]
        MIX = [K("bB", i) for i in range(8)]
        gab_ids = [B_GAB0, B_GAB1, B_GAB2, B_GAB3]
        for qd in range(4):
            wg2, kg2 = wblk(gc, gab_ids[qd], 1)
            wab, kab = wblk(gc, B_WAB0 if qd < 2 else B_WAB1, 1)
            for e in range(2):
                i = 2 * qd + e
                col = (i % 4) * 128
                res = []
                for br in range(2):
                    pgt = rP.nxt()
                    for kc in range(8):
                        mm(Pf[pgt][:, :], wg2[:, kc, br * 256 + e * 128: br * 256 + (e + 1) * 128], hT[:, kc, :],
                           kc == 0, kc == 7, HTK + [kg2], [K("P", pgt)], kc == 7)
                    pab = rP.nxt()
                    src = oaT if br == 0 else obT
                    srk = OAT if br == 0 else OBT
                    for kc in range(4):
                        mm(Pf[pab][:, :], wab[:, br * 4 + kc, col:col + 128], src[:, kc, :], kc == 0, kc == 3,
                           srk + [kab], [K("P", pab)], kc == 3)
                    ss_, ms_ = 0 + br, 2 + br
                    sch.add(ACT, lambda pgt=pgt, ss_=ss_: nc.scalar.activation(out=fS[:, ss_, :], in_=Pf[pgt][:, :],
                                                                             func=AF.Sigmoid),
                            reads=[K("P", pgt)], writes=[K("fS", ss_)])
                    sch.add(DVE, lambda pab=pab, ss_=ss_, ms_=ms_: nc.vector.tensor_tensor(
                        out=fS[:, ms_, :], in0=fS[:, ss_, :], in1=Pf[pab][:, :], op=ALU.mult),
                        reads=[K("fS", ss_), K("P", pab)], writes=[K("fS", ms_)])
                sch.add(POOL, lambda i=i: nc.gpsimd.tensor_tensor(out=bB[:, i, :], in0=fS[:, 2, :], in1=fS[:, 3, :],
                                                                  op=ALU.add),
                        reads=[K("fS", 2), K("fS", 3)], writes=[K("bB", i)])
        if first:
            dump("mixT", bB[:, 0:8, :], MIX, [128, 8, 512], BF16)
        wo0, ko0 = wblk(gc, B_WO0)
        wo1, ko1 = wblk(gc, B_WO1, 1)

        def w_out_tile(tt):
            xi = rxt.nxt()
            xs = 12 + 2 * xi
            xap = fS[:, xs:xs + 2, :].rearrange("p a b -> p (a b)")
            xk = [K("fS", xs), K("fS", xs + 1)]
            r0 = c * CH + tt * 128
            sch.add(XQ[xi], lambda: nc.sync.dma_start(out=xap, in_=x[s, r0:r0 + 128, :]), writes=xk)
            for nh, (wo, ko) in enumerate([(wo0, ko0), (wo1, ko1)]):
                pw = rP.nxt()
                for kc in range(8):
                    mm(Pf[pw][:, :], bB[:, kc, tt * 128:(tt + 1) * 128], wo[:, kc, :], kc == 0, kc == 7,
                       MIX + [ko], [K("P", pw)], kc == 7)
                sch.add(DVE, lambda pw=pw, nh=nh: nc.vector.tensor_tensor(
                    out=xmid[:, tt, nh * 512:(nh + 1) * 512], in0=xap[:, nh * 512:(nh + 1) * 512], in1=Pf[pw][:, :],
                    op=ALU.add), reads=xk + [K("P", pw)], writes=[K("xmid", tt, nh)])

        def n2A(tt):
            norm_A(xmid[:, tt, :], [K("xmid", tt, 0), K("xmid", tt, 1)], g2b, K("g2b"), tt, 4 * tt)

        w_out_tile(0); w_out_tile(1); n2A(0); w_out_tile(2); n2A(1); norm_B(0); w_out_tile(3)
        if first:
            dump("xmid", xmid[:], [K("xmid", t, n) for t in range(4) for n in range(2)], [128, 4, D])
        n2A(2); norm_B(1); n2A(3); norm_B(2); norm_B(3)
        par = c % 2
        if c == 0:
            sch.add(POOL, lambda: nc.gpsimd.memset(halo[:, 0, :, :], 0.0), writes=[K("halo", 0)])
        GT = [K("bB", i) for i in range(NKF)]
        ngc = gc + 1
        has_next = ngc < NSEQ * NCH
        for u in range(11):
            wu, ku = wblk(gc, B_UP0 + u)
            if has_next and u in (7, 9):
                ptt = 0 if u == 7 else 1
                x_norm_A(ngc // NCH, ngc % NCH, ptt)
            for e in range(2):
                i = 2 * u + e
                rb = i % 2
                ub = 0 + 2 * rb
                ac = 4 + rb
                gl = 6 + rb
                ubuf = fS[:, ub:ub + 2, :].rearrange("p a b -> p (a b)")
                UBK = [K("fS", ub), K("fS", ub + 1)]
                pu = rP.nxt()
                for kc in range(8):
                    mm(Pf[pu][:, :], wu[:, kc, e * 128:(e + 1) * 128], hT[:, kc, :], kc == 0, kc == 7, HTK + [ku],
                       [K("P", pu)], kc == 7)
                pv2 = rP.nxt()
                for kc in range(8):
                    mm(Pf[pv2][:, :], wu[:, kc, 256 + e * 128:256 + (e + 1) * 128], hT[:, kc, :], kc == 0, kc == 7,
                       HTK + [ku], [K("P", pv2)], kc == 7)
                sch.add(ACT, lambda pu=pu, ubuf=ubuf: nc.scalar.copy(out=ubuf[:, 2:514], in_=Pf[pu][:, :]),
                        reads=[K("P", pu)], writes=UBK)
                sch.add(POOL, lambda ubuf=ubuf, i=i: nc.gpsimd.tensor_copy(out=ubuf[:, 0:2], in_=halo[:, par, i, :]),
                        reads=[K("halo", par)], writes=UBK)
                sch.add(POOL, lambda ubuf=ubuf, i=i: nc.gpsimd.tensor_copy(out=halo[:, 1 - par, i, :],
                                                                           in_=ubuf[:, 512:514]),
                        reads=UBK, writes=[K("halo", 1 - par)])
                sch.add(DVE, lambda ubuf=ubuf, i=i, ac=ac: nc.vector.tensor_scalar(
                    out=fS[:, ac, :], in0=ubuf[:, 2:514], scalar1=cw[:, 2, i:i + 1], scalar2=cb[:, i:i + 1],
                    op0=ALU.mult, op1=ALU.add), reads=UBK + [K("cw"), K("cb")], writes=[K("fS", ac)])
                sch.add(DVE, lambda ubuf=ubuf, i=i, ac=ac: nc.vector.scalar_tensor_tensor(
                    out=fS[:, ac, :], in0=ubuf[:, 1:513], scalar=cw[:, 1, i:i + 1], in1=fS[:, ac, :], op0=ALU.mult,
                    op1=ALU.add), reads=UBK + [K("cw"), K("fS", ac)], writes=[K("fS", ac)])
                sch.add(DVE, lambda ubuf=ubuf, i=i, ac=ac: nc.vector.scalar_tensor_tensor(
                    out=fS[:, ac, :], in0=ubuf[:, 0:512], scalar=cw[:, 0, i:i + 1], in1=fS[:, ac, :], op0=ALU.mult,
                    op1=ALU.add), reads=UBK + [K("cw"), K("fS", ac)], writes=[K("fS", ac)])
                sch.add(ACT, lambda ac=ac, gl=gl: nc.scalar.activation(out=fS[:, gl, :], in_=fS[:, ac, :], func=AF.Gelu),
                        reads=[K("fS", ac)], writes=[K("fS", gl)])
                sch.add(DVE, lambda gl=gl, pv2=pv2, i=i: nc.vector.tensor_tensor(out=bB[:, i, :], in0=fS[:, gl, :],
                                                                                 in1=Pf[pv2][:, :], op=ALU.mult),
                        reads=[K("fS", gl), K("P", pv2)], writes=[K("bB", i)])
        if first:
            dump("gT", bB[:, 0:NKF, :], GT, [128, NKF, 512], BF16)
        if has_next:
            for ptt in (0, 1):
                norm_B(ptt)
                prefetched.add((ngc, ptt))
            for ptt in (2, 3):
                x_norm_A(ngc // NCH, ngc % NCH, ptt)
        for nh in range(2):
            banks = [0, 1, 2, 3] if nh == 0 else [4, 5, 0, 1]
            for pi_, (k0, nk) in enumerate(DN_P):
                wd, kd = wblk(gc, B_DN0 + nh * 3 + pi_)
                for tt in range(4):
                    for kk in range(nk):
                        kc = k0 + kk
                        mm(Pf[banks[tt]][:, :], bB[:, kc, tt * 128:(tt + 1) * 128], wd[:, kk, :], kc == 0,
                           kc == NKF - 1, [K("bB", kc), kd], [K("P", banks[tt])], kk == nk - 1)
            for tt in range(4):
                sch.add(DVE, lambda nh=nh, tt=tt, b=banks[tt]: nc.vector.tensor_tensor(
                    out=xmid[:, tt, nh * 512:(nh + 1) * 512], in0=xmid[:, tt, nh * 512:(nh + 1) * 512],
                    in1=Pf[b][:, :], op=ALU.add),
                    reads=[K("xmid", tt, nh), K("P", banks[tt])], writes=[K("xmid", tt, nh)])
            if nh == 0 and has_next:
                for ptt in (2, 3):
                    norm_B(ptt)
                    prefetched.add((ngc, ptt))
        sch.add(OQ, lambda: nc.gpsimd.dma_start(
            out=out[s, c * CH:(c + 1) * CH, :].rearrange("(t p) d -> p t d", p=128), in_=xmid[:]),
            reads=[K("xmid", t, n) for t in range(4) for n in range(2)])

    for s in range(NSEQ):
        for c in range(NCH):
            chunk(s, c)
    stats = sch.finalize(nc.sync)
    es.close()
    return nc, dumps, stats


_CACHE = {}


def kernel(**inputs):
    nseq = 32 // NCORES
    if "nc" not in _CACHE:
        _CACHE["nc"] = build(nseq)[0]
    nc = _CACHE["nc"]
    consts = host_consts()
    x = np.ascontiguousarray(np.asarray(inputs["x"], dtype=np.float32))
    shared = {
        "norm1_g": np.asarray(inputs["norm1_g"], np.float32).reshape(1, D),
        "norm2_g": np.asarray(inputs["norm2_g"], np.float32).reshape(1, D),
        "w_in": np.asarray(inputs["w_in"], np.float32).reshape(D, 5632),
        "hg_lb_logits": np.asarray(inputs["hg_lb_logits"], np.float32).reshape(2, 512),
        "hg_onorm_g": np.asarray(inputs["hg_onorm_g"], np.float32).reshape(1, 512),
        "q_norm_g": np.asarray(inputs["q_norm_g"], np.float32).reshape(1, 64),
        "k_norm_g": np.asarray(inputs["k_norm_g"], np.float32).reshape(1, 64),
        "w_a": np.asarray(inputs["w_a"], np.float32).reshape(512, D),
        "w_b": np.asarray(inputs["w_b"], np.float32).reshape(512, D),
        "w_out": np.asarray(inputs["w_out"], np.float32).reshape(D, D),
        "w_up": np.asarray(inputs["w_up"], np.float32).reshape(D, 2 * DFF),
        "conv_w": np.asarray(inputs["conv_w"], np.float32).reshape(3, DFF),
        "conv_b": np.asarray(inputs["conv_b"], np.float32).reshape(1, DFF),
        "w_down": np.asarray(inputs["w_down"], np.float32).reshape(DFF, D),
    }
    shared.update(consts)
    in_maps = []
    for i in range(NCORES):
        m = dict(shared)
        m["x"] = x[i * nseq:(i + 1) * nseq]
        in_maps.append(m)
    res = run_bass_kernel_spmd(nc, in_maps, core_ids=list(range(NCORES)))
    return np.concatenate([np.asarray(r["out"]) for r in res.results], axis=0).astype(np.float32)
```

```python
from contextlib import ExitStack
import numpy as np
import concourse.bass as bass
import concourse.mybir as mybir
from concourse.bass_utils import run_bass_kernel_spmd

F32 = mybir.dt.float32
BF16 = mybir.dt.bfloat16
AF = mybir.ActivationFunctionType
ALU = mybir.AluOpType
AX = mybir.AxisListType

NCORES = 8
S = 2048
D = 1024
CH = 512
NCH = S // CH
DFF = 2816
NKF = DFF // 128
EPS = 1e-6
BIG = 30000.0
NBLK = 32
RING = 3

(B_HQ, B_HF, B_HI, B_HG, B_MK, B_MQ, B_MV, B_GAB0, B_WAB0, B_GAB1, B_GAB2, B_WAB1, B_GAB3,
 B_WO0, B_WO1) = range(15)
B_UP0 = 15
B_DN0 = 26


class Q:
    def __init__(self, name, issuer, sem, inc, kind):
        self.name, self.issuer, self.sem, self.inc, self.kind = name, issuer, sem, inc, kind
        self.nsig = 0
        self.last = None


class Sched:
    def __init__(self, nc):
        self.nc = nc
        self.ins = []
        self.queues = []

    def queue(self, name, issuer, sem, inc, kind):
        q = Q(name, issuer, sem, inc, kind)
        self.queues.append(q)
        return q

    def add(self, q, fn, reads=(), writes=(), sig=True):
        self.ins.append((q, fn, tuple(reads), tuple(writes), sig or q.kind == 'dma'))

    def finalize(self, final_issuer):
        ins = self.ins
        n = len(ins)
        sigval = [0] * n
        nextsig = [None] * n
        for i, (q, fn, r, w, sig) in enumerate(ins):
            if sig:
                q.nsig += 1
                sigval[i] = q.nsig
        lastsig = {}
        for i in range(n - 1, -1, -1):
            q = ins[i][0]
            if ins[i][4]:
                lastsig[q] = i
            nextsig[i] = lastsig.get(q)
        writers, readers = {}, {}
        clocks = {}
        iclk = [None] * n
        nwaits = 0
        for i, (q, fn, rds, wrs, sig) in enumerate(ins):
            deps = set()
            for k in rds:
                for qq, j in writers.get(k, {}).items():
                    if qq is q and q.kind == 'pe':
                        continue
                    deps.add(j)
            for k in wrs:
                for qq, j in writers.get(k, {}).items():
                    if qq is q and q.kind != 'dma':
                        continue
                    deps.add(j)
                for qq, j in readers.get(k, {}).items():
                    if qq is q and q.kind != 'dma':
                        continue
                    deps.add(j)
            if q.kind == 'dma' and q.last is not None:
                deps.add(q.last)
            clk = clocks.setdefault(id(q.issuer), {})
            for j in sorted(deps):
                js = nextsig[j]
                assert js is not None and js < i, f"dep signal after waiter: ins {i} dep {j} sig {js}"
                qj = ins[js][0]
                val = sigval[js] * qj.inc
                if clk.get(qj, 0) >= val:
                    continue
                q.issuer.wait_ge(qj.sem, val)
                nwaits += 1
                for qq, v in iclk[js].items():
                    if clk.get(qq, 0) < v:
                        clk[qq] = v
            r = fn()
            if sig:
                r.then_inc(q.sem, q.inc)
                c2 = dict(clk)
                c2[q] = sigval[i] * q.inc
                iclk[i] = c2
            for k in rds:
                readers.setdefault(k, {})[q] = i
            for k in wrs:
                writers.setdefault(k, {})[q] = i
            if q.kind == 'dma':
                q.last = i
        for q in self.queues:
            if q.nsig:
                final_issuer.wait_ge(q.sem, q.nsig * q.inc)
        return n, nwaits


def host_consts():
    c = {}
    c["c_ident"] = np.eye(128, dtype=np.float32)
    s = np.arange(128)[:, None]
    t = np.arange(128)[None, :]
    same = (s // 64) == (t // 64)
    c["c_U"] = ((s > t) & same).astype(np.float32)
    c["c_bd"] = ((s <= t) & same).astype(np.float32)
    c["c_tri"] = (s <= t).astype(np.float32)
    c["c_cind"] = np.stack([(np.arange(128) < 64), (np.arange(128) >= 64)], 1).astype(np.float32)
    half = 32
    inv = 1.0 / (10000.0 ** (np.arange(half, dtype=np.float32) * 2.0 / 64))
    pos = np.arange(S, dtype=np.float32)
    ang = pos[:, None] * inv[None, :]
    cs = np.stack([np.cos(ang), np.sin(ang)], 1).astype(np.float32)
    c["c_cs"] = np.ascontiguousarray(cs.reshape(16, 128, 2, 32).transpose(1, 0, 2, 3))
    kind = (np.arange(S)[None, :] // 256 == np.arange(8)[:, None]).astype(np.float32)
    c["c_kind"] = kind
    return c


def build(NSEQ, dump_names=()):
    nc = bass.Bass("TRN2", target_bir_lowering=False)
    es = ExitStack()

    def din(name, shape, dt=F32):
        return nc.dram_tensor(name, list(shape), dt, kind="ExternalInput").ap()

    x = din("x", [NSEQ, S, D])
    norm1_g = din("norm1_g", [1, D]); norm2_g = din("norm2_g", [1, D])
    w_in = din("w_in", [D, 5632]); hg_lb = din("hg_lb_logits", [2, 512])
    hg_on = din("hg_onorm_g", [1, 512]); qng = din("q_norm_g", [1, 64]); kng = din("k_norm_g", [1, 64])
    w_a = din("w_a", [512, D]); w_b = din("w_b", [512, D]); w_out = din("w_out", [D, D])
    w_up = din("w_up", [D, 2 * DFF]); conv_w = din("conv_w", [3, DFF]); conv_b = din("conv_b", [1, DFF])
    w_down = din("w_down", [DFF, D])
    c_ident = din("c_ident", [128, 128]); c_U = din("c_U", [128, 128]); c_bd = din("c_bd", [128, 128])
    c_tri = din("c_tri", [128, 128]); c_cind = din("c_cind", [128, 2]); c_cs = din("c_cs", [128, 16, 2, 32])
    c_kind = din("c_kind", [8, S])
    out = nc.dram_tensor("out", [NSEQ, S, D], F32, kind="ExternalOutput").ap()
    wbf = nc.dram_tensor("wbf", [NBLK, 128, 4096], BF16).ap()
    dumps = {}

    def sb(name, shape, dt):
        return es.enter_context(nc.sbuf_tensor(name, list(shape), dt))

    def ps(name, shape, dt):
        return es.enter_context(nc.psum_tensor(name, list(shape), dt))

    def sem(name):
        return es.enter_context(nc.semaphore(name))

    sch = Sched(nc)
    PE = sch.queue("pe", nc.tensor, sem("s_pe"), 1, 'pe')
    ACT = sch.queue("act", nc.scalar, sem("s_act"), 1, 'cmp')
    DVE = sch.queue("dve", nc.vector, sem("s_dve"), 1, 'cmp')
    POOL = sch.queue("pool", nc.gpsimd, sem("s_pool"), 1, 'cmp')
    WQ = [sch.queue(f"wq{i}", nc.sync, sem(f"s_wq{i}"), 16, 'dma') for i in range(RING)]
    XQ = [sch.queue(f"xq{i}", nc.sync, sem(f"s_xq{i}"), 16, 'dma') for i in range(2)]
    OQ = sch.queue("oq", nc.gpsimd, sem("s_oq"), 16, 'dma')
    CQ = [sch.queue(f"cq{i}", nc.gpsimd, sem(f"s_cq{i}"), 16, 'dma') for i in range(4)]
    KQ = [sch.queue(f"kq{i}", nc.sync, sem(f"s_kq{i}"), 16, 'dma') for i in range(4)]
    DQ = sch.queue("dq", nc.sync, sem("s_dq"), 16, 'dma')

    wring = sb("wring", [128, RING, 4096], BF16)
    kTe = sb("kTe", [128, 8, S], BF16)
    vext = sb("vext", [128, 16, 8, 65], BF16)
    xmid = sb("xmid", [128, 4, D], F32)
    hT = sb("hT", [128, 8, CH], BF16)
    g1b = sb("g1b", [128, D], F32); g2b = sb("g2b", [128, D], F32)
    fS = sb("fS", [128, 18, 512], F32)
    bB = sb("bB", [128, 24, 512], BF16)
    Sst = sb("Sst", [128, 512], F32)
    qtil = sb("qtil", [128, 2, 512], BF16)
    attn = sb("attn", [128, 2, 512], BF16)
    sdb = sb("sdb", [128, 2, 512], BF16)
    oabf = sb("oabf", [128, 2, 512], BF16)
    oaT = sb("oaT", [128, 4, CH], BF16); obT = sb("obT", [128, 4, CH], BF16)
    ropeo = sb("ropeo", [128, 2, 512], BF16)
    bias = sb("bias", [128, 2, 8, 72], BF16)
    pt = sb("pt", [128, 3, 512], BF16)
    ident = sb("ident", [128, 128], BF16)
    Um = sb("Um", [128, 128], F32); bd = sb("bd", [128, 128], F32); tri = sb("tri", [128, 128], BF16)
    cind = sb("cind", [128, 2], F32)
    omlb = sb("omlb", [128, 512], F32); gob = sb("gob", [128, 512], F32)
    cs = sb("cs", [128, 16, 2, 32], F32)
    gq8 = sb("gq8", [128, 64], F32); gkb = sb("gkb", [128, 64], F32)
    cw = sb("cw", [128, 3, NKF], F32); cb = sb("cb", [128, NKF], F32)
    halo = sb("halo", [128, 2, NKF, 2], F32)
    elast = sb("elast", [128, 4, 4, 2], F32)
    st = sb("st", [128, 128], F32)
    kmT = sb("kmT", [128, 8, 8], BF16)
    kmf = sb("kmf", [128, 8], F32)
    epsb = sb("epsb", [128, 1], F32)
    gsm = sb("gsm", [128, 64], F32)
    cmpb = sb("cmpb", [128, 8 * 7 * 7], F32)
    cnt = sb("cnt", [128, 56], F32)
    rden = sb("rden", [128, 2, 4], F32)
    Pf = [ps(f"P{i}", [128, 512], F32) for i in range(6)]
    Tb = [ps(f"T{i}", [128, 1024], BF16) for i in range(2)]

    K = lambda *a: tuple(a)

    def fslot(i, n=1):
        return fS[:, i, :] if n == 1 else fS[:, i:i + n, :]

    class Rot:
        def __init__(self, n):
            self.n, self.i = n, 0

        def nxt(self):
            v = self.i % self.n
            self.i += 1
            return v

    rP = Rot(4)
    rP6 = Rot(6)
    rP3 = Rot(3)
    rT = Rot(2)
    rxt = Rot(2)
    rpt = Rot(3)

    def dump(name, ap, keys, shape, dt=F32):
        if name not in dump_names or name in dumps:
            return
        t = nc.dram_tensor("dbg_" + name, list(shape), dt, kind="ExternalOutput").ap()
        dumps[name] = t
        sch.add(DQ, lambda: nc.sync.dma_start(out=t, in_=ap), reads=keys)

    cqi = [0]

    def cast(dst, src, b):
        q = CQ[cqi[0] % 4]
        cqi[0] += 1
        sch.add(q, lambda: nc.gpsimd.dma_start(out=dst, in_=src), writes=[K("wbf", b)])

    kqi = [0]

    def kload(dst, src, key, eng=None):
        q = KQ[kqi[0] % 4]
        kqi[0] += 1
        sch.add(q, lambda: nc.sync.dma_start(out=dst, in_=src), writes=[key])

    def kcast(dst, src, key):
        q = CQ[cqi[0] % 4]
        cqi[0] += 1
        sch.add(q, lambda: nc.gpsimd.dma_start(out=dst, in_=src), writes=[key])

    kcast(ident[:], c_ident, K("ident"))
    kcast(tri[:], c_tri, K("tri"))
    kload(Um[:], c_U, K("Um")); kload(bd[:], c_bd, K("bd")); kload(cind[:], c_cind, K("cind"))
    kload(cs[:], c_cs, K("cs"))
    kload(g1b[:], norm1_g.partition_broadcast(128), K("g1b"))
    kload(g2b[:], norm2_g.partition_broadcast(128), K("g2b"))
    kload(gob[:], hg_on.partition_broadcast(128), K("gob"))
    kload(gq8[:], qng.partition_broadcast(128), K("gq8"))
    kload(gkb[:], kng.partition_broadcast(128), K("gkb"))
    kload(fS[:, 0, :], hg_lb[0:1, :].partition_broadcast(128), K("fS", 0))
    kload(fS[:, 1, :], hg_lb[1:2, :].partition_broadcast(128), K("fS", 1))
    for j in range(3):
        sch.add(KQ[j % 4], (lambda j=j: nc.sync.dma_start(
            out=cw[:, j, :], in_=conv_w[j:j + 1, :].rearrange("o (kc p) -> p (o kc)", p=128),
            allow_slow_non_contiguous=True)), writes=[K("cw")])
    sch.add(KQ[3], lambda: nc.sync.dma_start(
        out=cb[:], in_=conv_b.rearrange("o (kc p) -> p (o kc)", p=128), allow_slow_non_contiguous=True),
        writes=[K("cb")])
    for h in range(8):
        kcast(kTe[64:72, h, :], c_kind, K("kTe_ind"))
    sch.add(DVE, lambda: nc.vector.tensor_tensor(out=fS[:, 0, :], in0=fS[:, 0, :], in1=fS[:, 1, :], op=ALU.subtract),
            reads=[K("fS", 0), K("fS", 1)], writes=[K("fS", 0)])
    sch.add(ACT, lambda: nc.scalar.activation(out=omlb[:], in_=fS[:, 0, :], func=AF.Sigmoid, scale=-1.0),
            reads=[K("fS", 0)], writes=[K("omlb")])
    sch.add(ACT, lambda: nc.scalar.mul(out=gq8[:], in_=gq8[:], mul=0.125), reads=[K("gq8")], writes=[K("gq8")])
    sch.add(POOL, lambda: nc.gpsimd.memset(vext[:, :, :, 64:65], 1.0), writes=[K("vext_one")])
    sch.add(POOL, lambda: nc.gpsimd.memset(epsb[:], EPS), writes=[K("epsb")])
    sch.add(POOL, lambda: nc.gpsimd.memset(bias[:, :, :, 0:64], 0.0), writes=[K("bias", 0), K("bias", 1)])

    def blkv(b, kc0, nkc, n0, nn):
        v = wbf[b].rearrange("p (kc n) -> p kc n", n=512)
        return v[:, kc0:kc0 + nkc, n0:n0 + nn]

    def rows(w, r0, nkc, c0, ncol):
        return w[r0:r0 + nkc * 128, c0:c0 + ncol].rearrange("(kc p) n -> p kc n", p=128)

    for g, b in enumerate([B_HQ, B_HF, B_HI, B_HG, B_MQ, B_MK, B_MV]):
        cast(blkv(b, 0, 8, 0, 512), rows(w_in, 0, 8, g * 512, 512), b)
    for qd, b in enumerate([B_GAB0, B_GAB1, B_GAB2, B_GAB3]):
        cast(blkv(b, 0, 8, 0, 256), rows(w_in, 0, 8, 3584 + qd * 256, 256), b)
        cast(blkv(b, 0, 8, 256, 256), rows(w_in, 0, 8, 4608 + qd * 256, 256), b)
    for hf, b in enumerate([B_WAB0, B_WAB1]):
        cast(blkv(b, 0, 4, 0, 512), rows(w_a, 0, 4, hf * 512, 512), b)
        cast(blkv(b, 4, 4, 0, 512), rows(w_b, 0, 4, hf * 512, 512), b)
    for hf, b in enumerate([B_WO0, B_WO1]):
        cast(blkv(b, 0, 8, 0, 512), rows(w_out, 0, 8, hf * 512, 512), b)
    for u in range(11):
        cast(blkv(B_UP0 + u, 0, 8, 0, 256), rows(w_up, 0, 8, u * 256, 256), B_UP0 + u)
        cast(blkv(B_UP0 + u, 0, 8, 256, 256), rows(w_up, 0, 8, DFF + u * 256, 256), B_UP0 + u)
    DN_P = [(0, 8), (8, 8), (16, 6)]
    for nh in range(2):
        for pi, (k0, nk) in enumerate(DN_P):
            cast(blkv(B_DN0 + nh * 3 + pi, 0, nk, 0, 512), rows(w_down, k0 * 128, nk, nh * 512, 512), B_DN0 + nh * 3 + pi)

    wstate = {"next": 0}
    total_blocks = NSEQ * NCH * NBLK

    def wload_upto(gb):
        while wstate["next"] <= gb and wstate["next"] < total_blocks:
            g = wstate["next"]
            slot = g % RING
            b = g % NBLK
            ne = 3072 if b in (B_DN0 + 2, B_DN0 + 5) else 4096
            sch.add(WQ[slot], (lambda slot=slot, b=b, ne=ne: nc.sync.dma_start(out=wring[:, slot, 0:ne],
                                                                              in_=wbf[b][:, 0:ne])),
                    reads=[K("wbf", b)], writes=[K("w", slot)])
            wstate["next"] += 1

    def wblk(gc, b, ahead=2):
        gb = gc * NBLK + b
        wload_upto(gb + ahead)
        slot = gb % RING
        return wring[:, slot, :].rearrange("p (kc n) -> p kc n", n=512), K("w", slot)

    def mm(out_ap, lhsT, rhs, start, stop, reads, writes, sig, **kw):
        sch.add(PE, lambda: nc.tensor.matmul(out_ap, lhsT=lhsT, rhs=rhs, start=start, stop=stop, **kw),
                reads=reads, writes=writes, sig=sig)

    def tp(out_ap, in_ap, reads, writes, sig):
        sch.add(PE, lambda: nc.tensor.transpose(out_ap, in_ap, ident[:]), reads=list(reads) + [K("ident")],
                writes=writes, sig=sig)

    hbuf = sb("hbuf", [128, 2, D], BF16)

    def norm_A(src_ap, src_keys, gb_t, gkey, tt, stc):
        junk = bB[:, 22:24, :].rearrange("p a b -> p (a b)")
        sch.add(ACT, lambda: nc.scalar.activation(out=junk, in_=src_ap, func=AF.Square, scale=1.0 / 32.0,
                                                  accum_out=st[:, stc:stc + 1]),
                reads=src_keys, writes=[K("bB", 22), K("bB", 23), K("st", stc)])
        sch.add(ACT, lambda: nc.scalar.activation(out=st[:, stc + 2:stc + 3], in_=st[:, stc:stc + 1], func=AF.Ln,
                                                  bias=epsb[:, 0:1]),
                reads=[K("st", stc), K("epsb")], writes=[K("st", stc + 2)])
        sch.add(ACT, lambda: nc.scalar.activation(out=st[:, stc + 3:stc + 4], in_=st[:, stc + 2:stc + 3], func=AF.Exp,
                                                  scale=-0.5),
                reads=[K("st", stc + 2)], writes=[K("st", stc + 3)])
        hi_ = tt % 2
        sch.add(DVE, lambda: nc.vector.scalar_tensor_tensor(out=hbuf[:, hi_, :], in0=src_ap,
                                                            scalar=st[:, stc + 3:stc + 4],
                                                            in1=gb_t[:], op0=ALU.mult, op1=ALU.mult),
                reads=list(src_keys) + [K("st", stc + 3), gkey], writes=[K("hbuf", hi_)])

    def norm_B(tt):
        hi_ = tt % 2
        ti = rT.nxt()
        for kc in range(8):
            tp(Tb[ti][:, kc * 128:(kc + 1) * 128], hbuf[:, hi_, kc * 128:(kc + 1) * 128], [K("hbuf", hi_)],
               [K("T", ti)], kc == 7)
        sch.add(DVE, lambda: nc.vector.tensor_copy(out=hT[:, :, tt * 128:(tt + 1) * 128],
                                                   in_=Tb[ti][:, :].rearrange("p (kc t) -> p kc t", t=128)),
                reads=[K("T", ti)], writes=[K("hT", tt)])

    def norm_tile(src_ap, src_keys, gb_t, gkey, tt, stc):
        norm_A(src_ap, src_keys, gb_t, gkey, tt, stc)
        norm_B(tt)

    def x_norm_A(s_, c_, tt):
        xi = rxt.nxt()
        xs = 12 + 2 * xi
        xap = fS[:, xs:xs + 2, :].rearrange("p a b -> p (a b)")
        xk = [K("fS", xs), K("fS", xs + 1)]
        r0 = c_ * CH + tt * 128
        sch.add(XQ[xi], lambda: nc.sync.dma_start(out=xap, in_=x[s_, r0:r0 + 128, :]), writes=xk)
        norm_A(xap, xk, g1b, K("g1b"), tt, 4 * tt)

    prefetched = set()

    HTK = [K("hT", t) for t in range(4)]

    def chunk(s, c):
        gc = s * NCH + c
        first = (gc == 0)
        for tt in range(4):
            if (gc, tt) in prefetched:
                continue
            x_norm_A(s, c, tt)
            norm_B(tt)
        if first:
            dump("hT", hT[:], HTK, [128, 8, CH], BF16)
        wq_, kq_ = wblk(gc, B_HQ)
        wf_, kf_ = wblk(gc, B_HF, 1)
        if c == 0:
            sch.add(DVE, lambda: nc.vector.memset(Sst[:], 0.0), writes=[K("Sst")])
        KQT = [K("bB", i) for i in range(8)]

        def h_stageA(tt):
            r = tt % 2
            qs, ks, ls = 0 + r, 2 + r, 4 + r
            pq = rP6.nxt()
            for kc in range(8):
                mm(Pf[pq][:, :], hT[:, kc, tt * 128:(tt + 1) * 128], wq_[:, kc, :], kc == 0, kc == 7,
                   [K("hT", tt), kq_], [K("P", pq)], kc == 7)
            sch.add(ACT, lambda: nc.scalar.activation(out=fS[:, qs, :], in_=Pf[pq][:, :], func=AF.Silu),
                    reads=[K("P", pq)], writes=[K("fS", qs)])
            pf = rP6.nxt()
            for kc in range(8):
                mm(Pf[pf][:, :], hT[:, kc, tt * 128:(tt + 1) * 128], wf_[:, kc, :], kc == 0, kc == 7,
                   [K("hT", tt), kf_], [K("P", pf)], kc == 7)
            sch.add(ACT, lambda: nc.scalar.activation(out=fS[:, ks, :], in_=Pf[pf][:, :], func=AF.Sigmoid, scale=-1.0),
                    reads=[K("P", pf)], writes=[K("fS", ks)])
            sch.add(DVE, lambda: nc.vector.tensor_tensor(out=fS[:, ks, :], in0=fS[:, ks, :], in1=omlb[:], op=ALU.mult),
                    reads=[K("fS", ks), K("omlb")], writes=[K("fS", ks)])
            sch.add(ACT, lambda: nc.scalar.activation(out=fS[:, ls, :], in_=fS[:, ks, :], func=AF.Ln, scale=-1.0,
                                                      bias=1.0),
                    reads=[K("fS", ks)], writes=[K("fS", ls)])

        def h_stageB(tt):
            r = tt % 2
            qs, ks, ls, es_, ns = 0 + r, 2 + r, 4 + r, 6 + r, 8 + r
            pa = rP6.nxt()
            mm(Pf[pa][:, :], Um[:], fS[:, ls, :], True, True, [K("Um"), K("fS", ls)], [K("P", pa)], True)
            sch.add(ACT, lambda: nc.scalar.activation(out=fS[:, es_, :], in_=Pf[pa][:, :], func=AF.Exp),
                    reads=[K("P", pa)], writes=[K("fS", es_)])
            sch.add(ACT, lambda: nc.scalar.activation(out=fS[:, ns, :], in_=Pf[pa][:, :], func=AF.Exp, scale=-1.0),
                    reads=[K("P", pa)], writes=[K("fS", ns)])
            pl = rP6.nxt()
            for h in range(4):
                mm(Pf[pl][:, h * 2:h * 2 + 2], fS[:, ls, h * 128:(h + 1) * 128], cind[:], True, True,
                   [K("fS", ls), K("cind")], [K("P", pl)], h == 3)
            sch.add(ACT, lambda: nc.scalar.activation(
                out=elast[:, tt, :, :].rearrange("p h j -> p (h j)"), in_=Pf[pl][:, 0:8], func=AF.Exp),
                reads=[K("P", pl)], writes=[K("elast", tt)])
            sch.add(DVE, lambda: nc.vector.tensor_tensor(out=bB[:, 12 + tt, :], in0=fS[:, ks, :], in1=fS[:, es_, :],
                                                         op=ALU.mult),
                    reads=[K("fS", ks), K("fS", es_)], writes=[K("bB", 12 + tt)])
            sch.add(DVE, lambda: nc.vector.tensor_tensor(out=qtil[:, r, :], in0=fS[:, qs, :], in1=fS[:, ns, :],
                                                         op=ALU.mult),
                    reads=[K("fS", qs), K("fS", ns)], writes=[K("qtil", r)])
            if first and tt == 0:
                dump("khat0", bB[:, 12, :], [K("bB", 12)], [128, 512], BF16)
                dump("qtil0", qtil[:, 0, :], [K("qtil", 0)], [128, 512], BF16)

        def h_stageB2(tt):
            r = tt % 2
            ti = rT.nxt()
            for h in range(4):
                tp(Tb[ti][:, h * 128:(h + 1) * 128], bB[:, 12 + tt, h * 128:(h + 1) * 128], [K("bB", 12 + tt)],
                   [K("T", ti)], False)
            for h in range(4):
                tp(Tb[ti][:, (4 + h) * 128:(5 + h) * 128], qtil[:, r, h * 128:(h + 1) * 128], [K("qtil", r)],
                   [K("T", ti)], h == 3)
            sch.add(ACT, lambda: nc.scalar.copy(out=bB[:, 0:8, tt * 128:(tt + 1) * 128],
                                                in_=Tb[ti][:, :].rearrange("p (a t) -> p a t", t=128)),
                    reads=[K("T", ti)], writes=[K("kqT", tt)] + KQT)

        hi_state = {}

        def h_hi(tt):
            if "w" not in hi_state:
                hi_state["w"] = wblk(gc, B_HI)
            wi_, ki_ = hi_state["w"]
            pv = rP6.nxt()
            for kc in range(8):
                mm(Pf[pv][:, :], hT[:, kc, tt * 128:(tt + 1) * 128], wi_[:, kc, :], kc == 0, kc == 7,
                   [K("hT", tt), ki_], [K("P", pv)], kc == 7)
            sch.add(ACT, lambda: nc.scalar.copy(out=bB[:, 8 + tt, :], in_=Pf[pv][:, :]),
                    reads=[K("P", pv)], writes=[K("bB", 8 + tt)])

        h_stageA(0); h_stageA(1); h_stageB(0); h_stageA(2); h_stageB(1); h_stageB2(0); h_stageA(3); h_stageB(2)
        h_stageB2(1); h_hi(0); h_stageB(3); h_hi(1); h_stageB2(2); h_hi(2); h_hi(3); h_stageB2(3)
        wg_, kg_ = wblk(gc, B_HG)
        for tt in range(4):
            ph = rP6.nxt()
            for kc in range(8):
                mm(Pf[ph][:, :], hT[:, kc, tt * 128:(tt + 1) * 128], wg_[:, kc, :], kc == 0, kc == 7,
                   [K("hT", tt), kg_], [K("P", ph)], kc == 7)
            sch.add(ACT, lambda ph=ph, tt=tt: nc.scalar.activation(out=fS[:, tt, :], in_=Pf[ph][:, :], func=AF.Silu),
                    reads=[K("P", ph)], writes=[K("fS", tt)])
            sch.add(POOL, lambda tt=tt: nc.gpsimd.tensor_tensor(out=fS[:, tt, :], in0=fS[:, tt, :], in1=gob[:],
                                                                op=ALU.mult),
                    reads=[K("fS", tt), K("gob")], writes=[K("fS", tt)])

        def h_post_tr(tt):
            r = tt % 2
            ti = rT.nxt()
            for kc in range(4):
                tp(Tb[ti][:, kc * 128:(kc + 1) * 128], oabf[:, r, kc * 128:(kc + 1) * 128], [K("oabf", r)],
                   [K("T", ti)], kc == 3)
            sch.add(ACT, lambda: nc.scalar.copy(out=oaT[:, :, tt * 128:(tt + 1) * 128],
                                                in_=Tb[ti][:, 0:512].rearrange("p (a t) -> p a t", t=128)),
                    reads=[K("T", ti)], writes=[K("oaT", tt)])

        def h_attn(tt):
            r = tt % 2
            po = 4 + r
            pat = rP3.nxt()
            for h in range(4):
                mm(Pf[pat][:, h * 128:(h + 1) * 128], bB[:, h, tt * 128:(tt + 1) * 128],
                   bB[:, 4 + h, tt * 128:(tt + 1) * 128], True, True, KQT, [K("P", pat)], h == 3)
            sch.add(DVE, lambda: nc.vector.scalar_tensor_tensor(
                out=attn[:, r, :].rearrange("p (h t) -> p h t", h=4),
                in0=Pf[pat][:, :].rearrange("p (h t) -> p h t", h=4), scalar=1e30,
                in1=bd[:].unsqueeze(1).broadcast_to([128, 4, 128]), op0=ALU.min, op1=ALU.mult),
                reads=[K("P", pat), K("bd")], writes=[K("attn", r)])
            for h in range(4):
                mm(Pf[po][:, h * 128:(h + 1) * 128], attn[:, r, h * 128:(h + 1) * 128],
                   bB[:, 8 + tt, h * 128:(h + 1) * 128], h == 0, False, [K("attn", r), K("bB", 8 + tt)],
                   [K("P", po)], False, skip_group_check=True)

        def h_rec(tt):
            r = tt % 2
            po = 4 + r
            for j in range(2):
                n = 2 * tt + j
                sd = n % 2
                sch.add(DVE, lambda j=j: nc.vector.tensor_tensor(
                    out=fS[:, 16, :].rearrange("p (h v) -> p h v", h=4),
                    in0=Sst[:].rearrange("p (h v) -> p h v", h=4),
                    in1=elast[:, tt, :, j:j + 1].broadcast_to([128, 4, 128]), op=ALU.mult),
                    reads=[K("Sst"), K("elast", tt)], writes=[K("fS", 16)])
                sch.add(ACT, lambda sd=sd: nc.scalar.copy(out=sdb[:, sd, :], in_=fS[:, 16, :]),
                        reads=[K("fS", 16)], writes=[K("sdb", sd)])
                for h in range(4):
                    mm(Pf[3][:, h * 128:(h + 1) * 128], bB[j * 64:(j + 1) * 64, 12 + tt, h * 128:(h + 1) * 128],
                       bB[j * 64:(j + 1) * 64, 8 + tt, h * 128:(h + 1) * 128], True, True,
                       [K("bB", 12 + tt), K("bB", 8 + tt)], [K("P", 3)], h == 3)
                for h in range(4):
                    t0 = tt * 128 + j * 64
                    mm(Pf[po][j * 64:(j + 1) * 64, h * 128:(h + 1) * 128], bB[:, 4 + h, t0:t0 + 64],
                       sdb[:, sd, h * 128:(h + 1) * 128], False, (j == 1 and h == 3), KQT + [K("sdb", sd)],
                       [K("P", po)], (j == 1 and h == 3), skip_group_check=True)
                sch.add(DVE, lambda: nc.vector.tensor_tensor(out=Sst[:], in0=fS[:, 16, :], in1=Pf[3][:, :], op=ALU.add),
                        reads=[K("fS", 16), K("P", 3)], writes=[K("Sst")])

        def h_post(tt):
            r = tt % 2
            po = 4 + r
            so = 16 + 16 * r
            for h in range(4):
                sch.add(ACT, lambda h=h: nc.scalar.activation(
                    out=bB[:, 22, h * 128:(h + 1) * 128], in_=Pf[po][:, h * 128:(h + 1) * 128], func=AF.Square,
                    scale=float(1.0 / np.sqrt(128.0)), accum_out=st[:, so + h:so + h + 1]),
                    reads=[K("P", po)], writes=[K("bB", 22), K("st", so + h)])
            sch.add(ACT, lambda: nc.scalar.activation(out=st[:, so + 8:so + 12], in_=st[:, so:so + 4], func=AF.Ln,
                                                      bias=epsb[:, 0:1]),
                    reads=[K("st", so + h) for h in range(4)] + [K("epsb")], writes=[K("st", so + 8)])
            sch.add(ACT, lambda: nc.scalar.activation(out=st[:, so + 12:so + 16], in_=st[:, so + 8:so + 12],
                                                      func=AF.Exp, scale=-0.5),
                    reads=[K("st", so + 8)], writes=[K("st", so + 12)])
            for h in range(4):
                sch.add(DVE, lambda h=h: nc.vector.scalar_tensor_tensor(
                    out=oabf[:, r, h * 128:(h + 1) * 128], in0=Pf[po][:, h * 128:(h + 1) * 128],
                    scalar=st[:, so + 12 + h:so + 13 + h], in1=fS[:, tt, h * 128:(h + 1) * 128], op0=ALU.mult,
                    op1=ALU.mult),
                    reads=[K("P", po), K("st", so + 12), K("fS", tt)], writes=[K("oabf", r)])
            if first and tt == 0:
                dump("oa0", oabf[:, 0, :], [K("oabf", 0)], [128, 512], BF16)

        h_attn(0)
        for tt in range(4):
            h_rec(tt)
            if tt < 3:
                h_attn(tt + 1)
            h_post(tt)
            if tt >= 1:
                h_post_tr(tt - 1)
        QTE = [K("bB", 16 + i) for i in range(8)]
        qTe = bB[:, 16:24, :]
        wmk, kmk = wblk(gc, B_MK)
        wstore = {"k": (wmk, kmk)}

        def m_stageA(kind, tt, idx):
            is_q = kind == "q"
            if is_q and "q" not in wstore:
                wstore["q"] = wblk(gc, B_MQ)
            wv, wk = wstore[kind]
            par = idx % 3
            sq, mn = [4, 5, 10][par], [6, 7, 11][par]
            sc = 64 + 16 * par
            pm = rP.nxt()
            for kc in range(8):
                mm(Pf[pm][:, :], hT[:, kc, tt * 128:(tt + 1) * 128], wv[:, kc, :], kc == 0, kc == 7,
                   [K("hT", tt), wk], [K("P", pm)], kc == 7)
            sch.add(ACT, lambda: nc.scalar.activation(out=fS[:, sq, :], in_=Pf[pm][:, :], func=AF.Square, scale=0.125),
                    reads=[K("P", pm)], writes=[K("fS", sq)])
            sch.add(DVE, lambda: nc.vector.tensor_reduce(out=st[:, sc:sc + 8],
                                                         in_=fS[:, sq, :].rearrange("p (h d) -> p h d", h=8),
                                                         axis=AX.X, op=ALU.add),
                    reads=[K("fS", sq)], writes=[K("st", sc)])
            sch.add(ACT, lambda: nc.scalar.activation(out=st[:, sc + 8:sc + 16], in_=st[:, sc:sc + 8], func=AF.Ln,
                                                      bias=epsb[:, 0:1]),
                    reads=[K("st", sc), K("epsb")], writes=[K("st", sc + 8)])
            sch.add(ACT, lambda: nc.scalar.activation(out=st[:, sc:sc + 8], in_=st[:, sc + 8:sc + 16], func=AF.Exp,
                                                      scale=-0.5),
                    reads=[K("st", sc + 8)], writes=[K("st", sc)])
            sch.add(DVE, lambda: nc.vector.tensor_tensor(
                out=fS[:, mn, :].rearrange("p (h d) -> p h d", h=8),
                in0=Pf[pm][:, :].rearrange("p (h d) -> p h d", h=8),
                in1=st[:, sc:sc + 8].unsqueeze(2).broadcast_to([128, 8, 64]), op=ALU.mult),
                reads=[K("P", pm), K("st", sc)], writes=[K("fS", mn)])
            gvec, gkey = (gq8, K("gq8")) if is_q else (gkb, K("gkb"))
            m3 = fS[:, mn, :].rearrange("p (h d) -> p h d", h=8)
            sch.add(DVE, lambda: nc.vector.tensor_tensor(out=m3, in0=m3,
                                                         in1=gvec[:].unsqueeze(1).broadcast_to([128, 8, 64]),
                                                         op=ALU.mult),
                    reads=[K("fS", mn), gkey], writes=[K("fS", mn)])

        def m_stageB(kind, tt, idx):
            is_q = kind == "q"
            par = idx % 3
            mn, tB = [6, 7, 11][par], [8, 9, 17][par]
            tile_i = c * 4 + tt
            m4 = fS[:, mn, :].rearrange("p (h a d) -> p h a d", h=8, a=2)
            tB4 = fS[:, tB, :].rearrange("p (h a d) -> p h a d", h=8, a=2)
            cosb = cs[:, tile_i, 0, :]
            sinb3 = cs[:, tile_i, 1, :].unsqueeze(1).broadcast_to([128, 8, 32])
            sch.add(POOL, lambda: nc.gpsimd.tensor_tensor(out=tB4[:, :, 0, :], in0=m4[:, :, 1, :], in1=sinb3,
                                                          op=ALU.mult),
                    reads=[K("fS", mn), K("cs")], writes=[K("fS", tB)])
            sch.add(POOL, lambda: nc.gpsimd.tensor_tensor(out=tB4[:, :, 1, :], in0=m4[:, :, 0, :], in1=sinb3,
                                                          op=ALU.mult),
                    reads=[K("fS", mn), K("cs")], writes=[K("fS", tB)])
            sch.add(DVE, lambda: nc.vector.tensor_tensor(
                out=m4, in0=m4, in1=cosb.unsqueeze(1).unsqueeze(1).broadcast_to([128, 8, 2, 32]), op=ALU.mult),
                reads=[K("fS", mn), K("cs"), K("fS", tB)], writes=[K("fS", mn)])
            ro = idx % 2
            ro4 = ropeo[:, ro, :].rearrange("p (h a d) -> p h a d", h=8, a=2)
            sch.add(POOL, lambda: nc.gpsimd.tensor_tensor(out=ro4[:, :, 0, :], in0=m4[:, :, 0, :], in1=tB4[:, :, 0, :],
                                                          op=ALU.subtract),
                    reads=[K("fS", mn), K("fS", tB)], writes=[K("ropeo", ro)])
            sch.add(POOL, lambda: nc.gpsimd.tensor_tensor(out=ro4[:, :, 1, :], in0=m4[:, :, 1, :], in1=tB4[:, :, 1, :],
                                                          op=ALU.add),
                    reads=[K("fS", mn), K("fS", tB)], writes=[K("ropeo", ro)])

        def m_stageB2(kind, tt, idx):
            is_q = kind == "q"
            ro = idx % 2
            tile_i = c * 4 + tt
            ti = rT.nxt()
            for h in range(8):
                tp(Tb[ti][0:64, h * 128:(h + 1) * 128], ropeo[:, ro, h * 64:(h + 1) * 64], [K("ropeo", ro)],
                   [K("T", ti)], h == 7)
            src = Tb[ti][0:64, :].rearrange("p (h t) -> p h t", h=8)
            if is_q:
                sch.add(DVE, lambda: nc.vector.tensor_copy(out=qTe[0:64, :, tt * 128:(tt + 1) * 128], in_=src),
                        reads=[K("T", ti)], writes=QTE + [K("qTe", tt)])
            else:
                p0 = c * CH + tt * 128
                sch.add(DVE, lambda: nc.vector.tensor_copy(out=kTe[0:64, :, p0:p0 + 128], in_=src),
                        reads=[K("T", ti)], writes=[K("kTe", tile_i)])
                if tt % 2 == 1:
                    blk = 2 * c + tt // 2
                    sch.add(DVE, lambda: nc.vector.tensor_reduce(out=kmf[0:64, :],
                                                                 in_=kTe[0:64, :, blk * 256:(blk + 1) * 256],
                                                                 axis=AX.X, op=ALU.add),
                            reads=[K("kTe", 2 * blk), K("kTe", 2 * blk + 1)], writes=[K("kmf")])
                    sch.add(DVE, lambda: nc.vector.tensor_scalar(out=kmT[0:64, :, blk:blk + 1],
                                                                 in0=kmf[0:64, :].unsqueeze(2), scalar1=1.0 / 256.0,
                                                                 scalar2=None, op0=ALU.mult),
                            reads=[K("kmf")], writes=[K("kmT")])

        def m_stageC(tt):
            qb = 2 * c + tt // 2
            bi = tt % 2
            if qb >= 4:
                pg = rP.nxt()
                for h in range(8):
                    mm(Pf[pg][:, h * 8:h * 8 + qb], qTe[0:64, h, tt * 128:(tt + 1) * 128], kmT[0:64, h, 0:qb], True, True,
                       [K("qTe", tt), K("kmT")], [K("P", pg)], h == 7)
                sch.add(ACT, lambda: nc.scalar.copy(
                    out=gsm[:].rearrange("p (h j) -> p h j", h=8)[:, :, 0:qb],
                    in_=Pf[pg][:, 0:64].rearrange("p (h j) -> p h j", h=8)[:, :, 0:qb]),
                        reads=[K("P", pg)], writes=[K("gsm")])
                g3 = gsm[:].rearrange("p (h j) -> p h j", h=8)[:, :, 0:qb]
                c4 = cmpb[:, 0:8 * qb * qb].rearrange("p (h j k) -> p h j k", h=8, j=qb)
                sch.add(DVE, lambda: nc.vector.tensor_tensor(
                    out=c4, in0=g3.unsqueeze(2).broadcast_to([128, 8, qb, qb]),
                    in1=g3.unsqueeze(3).broadcast_to([128, 8, qb, qb]), op=ALU.is_gt),
                    reads=[K("gsm")], writes=[K("cmpb")])
                cn3 = cnt[:, 0:8 * qb].rearrange("p (h j) -> p h j", h=8)
                sch.add(DVE, lambda: nc.vector.tensor_reduce(out=cn3, in_=c4, axis=AX.X, op=ALU.add),
                        reads=[K("cmpb")], writes=[K("cnt")])
                sch.add(DVE, lambda: nc.vector.tensor_scalar(
                    out=bias[:, bi, :, 64:64 + qb], in0=cn3, scalar1=2.5, scalar2=-BIG, op0=ALU.is_gt, op1=ALU.mult),
                    reads=[K("cnt")], writes=[K("bias", bi)])
                sch.add(POOL, lambda: nc.gpsimd.memset(bias[:, bi, :, 64 + qb:65 + qb], 0.0), writes=[K("bias", bi)])
            else:
                sch.add(POOL, lambda: nc.gpsimd.memset(bias[:, bi, :, 64:65 + qb], 0.0), writes=[K("bias", bi)])
            if qb < 7:
                sch.add(POOL, lambda: nc.gpsimd.memset(bias[:, bi, :, 65 + qb:72], -BIG), writes=[K("bias", bi)])

        def m_stageC2(tt):
            bi = tt % 2
            for hh in range(2):
                pb = rP.nxt()
                for h4 in range(4):
                    h = hh * 4 + h4
                    mm(Pf[pb][0:72, h4 * 128:(h4 + 1) * 128], bias[:, bi, h, :], ident[:], True, True,
                       [K("bias", bi), K("ident")], [K("P", pb)], h4 == 3)
                sch.add(ACT, lambda pb=pb, hh=hh: nc.scalar.copy(
                    out=qTe[64:72, hh * 4:hh * 4 + 4, tt * 128:(tt + 1) * 128],
                    in_=Pf[pb][64:72, :].rearrange("p (h t) -> p h t", h=4)),
                    reads=[K("P", pb)], writes=QTE + [K("qTeb", tt)])

        def m_v(tt):
            if "v" not in wstore:
                wstore["v"] = wblk(gc, B_MV)
            wmv, kmv = wstore["v"]
            tile_i = c * 4 + tt
            pv = rP.nxt()
            for kc in range(8):
                mm(Pf[pv][:, :], hT[:, kc, tt * 128:(tt + 1) * 128], wmv[:, kc, :], kc == 0, kc == 7,
                   [K("hT", tt), kmv], [K("P", pv)], kc == 7)
            sch.add(ACT, lambda: nc.scalar.copy(
                out=vext[:, tile_i, :, 0:64], in_=Pf[pv][:, :].rearrange("p (h d) -> p h d", h=8)),
                reads=[K("P", pv)], writes=[K("vext", tile_i)])

        items2 = [("k", t) for t in range(4)] + [("q", t) for t in range(4)]
        m_stageA("k", 0, 0)
        h_post_tr(3)
        m_stageA("k", 1, 1)
        m_stageB("k", 0, 0)
        for i in range(2, 8):
            m_stageB(items2[i - 1][0], items2[i - 1][1], i - 1)
            m_stageA(items2[i][0], items2[i][1], i)
            m_stageB2(items2[i - 2][0], items2[i - 2][1], i - 2)
        m_stageB("q", 3, 7)
        m_v(0)
        m_stageB2("q", 2, 6)
        m_stageC(0)
        m_v(1)
        m_stageB2("q", 3, 7)
        m_stageC(1)
        m_stageC2(0)
        m_v(2)
        m_stageC2(1)
        m_stageC(2)
        m_v(3)
        m_stageC(3)
        m_stageC2(2)
        m_stageC2(3)
        if first:
            dump("kTe", kTe[0:72, :, 0:512], [K("kTe", i) for i in range(4)] + [K("kTe_ind")], [72, 8, 512], BF16)
            dump("qTe", qTe[0:72, :, :], QTE, [72, 8, 512], BF16)
        nkt = 4 * c + 4
        OB = [K("bB", 8 + i) for i in range(4)]
        obv = bB[:, 8:12, :].rearrange("p t (h d) -> p t h d", h=8)
        items = [(h, kt) for h in range(8) for kt in range(nkt)]
        pend = None

        def att_qk(h, kt):
            n0 = max(0, kt * 128 - c * CH)
            pst = rP.nxt()
            pi = rpt.nxt()
            mm(Pf[pst][:, n0:512], kTe[0:72, h, kt * 128:(kt + 1) * 128], qTe[0:72, h, n0:512], True, True,
               [K("kTe", kt), K("kTe_ind")] + QTE, [K("P", pst)], True)
            sch.add(ACT, lambda pst=pst, pi=pi, n0=n0: nc.scalar.activation(out=pt[:, pi, n0:512],
                                                                          in_=Pf[pst][:, n0:512], func=AF.Exp),
                    reads=[K("P", pst)], writes=[K("pt", pi)])
            if kt * 128 >= c * CH:
                sch.add(DVE, lambda pi=pi, n0=n0: nc.vector.tensor_tensor(out=pt[:, pi, n0:n0 + 128],
                                                                          in0=pt[:, pi, n0:n0 + 128], in1=tri[:],
                                                                          op=ALU.mult),
                        reads=[K("pt", pi), K("tri")], writes=[K("pt", pi)])
            return (h, kt, n0, pi)

        def att_pv(h, kt, n0, pi):
            pob = 4 + (h % 2)
            for sub in range(n0 // 128, 4):
                last = (kt == nkt - 1 and sub == 3)
                mm(Pf[pob][:, sub * 65:(sub + 1) * 65], pt[:, pi, sub * 128:(sub + 1) * 128], vext[:, kt, h, :],
                   (kt == 0 and sub == 0), last, [K("pt", pi), K("vext", kt), K("vext_one")], [K("P", pob)],
                   sub == 3, skip_group_check=True)
            if kt == nkt - 1:
                rd = h % 2
                po3 = Pf[pob][:, 0:260].rearrange("p (s e) -> p s e", e=65)
                sch.add(DVE, lambda po3=po3, rd=rd: nc.vector.reciprocal(out=rden[:, rd, :].unsqueeze(2),
                                                                         in_=po3[:, :, 64:65]),
                        reads=[K("P", pob)], writes=[K("rden", rd)])
                sch.add(DVE, lambda po3=po3, rd=rd, h=h: nc.vector.tensor_tensor(
                    out=obv[:, :, h, :], in0=po3[:, :, 0:64],
                    in1=rden[:, rd, :].unsqueeze(2).broadcast_to([128, 4, 64]), op=ALU.mult),
                    reads=[K("P", pob), K("rden", rd)], writes=OB)

        pendq = []
        for (h, kt) in items:
            pendq.append(att_qk(h, kt))
            if len(pendq) > 2:
                att_pv(*pendq.pop(0))
        while pendq:
            att_pv(*pendq.pop(0))
        if first:
            dump("ob", bB[:, 8:12, :], OB, [128, 4, 512], BF16)
        for tt in range(4):
            ti = rT.nxt()
            for kc in range(4):
                tp(Tb[ti][:, kc * 128:(kc + 1) * 128], bB[:, 8 + tt, kc * 128:(kc + 1) * 128], [K("bB", 8 + tt)],
                   [K("T", ti)], kc == 3)
            sch.add(ACT, lambda ti=ti, tt=tt: nc.scalar.copy(out=obT[:, :, tt * 128:(tt + 1) * 128],
                                                             in_=Tb[ti][:, 0:512].rearrange("p (a t) -> p a t", t=128)),
                    reads=[K("T", ti)], writes=[K("obT", tt)])
        OAT = [K("oaT", t) for t in range(4)]
        OBT = [K("obT", t) for t in range(4)]
        MIX = [K("bB", i) for i in range(8)]
        gab_ids = [B_GAB0, B_GAB1, B_GAB2, B_GAB3]
        for qd in range(4):
            wg2, kg2 = wblk(gc, gab_ids[qd], 1)
            wab, kab = wblk(gc, B_WAB0 if qd < 2 else B_WAB1, 1)
            for e in range(2):
                i = 2 * qd + e
                col = (i % 4) * 128
                res = []
                for br in range(2):
                    pgt = rP.nxt()
                    for kc in range(8):
                        mm(Pf[pgt][:, :], wg2[:, kc, br * 256 + e * 128: br * 256 + (e + 1) * 128], hT[:, kc, :],
                           kc == 0, kc == 7, HTK + [kg2], [K("P", pgt)], kc == 7)
                    pab = rP.nxt()
                    src = oaT if br == 0 else obT
                    srk = OAT if br == 0 else OBT
                    for kc in range(4):
                        mm(Pf[pab][:, :], wab[:, br * 4 + kc, col:col + 128], src[:, kc, :], kc == 0, kc == 3,
                           srk + [kab], [K("P", pab)], kc == 3)
                    ss_, ms_ = 0 + br, 2 + br
                    sch.add(ACT, lambda pgt=pgt, ss_=ss_: nc.scalar.activation(out=fS[:, ss_, :], in_=Pf[pgt][:, :],
                                                                             func=AF.Sigmoid),
                            reads=[K("P", pgt)], writes=[K("fS", ss_)])
                    sch.add(DVE, lambda pab=pab, ss_=ss_, ms_=ms_: nc.vector.tensor_tensor(
                        out=fS[:, ms_, :], in0=fS[:, ss_, :], in1=Pf[pab][:, :], op=ALU.mult),
                        reads=[K("fS", ss_), K("P", pab)], writes=[K("fS", ms_)])
                sch.add(POOL, lambda i=i: nc.gpsimd.tensor_tensor(out=bB[:, i, :], in0=fS[:, 2, :], in1=fS[:, 3, :],
                                                                  op=ALU.add),
                        reads=[K("fS", 2), K("fS", 3)], writes=[K("bB", i)])
        if first:
            dump("mixT", bB[:, 0:8, :], MIX, [128, 8, 512], BF16)
        wo0, ko0 = wblk(gc, B_WO0)
        wo1, ko1 = wblk(gc, B_WO1, 1)

        def w_out_tile(tt):
            xi = rxt.nxt()
            xs = 12 + 2 * xi
            xap = fS[:, xs:xs + 2, :].rearrange("p a b -> p (a b)")
            xk = [K("fS", xs), K("fS", xs + 1)]
            r0 = c * CH + tt * 128
            sch.add(XQ[xi], lambda: nc.sync.dma_start(out=xap, in_=x[s, r0:r0 + 128, :]), writes=xk)
            for nh, (wo, ko) in enumerate([(wo0, ko0), (wo1, ko1)]):
                pw = rP.nxt()
                for kc in range(8):
                    mm(Pf[pw][:, :], bB[:, kc, tt * 128:(tt + 1) * 128], wo[:, kc, :], kc == 0, kc == 7,
                       MIX + [ko], [K("P", pw)], kc == 7)
                sch.add(DVE, lambda pw=pw, nh=nh: nc.vector.tensor_tensor(
                    out=xmid[:, tt, nh * 512:(nh + 1) * 512], in0=xap[:, nh * 512:(nh + 1) * 512], in1=Pf[pw][:, :],
                    op=ALU.add), reads=xk + [K("P", pw)], writes=[K("xmid", tt, nh)])

        def n2A(tt):
            norm_A(xmid[:, tt, :], [K("xmid", tt, 0), K("xmid", tt, 1)], g2b, K("g2b"), tt, 4 * tt)

        w_out_tile(0); w_out_tile(1); n2A(0); w_out_tile(2); n2A(1); norm_B(0); w_out_tile(3)
        if first:
            dump("xmid", xmid[:], [K("xmid", t, n) for t in range(4) for n in range(2)], [128, 4, D])
        n2A(2); norm_B(1); n2A(3); norm_B(2); norm_B(3)
        par = c % 2
        if c == 0:
            sch.add(POOL, lambda: nc.gpsimd.memset(halo[:, 0, :, :], 0.0), writes=[K("halo", 0)])
        GT = [K("bB", i) for i in range(NKF)]
        ngc = gc + 1
        has_next = ngc < NSEQ * NCH
        for u in range(11):
            wu, ku = wblk(gc, B_UP0 + u)
            if has_next and u in (7, 9):
                ptt = 0 if u == 7 else 1
                x_norm_A(ngc // NCH, ngc % NCH, ptt)
            for e in range(2):
                i = 2 * u + e
                rb = i % 2
                ub = 0 + 2 * rb
                ac = 4 + rb
                gl = 6 + rb
                ubuf = fS[:, ub:ub + 2, :].rearrange("p a b -> p (a b)")
                UBK = [K("fS", ub), K("fS", ub + 1)]
                pu = rP.nxt()
                for kc in range(8):
                    mm(Pf[pu][:, :], wu[:, kc, e * 128:(e + 1) * 128], hT[:, kc, :], kc == 0, kc == 7, HTK + [ku],
                       [K("P", pu)], kc == 7)
                pv2 = rP.nxt()
                for kc in range(8):
                    mm(Pf[pv2][:, :], wu[:, kc, 256 + e * 128:256 + (e + 1) * 128], hT[:, kc, :], kc == 0, kc == 7,
                       HTK + [ku], [K("P", pv2)], kc == 7)
                sch.add(ACT, lambda pu=pu, ubuf=ubuf: nc.scalar.copy(out=ubuf[:, 2:514], in_=Pf[pu][:, :]),
                        reads=[K("P", pu)], writes=UBK)
                sch.add(POOL, lambda ubuf=ubuf, i=i: nc.gpsimd.tensor_copy(out=ubuf[:, 0:2], in_=halo[:, par, i, :]),
                        reads=[K("halo", par)], writes=UBK)
                sch.add(POOL, lambda ubuf=ubuf, i=i: nc.gpsimd.tensor_copy(out=halo[:, 1 - par, i, :],
                                                                           in_=ubuf[:, 512:514]),
                        reads=UBK, writes=[K("halo", 1 - par)])
                sch.add(DVE, lambda ubuf=ubuf, i=i, ac=ac: nc.vector.tensor_scalar(
                    out=fS[:, ac, :], in0=ubuf[:, 2:514], scalar1=cw[:, 2, i:i + 1], scalar2=cb[:, i:i + 1],
                    op0=ALU.mult, op1=ALU.add), reads=UBK + [K("cw"), K("cb")], writes=[K("fS", ac)])
                sch.add(DVE, lambda ubuf=ubuf, i=i, ac=ac: nc.vector.scalar_tensor_tensor(
                    out=fS[:, ac, :], in0=ubuf[:, 1:513], scalar=cw[:, 1, i:i + 1], in1=fS[:, ac, :], op0=ALU.mult,
                    op1=ALU.add), reads=UBK + [K("cw"), K("fS", ac)], writes=[K("fS", ac)])
                sch.add(DVE, lambda ubuf=ubuf, i=i, ac=ac: nc.vector.scalar_tensor_tensor(
                    out=fS[:, ac, :], in0=ubuf[:, 0:512], scalar=cw[:, 0, i:i + 1], in1=fS[:, ac, :], op0=ALU.mult,
                    op1=ALU.add), reads=UBK + [K("cw"), K("fS", ac)], writes=[K("fS", ac)])
                sch.add(ACT, lambda ac=ac, gl=gl: nc.scalar.activation(out=fS[:, gl, :], in_=fS[:, ac, :], func=AF.Gelu),
                        reads=[K("fS", ac)], writes=[K("fS", gl)])
                sch.add(DVE, lambda gl=gl, pv2=pv2, i=i: nc.vector.tensor_tensor(out=bB[:, i, :], in0=fS[:, gl, :],
                                                                                 in1=Pf[pv2][:, :], op=ALU.mult),
                        reads=[K("fS", gl), K("P", pv2)], writes=[K("bB", i)])
        if first:
            dump("gT", bB[:, 0:NKF, :], GT, [128, NKF, 512], BF16)
        if has_next:
            for ptt in (0, 1):
                norm_B(ptt)
                prefetched.add((ngc, ptt))
            for ptt in (2, 3):
                x_norm_A(ngc // NCH, ngc % NCH, ptt)
        for nh in range(2):
            banks = [0, 1, 2, 3] if nh == 0 else [4, 5, 0, 1]
            for pi_, (k0, nk) in enumerate(DN_P):
                wd, kd = wblk(gc, B_DN0 + nh * 3 + pi_)
                for tt in range(4):
                    for kk in range(nk):
                        kc = k0 + kk
                        mm(Pf[banks[tt]][:, :], bB[:, kc, tt * 128:(tt + 1) * 128], wd[:, kk, :], kc == 0,
                           kc == NKF - 1, [K("bB", kc), kd], [K("P", banks[tt])], kk == nk - 1)
            for tt in range(4):
                sch.add(DVE, lambda nh=nh, tt=tt, b=banks[tt]: nc.vector.tensor_tensor(
                    out=xmid[:, tt, nh * 512:(nh + 1) * 512], in0=xmid[:, tt, nh * 512:(nh + 1) * 512],
                    in1=Pf[b][:, :], op=ALU.add),
                    reads=[K("xmid", tt, nh), K("P", banks[tt])], writes=[K("xmid", tt, nh)])
            if nh == 0 and has_next:
                for ptt in (2, 3):
                    norm_B(ptt)
                    prefetched.add((ngc, ptt))
        sch.add(OQ, lambda: nc.gpsimd.dma_start(
            out=out[s, c * CH:(c + 1) * CH, :].rearrange("(t p) d -> p t d", p=128), in_=xmid[:]),
            reads=[K("xmid", t, n) for t in range(4) for n in range(2)])

    for s in range(NSEQ):
        for c in range(NCH):
            chunk(s, c)
    stats = sch.finalize(nc.sync)
    es.close()
    return nc, dumps, stats


_CACHE = {}


def kernel(**inputs):
    nseq = 32 // NCORES
    if "nc" not in _CACHE:
        _CACHE["nc"] = build(nseq)[0]
    nc = _CACHE["nc"]
    consts = host_consts()
    x = np.ascontiguousarray(np.asarray(inputs["x"], dtype=np.float32))
    shared = {
        "norm1_g": np.asarray(inputs["norm1_g"], np.float32).reshape(1, D),
        "norm2_g": np.asarray(inputs["norm2_g"], np.float32).reshape(1, D),
        "w_in": np.asarray(inputs["w_in"], np.float32).reshape(D, 5632),
        "hg_lb_logits": np.asarray(inputs["hg_lb_logits"], np.float32).reshape(2, 512),
        "hg_onorm_g": np.asarray(inputs["hg_onorm_g"], np.float32).reshape(1, 512),
        "q_norm_g": np.asarray(inputs["q_norm_g"], np.float32).reshape(1, 64),
        "k_norm_g": np.asarray(inputs["k_norm_g"], np.float32).reshape(1, 64),
        "w_a": np.asarray(inputs["w_a"], np.float32).reshape(512, D),
        "w_b": np.asarray(inputs["w_b"], np.float32).reshape(512, D),
        "w_out": np.asarray(inputs["w_out"], np.float32).reshape(D, D),
        "w_up": np.asarray(inputs["w_up"], np.float32).reshape(D, 2 * DFF),
        "conv_w": np.asarray(inputs["conv_w"], np.float32).reshape(3, DFF),
        "conv_b": np.asarray(inputs["conv_b"], np.float32).reshape(1, DFF),
        "w_down": np.asarray(inputs["w_down"], np.float32).reshape(DFF, D),
    }
    shared.update(consts)
    in_maps = []
    for i in range(NCORES):
        m = dict(shared)
        m["x"] = x[i * nseq:(i + 1) * nseq]
        in_maps.append(m)
    res = run_bass_kernel_spmd(nc, in_maps, core_ids=list(range(NCORES)))
    return np.concatenate([np.asarray(r["out"]) for r in res.results], axis=0).astype(np.float32)
```

```python
from contextlib import ExitStack
import numpy as np
import concourse.bass as bass
import concourse.mybir as mybir
from concourse.bass_utils import run_bass_kernel_spmd

F32 = mybir.dt.float32
BF16 = mybir.dt.bfloat16
AF = mybir.ActivationFunctionType
ALU = mybir.AluOpType
AX = mybir.AxisListType

NCORES = 8
S = 2048
D = 1024
CH = 512
NCH = S // CH
DFF = 2816
NKF = DFF // 128
EPS = 1e-6
BIG = 30000.0
NBLK = 32
RING = 3

(B_HQ, B_HF, B_HI, B_HG, B_MK, B_MQ, B_MV, B_GAB0, B_WAB0, B_GAB1, B_GAB2, B_WAB1, B_GAB3,
 B_WO0, B_WO1) = range(15)
B_UP0 = 15
B_DN0 = 26


class Q:
    def __init__(self, name, issuer, sem, inc, kind):
        self.name, self.issuer, self.sem, self.inc, self.kind = name, issuer, sem, inc, kind
        self.nsig = 0
        self.last = None


class Sched:
    def __init__(self, nc):
        self.nc = nc
        self.ins = []
        self.queues = []

    def queue(self, name, issuer, sem, inc, kind):
        q = Q(name, issuer, sem, inc, kind)
        self.queues.append(q)
        return q

    def add(self, q, fn, reads=(), writes=(), sig=True):
        self.ins.append((q, fn, tuple(reads), tuple(writes), sig or q.kind == 'dma'))

    def finalize(self, final_issuer):
        ins = self.ins
        n = len(ins)
        sigval = [0] * n
        nextsig = [None] * n
        for i, (q, fn, r, w, sig) in enumerate(ins):
            if sig:
                q.nsig += 1
                sigval[i] = q.nsig
        lastsig = {}
        for i in range(n - 1, -1, -1):
            q = ins[i][0]
            if ins[i][4]:
                lastsig[q] = i
            nextsig[i] = lastsig.get(q)
        writers, readers = {}, {}
        clocks = {}
        iclk = [None] * n
        nwaits = 0
        for i, (q, fn, rds, wrs, sig) in enumerate(ins):
            deps = set()
            for k in rds:
                for qq, j in writers.get(k, {}).items():
                    if qq is q and q.kind == 'pe':
                        continue
                    deps.add(j)
            for k in wrs:
                for qq, j in writers.get(k, {}).items():
                    if qq is q and q.kind != 'dma':
                        continue
                    deps.add(j)
                for qq, j in readers.get(k, {}).items():
                    if qq is q and q.kind != 'dma':
                        continue
                    deps.add(j)
            if q.kind == 'dma' and q.last is not None:
                deps.add(q.last)
            clk = clocks.setdefault(id(q.issuer), {})
            for j in sorted(deps):
                js = nextsig[j]
                assert js is not None and js < i, f"dep signal after waiter: ins {i} dep {j} sig {js}"
                qj = ins[js][0]
                val = sigval[js] * qj.inc
                if clk.get(qj, 0) >= val:
                    continue
                q.issuer.wait_ge(qj.sem, val)
                nwaits += 1
                for qq, v in iclk[js].items():
                    if clk.get(qq, 0) < v:
                        clk[qq] = v
            r = fn()
            if sig:
                r.then_inc(q.sem, q.inc)
                c2 = dict(clk)
                c2[q] = sigval[i] * q.inc
                iclk[i] = c2
            for k in rds:
                readers.setdefault(k, {})[q] = i
            for k in wrs:
                writers.setdefault(k, {})[q] = i
            if q.kind == 'dma':
                q.last = i
        for q in self.queues:
            if q.nsig:
                final_issuer.wait_ge(q.sem, q.nsig * q.inc)
        return n, nwaits


def host_consts():
    c = {}
    c["c_ident"] = np.eye(128, dtype=np.float32)
    s = np.arange(128)[:, None]
    t = np.arange(128)[None, :]
    same = (s // 64) == (t // 64)
    c["c_U"] = ((s > t) & same).astype(np.float32)
    c["c_bd"] = ((s <= t) & same).astype(np.float32)
    c["c_tri"] = (s <= t).astype(np.float32)
    c["c_cind"] = np.stack([(np.arange(128) < 64), (np.arange(128) >= 64)], 1).astype(np.float32)
    half = 32
    inv = 1.0 / (10000.0 ** (np.arange(half, dtype=np.float32) * 2.0 / 64))
    pos = np.arange(S, dtype=np.float32)
    ang = pos[:, None] * inv[None, :]
    cs = np.stack([np.cos(ang), np.sin(ang)], 1).astype(np.float32)
    c["c_cs"] = np.ascontiguousarray(cs.reshape(16, 128, 2, 32).transpose(1, 0, 2, 3))
    kind = (np.arange(S)[None, :] // 256 == np.arange(8)[:, None]).astype(np.float32)
    c["c_kind"] = kind
    return c


def build(NSEQ, dump_names=()):
    nc = bass.Bass("TRN2", target_bir_lowering=False)
    es = ExitStack()

    def din(name, shape, dt=F32):
        return nc.dram_tensor(name, list(shape), dt, kind="ExternalInput").ap()

    x = din("x", [NSEQ, S, D])
    norm1_g = din("norm1_g", [1, D]); norm2_g = din("norm2_g", [1, D])
    w_in = din("w_in", [D, 5632]); hg_lb = din("hg_lb_logits", [2, 512])
    hg_on = din("hg_onorm_g", [1, 512]); qng = din("q_norm_g", [1, 64]); kng = din("k_norm_g", [1, 64])
    w_a = din("w_a", [512, D]); w_b = din("w_b", [512, D]); w_out = din("w_out", [D, D])
    w_up = din("w_up", [D, 2 * DFF]); conv_w = din("conv_w", [3, DFF]); conv_b = din("conv_b", [1, DFF])
    w_down = din("w_down", [DFF, D])
    c_ident = din("c_ident", [128, 128]); c_U = din("c_U", [128, 128]); c_bd = din("c_bd", [128, 128])
    c_tri = din("c_tri", [128, 128]); c_cind = din("c_cind", [128, 2]); c_cs = din("c_cs", [128, 16, 2, 32])
    c_kind = din("c_kind", [8, S])
    out = nc.dram_tensor("out", [NSEQ, S, D], F32, kind="ExternalOutput").ap()
    wbf = nc.dram_tensor("wbf", [NBLK, 128, 4096], BF16).ap()
    dumps = {}

    def sb(name, shape, dt):
        return es.enter_context(nc.sbuf_tensor(name, list(shape), dt))

    def ps(name, shape, dt):
        return es.enter_context(nc.psum_tensor(name, list(shape), dt))

    def sem(name):
        return es.enter_context(nc.semaphore(name))

    sch = Sched(nc)
    PE = sch.queue("pe", nc.tensor, sem("s_pe"), 1, 'pe')
    ACT = sch.queue("act", nc.scalar, sem("s_act"), 1, 'cmp')
    DVE = sch.queue("dve", nc.vector, sem("s_dve"), 1, 'cmp')
    POOL = sch.queue("pool", nc.gpsimd, sem("s_pool"), 1, 'cmp')
    WQ = [sch.queue(f"wq{i}", nc.sync, sem(f"s_wq{i}"), 16, 'dma') for i in range(RING)]
    XQ = [sch.queue(f"xq{i}", nc.sync, sem(f"s_xq{i}"), 16, 'dma') for i in range(2)]
    OQ = sch.queue("oq", nc.gpsimd, sem("s_oq"), 16, 'dma')
    CQ = [sch.queue(f"cq{i}", nc.gpsimd, sem(f"s_cq{i}"), 16, 'dma') for i in range(4)]
    KQ = [sch.queue(f"kq{i}", nc.sync, sem(f"s_kq{i}"), 16, 'dma') for i in range(4)]
    DQ = sch.queue("dq", nc.sync, sem("s_dq"), 16, 'dma')

    wring = sb("wring", [128, RING, 4096], BF16)
    kTe = sb("kTe", [128, 8, S], BF16)
    vext = sb("vext", [128, 16, 8, 65], BF16)
    xmid = sb("xmid", [128, 4, D], F32)
    hT = sb("hT", [128, 8, CH], BF16)
    g1b = sb("g1b", [128, D], F32); g2b = sb("g2b", [128, D], F32)
    fS = sb("fS", [128, 18, 512], F32)
    bB = sb("bB", [128, 24, 512], BF16)
    Sst = sb("Sst", [128, 512], F32)
    qtil = sb("qtil", [128, 2, 512], BF16)
    attn = sb("attn", [128, 2, 512], BF16)
    sdb = sb("sdb", [128, 2, 512], BF16)
    oabf = sb("oabf", [128, 2, 512], BF16)
    oaT = sb("oaT", [128, 4, CH], BF16); obT = sb("obT", [128, 4, CH], BF16)
    ropeo = sb("ropeo", [128, 2, 512], BF16)
    bias = sb("bias", [128, 2, 8, 72], BF16)
    pt = sb("pt", [128, 3, 512], BF16)
    ident = sb("ident", [128, 128], BF16)
    Um = sb("Um", [128, 128], F32); bd = sb("bd", [128, 128], F32); tri = sb("tri", [128, 128], BF16)
    cind = sb("cind", [128, 2], F32)
    omlb = sb("omlb", [128, 512], F32); gob = sb("gob", [128, 512], F32)
    cs = sb("cs", [128, 16, 2, 32], F32)
    gq8 = sb("gq8", [128, 64], F32); gkb = sb("gkb", [128, 64], F32)
    cw = sb("cw", [128, 3, NKF], F32); cb = sb("cb", [128, NKF], F32)
    halo = sb("halo", [128, 2, NKF, 2], F32)
    elast = sb("elast", [128, 4, 4, 2], F32)
    st = sb("st", [128, 128], F32)
    kmT = sb("kmT", [128, 8, 8], BF16)
    kmf = sb("kmf", [128, 8], F32)
    epsb = sb("epsb", [128, 1], F32)
    gsm = sb("gsm", [128, 64], F32)
    cmpb = sb("cmpb", [128, 8 * 7 * 7], F32)
    cnt = sb("cnt", [128, 56], F32)
    rden = sb("rden", [128, 2, 4], F32)
    Pf = [ps(f"P{i}", [128, 512], F32) for i in range(6)]
    Tb = [ps(f"T{i}", [128, 1024], BF16) for i in range(2)]

    K = lambda *a: tuple(a)

    def fslot(i, n=1):
        return fS[:, i, :] if n == 1 else fS[:, i:i + n, :]

    class Rot:
        def __init__(self, n):
            self.n, self.i = n, 0

        def nxt(self):
            v = self.i % self.n
            self.i += 1
            return v

    rP = Rot(4)
    rP6 = Rot(6)
    rP3 = Rot(3)
    rT = Rot(2)
    rxt = Rot(2)
    rpt = Rot(3)

    def dump(name, ap, keys, shape, dt=F32):
        if name not in dump_names or name in dumps:
            return
        t = nc.dram_tensor("dbg_" + name, list(shape), dt, kind="ExternalOutput").ap()
        dumps[name] = t
        sch.add(DQ, lambda: nc.sync.dma_start(out=t, in_=ap), reads=keys)

    cqi = [0]

    cast_list = []

    def cast(dst, src, b):
        cast_list.append((b, dst, src))

    def cast_flush():
        for b, dst, src in sorted(cast_list, key=lambda t: t[0]):
            q = CQ[cqi[0] % 4]
            cqi[0] += 1
            sch.add(q, (lambda dst=dst, src=src: nc.gpsimd.dma_start(out=dst, in_=src)), writes=[K("wbf", b)])

    kqi = [0]

    def kload(dst, src, key, eng=None):
        q = KQ[kqi[0] % 4]
        kqi[0] += 1
        sch.add(q, lambda: nc.sync.dma_start(out=dst, in_=src), writes=[key])

    def kcast(dst, src, key):
        q = CQ[cqi[0] % 4]
        cqi[0] += 1
        sch.add(q, lambda: nc.gpsimd.dma_start(out=dst, in_=src), writes=[key])

    kcast(ident[:], c_ident, K("ident"))
    kcast(tri[:], c_tri, K("tri"))
    kload(Um[:], c_U, K("Um")); kload(bd[:], c_bd, K("bd")); kload(cind[:], c_cind, K("cind"))
    kload(cs[:], c_cs, K("cs"))
    kload(g1b[:], norm1_g.partition_broadcast(128), K("g1b"))
    kload(g2b[:], norm2_g.partition_broadcast(128), K("g2b"))
    kload(gob[:], hg_on.partition_broadcast(128), K("gob"))
    kload(gq8[:], qng.partition_broadcast(128), K("gq8"))
    kload(gkb[:], kng.partition_broadcast(128), K("gkb"))
    kload(fS[:, 0, :], hg_lb[0:1, :].partition_broadcast(128), K("fS", 0))
    kload(fS[:, 1, :], hg_lb[1:2, :].partition_broadcast(128), K("fS", 1))
    for j in range(3):
        sch.add(KQ[j % 4], (lambda j=j: nc.sync.dma_start(
            out=cw[:, j, :], in_=conv_w[j:j + 1, :].rearrange("o (kc p) -> p (o kc)", p=128),
            allow_slow_non_contiguous=True)), writes=[K("cw")])
    sch.add(KQ[3], lambda: nc.sync.dma_start(
        out=cb[:], in_=conv_b.rearrange("o (kc p) -> p (o kc)", p=128), allow_slow_non_contiguous=True),
        writes=[K("cb")])
    for h in range(8):
        kcast(kTe[64:72, h, :], c_kind, K("kTe_ind"))
    sch.add(DVE, lambda: nc.vector.tensor_tensor(out=fS[:, 0, :], in0=fS[:, 0, :], in1=fS[:, 1, :], op=ALU.subtract),
            reads=[K("fS", 0), K("fS", 1)], writes=[K("fS", 0)])
    sch.add(ACT, lambda: nc.scalar.activation(out=omlb[:], in_=fS[:, 0, :], func=AF.Sigmoid, scale=-1.0),
            reads=[K("fS", 0)], writes=[K("omlb")])
    sch.add(ACT, lambda: nc.scalar.mul(out=gq8[:], in_=gq8[:], mul=0.125), reads=[K("gq8")], writes=[K("gq8")])
    sch.add(POOL, lambda: nc.gpsimd.memset(vext[:, :, :, 64:65], 1.0), writes=[K("vext_one")])
    sch.add(POOL, lambda: nc.gpsimd.memset(epsb[:], EPS), writes=[K("epsb")])
    sch.add(POOL, lambda: nc.gpsimd.memset(bias[:, :, :, 0:64], 0.0), writes=[K("bias", 0), K("bias", 1)])

    def blkv(b, kc0, nkc, n0, nn):
        v = wbf[b].rearrange("p (kc n) -> p kc n", n=512)
        return v[:, kc0:kc0 + nkc, n0:n0 + nn]

    def rows(w, r0, nkc, c0, ncol):
        return w[r0:r0 + nkc * 128, c0:c0 + ncol].rearrange("(kc p) n -> p kc n", p=128)

    for g, b in enumerate([B_HQ, B_HF, B_HI, B_HG, B_MQ, B_MK, B_MV]):
        cast(blkv(b, 0, 8, 0, 512), rows(w_in, 0, 8, g * 512, 512), b)
    for qd, b in enumerate([B_GAB0, B_GAB1, B_GAB2, B_GAB3]):
        cast(blkv(b, 0, 8, 0, 256), rows(w_in, 0, 8, 3584 + qd * 256, 256), b)
        cast(blkv(b, 0, 8, 256, 256), rows(w_in, 0, 8, 4608 + qd * 256, 256), b)
    for hf, b in enumerate([B_WAB0, B_WAB1]):
        cast(blkv(b, 0, 4, 0, 512), rows(w_a, 0, 4, hf * 512, 512), b)
        cast(blkv(b, 4, 4, 0, 512), rows(w_b, 0, 4, hf * 512, 512), b)
    for hf, b in enumerate([B_WO0, B_WO1]):
        cast(blkv(b, 0, 8, 0, 512), rows(w_out, 0, 8, hf * 512, 512), b)
    for u in range(11):
        cast(blkv(B_UP0 + u, 0, 8, 0, 256), rows(w_up, 0, 8, u * 256, 256), B_UP0 + u)
        cast(blkv(B_UP0 + u, 0, 8, 256, 256), rows(w_up, 0, 8, DFF + u * 256, 256), B_UP0 + u)
    DN_P = [(0, 8), (8, 8), (16, 6)]
    for nh in range(2):
        for pi, (k0, nk) in enumerate(DN_P):
            cast(blkv(B_DN0 + nh * 3 + pi, 0, nk, 0, 512), rows(w_down, k0 * 128, nk, nh * 512, 512), B_DN0 + nh * 3 + pi)

    cast_flush()
    wstate = {"next": 0}
    total_blocks = NSEQ * NCH * NBLK

    def wload_upto(gb):
        while wstate["next"] <= gb and wstate["next"] < total_blocks:
            g = wstate["next"]
            slot = g % RING
            b = g % NBLK
            ne = 3072 if b in (B_DN0 + 2, B_DN0 + 5) else 4096
            sch.add(WQ[slot], (lambda slot=slot, b=b, ne=ne: nc.sync.dma_start(out=wring[:, slot, 0:ne],
                                                                              in_=wbf[b][:, 0:ne])),
                    reads=[K("wbf", b)], writes=[K("w", slot)])
            wstate["next"] += 1

    def wblk(gc, b, ahead=2):
        gb = gc * NBLK + b
        wload_upto(gb + ahead)
        slot = gb % RING
        return wring[:, slot, :].rearrange("p (kc n) -> p kc n", n=512), K("w", slot)

    def mm(out_ap, lhsT, rhs, start, stop, reads, writes, sig, **kw):
        sch.add(PE, lambda: nc.tensor.matmul(out_ap, lhsT=lhsT, rhs=rhs, start=start, stop=stop, **kw),
                reads=reads, writes=writes, sig=sig)

    def tp(out_ap, in_ap, reads, writes, sig):
        sch.add(PE, lambda: nc.tensor.transpose(out_ap, in_ap, ident[:]), reads=list(reads) + [K("ident")],
                writes=writes, sig=sig)

    hbuf = sb("hbuf", [128, 2, D], BF16)

    def norm_A(src_ap, src_keys, gb_t, gkey, tt, stc):
        junk = bB[:, 22:24, :].rearrange("p a b -> p (a b)")
        sch.add(ACT, lambda: nc.scalar.activation(out=junk, in_=src_ap, func=AF.Square, scale=1.0 / 32.0,
                                                  accum_out=st[:, stc:stc + 1]),
                reads=src_keys, writes=[K("bB", 22), K("bB", 23), K("st", stc)])
        sch.add(ACT, lambda: nc.scalar.activation(out=st[:, stc + 2:stc + 3], in_=st[:, stc:stc + 1], func=AF.Ln,
                                                  bias=epsb[:, 0:1]),
                reads=[K("st", stc), K("epsb")], writes=[K("st", stc + 2)])
        sch.add(ACT, lambda: nc.scalar.activation(out=st[:, stc + 3:stc + 4], in_=st[:, stc + 2:stc + 3], func=AF.Exp,
                                                  scale=-0.5),
                reads=[K("st", stc + 2)], writes=[K("st", stc + 3)])
        hi_ = tt % 2
        sch.add(DVE, lambda: nc.vector.scalar_tensor_tensor(out=hbuf[:, hi_, :], in0=src_ap,
                                                            scalar=st[:, stc + 3:stc + 4],
                                                            in1=gb_t[:], op0=ALU.mult, op1=ALU.mult),
                reads=list(src_keys) + [K("st", stc + 3), gkey], writes=[K("hbuf", hi_)])

    def norm_B(tt):
        hi_ = tt % 2
        ti = rT.nxt()
        for kc in range(8):
            tp(Tb[ti][:, kc * 128:(kc + 1) * 128], hbuf[:, hi_, kc * 128:(kc + 1) * 128], [K("hbuf", hi_)],
               [K("T", ti)], kc == 7)
        sch.add(DVE, lambda: nc.vector.tensor_copy(out=hT[:, :, tt * 128:(tt + 1) * 128],
                                                   in_=Tb[ti][:, :].rearrange("p (kc t) -> p kc t", t=128)),
                reads=[K("T", ti)], writes=[K("hT", tt)])

    def norm_tile(src_ap, src_keys, gb_t, gkey, tt, stc):
        norm_A(src_ap, src_keys, gb_t, gkey, tt, stc)
        norm_B(tt)

    def x_norm_A(s_, c_, tt):
        xi = rxt.nxt()
        xs = 12 + 2 * xi
        xap = fS[:, xs:xs + 2, :].rearrange("p a b -> p (a b)")
        xk = [K("fS", xs), K("fS", xs + 1)]
        r0 = c_ * CH + tt * 128
        sch.add(XQ[xi], lambda: nc.sync.dma_start(out=xap, in_=x[s_, r0:r0 + 128, :]), writes=xk)
        norm_A(xap, xk, g1b, K("g1b"), tt, 4 * tt)

    prefetched = set()

    HTK = [K("hT", t) for t in range(4)]

    def chunk(s, c):
        gc = s * NCH + c
        first = (gc == 0)
        for tt in range(4):
            if (gc, tt) in prefetched:
                continue
            x_norm_A(s, c, tt)
            norm_B(tt)
        if first:
            dump("hT", hT[:], HTK, [128, 8, CH], BF16)
        wq_, kq_ = wblk(gc, B_HQ)
        wf_, kf_ = wblk(gc, B_HF, 1)
        if c == 0:
            sch.add(DVE, lambda: nc.vector.memset(Sst[:], 0.0), writes=[K("Sst")])
        KQT = [K("bB", i) for i in range(8)]

        def h_stageA(tt):
            r = tt % 2
            qs, ks, ls = 0 + r, 2 + r, 4 + r
            pq = rP6.nxt()
            for kc in range(8):
                mm(Pf[pq][:, :], hT[:, kc, tt * 128:(tt + 1) * 128], wq_[:, kc, :], kc == 0, kc == 7,
                   [K("hT", tt), kq_], [K("P", pq)], kc == 7)
            sch.add(ACT, lambda: nc.scalar.activation(out=fS[:, qs, :], in_=Pf[pq][:, :], func=AF.Silu),
                    reads=[K("P", pq)], writes=[K("fS", qs)])
            pf = rP6.nxt()
            for kc in range(8):
                mm(Pf[pf][:, :], hT[:, kc, tt * 128:(tt + 1) * 128], wf_[:, kc, :], kc == 0, kc == 7,
                   [K("hT", tt), kf_], [K("P", pf)], kc == 7)
            sch.add(ACT, lambda: nc.scalar.activation(out=fS[:, ks, :], in_=Pf[pf][:, :], func=AF.Sigmoid, scale=-1.0),
                    reads=[K("P", pf)], writes=[K("fS", ks)])
            sch.add(DVE, lambda: nc.vector.tensor_tensor(out=fS[:, ks, :], in0=fS[:, ks, :], in1=omlb[:], op=ALU.mult),
                    reads=[K("fS", ks), K("omlb")], writes=[K("fS", ks)])
            sch.add(ACT, lambda: nc.scalar.activation(out=fS[:, ls, :], in_=fS[:, ks, :], func=AF.Ln, scale=-1.0,
                                                      bias=1.0),
                    reads=[K("fS", ks)], writes=[K("fS", ls)])

        def h_stageB(tt):
            r = tt % 2
            qs, ks, ls, es_, ns = 0 + r, 2 + r, 4 + r, 6 + r, 8 + r
            pa = rP6.nxt()
            mm(Pf[pa][:, :], Um[:], fS[:, ls, :], True, True, [K("Um"), K("fS", ls)], [K("P", pa)], True)
            sch.add(ACT, lambda: nc.scalar.activation(out=fS[:, es_, :], in_=Pf[pa][:, :], func=AF.Exp),
                    reads=[K("P", pa)], writes=[K("fS", es_)])
            sch.add(ACT, lambda: nc.scalar.activation(out=fS[:, ns, :], in_=Pf[pa][:, :], func=AF.Exp, scale=-1.0),
                    reads=[K("P", pa)], writes=[K("fS", ns)])
            pl = rP6.nxt()
            for h in range(4):
                mm(Pf[pl][:, h * 2:h * 2 + 2], fS[:, ls, h * 128:(h + 1) * 128], cind[:], True, True,
                   [K("fS", ls), K("cind")], [K("P", pl)], h == 3)
            sch.add(ACT, lambda: nc.scalar.activation(
                out=elast[:, tt, :, :].rearrange("p h j -> p (h j)"), in_=Pf[pl][:, 0:8], func=AF.Exp),
                reads=[K("P", pl)], writes=[K("elast", tt)])
            sch.add(DVE, lambda: nc.vector.tensor_tensor(out=bB[:, 12 + tt, :], in0=fS[:, ks, :], in1=fS[:, es_, :],
                                                         op=ALU.mult),
                    reads=[K("fS", ks), K("fS", es_)], writes=[K("bB", 12 + tt)])
            sch.add(DVE, lambda: nc.vector.tensor_tensor(out=qtil[:, r, :], in0=fS[:, qs, :], in1=fS[:, ns, :],
                                                         op=ALU.mult),
                    reads=[K("fS", qs), K("fS", ns)], writes=[K("qtil", r)])
            if first and tt == 0:
                dump("khat0", bB[:, 12, :], [K("bB", 12)], [128, 512], BF16)
                dump("qtil0", qtil[:, 0, :], [K("qtil", 0)], [128, 512], BF16)

        def h_stageB2(tt):
            r = tt % 2
            ti = rT.nxt()
            for h in range(4):
                tp(Tb[ti][:, h * 128:(h + 1) * 128], bB[:, 12 + tt, h * 128:(h + 1) * 128], [K("bB", 12 + tt)],
                   [K("T", ti)], False)
            for h in range(4):
                tp(Tb[ti][:, (4 + h) * 128:(5 + h) * 128], qtil[:, r, h * 128:(h + 1) * 128], [K("qtil", r)],
                   [K("T", ti)], h == 3)
            sch.add(ACT, lambda: nc.scalar.copy(out=bB[:, 0:8, tt * 128:(tt + 1) * 128],
                                                in_=Tb[ti][:, :].rearrange("p (a t) -> p a t", t=128)),
                    reads=[K("T", ti)], writes=[K("kqT", tt)] + KQT)

        hi_state = {}

        def h_hi(tt):
            if "w" not in hi_state:
                hi_state["w"] = wblk(gc, B_HI)
            wi_, ki_ = hi_state["w"]
            pv = rP6.nxt()
            for kc in range(8):
                mm(Pf[pv][:, :], hT[:, kc, tt * 128:(tt + 1) * 128], wi_[:, kc, :], kc == 0, kc == 7,
                   [K("hT", tt), ki_], [K("P", pv)], kc == 7)
            sch.add(ACT, lambda: nc.scalar.copy(out=bB[:, 8 + tt, :], in_=Pf[pv][:, :]),
                    reads=[K("P", pv)], writes=[K("bB", 8 + tt)])

        h_stageA(0); h_stageA(1); h_stageB(0); h_stageA(2); h_stageB(1); h_stageB2(0); h_stageA(3); h_stageB(2)
        h_stageB2(1); h_hi(0); h_stageB(3); h_hi(1); h_stageB2(2); h_hi(2); h_hi(3); h_stageB2(3)
        wg_, kg_ = wblk(gc, B_HG)
        for tt in range(4):
            ph = rP6.nxt()
            for kc in range(8):
                mm(Pf[ph][:, :], hT[:, kc, tt * 128:(tt + 1) * 128], wg_[:, kc, :], kc == 0, kc == 7,
                   [K("hT", tt), kg_], [K("P", ph)], kc == 7)
            sch.add(ACT, lambda ph=ph, tt=tt: nc.scalar.activation(out=fS[:, tt, :], in_=Pf[ph][:, :], func=AF.Silu),
                    reads=[K("P", ph)], writes=[K("fS", tt)])
            sch.add(POOL, lambda tt=tt: nc.gpsimd.tensor_tensor(out=fS[:, tt, :], in0=fS[:, tt, :], in1=gob[:],
                                                                op=ALU.mult),
                    reads=[K("fS", tt), K("gob")], writes=[K("fS", tt)])

        def h_post_tr(tt):
            r = tt % 2
            ti = rT.nxt()
            for kc in range(4):
                tp(Tb[ti][:, kc * 128:(kc + 1) * 128], oabf[:, r, kc * 128:(kc + 1) * 128], [K("oabf", r)],
                   [K("T", ti)], kc == 3)
            sch.add(ACT, lambda: nc.scalar.copy(out=oaT[:, :, tt * 128:(tt + 1) * 128],
                                                in_=Tb[ti][:, 0:512].rearrange("p (a t) -> p a t", t=128)),
                    reads=[K("T", ti)], writes=[K("oaT", tt)])

        def h_attn(tt):
            r = tt % 2
            po = 4 + r
            pat = rP3.nxt()
            for h in range(4):
                mm(Pf[pat][:, h * 128:(h + 1) * 128], bB[:, h, tt * 128:(tt + 1) * 128],
                   bB[:, 4 + h, tt * 128:(tt + 1) * 128], True, True, KQT, [K("P", pat)], h == 3)
            sch.add(DVE, lambda: nc.vector.scalar_tensor_tensor(
                out=attn[:, r, :].rearrange("p (h t) -> p h t", h=4),
                in0=Pf[pat][:, :].rearrange("p (h t) -> p h t", h=4), scalar=1e30,
                in1=bd[:].unsqueeze(1).broadcast_to([128, 4, 128]), op0=ALU.min, op1=ALU.mult),
                reads=[K("P", pat), K("bd")], writes=[K("attn", r)])
            for h in range(4):
                mm(Pf[po][:, h * 128:(h + 1) * 128], attn[:, r, h * 128:(h + 1) * 128],
                   bB[:, 8 + tt, h * 128:(h + 1) * 128], h == 0, False, [K("attn", r), K("bB", 8 + tt)],
                   [K("P", po)], False, skip_group_check=True)

        def h_rec(tt):
            r = tt % 2
            po = 4 + r
            for j in range(2):
                n = 2 * tt + j
                sd = n % 2
                sch.add(DVE, lambda j=j: nc.vector.tensor_tensor(
                    out=fS[:, 16, :].rearrange("p (h v) -> p h v", h=4),
                    in0=Sst[:].rearrange("p (h v) -> p h v", h=4),
                    in1=elast[:, tt, :, j:j + 1].broadcast_to([128, 4, 128]), op=ALU.mult),
                    reads=[K("Sst"), K("elast", tt)], writes=[K("fS", 16)])
                sch.add(ACT, lambda sd=sd: nc.scalar.copy(out=sdb[:, sd, :], in_=fS[:, 16, :]),
                        reads=[K("fS", 16)], writes=[K("sdb", sd)])
                for h in range(4):
                    mm(Pf[3][:, h * 128:(h + 1) * 128], bB[j * 64:(j + 1) * 64, 12 + tt, h * 128:(h + 1) * 128],
                       bB[j * 64:(j + 1) * 64, 8 + tt, h * 128:(h + 1) * 128], True, True,
                       [K("bB", 12 + tt), K("bB", 8 + tt)], [K("P", 3)], h == 3)
                for h in range(4):
                    t0 = tt * 128 + j * 64
                    mm(Pf[po][j * 64:(j + 1) * 64, h * 128:(h + 1) * 128], bB[:, 4 + h, t0:t0 + 64],
                       sdb[:, sd, h * 128:(h + 1) * 128], False, (j == 1 and h == 3), KQT + [K("sdb", sd)],
                       [K("P", po)], (j == 1 and h == 3), skip_group_check=True)
                sch.add(DVE, lambda: nc.vector.tensor_tensor(out=Sst[:], in0=fS[:, 16, :], in1=Pf[3][:, :], op=ALU.add),
                        reads=[K("fS", 16), K("P", 3)], writes=[K("Sst")])

        def h_post(tt):
            r = tt % 2
            po = 4 + r
            so = 16 + 16 * r
            for h in range(4):
                sch.add(ACT, lambda h=h: nc.scalar.activation(
                    out=bB[:, 22, h * 128:(h + 1) * 128], in_=Pf[po][:, h * 128:(h + 1) * 128], func=AF.Square,
                    scale=float(1.0 / np.sqrt(128.0)), accum_out=st[:, so + h:so + h + 1]),
                    reads=[K("P", po)], writes=[K("bB", 22), K("st", so + h)])
            sch.add(ACT, lambda: nc.scalar.activation(out=st[:, so + 8:so + 12], in_=st[:, so:so + 4], func=AF.Ln,
                                                      bias=epsb[:, 0:1]),
                    reads=[K("st", so + h) for h in range(4)] + [K("epsb")], writes=[K("st", so + 8)])
            sch.add(ACT, lambda: nc.scalar.activation(out=st[:, so + 12:so + 16], in_=st[:, so + 8:so + 12],
                                                      func=AF.Exp, scale=-0.5),
                    reads=[K("st", so + 8)], writes=[K("st", so + 12)])
            for h in range(4):
                sch.add(DVE, lambda h=h: nc.vector.scalar_tensor_tensor(
                    out=oabf[:, r, h * 128:(h + 1) * 128], in0=Pf[po][:, h * 128:(h + 1) * 128],
                    scalar=st[:, so + 12 + h:so + 13 + h], in1=fS[:, tt, h * 128:(h + 1) * 128], op0=ALU.mult,
                    op1=ALU.mult),
                    reads=[K("P", po), K("st", so + 12), K("fS", tt)], writes=[K("oabf", r)])
            if first and tt == 0:
                dump("oa0", oabf[:, 0, :], [K("oabf", 0)], [128, 512], BF16)

        h_attn(0)
        for tt in range(4):
            h_rec(tt)
            if tt < 3:
                h_attn(tt + 1)
            h_post(tt)
            if tt >= 1:
                h_post_tr(tt - 1)
        QTE = [K("bB", 16 + i) for i in range(8)]
        qTe = bB[:, 16:24, :]
        wmk, kmk = wblk(gc, B_MK)
        wstore = {"k": (wmk, kmk)}

        def m_stageA(kind, tt, idx):
            is_q = kind == "q"
            if is_q and "q" not in wstore:
                wstore["q"] = wblk(gc, B_MQ)
            wv, wk = wstore[kind]
            par = idx % 3
            sq, mn = [4, 5, 10][par], [6, 7, 11][par]
            sc = 64 + 16 * par
            pm = rP.nxt()
            for kc in range(8):
                mm(Pf[pm][:, :], hT[:, kc, tt * 128:(tt + 1) * 128], wv[:, kc, :], kc == 0, kc == 7,
                   [K("hT", tt), wk], [K("P", pm)], kc == 7)
            sch.add(ACT, lambda: nc.scalar.activation(out=fS[:, sq, :], in_=Pf[pm][:, :], func=AF.Square, scale=0.125),
                    reads=[K("P", pm)], writes=[K("fS", sq)])
            sch.add(DVE, lambda: nc.vector.tensor_reduce(out=st[:, sc:sc + 8],
                                                         in_=fS[:, sq, :].rearrange("p (h d) -> p h d", h=8),
                                                         axis=AX.X, op=ALU.add),
                    reads=[K("fS", sq)], writes=[K("st", sc)])
            sch.add(ACT, lambda: nc.scalar.activation(out=st[:, sc + 8:sc + 16], in_=st[:, sc:sc + 8], func=AF.Ln,
                                                      bias=epsb[:, 0:1]),
                    reads=[K("st", sc), K("epsb")], writes=[K("st", sc + 8)])
            sch.add(ACT, lambda: nc.scalar.activation(out=st[:, sc:sc + 8], in_=st[:, sc + 8:sc + 16], func=AF.Exp,
                                                      scale=-0.5),
                    reads=[K("st", sc + 8)], writes=[K("st", sc)])
            sch.add(DVE, lambda: nc.vector.tensor_tensor(
                out=fS[:, mn, :].rearrange("p (h d) -> p h d", h=8),
                in0=Pf[pm][:, :].rearrange("p (h d) -> p h d", h=8),
                in1=st[:, sc:sc + 8].unsqueeze(2).broadcast_to([128, 8, 64]), op=ALU.mult),
                reads=[K("P", pm), K("st", sc)], writes=[K("fS", mn)])
            gvec, gkey = (gq8, K("gq8")) if is_q else (gkb, K("gkb"))
            m3 = fS[:, mn, :].rearrange("p (h d) -> p h d", h=8)
            sch.add(DVE, lambda: nc.vector.tensor_tensor(out=m3, in0=m3,
                                                         in1=gvec[:].unsqueeze(1).broadcast_to([128, 8, 64]),
                                                         op=ALU.mult),
                    reads=[K("fS", mn), gkey], writes=[K("fS", mn)])

        def m_stageB(kind, tt, idx):
            is_q = kind == "q"
            par = idx % 3
            mn, tB = [6, 7, 11][par], [8, 9, 17][par]
            tile_i = c * 4 + tt
            m4 = fS[:, mn, :].rearrange("p (h a d) -> p h a d", h=8, a=2)
            tB4 = fS[:, tB, :].rearrange("p (h a d) -> p h a d", h=8, a=2)
            cosb = cs[:, tile_i, 0, :]
            sinb3 = cs[:, tile_i, 1, :].unsqueeze(1).broadcast_to([128, 8, 32])
            sch.add(POOL, lambda: nc.gpsimd.tensor_tensor(out=tB4[:, :, 0, :], in0=m4[:, :, 1, :], in1=sinb3,
                                                          op=ALU.mult),
                    reads=[K("fS", mn), K("cs")], writes=[K("fS", tB)])
            sch.add(POOL, lambda: nc.gpsimd.tensor_tensor(out=tB4[:, :, 1, :], in0=m4[:, :, 0, :], in1=sinb3,
                                                          op=ALU.mult),
                    reads=[K("fS", mn), K("cs")], writes=[K("fS", tB)])
            sch.add(DVE, lambda: nc.vector.tensor_tensor(
                out=m4, in0=m4, in1=cosb.unsqueeze(1).unsqueeze(1).broadcast_to([128, 8, 2, 32]), op=ALU.mult),
                reads=[K("fS", mn), K("cs"), K("fS", tB)], writes=[K("fS", mn)])
            ro = idx % 2
            ro4 = ropeo[:, ro, :].rearrange("p (h a d) -> p h a d", h=8, a=2)
            sch.add(POOL, lambda: nc.gpsimd.tensor_tensor(out=ro4[:, :, 0, :], in0=m4[:, :, 0, :], in1=tB4[:, :, 0, :],
                                                          op=ALU.subtract),
                    reads=[K("fS", mn), K("fS", tB)], writes=[K("ropeo", ro)])
            sch.add(POOL, lambda: nc.gpsimd.tensor_tensor(out=ro4[:, :, 1, :], in0=m4[:, :, 1, :], in1=tB4[:, :, 1, :],
                                                          op=ALU.add),
                    reads=[K("fS", mn), K("fS", tB)], writes=[K("ropeo", ro)])

        def m_stageB2(kind, tt, idx):
            is_q = kind == "q"
            ro = idx % 2
            tile_i = c * 4 + tt
            ti = rT.nxt()
            for h in range(8):
                tp(Tb[ti][0:64, h * 128:(h + 1) * 128], ropeo[:, ro, h * 64:(h + 1) * 64], [K("ropeo", ro)],
                   [K("T", ti)], h == 7)
            src = Tb[ti][0:64, :].rearrange("p (h t) -> p h t", h=8)
            if is_q:
                sch.add(DVE, lambda: nc.vector.tensor_copy(out=qTe[0:64, :, tt * 128:(tt + 1) * 128], in_=src),
                        reads=[K("T", ti)], writes=QTE + [K("qTe", tt)])
            else:
                p0 = c * CH + tt * 128
                sch.add(DVE, lambda: nc.vector.tensor_copy(out=kTe[0:64, :, p0:p0 + 128], in_=src),
                        reads=[K("T", ti)], writes=[K("kTe", tile_i)])
                if tt % 2 == 1:
                    blk = 2 * c + tt // 2
                    sch.add(DVE, lambda: nc.vector.tensor_reduce(out=kmf[0:64, :],
                                                                 in_=kTe[0:64, :, blk * 256:(blk + 1) * 256],
                                                                 axis=AX.X, op=ALU.add),
                            reads=[K("kTe", 2 * blk), K("kTe", 2 * blk + 1)], writes=[K("kmf")])
                    sch.add(DVE, lambda: nc.vector.tensor_scalar(out=kmT[0:64, :, blk:blk + 1],
                                                                 in0=kmf[0:64, :].unsqueeze(2), scalar1=1.0 / 256.0,
                                                                 scalar2=None, op0=ALU.mult),
                            reads=[K("kmf")], writes=[K("kmT")])

        def m_stageC(tt):
            qb = 2 * c + tt // 2
            bi = tt % 2
            if qb >= 4:
                pg = rP.nxt()
                for h in range(8):
                    mm(Pf[pg][:, h * 8:h * 8 + qb], qTe[0:64, h, tt * 128:(tt + 1) * 128], kmT[0:64, h, 0:qb], True, True,
                       [K("qTe", tt), K("kmT")], [K("P", pg)], h == 7)
                sch.add(ACT, lambda: nc.scalar.copy(
                    out=gsm[:].rearrange("p (h j) -> p h j", h=8)[:, :, 0:qb],
                    in_=Pf[pg][:, 0:64].rearrange("p (h j) -> p h j", h=8)[:, :, 0:qb]),
                        reads=[K("P", pg)], writes=[K("gsm")])
                g3 = gsm[:].rearrange("p (h j) -> p h j", h=8)[:, :, 0:qb]
                c4 = cmpb[:, 0:8 * qb * qb].rearrange("p (h j k) -> p h j k", h=8, j=qb)
                sch.add(DVE, lambda: nc.vector.tensor_tensor(
                    out=c4, in0=g3.unsqueeze(2).broadcast_to([128, 8, qb, qb]),
                    in1=g3.unsqueeze(3).broadcast_to([128, 8, qb, qb]), op=ALU.is_gt),
                    reads=[K("gsm")], writes=[K("cmpb")])
                cn3 = cnt[:, 0:8 * qb].rearrange("p (h j) -> p h j", h=8)
                sch.add(DVE, lambda: nc.vector.tensor_reduce(out=cn3, in_=c4, axis=AX.X, op=ALU.add),
                        reads=[K("cmpb")], writes=[K("cnt")])
                sch.add(DVE, lambda: nc.vector.tensor_scalar(
                    out=bias[:, bi, :, 64:64 + qb], in0=cn3, scalar1=2.5, scalar2=-BIG, op0=ALU.is_gt, op1=ALU.mult),
                    reads=[K("cnt")], writes=[K("bias", bi)])
                sch.add(POOL, lambda: nc.gpsimd.memset(bias[:, bi, :, 64 + qb:65 + qb], 0.0), writes=[K("bias", bi)])
            else:
                sch.add(POOL, lambda: nc.gpsimd.memset(bias[:, bi, :, 64:65 + qb], 0.0), writes=[K("bias", bi)])
            if qb < 7:
                sch.add(POOL, lambda: nc.gpsimd.memset(bias[:, bi, :, 65 + qb:72], -BIG), writes=[K("bias", bi)])

        def m_stageC2(tt):
            bi = tt % 2
            for hh in range(2):
                pb = rP.nxt()
                for h4 in range(4):
                    h = hh * 4 + h4
                    mm(Pf[pb][0:72, h4 * 128:(h4 + 1) * 128], bias[:, bi, h, :], ident[:], True, True,
                       [K("bias", bi), K("ident")], [K("P", pb)], h4 == 3)
                sch.add(ACT, lambda pb=pb, hh=hh: nc.scalar.copy(
                    out=qTe[64:72, hh * 4:hh * 4 + 4, tt * 128:(tt + 1) * 128],
                    in_=Pf[pb][64:72, :].rearrange("p (h t) -> p h t", h=4)),
                    reads=[K("P", pb)], writes=QTE + [K("qTeb", tt)])

        def m_v(tt):
            if "v" not in wstore:
                wstore["v"] = wblk(gc, B_MV)
            wmv, kmv = wstore["v"]
            tile_i = c * 4 + tt
            pv = rP.nxt()
            for kc in range(8):
                mm(Pf[pv][:, :], hT[:, kc, tt * 128:(tt + 1) * 128], wmv[:, kc, :], kc == 0, kc == 7,
                   [K("hT", tt), kmv], [K("P", pv)], kc == 7)
            sch.add(ACT, lambda: nc.scalar.copy(
                out=vext[:, tile_i, :, 0:64], in_=Pf[pv][:, :].rearrange("p (h d) -> p h d", h=8)),
                reads=[K("P", pv)], writes=[K("vext", tile_i)])

        items2 = [("k", t) for t in range(4)] + [("q", t) for t in range(4)]
        m_stageA("k", 0, 0)
        h_post_tr(3)
        m_stageA("k", 1, 1)
        m_stageB("k", 0, 0)
        for i in range(2, 8):
            m_stageB(items2[i - 1][0], items2[i - 1][1], i - 1)
            m_stageA(items2[i][0], items2[i][1], i)
            m_stageB2(items2[i - 2][0], items2[i - 2][1], i - 2)
        m_stageB("q", 3, 7)
        m_v(0)
        m_stageB2("q", 2, 6)
        m_stageC(0)
        m_v(1)
        m_stageB2("q", 3, 7)
        m_stageC(1)
        m_stageC2(0)
        m_v(2)
        m_stageC2(1)
        m_stageC(2)
        m_v(3)
        m_stageC(3)
        m_stageC2(2)
        m_stageC2(3)
        if first:
            dump("kTe", kTe[0:72, :, 0:512], [K("kTe", i) for i in range(4)] + [K("kTe_ind")], [72, 8, 512], BF16)
            dump("qTe", qTe[0:72, :, :], QTE, [72, 8, 512], BF16)
        nkt = 4 * c + 4
        OB = [K("bB", 8 + i) for i in range(4)]
        obv = bB[:, 8:12, :].rearrange("p t (h d) -> p t h d", h=8)
        items = [(h, kt) for h in range(8) for kt in range(nkt)]
        pend = None

        def att_qk(h, kt):
            n0 = max(0, kt * 128 - c * CH)
            pst = rP.nxt()
            pi = rpt.nxt()
            mm(Pf[pst][:, n0:512], kTe[0:72, h, kt * 128:(kt + 1) * 128], qTe[0:72, h, n0:512], True, True,
               [K("kTe", kt), K("kTe_ind")] + QTE, [K("P", pst)], True)
            sch.add(ACT, lambda pst=pst, pi=pi, n0=n0: nc.scalar.activation(out=pt[:, pi, n0:512],
                                                                          in_=Pf[pst][:, n0:512], func=AF.Exp),
                    reads=[K("P", pst)], writes=[K("pt", pi)])
            if kt * 128 >= c * CH:
                sch.add(DVE, lambda pi=pi, n0=n0: nc.vector.tensor_tensor(out=pt[:, pi, n0:n0 + 128],
                                                                          in0=pt[:, pi, n0:n0 + 128], in1=tri[:],
                                                                          op=ALU.mult),
                        reads=[K("pt", pi), K("tri")], writes=[K("pt", pi)])
            return (h, kt, n0, pi)

        def att_pv(h, kt, n0, pi):
            pob = 4 + (h % 2)
            for sub in range(n0 // 128, 4):
                last = (kt == nkt - 1 and sub == 3)
                mm(Pf[pob][:, sub * 65:(sub + 1) * 65], pt[:, pi, sub * 128:(sub + 1) * 128], vext[:, kt, h, :],
                   (kt == 0 and sub == 0), last, [K("pt", pi), K("vext", kt), K("vext_one")], [K("P", pob)],
                   sub == 3, skip_group_check=True)
            if kt == nkt - 1:
                rd = h % 2
                po3 = Pf[pob][:, 0:260].rearrange("p (s e) -> p s e", e=65)
                sch.add(DVE, lambda po3=po3, rd=rd: nc.vector.reciprocal(out=rden[:, rd, :].unsqueeze(2),
                                                                         in_=po3[:, :, 64:65]),
                        reads=[K("P", pob)], writes=[K("rden", rd)])
                sch.add(DVE, lambda po3=po3, rd=rd, h=h: nc.vector.tensor_tensor(
                    out=obv[:, :, h, :], in0=po3[:, :, 0:64],
                    in1=rden[:, rd, :].unsqueeze(2).broadcast_to([128, 4, 64]), op=ALU.mult),
                    reads=[K("P", pob), K("rden", rd)], writes=OB)

        pendq = []
        for (h, kt) in items:
            pendq.append(att_qk(h, kt))
            if len(pendq) > 2:
                att_pv(*pendq.pop(0))
        while pendq:
            att_pv(*pendq.pop(0))
        if first:
            dump("ob", bB[:, 8:12, :], OB, [128, 4, 512], BF16)
        for tt in range(4):
            ti = rT.nxt()
            for kc in range(4):
                tp(Tb[ti][:, kc * 128:(kc + 1) * 128], bB[:, 8 + tt, kc * 128:(kc + 1) * 128], [K("bB", 8 + tt)],
                   [K("T", ti)], kc == 3)
            sch.add(ACT, lambda ti=ti, tt=tt: nc.scalar.copy(out=obT[:, :, tt * 128:(tt + 1) * 128],
                                                             in_=Tb[ti][:, 0:512].rearrange("p (a t) -> p a t", t=128)),
                    reads=[K("T", ti)], writes=[K("obT", tt)])
        OAT = [K("oaT", t) for t in range(4)]
        OBT = [K("obT", t) for t in range(4)]
        MIX = [K("bB", i) for i in range(8)]
        gab_ids = [B_GAB0, B_GAB1, B_GAB2, B_GAB3]
        for qd in range(4):
            wg2, kg2 = wblk(gc, gab_ids[qd], 1)
            wab, kab = wblk(gc, B_WAB0 if qd < 2 else B_WAB1, 1)
            for e in range(2):
                i = 2 * qd + e
                col = (i % 4) * 128
                res = []
                for br in range(2):
                    pgt = rP.nxt()
                    for kc in range(8):
                        mm(Pf[pgt][:, :], wg2[:, kc, br * 256 + e * 128: br * 256 + (e + 1) * 128], hT[:, kc, :],
                           kc == 0, kc == 7, HTK + [kg2], [K("P", pgt)], kc == 7)
                    pab = rP.nxt()
                    src = oaT if br == 0 else obT
                    srk = OAT if br == 0 else OBT
                    for kc in range(4):
                        mm(Pf[pab][:, :], wab[:, br * 4 + kc, col:col + 128], src[:, kc, :], kc == 0, kc == 3,
                           srk + [kab], [K("P", pab)], kc == 3)
                    ss_, ms_ = 0 + br, 2 + br
                    sch.add(ACT, lambda pgt=pgt, ss_=ss_: nc.scalar.activation(out=fS[:, ss_, :], in_=Pf[pgt][:, :],
                                                                             func=AF.Sigmoid),
                            reads=[K("P", pgt)], writes=[K("fS", ss_)])
                    sch.add(DVE, lambda pab=pab, ss_=ss_, ms_=ms_: nc.vector.tensor_tensor(
                        out=fS[:, ms_, :], in0=fS[:, ss_, :], in1=Pf[pab][:, :], op=ALU.mult),
                        reads=[K("fS", ss_), K("P", pab)], writes=[K("fS", ms_)])
                sch.add(POOL, lambda i=i: nc.gpsimd.tensor_tensor(out=bB[:, i, :], in0=fS[:, 2, :], in1=fS[:, 3, :],
                                                                  op=ALU.add),
                        reads=[K("fS", 2), K("fS", 3)], writes=[K("bB", i)])
        if first:
            dump("mixT", bB[:, 0:8, :], MIX, [128, 8, 512], BF16)
        wo0, ko0 = wblk(gc, B_WO0)
        wo1, ko1 = wblk(gc, B_WO1, 1)

        def w_out_tile(tt):
            xi = rxt.nxt()
            xs = 12 + 2 * xi
            xap = fS[:, xs:xs + 2, :].rearrange("p a b -> p (a b)")
            xk = [K("fS", xs), K("fS", xs + 1)]
            r0 = c * CH + tt * 128
            sch.add(XQ[xi], lambda: nc.sync.dma_start(out=xap, in_=x[s, r0:r0 + 128, :]), writes=xk)
            for nh, (wo, ko) in enumerate([(wo0, ko0), (wo1, ko1)]):
                pw = rP.nxt()
                for kc in range(8):
                    mm(Pf[pw][:, :], bB[:, kc, tt * 128:(tt + 1) * 128], wo[:, kc, :], kc == 0, kc == 7,
                       MIX + [ko], [K("P", pw)], kc == 7)
                sch.add(DVE, lambda pw=pw, nh=nh: nc.vector.tensor_tensor(
                    out=xmid[:, tt, nh * 512:(nh + 1) * 512], in0=xap[:, nh * 512:(nh + 1) * 512], in1=Pf[pw][:, :],
                    op=ALU.add), reads=xk + [K("P", pw)], writes=[K("xmid", tt, nh)])

        def n2A(tt):
            norm_A(xmid[:, tt, :], [K("xmid", tt, 0), K("xmid", tt, 1)], g2b, K("g2b"), tt, 4 * tt)

        w_out_tile(0); w_out_tile(1); n2A(0); w_out_tile(2); n2A(1); norm_B(0); w_out_tile(3)
        if first:
            dump("xmid", xmid[:], [K("xmid", t, n) for t in range(4) for n in range(2)], [128, 4, D])
        n2A(2); norm_B(1); n2A(3); norm_B(2); norm_B(3)
        par = c % 2
        if c == 0:
            sch.add(POOL, lambda: nc.gpsimd.memset(halo[:, 0, :, :], 0.0), writes=[K("halo", 0)])
        GT = [K("bB", i) for i in range(NKF)]
        ngc = gc + 1
        has_next = ngc < NSEQ * NCH
        for u in range(11):
            wu, ku = wblk(gc, B_UP0 + u)
            if has_next and u in (7, 9):
                ptt = 0 if u == 7 else 1
                x_norm_A(ngc // NCH, ngc % NCH, ptt)
            for e in range(2):
                i = 2 * u + e
                rb = i % 2
                ub = 0 + 2 * rb
                ac = 4 + rb
                gl = 6 + rb
                ubuf = fS[:, ub:ub + 2, :].rearrange("p a b -> p (a b)")
                UBK = [K("fS", ub), K("fS", ub + 1)]
                pu = rP.nxt()
                for kc in range(8):
                    mm(Pf[pu][:, :], wu[:, kc, e * 128:(e + 1) * 128], hT[:, kc, :], kc == 0, kc == 7, HTK + [ku],
                       [K("P", pu)], kc == 7)
                pv2 = rP.nxt()
                for kc in range(8):
                    mm(Pf[pv2][:, :], wu[:, kc, 256 + e * 128:256 + (e + 1) * 128], hT[:, kc, :], kc == 0, kc == 7,
                       HTK + [ku], [K("P", pv2)], kc == 7)
                sch.add(ACT, lambda pu=pu, ubuf=ubuf: nc.scalar.copy(out=ubuf[:, 2:514], in_=Pf[pu][:, :]),
                        reads=[K("P", pu)], writes=UBK)
                sch.add(POOL, lambda ubuf=ubuf, i=i: nc.gpsimd.tensor_copy(out=ubuf[:, 0:2], in_=halo[:, par, i, :]),
                        reads=[K("halo", par)], writes=UBK)
                sch.add(POOL, lambda ubuf=ubuf, i=i: nc.gpsimd.tensor_copy(out=halo[:, 1 - par, i, :],
                                                                           in_=ubuf[:, 512:514]),
                        reads=UBK, writes=[K("halo", 1 - par)])
                sch.add(DVE, lambda ubuf=ubuf, i=i, ac=ac: nc.vector.tensor_scalar(
                    out=fS[:, ac, :], in0=ubuf[:, 2:514], scalar1=cw[:, 2, i:i + 1], scalar2=cb[:, i:i + 1],
                    op0=ALU.mult, op1=ALU.add), reads=UBK + [K("cw"), K("cb")], writes=[K("fS", ac)])
                sch.add(DVE, lambda ubuf=ubuf, i=i, ac=ac: nc.vector.scalar_tensor_tensor(
                    out=fS[:, ac, :], in0=ubuf[:, 1:513], scalar=cw[:, 1, i:i + 1], in1=fS[:, ac, :], op0=ALU.mult,
                    op1=ALU.add), reads=UBK + [K("cw"), K("fS", ac)], writes=[K("fS", ac)])
                sch.add(DVE, lambda ubuf=ubuf, i=i, ac=ac: nc.vector.scalar_tensor_tensor(
                    out=fS[:, ac, :], in0=ubuf[:, 0:512], scalar=cw[:, 0, i:i + 1], in1=fS[:, ac, :], op0=ALU.mult,
                    op1=ALU.add), reads=UBK + [K("cw"), K("fS", ac)], writes=[K("fS", ac)])
                sch.add(ACT, lambda ac=ac, gl=gl: nc.scalar.activation(out=fS[:, gl, :], in_=fS[:, ac, :], func=AF.Gelu),
                        reads=[K("fS", ac)], writes=[K("fS", gl)])
                sch.add(DVE, lambda gl=gl, pv2=pv2, i=i: nc.vector.tensor_tensor(out=bB[:, i, :], in0=fS[:, gl, :],
                                                                                 in1=Pf[pv2][:, :], op=ALU.mult),
                        reads=[K("fS", gl), K("P", pv2)], writes=[K("bB", i)])
        if first:
            dump("gT", bB[:, 0:NKF, :], GT, [128, NKF, 512], BF16)
        if has_next:
            for ptt in (0, 1):
                norm_B(ptt)
                prefetched.add((ngc, ptt))
            for ptt in (2, 3):
                x_norm_A(ngc // NCH, ngc % NCH, ptt)
        for nh in range(2):
            banks = [0, 1, 2, 3] if nh == 0 else [4, 5, 0, 1]
            for pi_, (k0, nk) in enumerate(DN_P):
                wd, kd = wblk(gc, B_DN0 + nh * 3 + pi_)
                for tt in range(4):
                    for kk in range(nk):
                        kc = k0 + kk
                        mm(Pf[banks[tt]][:, :], bB[:, kc, tt * 128:(tt + 1) * 128], wd[:, kk, :], kc == 0,
                           kc == NKF - 1, [K("bB", kc), kd], [K("P", banks[tt])], kk == nk - 1)
            for tt in range(4):
                sch.add(DVE, lambda nh=nh, tt=tt, b=banks[tt]: nc.vector.tensor_tensor(
                    out=xmid[:, tt, nh * 512:(nh + 1) * 512], in0=xmid[:, tt, nh * 512:(nh + 1) * 512],
                    in1=Pf[b][:, :], op=ALU.add),
                    reads=[K("xmid", tt, nh), K("P", banks[tt])], writes=[K("xmid", tt, nh)])
            if nh == 0 and has_next:
                for ptt in (2, 3):
                    norm_B(ptt)
                    prefetched.add((ngc, ptt))
        sch.add(OQ, lambda: nc.gpsimd.dma_start(
            out=out[s, c * CH:(c + 1) * CH, :].rearrange("(t p) d -> p t d", p=128), in_=xmid[:]),
            reads=[K("xmid", t, n) for t in range(4) for n in range(2)])

    for s in range(NSEQ):
        for c in range(NCH):
            chunk(s, c)
    stats = sch.finalize(nc.sync)
    es.close()
    return nc, dumps, stats


_CACHE = {}


def kernel(**inputs):
    nseq = 32 // NCORES
    if "nc" not in _CACHE:
        _CACHE["nc"] = build(nseq)[0]
    nc = _CACHE["nc"]
    consts = host_consts()
    x = np.ascontiguousarray(np.asarray(inputs["x"], dtype=np.float32))
    shared = {
        "norm1_g": np.asarray(inputs["norm1_g"], np.float32).reshape(1, D),
        "norm2_g": np.asarray(inputs["norm2_g"], np.float32).reshape(1, D),
        "w_in": np.asarray(inputs["w_in"], np.float32).reshape(D, 5632),
        "hg_lb_logits": np.asarray(inputs["hg_lb_logits"], np.float32).reshape(2, 512),
        "hg_onorm_g": np.asarray(inputs["hg_onorm_g"], np.float32).reshape(1, 512),
        "q_norm_g": np.asarray(inputs["q_norm_g"], np.float32).reshape(1, 64),
        "k_norm_g": np.asarray(inputs["k_norm_g"], np.float32).reshape(1, 64),
        "w_a": np.asarray(inputs["w_a"], np.float32).reshape(512, D),
        "w_b": np.asarray(inputs["w_b"], np.float32).reshape(512, D),
        "w_out": np.asarray(inputs["w_out"], np.float32).reshape(D, D),
        "w_up": np.asarray(inputs["w_up"], np.float32).reshape(D, 2 * DFF),
        "conv_w": np.asarray(inputs["conv_w"], np.float32).reshape(3, DFF),
        "conv_b": np.asarray(inputs["conv_b"], np.float32).reshape(1, DFF),
        "w_down": np.asarray(inputs["w_down"], np.float32).reshape(DFF, D),
    }
    shared.update(consts)
    in_maps = []
    for i in range(NCORES):
        m = dict(shared)
        m["x"] = x[i * nseq:(i + 1) * nseq]
        in_maps.append(m)
    res = run_bass_kernel_spmd(nc, in_maps, core_ids=list(range(NCORES)))
    return np.concatenate([np.asarray(r["out"]) for r in res.results], axis=0).astype(np.float32)
```

```python
from contextlib import ExitStack
import numpy as np
import concourse.bass as bass
import concourse.mybir as mybir
from concourse.bass_utils import run_bass_kernel_spmd

F32 = mybir.dt.float32
BF16 = mybir.dt.bfloat16
AF = mybir.ActivationFunctionType
ALU = mybir.AluOpType
AX = mybir.AxisListType

NCORES = 8
S = 2048
D = 1024
CH = 512
NCH = S // CH
DFF = 2816
NKF = DFF // 128
EPS = 1e-6
BIG = 30000.0
NBLK = 32
RING = 3

(B_HQ, B_HF, B_HI, B_HG, B_MK, B_MQ, B_MV, B_GAB0, B_WAB0, B_GAB1, B_GAB2, B_WAB1, B_GAB3,
 B_WO0, B_WO1) = range(15)
B_UP0 = 15
B_DN0 = 26


class Q:
    def __init__(self, name, issuer, sem, inc, kind):
        self.name, self.issuer, self.sem, self.inc, self.kind = name, issuer, sem, inc, kind
        self.nsig = 0
        self.last = None


class Sched:
    def __init__(self, nc):
        self.nc = nc
        self.ins = []
        self.queues = []

    def queue(self, name, issuer, sem, inc, kind):
        q = Q(name, issuer, sem, inc, kind)
        self.queues.append(q)
        return q

    def add(self, q, fn, reads=(), writes=(), sig=True):
        self.ins.append((q, fn, tuple(reads), tuple(writes), sig or q.kind == 'dma'))

    def finalize(self, final_issuer):
        ins = self.ins
        n = len(ins)
        sigval = [0] * n
        nextsig = [None] * n
        for i, (q, fn, r, w, sig) in enumerate(ins):
            if sig:
                q.nsig += 1
                sigval[i] = q.nsig
        lastsig = {}
        for i in range(n - 1, -1, -1):
            q = ins[i][0]
            if ins[i][4]:
                lastsig[q] = i
            nextsig[i] = lastsig.get(q)
        writers, readers = {}, {}
        clocks = {}
        iclk = [None] * n
        nwaits = 0
        for i, (q, fn, rds, wrs, sig) in enumerate(ins):
            deps = set()
            for k in rds:
                for qq, j in writers.get(k, {}).items():
                    if qq is q and q.kind == 'pe':
                        continue
                    deps.add(j)
            for k in wrs:
                for qq, j in writers.get(k, {}).items():
                    if qq is q and q.kind != 'dma':
                        continue
                    deps.add(j)
                for qq, j in readers.get(k, {}).items():
                    if qq is q and q.kind != 'dma':
                        continue
                    deps.add(j)
            if q.kind == 'dma' and q.last is not None:
                deps.add(q.last)
            clk = clocks.setdefault(id(q.issuer), {})
            for j in sorted(deps):
                js = nextsig[j]
                assert js is not None and js < i, f"dep signal after waiter: ins {i} dep {j} sig {js}"
                qj = ins[js][0]
                val = sigval[js] * qj.inc
                if clk.get(qj, 0) >= val:
                    continue
                q.issuer.wait_ge(qj.sem, val)
                nwaits += 1
                for qq, v in iclk[js].items():
                    if clk.get(qq, 0) < v:
                        clk[qq] = v
            r = fn()
            if sig:
                r.then_inc(q.sem, q.inc)
                c2 = dict(clk)
                c2[q] = sigval[i] * q.inc
                iclk[i] = c2
            for k in rds:
                readers.setdefault(k, {})[q] = i
            for k in wrs:
                writers.setdefault(k, {})[q] = i
            if q.kind == 'dma':
                q.last = i
        for q in self.queues:
            if q.nsig:
                final_issuer.wait_ge(q.sem, q.nsig * q.inc)
        return n, nwaits


def host_consts():
    c = {}
    c["c_ident"] = np.eye(128, dtype=np.float32)
    s = np.arange(128)[:, None]
    t = np.arange(128)[None, :]
    same = (s // 64) == (t // 64)
    c["c_U"] = ((s > t) & same).astype(np.float32)
    c["c_bd"] = ((s <= t) & same).astype(np.float32)
    c["c_tri"] = (s <= t).astype(np.float32)
    c["c_cind"] = np.stack([(np.arange(128) < 64), (np.arange(128) >= 64)], 1).astype(np.float32)
    half = 32
    inv = 1.0 / (10000.0 ** (np.arange(half, dtype=np.float32) * 2.0 / 64))
    pos = np.arange(S, dtype=np.float32)
    ang = pos[:, None] * inv[None, :]
    cs = np.stack([np.cos(ang), np.sin(ang)], 1).astype(np.float32)
    c["c_cs"] = np.ascontiguousarray(cs.reshape(16, 128, 2, 32).transpose(1, 0, 2, 3))
    kind = (np.arange(S)[None, :] // 256 == np.arange(8)[:, None]).astype(np.float32)
    c["c_kind"] = kind
    return c


def build(NSEQ, dump_names=()):
    nc = bass.Bass("TRN2", target_bir_lowering=False)
    es = ExitStack()

    def din(name, shape, dt=F32):
        return nc.dram_tensor(name, list(shape), dt, kind="ExternalInput").ap()

    x = din("x", [NSEQ, S, D])
    norm1_g = din("norm1_g", [1, D]); norm2_g = din("norm2_g", [1, D])
    w_in = din("w_in", [D, 5632]); hg_lb = din("hg_lb_logits", [2, 512])
    hg_on = din("hg_onorm_g", [1, 512]); qng = din("q_norm_g", [1, 64]); kng = din("k_norm_g", [1, 64])
    w_a = din("w_a", [512, D]); w_b = din("w_b", [512, D]); w_out = din("w_out", [D, D])
    w_up = din("w_up", [D, 2 * DFF]); conv_w = din("conv_w", [3, DFF]); conv_b = din("conv_b", [1, DFF])
    w_down = din("w_down", [DFF, D])
    c_ident = din("c_ident", [128, 128]); c_U = din("c_U", [128, 128]); c_bd = din("c_bd", [128, 128])
    c_tri = din("c_tri", [128, 128]); c_cind = din("c_cind", [128, 2]); c_cs = din("c_cs", [128, 16, 2, 32])
    c_kind = din("c_kind", [8, S])
    out = nc.dram_tensor("out", [NSEQ, S, D], F32, kind="ExternalOutput").ap()
    wbf = nc.dram_tensor("wbf", [NBLK, 128, 4096], BF16).ap()
    dumps = {}

    def sb(name, shape, dt):
        return es.enter_context(nc.sbuf_tensor(name, list(shape), dt))

    def ps(name, shape, dt):
        return es.enter_context(nc.psum_tensor(name, list(shape), dt))

    def sem(name):
        return es.enter_context(nc.semaphore(name))

    sch = Sched(nc)
    PE = sch.queue("pe", nc.tensor, sem("s_pe"), 1, 'pe')
    ACT = sch.queue("act", nc.scalar, sem("s_act"), 1, 'cmp')
    DVE = sch.queue("dve", nc.vector, sem("s_dve"), 1, 'cmp')
    POOL = sch.queue("pool", nc.gpsimd, sem("s_pool"), 1, 'cmp')
    WQ = [sch.queue(f"wq{i}", nc.sync, sem(f"s_wq{i}"), 16, 'dma') for i in range(RING)]
    XQ = [sch.queue(f"xq{i}", nc.sync, sem(f"s_xq{i}"), 16, 'dma') for i in range(2)]
    OQ = sch.queue("oq", nc.gpsimd, sem("s_oq"), 16, 'dma')
    CQ = [sch.queue(f"cq{i}", nc.gpsimd, sem(f"s_cq{i}"), 16, 'dma') for i in range(4)]
    KQ = [sch.queue(f"kq{i}", nc.sync, sem(f"s_kq{i}"), 16, 'dma') for i in range(4)]
    DQ = sch.queue("dq", nc.sync, sem("s_dq"), 16, 'dma')

    wring = sb("wring", [128, RING, 4096], BF16)
    kTe = sb("kTe", [128, 8, S], BF16)
    vext = sb("vext", [128, 16, 8, 65], BF16)
    xmid = sb("xmid", [128, 4, D], F32)
    hT = sb("hT", [128, 8, CH], BF16)
    g1b = sb("g1b", [128, D], F32); g2b = sb("g2b", [128, D], F32)
    fS = sb("fS", [128, 18, 512], F32)
    bB = sb("bB", [128, 24, 512], BF16)
    Sst = sb("Sst", [128, 512], F32)
    qtil = sb("qtil", [128, 2, 512], BF16)
    attn = sb("attn", [128, 2, 512], BF16)
    sdb = sb("sdb", [128, 2, 512], BF16)
    oabf = sb("oabf", [128, 2, 512], BF16)
    oaT = sb("oaT", [128, 4, CH], BF16); obT = sb("obT", [128, 4, CH], BF16)
    ropeo = sb("ropeo", [128, 2, 512], BF16)
    bias = sb("bias", [128, 2, 8, 72], BF16)
    pt = sb("pt", [128, 3, 512], BF16)
    ident = sb("ident", [128, 128], BF16)
    Um = sb("Um", [128, 128], F32); bd = sb("bd", [128, 128], F32); tri = sb("tri", [128, 128], BF16)
    cind = sb("cind", [128, 2], F32)
    omlb = sb("omlb", [128, 512], F32); gob = sb("gob", [128, 512], F32)
    cs = sb("cs", [128, 16, 2, 32], F32)
    gq8 = sb("gq8", [128, 64], F32); gkb = sb("gkb", [128, 64], F32)
    cw = sb("cw", [128, 3, NKF], F32); cb = sb("cb", [128, NKF], F32)
    halo = sb("halo", [128, 2, NKF, 2], F32)
    elast = sb("elast", [128, 4, 4, 2], F32)
    st = sb("st", [128, 128], F32)
    kmT = sb("kmT", [128, 8, 8], BF16)
    kmf = sb("kmf", [128, 8], F32)
    epsb = sb("epsb", [128, 1], F32)
    gsm = sb("gsm", [128, 64], F32)
    cmpb = sb("cmpb", [128, 8 * 7 * 7], F32)
    cnt = sb("cnt", [128, 56], F32)
    rden = sb("rden", [128, 2, 4], F32)
    Pf = [ps(f"P{i}", [128, 512], F32) for i in range(6)]
    Tb = [ps(f"T{i}", [128, 1024], BF16) for i in range(2)]

    K = lambda *a: tuple(a)

    def fslot(i, n=1):
        return fS[:, i, :] if n == 1 else fS[:, i:i + n, :]

    class Rot:
        def __init__(self, n):
            self.n, self.i = n, 0

        def nxt(self):
            v = self.i % self.n
            self.i += 1
            return v

    rP = Rot(4)
    rP6 = Rot(6)
    rP3 = Rot(3)
    rT = Rot(2)
    rxt = Rot(2)
    rpt = Rot(3)

    def dump(name, ap, keys, shape, dt=F32):
        if name not in dump_names or name in dumps:
            return
        t = nc.dram_tensor("dbg_" + name, list(shape), dt, kind="ExternalOutput").ap()
        dumps[name] = t
        sch.add(DQ, lambda: nc.sync.dma_start(out=t, in_=ap), reads=keys)

    cqi = [0]

    def cast(dst, src, b):
        q = CQ[cqi[0] % 4]
        cqi[0] += 1
        sch.add(q, lambda: nc.gpsimd.dma_start(out=dst, in_=src), writes=[K("wbf", b)])

    kqi = [0]

    def kload(dst, src, key, eng=None):
        q = KQ[kqi[0] % 4]
        kqi[0] += 1
        sch.add(q, lambda: nc.sync.dma_start(out=dst, in_=src), writes=[key])

    def kcast(dst, src, key):
        q = CQ[cqi[0] % 4]
        cqi[0] += 1
        sch.add(q, lambda: nc.gpsimd.dma_start(out=dst, in_=src), writes=[key])

    kcast(ident[:], c_ident, K("ident"))
    kcast(tri[:], c_tri, K("tri"))
    kload(Um[:], c_U, K("Um")); kload(bd[:], c_bd, K("bd")); kload(cind[:], c_cind, K("cind"))
    kload(cs[:], c_cs, K("cs"))
    kload(g1b[:], norm1_g.partition_broadcast(128), K("g1b"))
    kload(g2b[:], norm2_g.partition_broadcast(128), K("g2b"))
    kload(gob[:], hg_on.partition_broadcast(128), K("gob"))
    kload(gq8[:], qng.partition_broadcast(128), K("gq8"))
    kload(gkb[:], kng.partition_broadcast(128), K("gkb"))
    kload(fS[:, 0, :], hg_lb[0:1, :].partition_broadcast(128), K("fS", 0))
    kload(fS[:, 1, :], hg_lb[1:2, :].partition_broadcast(128), K("fS", 1))
    for j in range(3):
        sch.add(KQ[j % 4], (lambda j=j: nc.sync.dma_start(
            out=cw[:, j, :], in_=conv_w[j:j + 1, :].rearrange("o (kc p) -> p (o kc)", p=128),
            allow_slow_non_contiguous=True)), writes=[K("cw")])
    sch.add(KQ[3], lambda: nc.sync.dma_start(
        out=cb[:], in_=conv_b.rearrange("o (kc p) -> p (o kc)", p=128), allow_slow_non_contiguous=True),
        writes=[K("cb")])
    for h in range(8):
        kcast(kTe[64:72, h, :], c_kind, K("kTe_ind"))
    sch.add(DVE, lambda: nc.vector.tensor_tensor(out=fS[:, 0, :], in0=fS[:, 0, :], in1=fS[:, 1, :], op=ALU.subtract),
            reads=[K("fS", 0), K("fS", 1)], writes=[K("fS", 0)])
    sch.add(ACT, lambda: nc.scalar.activation(out=omlb[:], in_=fS[:, 0, :], func=AF.Sigmoid, scale=-1.0),
            reads=[K("fS", 0)], writes=[K("omlb")])
    sch.add(ACT, lambda: nc.scalar.mul(out=gq8[:], in_=gq8[:], mul=0.125), reads=[K("gq8")], writes=[K("gq8")])
    sch.add(POOL, lambda: nc.gpsimd.memset(vext[:, :, :, 64:65], 1.0), writes=[K("vext_one")])
    sch.add(POOL, lambda: nc.gpsimd.memset(epsb[:], EPS), writes=[K("epsb")])
    sch.add(POOL, lambda: nc.gpsimd.memset(bias[:, :, :, 0:64], 0.0), writes=[K("bias", 0), K("bias", 1)])

    def blkv(b, kc0, nkc, n0, nn):
        v = wbf[b].rearrange("p (kc n) -> p kc n", n=512)
        return v[:, kc0:kc0 + nkc, n0:n0 + nn]

    def rows(w, r0, nkc, c0, ncol):
        return w[r0:r0 + nkc * 128, c0:c0 + ncol].rearrange("(kc p) n -> p kc n", p=128)

    for g, b in enumerate([B_HQ, B_HF, B_HI, B_HG, B_MQ, B_MK, B_MV]):
        cast(blkv(b, 0, 8, 0, 512), rows(w_in, 0, 8, g * 512, 512), b)
    for qd, b in enumerate([B_GAB0, B_GAB1, B_GAB2, B_GAB3]):
        cast(blkv(b, 0, 8, 0, 256), rows(w_in, 0, 8, 3584 + qd * 256, 256), b)
        cast(blkv(b, 0, 8, 256, 256), rows(w_in, 0, 8, 4608 + qd * 256, 256), b)
    for hf, b in enumerate([B_WAB0, B_WAB1]):
        cast(blkv(b, 0, 4, 0, 512), rows(w_a, 0, 4, hf * 512, 512), b)
        cast(blkv(b, 4, 4, 0, 512), rows(w_b, 0, 4, hf * 512, 512), b)
    for hf, b in enumerate([B_WO0, B_WO1]):
        cast(blkv(b, 0, 8, 0, 512), rows(w_out, 0, 8, hf * 512, 512), b)
    for u in range(11):
        cast(blkv(B_UP0 + u, 0, 8, 0, 256), rows(w_up, 0, 8, u * 256, 256), B_UP0 + u)
        cast(blkv(B_UP0 + u, 0, 8, 256, 256), rows(w_up, 0, 8, DFF + u * 256, 256), B_UP0 + u)
    DN_P = [(0, 8), (8, 8), (16, 6)]
    for nh in range(2):
        for pi, (k0, nk) in enumerate(DN_P):
            cast(blkv(B_DN0 + nh * 3 + pi, 0, nk, 0, 512), rows(w_down, k0 * 128, nk, nh * 512, 512), B_DN0 + nh * 3 + pi)

    wstate = {"next": 0}
    total_blocks = NSEQ * NCH * NBLK

    def wload_upto(gb):
        while wstate["next"] <= gb and wstate["next"] < total_blocks:
            g = wstate["next"]
            slot = g % RING
            b = g % NBLK
            ne = 3072 if b in (B_DN0 + 2, B_DN0 + 5) else 4096
            sch.add(WQ[slot], (lambda slot=slot, b=b, ne=ne: nc.sync.dma_start(out=wring[:, slot, 0:ne],
                                                                              in_=wbf[b][:, 0:ne])),
                    reads=[K("wbf", b)], writes=[K("w", slot)])
            wstate["next"] += 1

    def wblk(gc, b, ahead=2):
        gb = gc * NBLK + b
        wload_upto(gb + ahead)
        slot = gb % RING
        return wring[:, slot, :].rearrange("p (kc n) -> p kc n", n=512), K("w", slot)

    def mm(out_ap, lhsT, rhs, start, stop, reads, writes, sig, **kw):
        sch.add(PE, lambda: nc.tensor.matmul(out_ap, lhsT=lhsT, rhs=rhs, start=start, stop=stop, **kw),
                reads=reads, writes=writes, sig=sig)

    def tp(out_ap, in_ap, reads, writes, sig):
        sch.add(PE, lambda: nc.tensor.transpose(out_ap, in_ap, ident[:]), reads=list(reads) + [K("ident")],
                writes=writes, sig=sig)

    hbuf = sb("hbuf", [128, 2, D], BF16)

    def norm_A(src_ap, src_keys, gb_t, gkey, tt, stc):
        junk = bB[:, 22:24, :].rearrange("p a b -> p (a b)")
        sch.add(ACT, lambda: nc.scalar.activation(out=junk, in_=src_ap, func=AF.Square, scale=1.0 / 32.0,
                                                  accum_out=st[:, stc:stc + 1]),
                reads=src_keys, writes=[K("bB", 22), K("bB", 23), K("st", stc)])
        sch.add(ACT, lambda: nc.scalar.activation(out=st[:, stc + 2:stc + 3], in_=st[:, stc:stc + 1], func=AF.Ln,
                                                  bias=epsb[:, 0:1]),
                reads=[K("st", stc), K("epsb")], writes=[K("st", stc + 2)])
        sch.add(ACT, lambda: nc.scalar.activation(out=st[:, stc + 3:stc + 4], in_=st[:, stc + 2:stc + 3], func=AF.Exp,
                                                  scale=-0.5),
                reads=[K("st", stc + 2)], writes=[K("st", stc + 3)])
        hi_ = tt % 2
        sch.add(DVE, lambda: nc.vector.scalar_tensor_tensor(out=hbuf[:, hi_, :], in0=src_ap,
                                                            scalar=st[:, stc + 3:stc + 4],
                                                            in1=gb_t[:], op0=ALU.mult, op1=ALU.mult),
                reads=list(src_keys) + [K("st", stc + 3), gkey], writes=[K("hbuf", hi_)])

    def norm_B(tt):
        hi_ = tt % 2
        ti = rT.nxt()
        for kc in range(8):
            tp(Tb[ti][:, kc * 128:(kc + 1) * 128], hbuf[:, hi_, kc * 128:(kc + 1) * 128], [K("hbuf", hi_)],
               [K("T", ti)], kc == 7)
        sch.add(DVE, lambda: nc.vector.tensor_copy(out=hT[:, :, tt * 128:(tt + 1) * 128],
                                                   in_=Tb[ti][:, :].rearrange("p (kc t) -> p kc t", t=128)),
                reads=[K("T", ti)], writes=[K("hT", tt)])

    def norm_tile(src_ap, src_keys, gb_t, gkey, tt, stc):
        norm_A(src_ap, src_keys, gb_t, gkey, tt, stc)
        norm_B(tt)

    def x_norm_A(s_, c_, tt):
        xi = rxt.nxt()
        xs = 12 + 2 * xi
        xap = fS[:, xs:xs + 2, :].rearrange("p a b -> p (a b)")
        xk = [K("fS", xs), K("fS", xs + 1)]
        r0 = c_ * CH + tt * 128
        sch.add(XQ[xi], lambda: nc.sync.dma_start(out=xap, in_=x[s_, r0:r0 + 128, :]), writes=xk)
        norm_A(xap, xk, g1b, K("g1b"), tt, 4 * tt)

    prefetched = set()

    HTK = [K("hT", t) for t in range(4)]

    def chunk(s, c):
        gc = s * NCH + c
        first = (gc == 0)
        for tt in range(4):
            if (gc, tt) in prefetched:
                continue
            x_norm_A(s, c, tt)
            norm_B(tt)
        if first:
            dump("hT", hT[:], HTK, [128, 8, CH], BF16)
        wq_, kq_ = wblk(gc, B_HQ)
        wf_, kf_ = wblk(gc, B_HF, 1)
        if c == 0:
            sch.add(DVE, lambda: nc.vector.memset(Sst[:], 0.0), writes=[K("Sst")])
        KQT = [K("bB", i) for i in range(8)]

        def h_stageA(tt):
            r = tt % 2
            qs, ks, ls = 0 + r, 2 + r, 4 + r
            pq = rP6.nxt()
            for kc in range(8):
                mm(Pf[pq][:, :], hT[:, kc, tt * 128:(tt + 1) * 128], wq_[:, kc, :], kc == 0, kc == 7,
                   [K("hT", tt), kq_], [K("P", pq)], kc == 7)
            sch.add(ACT, lambda: nc.scalar.activation(out=fS[:, qs, :], in_=Pf[pq][:, :], func=AF.Sigmoid),
                    reads=[K("P", pq)], writes=[K("fS", qs)])
            sch.add(DVE, lambda: nc.vector.tensor_tensor(out=fS[:, qs, :], in0=fS[:, qs, :], in1=Pf[pq][:, :],
                                                         op=ALU.mult),
                    reads=[K("fS", qs), K("P", pq)], writes=[K("fS", qs)])
            pf = rP6.nxt()
            for kc in range(8):
                mm(Pf[pf][:, :], hT[:, kc, tt * 128:(tt + 1) * 128], wf_[:, kc, :], kc == 0, kc == 7,
                   [K("hT", tt), kf_], [K("P", pf)], kc == 7)
            sch.add(ACT, lambda: nc.scalar.activation(out=fS[:, ks, :], in_=Pf[pf][:, :], func=AF.Sigmoid, scale=-1.0),
                    reads=[K("P", pf)], writes=[K("fS", ks)])
            sch.add(DVE, lambda: nc.vector.tensor_tensor(out=fS[:, ks, :], in0=fS[:, ks, :], in1=omlb[:], op=ALU.mult),
                    reads=[K("fS", ks), K("omlb")], writes=[K("fS", ks)])
            sch.add(ACT, lambda: nc.scalar.activation(out=fS[:, ls, :], in_=fS[:, ks, :], func=AF.Ln, scale=-1.0,
                                                      bias=1.0),
                    reads=[K("fS", ks)], writes=[K("fS", ls)])

        def h_stageB(tt):
            r = tt % 2
            qs, ks, ls, es_, ns = 0 + r, 2 + r, 4 + r, 6 + r, 8 + r
            pa = rP6.nxt()
            mm(Pf[pa][:, :], Um[:], fS[:, ls, :], True, True, [K("Um"), K("fS", ls)], [K("P", pa)], True)
            sch.add(ACT, lambda: nc.scalar.activation(out=fS[:, es_, :], in_=Pf[pa][:, :], func=AF.Exp),
                    reads=[K("P", pa)], writes=[K("fS", es_)])
            sch.add(ACT, lambda: nc.scalar.activation(out=fS[:, ns, :], in_=Pf[pa][:, :], func=AF.Exp, scale=-1.0),
                    reads=[K("P", pa)], writes=[K("fS", ns)])
            pl = rP6.nxt()
            for h in range(4):
                mm(Pf[pl][:, h * 2:h * 2 + 2], fS[:, ls, h * 128:(h + 1) * 128], cind[:], True, True,
                   [K("fS", ls), K("cind")], [K("P", pl)], h == 3)
            sch.add(ACT, lambda: nc.scalar.activation(
                out=elast[:, tt, :, :].rearrange("p h j -> p (h j)"), in_=Pf[pl][:, 0:8], func=AF.Exp),
                reads=[K("P", pl)], writes=[K("elast", tt)])
            sch.add(DVE, lambda: nc.vector.tensor_tensor(out=bB[:, 12 + tt, :], in0=fS[:, ks, :], in1=fS[:, es_, :],
                                                         op=ALU.mult),
                    reads=[K("fS", ks), K("fS", es_)], writes=[K("bB", 12 + tt)])
            sch.add(DVE, lambda: nc.vector.tensor_tensor(out=qtil[:, r, :], in0=fS[:, qs, :], in1=fS[:, ns, :],
                                                         op=ALU.mult),
                    reads=[K("fS", qs), K("fS", ns)], writes=[K("qtil", r)])
            if first and tt == 0:
                dump("khat0", bB[:, 12, :], [K("bB", 12)], [128, 512], BF16)
                dump("qtil0", qtil[:, 0, :], [K("qtil", 0)], [128, 512], BF16)

        def h_stageB2(tt):
            r = tt % 2
            ti = rT.nxt()
            for h in range(4):
                tp(Tb[ti][:, h * 128:(h + 1) * 128], bB[:, 12 + tt, h * 128:(h + 1) * 128], [K("bB", 12 + tt)],
                   [K("T", ti)], False)
            for h in range(4):
                tp(Tb[ti][:, (4 + h) * 128:(5 + h) * 128], qtil[:, r, h * 128:(h + 1) * 128], [K("qtil", r)],
                   [K("T", ti)], h == 3)
            sch.add(ACT, lambda: nc.scalar.copy(out=bB[:, 0:8, tt * 128:(tt + 1) * 128],
                                                in_=Tb[ti][:, :].rearrange("p (a t) -> p a t", t=128)),
                    reads=[K("T", ti)], writes=[K("kqT", tt)] + KQT)

        hi_state = {}

        def h_hi(tt):
            if "w" not in hi_state:
                hi_state["w"] = wblk(gc, B_HI)
            wi_, ki_ = hi_state["w"]
            pv = rP6.nxt()
            for kc in range(8):
                mm(Pf[pv][:, :], hT[:, kc, tt * 128:(tt + 1) * 128], wi_[:, kc, :], kc == 0, kc == 7,
                   [K("hT", tt), ki_], [K("P", pv)], kc == 7)
            sch.add(ACT, lambda: nc.scalar.copy(out=bB[:, 8 + tt, :], in_=Pf[pv][:, :]),
                    reads=[K("P", pv)], writes=[K("bB", 8 + tt)])

        h_stageA(0); h_stageA(1); h_stageB(0); h_stageA(2); h_stageB(1); h_stageB2(0); h_stageA(3); h_stageB(2)
        h_stageB2(1); h_hi(0); h_stageB(3); h_hi(1); h_stageB2(2); h_hi(2); h_hi(3); h_stageB2(3)
        wg_, kg_ = wblk(gc, B_HG)
        for tt in range(4):
            ph = rP6.nxt()
            for kc in range(8):
                mm(Pf[ph][:, :], hT[:, kc, tt * 128:(tt + 1) * 128], wg_[:, kc, :], kc == 0, kc == 7,
                   [K("hT", tt), kg_], [K("P", ph)], kc == 7)
            sch.add(ACT, lambda ph=ph, tt=tt: nc.scalar.activation(out=fS[:, tt, :], in_=Pf[ph][:, :], func=AF.Silu),
                    reads=[K("P", ph)], writes=[K("fS", tt)])
            sch.add(POOL, lambda tt=tt: nc.gpsimd.tensor_tensor(out=fS[:, tt, :], in0=fS[:, tt, :], in1=gob[:],
                                                                op=ALU.mult),
                    reads=[K("fS", tt), K("gob")], writes=[K("fS", tt)])

        def h_post_tr(tt):
            r = tt % 2
            ti = rT.nxt()
            for kc in range(4):
                tp(Tb[ti][:, kc * 128:(kc + 1) * 128], oabf[:, r, kc * 128:(kc + 1) * 128], [K("oabf", r)],
                   [K("T", ti)], kc == 3)
            sch.add(ACT, lambda: nc.scalar.copy(out=oaT[:, :, tt * 128:(tt + 1) * 128],
                                                in_=Tb[ti][:, 0:512].rearrange("p (a t) -> p a t", t=128)),
                    reads=[K("T", ti)], writes=[K("oaT", tt)])

        def h_attn(tt):
            r = tt % 2
            po = 4 + r
            pat = rP3.nxt()
            for h in range(4):
                mm(Pf[pat][:, h * 128:(h + 1) * 128], bB[:, h, tt * 128:(tt + 1) * 128],
                   bB[:, 4 + h, tt * 128:(tt + 1) * 128], True, True, KQT, [K("P", pat)], h == 3)
            sch.add(DVE, lambda: nc.vector.scalar_tensor_tensor(
                out=attn[:, r, :].rearrange("p (h t) -> p h t", h=4),
                in0=Pf[pat][:, :].rearrange("p (h t) -> p h t", h=4), scalar=1e30,
                in1=bd[:].unsqueeze(1).broadcast_to([128, 4, 128]), op0=ALU.min, op1=ALU.mult),
                reads=[K("P", pat), K("bd")], writes=[K("attn", r)])
            for h in range(4):
                mm(Pf[po][:, h * 128:(h + 1) * 128], attn[:, r, h * 128:(h + 1) * 128],
                   bB[:, 8 + tt, h * 128:(h + 1) * 128], h == 0, False, [K("attn", r), K("bB", 8 + tt)],
                   [K("P", po)], False, skip_group_check=True)

        def h_rec(tt):
            r = tt % 2
            po = 4 + r
            for j in range(2):
                n = 2 * tt + j
                sd = n % 2
                sch.add(DVE, lambda j=j: nc.vector.tensor_tensor(
                    out=fS[:, 16, :].rearrange("p (h v) -> p h v", h=4),
                    in0=Sst[:].rearrange("p (h v) -> p h v", h=4),
                    in1=elast[:, tt, :, j:j + 1].broadcast_to([128, 4, 128]), op=ALU.mult),
                    reads=[K("Sst"), K("elast", tt)], writes=[K("fS", 16)])
                sch.add(ACT, lambda sd=sd: nc.scalar.copy(out=sdb[:, sd, :], in_=fS[:, 16, :]),
                        reads=[K("fS", 16)], writes=[K("sdb", sd)])
                for h in range(4):
                    mm(Pf[3][:, h * 128:(h + 1) * 128], bB[j * 64:(j + 1) * 64, 12 + tt, h * 128:(h + 1) * 128],
                       bB[j * 64:(j + 1) * 64, 8 + tt, h * 128:(h + 1) * 128], True, True,
                       [K("bB", 12 + tt), K("bB", 8 + tt)], [K("P", 3)], h == 3)
                for h in range(4):
                    t0 = tt * 128 + j * 64
                    mm(Pf[po][j * 64:(j + 1) * 64, h * 128:(h + 1) * 128], bB[:, 4 + h, t0:t0 + 64],
                       sdb[:, sd, h * 128:(h + 1) * 128], False, (j == 1 and h == 3), KQT + [K("sdb", sd)],
                       [K("P", po)], (j == 1 and h == 3), skip_group_check=True)
                sch.add(DVE, lambda: nc.vector.tensor_tensor(out=Sst[:], in0=fS[:, 16, :], in1=Pf[3][:, :], op=ALU.add),
                        reads=[K("fS", 16), K("P", 3)], writes=[K("Sst")])

        def h_post(tt):
            r = tt % 2
            po = 4 + r
            so = 16 + 16 * r
            for h in range(4):
                sch.add(ACT, lambda h=h: nc.scalar.activation(
                    out=bB[:, 22, h * 128:(h + 1) * 128], in_=Pf[po][:, h * 128:(h + 1) * 128], func=AF.Square,
                    scale=float(1.0 / np.sqrt(128.0)), accum_out=st[:, so + h:so + h + 1]),
                    reads=[K("P", po)], writes=[K("bB", 22), K("st", so + h)])
            sch.add(ACT, lambda: nc.scalar.activation(out=st[:, so + 8:so + 12], in_=st[:, so:so + 4], func=AF.Ln,
                                                      bias=epsb[:, 0:1]),
                    reads=[K("st", so + h) for h in range(4)] + [K("epsb")], writes=[K("st", so + 8)])
            sch.add(ACT, lambda: nc.scalar.activation(out=st[:, so + 12:so + 16], in_=st[:, so + 8:so + 12],
                                                      func=AF.Exp, scale=-0.5),
                    reads=[K("st", so + 8)], writes=[K("st", so + 12)])
            for h in range(4):
                sch.add(DVE, lambda h=h: nc.vector.scalar_tensor_tensor(
                    out=oabf[:, r, h * 128:(h + 1) * 128], in0=Pf[po][:, h * 128:(h + 1) * 128],
                    scalar=st[:, so + 12 + h:so + 13 + h], in1=fS[:, tt, h * 128:(h + 1) * 128], op0=ALU.mult,
                    op1=ALU.mult),
                    reads=[K("P", po), K("st", so + 12), K("fS", tt)], writes=[K("oabf", r)])
            if first and tt == 0:
                dump("oa0", oabf[:, 0, :], [K("oabf", 0)], [128, 512], BF16)

        h_attn(0)
        for tt in range(4):
            h_rec(tt)
            if tt < 3:
                h_attn(tt + 1)
            h_post(tt)
            if tt >= 1:
                h_post_tr(tt - 1)
        QTE = [K("bB", 16 + i) for i in range(8)]
        qTe = bB[:, 16:24, :]
        wmk, kmk = wblk(gc, B_MK)
        wstore = {"k": (wmk, kmk)}

        def m_stageA(kind, tt, idx):
            is_q = kind == "q"
            if is_q and "q" not in wstore:
                wstore["q"] = wblk(gc, B_MQ)
            wv, wk = wstore[kind]
            par = idx % 3
            sq, mn = [4, 5, 10][par], [6, 7, 11][par]
            sc = 64 + 16 * par
            pm = rP.nxt()
            for kc in range(8):
                mm(Pf[pm][:, :], hT[:, kc, tt * 128:(tt + 1) * 128], wv[:, kc, :], kc == 0, kc == 7,
                   [K("hT", tt), wk], [K("P", pm)], kc == 7)
            sch.add(ACT, lambda: nc.scalar.activation(out=fS[:, sq, :], in_=Pf[pm][:, :], func=AF.Square, scale=0.125),
                    reads=[K("P", pm)], writes=[K("fS", sq)])
            sch.add(DVE, lambda: nc.vector.tensor_reduce(out=st[:, sc:sc + 8],
                                                         in_=fS[:, sq, :].rearrange("p (h d) -> p h d", h=8),
                                                         axis=AX.X, op=ALU.add),
                    reads=[K("fS", sq)], writes=[K("st", sc)])
            sch.add(ACT, lambda: nc.scalar.activation(out=st[:, sc + 8:sc + 16], in_=st[:, sc:sc + 8], func=AF.Ln,
                                                      bias=epsb[:, 0:1]),
                    reads=[K("st", sc), K("epsb")], writes=[K("st", sc + 8)])
            sch.add(ACT, lambda: nc.scalar.activation(out=st[:, sc:sc + 8], in_=st[:, sc + 8:sc + 16], func=AF.Exp,
                                                      scale=-0.5),
                    reads=[K("st", sc + 8)], writes=[K("st", sc)])
            sch.add(DVE, lambda: nc.vector.tensor_tensor(
                out=fS[:, mn, :].rearrange("p (h d) -> p h d", h=8),
                in0=Pf[pm][:, :].rearrange("p (h d) -> p h d", h=8),
                in1=st[:, sc:sc + 8].unsqueeze(2).broadcast_to([128, 8, 64]), op=ALU.mult),
                reads=[K("P", pm), K("st", sc)], writes=[K("fS", mn)])
            gvec, gkey = (gq8, K("gq8")) if is_q else (gkb, K("gkb"))
            m3 = fS[:, mn, :].rearrange("p (h d) -> p h d", h=8)
            sch.add(DVE, lambda: nc.vector.tensor_tensor(out=m3, in0=m3,
                                                         in1=gvec[:].unsqueeze(1).broadcast_to([128, 8, 64]),
                                                         op=ALU.mult),
                    reads=[K("fS", mn), gkey], writes=[K("fS", mn)])

        def m_stageB(kind, tt, idx):
            is_q = kind == "q"
            par = idx % 3
            mn, tB = [6, 7, 11][par], [8, 9, 17][par]
            tile_i = c * 4 + tt
            m4 = fS[:, mn, :].rearrange("p (h a d) -> p h a d", h=8, a=2)
            tB4 = fS[:, tB, :].rearrange("p (h a d) -> p h a d", h=8, a=2)
            cosb = cs[:, tile_i, 0, :]
            sinb3 = cs[:, tile_i, 1, :].unsqueeze(1).broadcast_to([128, 8, 32])
            sch.add(POOL, lambda: nc.gpsimd.tensor_tensor(out=tB4[:, :, 0, :], in0=m4[:, :, 1, :], in1=sinb3,
                                                          op=ALU.mult),
                    reads=[K("fS", mn), K("cs")], writes=[K("fS", tB)])
            sch.add(POOL, lambda: nc.gpsimd.tensor_tensor(out=tB4[:, :, 1, :], in0=m4[:, :, 0, :], in1=sinb3,
                                                          op=ALU.mult),
                    reads=[K("fS", mn), K("cs")], writes=[K("fS", tB)])
            sch.add(DVE, lambda: nc.vector.tensor_tensor(
                out=m4, in0=m4, in1=cosb.unsqueeze(1).unsqueeze(1).broadcast_to([128, 8, 2, 32]), op=ALU.mult),
                reads=[K("fS", mn), K("cs"), K("fS", tB)], writes=[K("fS", mn)])
            ro = idx % 2
            ro4 = ropeo[:, ro, :].rearrange("p (h a d) -> p h a d", h=8, a=2)
            sch.add(POOL, lambda: nc.gpsimd.tensor_tensor(out=ro4[:, :, 0, :], in0=m4[:, :, 0, :], in1=tB4[:, :, 0, :],
                                                          op=ALU.subtract),
                    reads=[K("fS", mn), K("fS", tB)], writes=[K("ropeo", ro)])
            sch.add(POOL, lambda: nc.gpsimd.tensor_tensor(out=ro4[:, :, 1, :], in0=m4[:, :, 1, :], in1=tB4[:, :, 1, :],
                                                          op=ALU.add),
                    reads=[K("fS", mn), K("fS", tB)], writes=[K("ropeo", ro)])

        def m_stageB2(kind, tt, idx):
            is_q = kind == "q"
            ro = idx % 2
            tile_i = c * 4 + tt
            ti = rT.nxt()
            for h in range(8):
                tp(Tb[ti][0:64, h * 128:(h + 1) * 128], ropeo[:, ro, h * 64:(h + 1) * 64], [K("ropeo", ro)],
                   [K("T", ti)], h == 7)
            src = Tb[ti][0:64, :].rearrange("p (h t) -> p h t", h=8)
            if is_q:
                sch.add(DVE, lambda: nc.vector.tensor_copy(out=qTe[0:64, :, tt * 128:(tt + 1) * 128], in_=src),
                        reads=[K("T", ti)], writes=QTE + [K("qTe", tt)])
            else:
                p0 = c * CH + tt * 128
                sch.add(DVE, lambda: nc.vector.tensor_copy(out=kTe[0:64, :, p0:p0 + 128], in_=src),
                        reads=[K("T", ti)], writes=[K("kTe", tile_i)])
                if tt % 2 == 1:
                    blk = 2 * c + tt // 2
                    sch.add(DVE, lambda: nc.vector.tensor_reduce(out=kmf[0:64, :],
                                                                 in_=kTe[0:64, :, blk * 256:(blk + 1) * 256],
                                                                 axis=AX.X, op=ALU.add),
                            reads=[K("kTe", 2 * blk), K("kTe", 2 * blk + 1)], writes=[K("kmf")])
                    sch.add(DVE, lambda: nc.vector.tensor_scalar(out=kmT[0:64, :, blk:blk + 1],
                                                                 in0=kmf[0:64, :].unsqueeze(2), scalar1=1.0 / 256.0,
                                                                 scalar2=None, op0=ALU.mult),
                            reads=[K("kmf")], writes=[K("kmT")])

        def m_stageC(tt):
            qb = 2 * c + tt // 2
            bi = tt % 2
            if qb >= 4:
                pg = rP.nxt()
                for h in range(8):
                    mm(Pf[pg][:, h * 8:h * 8 + qb], qTe[0:64, h, tt * 128:(tt + 1) * 128], kmT[0:64, h, 0:qb], True, True,
                       [K("qTe", tt), K("kmT")], [K("P", pg)], h == 7)
                sch.add(ACT, lambda: nc.scalar.copy(
                    out=gsm[:].rearrange("p (h j) -> p h j", h=8)[:, :, 0:qb],
                    in_=Pf[pg][:, 0:64].rearrange("p (h j) -> p h j", h=8)[:, :, 0:qb]),
                        reads=[K("P", pg)], writes=[K("gsm")])
                g3 = gsm[:].rearrange("p (h j) -> p h j", h=8)[:, :, 0:qb]
                c4 = cmpb[:, 0:8 * qb * qb].rearrange("p (h j k) -> p h j k", h=8, j=qb)
                sch.add(DVE, lambda: nc.vector.tensor_tensor(
                    out=c4, in0=g3.unsqueeze(2).broadcast_to([128, 8, qb, qb]),
                    in1=g3.unsqueeze(3).broadcast_to([128, 8, qb, qb]), op=ALU.is_gt),
                    reads=[K("gsm")], writes=[K("cmpb")])
                cn3 = cnt[:, 0:8 * qb].rearrange("p (h j) -> p h j", h=8)
                sch.add(DVE, lambda: nc.vector.tensor_reduce(out=cn3, in_=c4, axis=AX.X, op=ALU.add),
                        reads=[K("cmpb")], writes=[K("cnt")])
                sch.add(DVE, lambda: nc.vector.tensor_scalar(
                    out=bias[:, bi, :, 64:64 + qb], in0=cn3, scalar1=2.5, scalar2=-BIG, op0=ALU.is_gt, op1=ALU.mult),
                    reads=[K("cnt")], writes=[K("bias", bi)])
                sch.add(POOL, lambda: nc.gpsimd.memset(bias[:, bi, :, 64 + qb:65 + qb], 0.0), writes=[K("bias", bi)])
            else:
                sch.add(POOL, lambda: nc.gpsimd.memset(bias[:, bi, :, 64:65 + qb], 0.0), writes=[K("bias", bi)])
            if qb < 7:
                sch.add(POOL, lambda: nc.gpsimd.memset(bias[:, bi, :, 65 + qb:72], -BIG), writes=[K("bias", bi)])

        def m_stageC2(tt):
            bi = tt % 2
            for hh in range(2):
                pb = rP.nxt()
                for h4 in range(4):
                    h = hh * 4 + h4
                    mm(Pf[pb][0:72, h4 * 128:(h4 + 1) * 128], bias[:, bi, h, :], ident[:], True, True,
                       [K("bias", bi), K("ident")], [K("P", pb)], h4 == 3)
                sch.add(ACT, lambda pb=pb, hh=hh: nc.scalar.copy(
                    out=qTe[64:72, hh * 4:hh * 4 + 4, tt * 128:(tt + 1) * 128],
                    in_=Pf[pb][64:72, :].rearrange("p (h t) -> p h t", h=4)),
                    reads=[K("P", pb)], writes=QTE + [K("qTeb", tt)])

        def m_v(tt):
            if "v" not in wstore:
                wstore["v"] = wblk(gc, B_MV)
            wmv, kmv = wstore["v"]
            tile_i = c * 4 + tt
            pv = rP.nxt()
            for kc in range(8):
                mm(Pf[pv][:, :], hT[:, kc, tt * 128:(tt + 1) * 128], wmv[:, kc, :], kc == 0, kc == 7,
                   [K("hT", tt), kmv], [K("P", pv)], kc == 7)
            sch.add(ACT, lambda: nc.scalar.copy(
                out=vext[:, tile_i, :, 0:64], in_=Pf[pv][:, :].rearrange("p (h d) -> p h d", h=8)),
                reads=[K("P", pv)], writes=[K("vext", tile_i)])

        items2 = [("k", t) for t in range(4)] + [("q", t) for t in range(4)]
        m_stageA("k", 0, 0)
        h_post_tr(3)
        m_stageA("k", 1, 1)
        m_stageB("k", 0, 0)
        for i in range(2, 8):
            m_stageB(items2[i - 1][0], items2[i - 1][1], i - 1)
            m_stageA(items2[i][0], items2[i][1], i)
            m_stageB2(items2[i - 2][0], items2[i - 2][1], i - 2)
        m_stageB("q", 3, 7)
        m_v(0)
        m_stageB2("q", 2, 6)
        m_stageC(0)
        m_v(1)
        m_stageB2("q", 3, 7)
        m_stageC(1)
        m_stageC2(0)
        m_v(2)
        m_stageC2(1)
        m_stageC(2)
        m_v(3)
        m_stageC(3)
        m_stageC2(2)
        m_stageC2(3)
        if first:
            dump("kTe", kTe[0:72, :, 0:512], [K("kTe", i) for i in range(4)] + [K("kTe_ind")], [72, 8, 512], BF16)
            dump("qTe", qTe[0:72, :, :], QTE, [72, 8, 512], BF16)
        nkt = 4 * c + 4
        OB = [K("bB", 8 + i) for i in range(4)]
        obv = bB[:, 8:12, :].rearrange("p t (h d) -> p t h d", h=8)
        items = [(h, kt) for h in range(8) for kt in range(nkt)]
        pend = None

        def att_qk(h, kt):
            n0 = max(0, kt * 128 - c * CH)
            pst = rP.nxt()
            pi = rpt.nxt()
            mm(Pf[pst][:, n0:512], kTe[0:72, h, kt * 128:(kt + 1) * 128], qTe[0:72, h, n0:512], True, True,
               [K("kTe", kt), K("kTe_ind")] + QTE, [K("P", pst)], True)
            sch.add(ACT, lambda pst=pst, pi=pi, n0=n0: nc.scalar.activation(out=pt[:, pi, n0:512],
                                                                          in_=Pf[pst][:, n0:512], func=AF.Exp),
                    reads=[K("P", pst)], writes=[K("pt", pi)])
            if kt * 128 >= c * CH:
                sch.add(DVE, lambda pi=pi, n0=n0: nc.vector.tensor_tensor(out=pt[:, pi, n0:n0 + 128],
                                                                          in0=pt[:, pi, n0:n0 + 128], in1=tri[:],
                                                                          op=ALU.mult),
                        reads=[K("pt", pi), K("tri")], writes=[K("pt", pi)])
            return (h, kt, n0, pi)

        def att_pv(h, kt, n0, pi):
            pob = 4 + (h % 2)
            for sub in range(n0 // 128, 4):
                last = (kt == nkt - 1 and sub == 3)
                mm(Pf[pob][:, sub * 65:(sub + 1) * 65], pt[:, pi, sub * 128:(sub + 1) * 128], vext[:, kt, h, :],
                   (kt == 0 and sub == 0), last, [K("pt", pi), K("vext", kt), K("vext_one")], [K("P", pob)],
                   sub == 3, skip_group_check=True)
            if kt == nkt - 1:
                rd = h % 2
                po3 = Pf[pob][:, 0:260].rearrange("p (s e) -> p s e", e=65)
                sch.add(DVE, lambda po3=po3, rd=rd: nc.vector.reciprocal(out=rden[:, rd, :].unsqueeze(2),
                                                                         in_=po3[:, :, 64:65]),
                        reads=[K("P", pob)], writes=[K("rden", rd)])
                sch.add(DVE, lambda po3=po3, rd=rd, h=h: nc.vector.tensor_tensor(
                    out=obv[:, :, h, :], in0=po3[:, :, 0:64],
                    in1=rden[:, rd, :].unsqueeze(2).broadcast_to([128, 4, 64]), op=ALU.mult),
                    reads=[K("P", pob), K("rden", rd)], writes=OB)

        pendq = []
        for (h, kt) in items:
            pendq.append(att_qk(h, kt))
            if len(pendq) > 2:
                att_pv(*pendq.pop(0))
        while pendq:
            att_pv(*pendq.pop(0))
        if first:
            dump("ob", bB[:, 8:12, :], OB, [128, 4, 512], BF16)
        for tt in range(4):
            ti = rT.nxt()
            for kc in range(4):
                tp(Tb[ti][:, kc * 128:(kc + 1) * 128], bB[:, 8 + tt, kc * 128:(kc + 1) * 128], [K("bB", 8 + tt)],
                   [K("T", ti)], kc == 3)
            sch.add(ACT, lambda ti=ti, tt=tt: nc.scalar.copy(out=obT[:, :, tt * 128:(tt + 1) * 128],
                                                             in_=Tb[ti][:, 0:512].rearrange("p (a t) -> p a t", t=128)),
                    reads=[K("T", ti)], writes=[K("obT", tt)])
        OAT = [K("oaT", t) for t in range(4)]
        OBT = [K("obT", t) for t in range(4)]
        MIX = [K("bB", i) for i in range(8)]
        gab_ids = [B_GAB0, B_GAB1, B_GAB2, B_GAB3]
        for qd in range(4):
            wg2, kg2 = wblk(gc, gab_ids[qd], 1)
            wab, kab = wblk(gc, B_WAB0 if qd < 2 else B_WAB1, 1)
            for e in range(2):
                i = 2 * qd + e
                col = (i % 4) * 128
                res = []
                for br in range(2):
                    pgt = rP.nxt()
                    for kc in range(8):
                        mm(Pf[pgt][:, :], wg2[:, kc, br * 256 + e * 128: br * 256 + (e + 1) * 128], hT[:, kc, :],
                           kc == 0, kc == 7, HTK + [kg2], [K("P", pgt)], kc == 7)
                    pab = rP.nxt()
                    src = oaT if br == 0 else obT
                    srk = OAT if br == 0 else OBT
                    for kc in range(4):
                        mm(Pf[pab][:, :], wab[:, br * 4 + kc, col:col + 128], src[:, kc, :], kc == 0, kc == 3,
                           srk + [kab], [K("P", pab)], kc == 3)
                    ss_, ms_ = 0 + br, 2 + br
                    sch.add(ACT, lambda pgt=pgt, ss_=ss_: nc.scalar.activation(out=fS[:, ss_, :], in_=Pf[pgt][:, :],
                                                                             func=AF.Sigmoid),
                            reads=[K("P", pgt)], writes=[K("fS", ss_)])
                    sch.add(DVE, lambda pab=pab, ss_=ss_, ms_=ms_: nc.vector.tensor_tensor(
                        out=fS[:, ms_, :], in0=fS[:, ss_, :], in1=Pf[pab][:, :], op=ALU.mult),
                        reads=[K("fS", ss_), K("P", pab)], writes=[K("fS", ms_)])
                sch.add(POOL, lambda i=i: nc.gpsimd.tensor_tensor(out=bB[:, i, :], in0=fS[:, 2, :], in1=fS[:, 3, :],
                                                                  op=ALU.add),
                        reads=[K("fS", 2), K("fS", 3)], writes=[K("bB", i)])
        if first:
            dump("mixT", bB[:, 0:8, :], MIX, [128, 8, 512], BF16)
        wo0, ko0 = wblk(gc, B_WO0)
        wo1, ko1 = wblk(gc, B_WO1, 1)

        def w_out_tile(tt):
            xi = rxt.nxt()
            xs = 12 + 2 * xi
            xap = fS[:, xs:xs + 2, :].rearrange("p a b -> p (a b)")
            xk = [K("fS", xs), K("fS", xs + 1)]
            r0 = c * CH + tt * 128
            sch.add(XQ[xi], lambda: nc.sync.dma_start(out=xap, in_=x[s, r0:r0 + 128, :]), writes=xk)
            for nh, (wo, ko) in enumerate([(wo0, ko0), (wo1, ko1)]):
                pw = rP.nxt()
                for kc in range(8):
                    mm(Pf[pw][:, :], bB[:, kc, tt * 128:(tt + 1) * 128], wo[:, kc, :], kc == 0, kc == 7,
                       MIX + [ko], [K("P", pw)], kc == 7)
                sch.add(DVE, lambda pw=pw, nh=nh: nc.vector.tensor_tensor(
                    out=xmid[:, tt, nh * 512:(nh + 1) * 512], in0=xap[:, nh * 512:(nh + 1) * 512], in1=Pf[pw][:, :],
                    op=ALU.add), reads=xk + [K("P", pw)], writes=[K("xmid", tt, nh)])

        def n2A(tt):
            norm_A(xmid[:, tt, :], [K("xmid", tt, 0), K("xmid", tt, 1)], g2b, K("g2b"), tt, 4 * tt)

        w_out_tile(0); w_out_tile(1); n2A(0); w_out_tile(2); n2A(1); norm_B(0); w_out_tile(3)
        if first:
            dump("xmid", xmid[:], [K("xmid", t, n) for t in range(4) for n in range(2)], [128, 4, D])
        n2A(2); norm_B(1); n2A(3); norm_B(2); norm_B(3)
        par = c % 2
        if c == 0:
            sch.add(POOL, lambda: nc.gpsimd.memset(halo[:, 0, :, :], 0.0), writes=[K("halo", 0)])
        GT = [K("bB", i) for i in range(NKF)]
        ngc = gc + 1
        has_next = ngc < NSEQ * NCH
        for u in range(11):
            wu, ku = wblk(gc, B_UP0 + u)
            if has_next and u in (7, 9):
                ptt = 0 if u == 7 else 1
                x_norm_A(ngc // NCH, ngc % NCH, ptt)
            for e in range(2):
                i = 2 * u + e
                rb = i % 2
                ub = 0 + 2 * rb
                ac = 4 + rb
                gl = 6 + rb
                ubuf = fS[:, ub:ub + 2, :].rearrange("p a b -> p (a b)")
                UBK = [K("fS", ub), K("fS", ub + 1)]
                pu = rP.nxt()
                for kc in range(8):
                    mm(Pf[pu][:, :], wu[:, kc, e * 128:(e + 1) * 128], hT[:, kc, :], kc == 0, kc == 7, HTK + [ku],
                       [K("P", pu)], kc == 7)
                pv2 = rP.nxt()
                for kc in range(8):
                    mm(Pf[pv2][:, :], wu[:, kc, 256 + e * 128:256 + (e + 1) * 128], hT[:, kc, :], kc == 0, kc == 7,
                       HTK + [ku], [K("P", pv2)], kc == 7)
                sch.add(ACT, lambda pu=pu, ubuf=ubuf: nc.scalar.copy(out=ubuf[:, 2:514], in_=Pf[pu][:, :]),
                        reads=[K("P", pu)], writes=UBK)
                sch.add(POOL, lambda ubuf=ubuf, i=i: nc.gpsimd.tensor_copy(out=ubuf[:, 0:2], in_=halo[:, par, i, :]),
                        reads=[K("halo", par)], writes=UBK)
                sch.add(POOL, lambda ubuf=ubuf, i=i: nc.gpsimd.tensor_copy(out=halo[:, 1 - par, i, :],
                                                                           in_=ubuf[:, 512:514]),
                        reads=UBK, writes=[K("halo", 1 - par)])
                sch.add(DVE, lambda ubuf=ubuf, i=i, ac=ac: nc.vector.tensor_scalar(
                    out=fS[:, ac, :], in0=ubuf[:, 2:514], scalar1=cw[:, 2, i:i + 1], scalar2=cb[:, i:i + 1],
                    op0=ALU.mult, op1=ALU.add), reads=UBK + [K("cw"), K("cb")], writes=[K("fS", ac)])
                sch.add(DVE, lambda ubuf=ubuf, i=i, ac=ac: nc.vector.scalar_tensor_tensor(
                    out=fS[:, ac, :], in0=ubuf[:, 1:513], scalar=cw[:, 1, i:i + 1], in1=fS[:, ac, :], op0=ALU.mult,
                    op1=ALU.add), reads=UBK + [K("cw"), K("fS", ac)], writes=[K("fS", ac)])
                sch.add(DVE, lambda ubuf=ubuf, i=i, ac=ac: nc.vector.scalar_tensor_tensor(
                    out=fS[:, ac, :], in0=ubuf[:, 0:512], scalar=cw[:, 0, i:i + 1], in1=fS[:, ac, :], op0=ALU.mult,
                    op1=ALU.add), reads=UBK + [K("cw"), K("fS", ac)], writes=[K("fS", ac)])
                sch.add(ACT, lambda ac=ac, gl=gl: nc.scalar.activation(out=fS[:, gl, :], in_=fS[:, ac, :], func=AF.Gelu),
                        reads=[K("fS", ac)], writes=[K("fS", gl)])
                sch.add(DVE, lambda gl=gl, pv2=pv2, i=i: nc.vector.tensor_tensor(out=bB[:, i, :], in0=fS[:, gl, :],
                                                                                 in1=Pf[pv2][:, :], op=ALU.mult),
                        reads=[K("fS", gl), K("P", pv2)], writes=[K("bB", i)])
        if first:
            dump("gT", bB[:, 0:NKF, :], GT, [128, NKF, 512], BF16)
        if has_next:
            for ptt in (0, 1):
                norm_B(ptt)
                prefetched.add((ngc, ptt))
            for ptt in (2, 3):
                x_norm_A(ngc // NCH, ngc % NCH, ptt)
        for nh in range(2):
            banks = [0, 1, 2, 3] if nh == 0 else [4, 5, 0, 1]
            for pi_, (k0, nk) in enumerate(DN_P):
                wd, kd = wblk(gc, B_DN0 + nh * 3 + pi_)
                for tt in range(4):
                    for kk in range(nk):
                        kc = k0 + kk
                        mm(Pf[banks[tt]][:, :], bB[:, kc, tt * 128:(tt + 1) * 128], wd[:, kk, :], kc == 0,
                           kc == NKF - 1, [K("bB", kc), kd], [K("P", banks[tt])], kk == nk - 1)
            for tt in range(4):
                sch.add(DVE, lambda nh=nh, tt=tt, b=banks[tt]: nc.vector.tensor_tensor(
                    out=xmid[:, tt, nh * 512:(nh + 1) * 512], in0=xmid[:, tt, nh * 512:(nh + 1) * 512],
                    in1=Pf[b][:, :], op=ALU.add),
                    reads=[K("xmid", tt, nh), K("P", banks[tt])], writes=[K("xmid", tt, nh)])
            if nh == 0 and has_next:
                for ptt in (2, 3):
                    norm_B(ptt)
                    prefetched.add((ngc, ptt))
        sch.add(OQ, lambda: nc.gpsimd.dma_start(
            out=out[s, c * CH:(c + 1) * CH, :].rearrange("(t p) d -> p t d", p=128), in_=xmid[:]),
            reads=[K("xmid", t, n) for t in range(4) for n in range(2)])

    for s in range(NSEQ):
        for c in range(NCH):
            chunk(s, c)
    stats = sch.finalize(nc.sync)
    es.close()
    return nc, dumps, stats


_CACHE = {}


def kernel(**inputs):
    nseq = 32 // NCORES
    if "nc" not in _CACHE:
        _CACHE["nc"] = build(nseq)[0]
    nc = _CACHE["nc"]
    consts = host_consts()
    x = np.ascontiguousarray(np.asarray(inputs["x"], dtype=np.float32))
    shared = {
        "norm1_g": np.asarray(inputs["norm1_g"], np.float32).reshape(1, D),
        "norm2_g": np.asarray(inputs["norm2_g"], np.float32).reshape(1, D),
        "w_in": np.asarray(inputs["w_in"], np.float32).reshape(D, 5632),
        "hg_lb_logits": np.asarray(inputs["hg_lb_logits"], np.float32).reshape(2, 512),
        "hg_onorm_g": np.asarray(inputs["hg_onorm_g"], np.float32).reshape(1, 512),
        "q_norm_g": np.asarray(inputs["q_norm_g"], np.float32).reshape(1, 64),
        "k_norm_g": np.asarray(inputs["k_norm_g"], np.float32).reshape(1, 64),
        "w_a": np.asarray(inputs["w_a"], np.float32).reshape(512, D),
        "w_b": np.asarray(inputs["w_b"], np.float32).reshape(512, D),
        "w_out": np.asarray(inputs["w_out"], np.float32).reshape(D, D),
        "w_up": np.asarray(inputs["w_up"], np.float32).reshape(D, 2 * DFF),
        "conv_w": np.asarray(inputs["conv_w"], np.float32).reshape(3, DFF),
        "conv_b": np.asarray(inputs["conv_b"], np.float32).reshape(1, DFF),
        "w_down": np.asarray(inputs["w_down"], np.float32).reshape(DFF, D),
    }
    shared.update(consts)
    in_maps = []
    for i in range(NCORES):
        m = dict(shared)
        m["x"] = x[i * nseq:(i + 1) * nseq]
        in_maps.append(m)
    res = run_bass_kernel_spmd(nc, in_maps, core_ids=list(range(NCORES)))
    return np.concatenate([np.asarray(r["out"]) for r in res.results], axis=0).astype(np.float32)
```

```python
from contextlib import ExitStack
import numpy as np
import concourse.bass as bass
import concourse.mybir as mybir
from concourse.bass_utils import run_bass_kernel_spmd

F32 = mybir.dt.float32
BF16 = mybir.dt.bfloat16
AF = mybir.ActivationFunctionType
ALU = mybir.AluOpType
AX = mybir.AxisListType

NCORES = 8
S = 2048
D = 1024
CH = 512
NCH = S // CH
DFF = 2816
NKF = DFF // 128
EPS = 1e-6
BIG = 30000.0
NBLK = 32
RING = 3

(B_HQ, B_HF, B_HI, B_HG, B_MK, B_MQ, B_MV, B_GAB0, B_WAB0, B_GAB1, B_GAB2, B_WAB1, B_GAB3,
 B_WO0, B_WO1) = range(15)
B_UP0 = 15
B_DN0 = 26


class Q:
    def __init__(self, name, issuer, sem, inc, kind):
        self.name, self.issuer, self.sem, self.inc, self.kind = name, issuer, sem, inc, kind
        self.nsig = 0
        self.last = None


class Sched:
    def __init__(self, nc):
        self.nc = nc
        self.ins = []
        self.queues = []

    def queue(self, name, issuer, sem, inc, kind):
        q = Q(name, issuer, sem, inc, kind)
        self.queues.append(q)
        return q

    def add(self, q, fn, reads=(), writes=(), sig=True):
        self.ins.append((q, fn, tuple(reads), tuple(writes), sig or q.kind == 'dma'))

    def finalize(self, final_issuer):
        ins = self.ins
        n = len(ins)
        sigval = [0] * n
        nextsig = [None] * n
        for i, (q, fn, r, w, sig) in enumerate(ins):
            if sig:
                q.nsig += 1
                sigval[i] = q.nsig
        lastsig = {}
        for i in range(n - 1, -1, -1):
            q = ins[i][0]
            if ins[i][4]:
                lastsig[q] = i
            nextsig[i] = lastsig.get(q)
        writers, readers = {}, {}
        clocks = {}
        iclk = [None] * n
        nwaits = 0
        for i, (q, fn, rds, wrs, sig) in enumerate(ins):
            deps = set()
            for k in rds:
                for qq, j in writers.get(k, {}).items():
                    if qq is q and q.kind == 'pe':
                        continue
                    deps.add(j)
            for k in wrs:
                for qq, j in writers.get(k, {}).items():
                    if qq is q and q.kind != 'dma':
                        continue
                    deps.add(j)
                for qq, j in readers.get(k, {}).items():
                    if qq is q and q.kind != 'dma':
                        continue
                    deps.add(j)
            if q.kind == 'dma' and q.last is not None:
                deps.add(q.last)
            clk = clocks.setdefault(id(q.issuer), {})
            for j in sorted(deps):
                js = nextsig[j]
                assert js is not None and js < i, f"dep signal after waiter: ins {i} dep {j} sig {js}"
                qj = ins[js][0]
                val = sigval[js] * qj.inc
                if clk.get(qj, 0) >= val:
                    continue
                q.issuer.wait_ge(qj.sem, val)
                nwaits += 1
                for qq, v in iclk[js].items():
                    if clk.get(qq, 0) < v:
                        clk[qq] = v
            r = fn()
            if sig:
                r.then_inc(q.sem, q.inc)
                c2 = dict(clk)
                c2[q] = sigval[i] * q.inc
                iclk[i] = c2
            for k in rds:
                readers.setdefault(k, {})[q] = i
            for k in wrs:
                writers.setdefault(k, {})[q] = i
            if q.kind == 'dma':
                q.last = i
        for q in self.queues:
            if q.nsig:
                final_issuer.wait_ge(q.sem, q.nsig * q.inc)
        return n, nwaits


def host_consts():
    c = {}
    c["c_ident"] = np.eye(128, dtype=np.float32)
    s = np.arange(128)[:, None]
    t = np.arange(128)[None, :]
    same = (s // 64) == (t // 64)
    c["c_U"] = ((s > t) & same).astype(np.float32)
    c["c_bd"] = ((s <= t) & same).astype(np.float32)
    c["c_tri"] = (s <= t).astype(np.float32)
    c["c_cind"] = np.stack([(np.arange(128) < 64), (np.arange(128) >= 64)], 1).astype(np.float32)
    half = 32
    inv = 1.0 / (10000.0 ** (np.arange(half, dtype=np.float32) * 2.0 / 64))
    pos = np.arange(S, dtype=np.float32)
    ang = pos[:, None] * inv[None, :]
    cs = np.stack([np.cos(ang), np.sin(ang)], 1).astype(np.float32)
    c["c_cs"] = np.ascontiguousarray(cs.reshape(16, 128, 2, 32).transpose(1, 0, 2, 3))
    kind = (np.arange(S)[None, :] // 256 == np.arange(8)[:, None]).astype(np.float32)
    c["c_kind"] = kind
    return c


def build(NSEQ, dump_names=()):
    nc = bass.Bass("TRN2", target_bir_lowering=False)
    es = ExitStack()

    def din(name, shape, dt=F32):
        return nc.dram_tensor(name, list(shape), dt, kind="ExternalInput").ap()

    x = din("x", [NSEQ, S, D])
    norm1_g = din("norm1_g", [1, D]); norm2_g = din("norm2_g", [1, D])
    w_in = din("w_in", [D, 5632]); hg_lb = din("hg_lb_logits", [2, 512])
    hg_on = din("hg_onorm_g", [1, 512]); qng = din("q_norm_g", [1, 64]); kng = din("k_norm_g", [1, 64])
    w_a = din("w_a", [512, D]); w_b = din("w_b", [512, D]); w_out = din("w_out", [D, D])
    w_up = din("w_up", [D, 2 * DFF]); conv_w = din("conv_w", [3, DFF]); conv_b = din("conv_b", [1, DFF])
    w_down = din("w_down", [DFF, D])
    c_ident = din("c_ident", [128, 128]); c_U = din("c_U", [128, 128]); c_bd = din("c_bd", [128, 128])
    c_tri = din("c_tri", [128, 128]); c_cind = din("c_cind", [128, 2]); c_cs = din("c_cs", [128, 16, 2, 32])
    c_kind = din("c_kind", [8, S])
    out = nc.dram_tensor("out", [NSEQ, S, D], F32, kind="ExternalOutput").ap()
    wbf = nc.dram_tensor("wbf", [NBLK, 128, 4096], BF16).ap()
    dumps = {}

    def sb(name, shape, dt):
        return es.enter_context(nc.sbuf_tensor(name, list(shape), dt))

    def ps(name, shape, dt):
        return es.enter_context(nc.psum_tensor(name, list(shape), dt))

    def sem(name):
        return es.enter_context(nc.semaphore(name))

    sch = Sched(nc)
    PE = sch.queue("pe", nc.tensor, sem("s_pe"), 1, 'pe')
    ACT = sch.queue("act", nc.scalar, sem("s_act"), 1, 'cmp')
    DVE = sch.queue("dve", nc.vector, sem("s_dve"), 1, 'cmp')
    POOL = sch.queue("pool", nc.gpsimd, sem("s_pool"), 1, 'cmp')
    WQ = [sch.queue(f"wq{i}", nc.sync, sem(f"s_wq{i}"), 16, 'dma') for i in range(RING)]
    XQ = [sch.queue(f"xq{i}", nc.sync, sem(f"s_xq{i}"), 16, 'dma') for i in range(2)]
    OQ = sch.queue("oq", nc.gpsimd, sem("s_oq"), 16, 'dma')
    CQ = [sch.queue(f"cq{i}", nc.gpsimd, sem(f"s_cq{i}"), 16, 'dma') for i in range(4)]
    KQ = [sch.queue(f"kq{i}", nc.sync, sem(f"s_kq{i}"), 16, 'dma') for i in range(4)]
    DQ = sch.queue("dq", nc.sync, sem("s_dq"), 16, 'dma')

    wring = sb("wring", [128, RING, 4096], BF16)
    kTe = sb("kTe", [128, 8, S], BF16)
    vext = sb("vext", [128, 16, 8, 65], BF16)
    xmid = sb("xmid", [128, 4, D], F32)
    hT = sb("hT", [128, 8, CH], BF16)
    g1b = sb("g1b", [128, D], F32); g2b = sb("g2b", [128, D], F32)
    fS = sb("fS", [128, 18, 512], F32)
    bB = sb("bB", [128, 24, 512], BF16)
    Sst = sb("Sst", [128, 512], F32)
    qtil = sb("qtil", [128, 2, 512], BF16)
    attn = sb("attn", [128, 2, 512], BF16)
    sdb = sb("sdb", [128, 2, 512], BF16)
    oabf = sb("oabf", [128, 2, 512], BF16)
    oaT = sb("oaT", [128, 4, CH], BF16); obT = sb("obT", [128, 4, CH], BF16)
    ropeo = sb("ropeo", [128, 2, 512], BF16)
    bias = sb("bias", [128, 2, 8, 72], BF16)
    pt = sb("pt", [128, 3, 512], BF16)
    ident = sb("ident", [128, 128], BF16)
    Um = sb("Um", [128, 128], F32); bd = sb("bd", [128, 128], F32); tri = sb("tri", [128, 128], BF16)
    cind = sb("cind", [128, 2], F32)
    omlb = sb("omlb", [128, 512], F32); gob = sb("gob", [128, 512], F32)
    cs = sb("cs", [128, 16, 2, 32], F32)
    gq8 = sb("gq8", [128, 64], F32); gkb = sb("gkb", [128, 64], F32)
    cw = sb("cw", [128, 3, NKF], F32); cb = sb("cb", [128, NKF], F32)
    halo = sb("halo", [128, 2, NKF, 2], F32)
    elast = sb("elast", [128, 4, 4, 2], F32)
    st = sb("st", [128, 128], F32)
    kmT = sb("kmT", [128, 8, 8], BF16)
    kmf = sb("kmf", [128, 8], F32)
    epsb = sb("epsb", [128, 1], F32)
    gsm = sb("gsm", [128, 64], F32)
    cmpb = sb("cmpb", [128, 8 * 7 * 7], F32)
    cnt = sb("cnt", [128, 56], F32)
    rden = sb("rden", [128, 2, 4], F32)
    Pf = [ps(f"P{i}", [128, 512], F32) for i in range(6)]
    Tb = [ps(f"T{i}", [128, 1024], BF16) for i in range(2)]

    K = lambda *a: tuple(a)

    def fslot(i, n=1):
        return fS[:, i, :] if n == 1 else fS[:, i:i + n, :]

    class Rot:
        def __init__(self, n):
            self.n, self.i = n, 0

        def nxt(self):
            v = self.i % self.n
            self.i += 1
            return v

    rP = Rot(4)
    rP6 = Rot(6)
    rP3 = Rot(3)
    rT = Rot(2)
    rxt = Rot(2)
    rpt = Rot(3)

    def dump(name, ap, keys, shape, dt=F32):
        if name not in dump_names or name in dumps:
            return
        t = nc.dram_tensor("dbg_" + name, list(shape), dt, kind="ExternalOutput").ap()
        dumps[name] = t
        sch.add(DQ, lambda: nc.sync.dma_start(out=t, in_=ap), reads=keys)

    cqi = [0]

    def cast(dst, src, b):
        q = CQ[cqi[0] % 4]
        cqi[0] += 1
        sch.add(q, lambda: nc.gpsimd.dma_start(out=dst, in_=src), writes=[K("wbf", b)])

    kqi = [0]

    def kload(dst, src, key, eng=None):
        q = KQ[kqi[0] % 4]
        kqi[0] += 1
        sch.add(q, lambda: nc.sync.dma_start(out=dst, in_=src), writes=[key])

    def kcast(dst, src, key):
        q = CQ[cqi[0] % 4]
        cqi[0] += 1
        sch.add(q, lambda: nc.gpsimd.dma_start(out=dst, in_=src), writes=[key])

    kcast(ident[:], c_ident, K("ident"))
    kcast(tri[:], c_tri, K("tri"))
    kload(Um[:], c_U, K("Um")); kload(bd[:], c_bd, K("bd")); kload(cind[:], c_cind, K("cind"))
    kload(cs[:], c_cs, K("cs"))
    kload(g1b[:], norm1_g.partition_broadcast(128), K("g1b"))
    kload(g2b[:], norm2_g.partition_broadcast(128), K("g2b"))
    kload(gob[:], hg_on.partition_broadcast(128), K("gob"))
    kload(gq8[:], qng.partition_broadcast(128), K("gq8"))
    kload(gkb[:], kng.partition_broadcast(128), K("gkb"))
    kload(fS[:, 0, :], hg_lb[0:1, :].partition_broadcast(128), K("fS", 0))
    kload(fS[:, 1, :], hg_lb[1:2, :].partition_broadcast(128), K("fS", 1))
    for j in range(3):
        sch.add(KQ[j % 4], (lambda j=j: nc.sync.dma_start(
            out=cw[:, j, :], in_=conv_w[j:j + 1, :].rearrange("o (kc p) -> p (o kc)", p=128),
            allow_slow_non_contiguous=True)), writes=[K("cw")])
    sch.add(KQ[3], lambda: nc.sync.dma_start(
        out=cb[:], in_=conv_b.rearrange("o (kc p) -> p (o kc)", p=128), allow_slow_non_contiguous=True),
        writes=[K("cb")])
    for h in range(8):
        kcast(kTe[64:72, h, :], c_kind, K("kTe_ind"))
    sch.add(DVE, lambda: nc.vector.tensor_tensor(out=fS[:, 0, :], in0=fS[:, 0, :], in1=fS[:, 1, :], op=ALU.subtract),
            reads=[K("fS", 0), K("fS", 1)], writes=[K("fS", 0)])
    sch.add(ACT, lambda: nc.scalar.activation(out=omlb[:], in_=fS[:, 0, :], func=AF.Sigmoid, scale=-1.0),
            reads=[K("fS", 0)], writes=[K("omlb")])
    sch.add(ACT, lambda: nc.scalar.mul(out=gq8[:], in_=gq8[:], mul=0.125), reads=[K("gq8")], writes=[K("gq8")])
    sch.add(POOL, lambda: nc.gpsimd.memset(vext[:, :, :, 64:65], 1.0), writes=[K("vext_one")])
    sch.add(POOL, lambda: nc.gpsimd.memset(epsb[:], EPS), writes=[K("epsb")])
    sch.add(POOL, lambda: nc.gpsimd.memset(bias[:, :, :, 0:64], 0.0), writes=[K("bias", 0), K("bias", 1)])

    def blkv(b, kc0, nkc, n0, nn):
        v = wbf[b].rearrange("p (kc n) -> p kc n", n=512)
        return v[:, kc0:kc0 + nkc, n0:n0 + nn]

    def rows(w, r0, nkc, c0, ncol):
        return w[r0:r0 + nkc * 128, c0:c0 + ncol].rearrange("(kc p) n -> p kc n", p=128)

    for g, b in enumerate([B_HQ, B_HF, B_HI, B_HG, B_MQ, B_MK, B_MV]):
        cast(blkv(b, 0, 8, 0, 512), rows(w_in, 0, 8, g * 512, 512), b)
    for qd, b in enumerate([B_GAB0, B_GAB1, B_GAB2, B_GAB3]):
        cast(blkv(b, 0, 8, 0, 256), rows(w_in, 0, 8, 3584 + qd * 256, 256), b)
        cast(blkv(b, 0, 8, 256, 256), rows(w_in, 0, 8, 4608 + qd * 256, 256), b)
    for hf, b in enumerate([B_WAB0, B_WAB1]):
        cast(blkv(b, 0, 4, 0, 512), rows(w_a, 0, 4, hf * 512, 512), b)
        cast(blkv(b, 4, 4, 0, 512), rows(w_b, 0, 4, hf * 512, 512), b)
    for hf, b in enumerate([B_WO0, B_WO1]):
        cast(blkv(b, 0, 8, 0, 512), rows(w_out, 0, 8, hf * 512, 512), b)
    for u in range(11):
        cast(blkv(B_UP0 + u, 0, 8, 0, 256), rows(w_up, 0, 8, u * 256, 256), B_UP0 + u)
        cast(blkv(B_UP0 + u, 0, 8, 256, 256), rows(w_up, 0, 8, DFF + u * 256, 256), B_UP0 + u)
    DN_P = [(0, 8), (8, 8), (16, 6)]
    for nh in range(2):
        for pi, (k0, nk) in enumerate(DN_P):
            cast(blkv(B_DN0 + nh * 3 + pi, 0, nk, 0, 512), rows(w_down, k0 * 128, nk, nh * 512, 512), B_DN0 + nh * 3 + pi)

    wstate = {"next": 0}
    total_blocks = NSEQ * NCH * NBLK

    def wload_upto(gb):
        while wstate["next"] <= gb and wstate["next"] < total_blocks:
            g = wstate["next"]
            slot = g % RING
            b = g % NBLK
            ne = 3072 if b in (B_DN0 + 2, B_DN0 + 5) else 4096
            sch.add(WQ[slot], (lambda slot=slot, b=b, ne=ne: nc.sync.dma_start(out=wring[:, slot, 0:ne],
                                                                              in_=wbf[b][:, 0:ne])),
                    reads=[K("wbf", b)], writes=[K("w", slot)])
            wstate["next"] += 1

    def wblk(gc, b, ahead=2):
        gb = gc * NBLK + b
        wload_upto(gb + ahead)
        slot = gb % RING
        return wring[:, slot, :].rearrange("p (kc n) -> p kc n", n=512), K("w", slot)

    def mm(out_ap, lhsT, rhs, start, stop, reads, writes, sig, **kw):
        sch.add(PE, lambda: nc.tensor.matmul(out_ap, lhsT=lhsT, rhs=rhs, start=start, stop=stop, **kw),
                reads=reads, writes=writes, sig=sig)

    def tp(out_ap, in_ap, reads, writes, sig):
        sch.add(PE, lambda: nc.tensor.transpose(out_ap, in_ap, ident[:]), reads=list(reads) + [K("ident")],
                writes=writes, sig=sig)

    hbuf = sb("hbuf", [128, 2, D], BF16)

    def norm_A(src_ap, src_keys, gb_t, gkey, tt, stc):
        junk = bB[:, 22:24, :].rearrange("p a b -> p (a b)")
        sch.add(ACT, lambda: nc.scalar.activation(out=junk, in_=src_ap, func=AF.Square, scale=1.0 / 32.0,
                                                  accum_out=st[:, stc:stc + 1]),
                reads=src_keys, writes=[K("bB", 22), K("bB", 23), K("st", stc)])
        sch.add(ACT, lambda: nc.scalar.activation(out=st[:, stc + 2:stc + 3], in_=st[:, stc:stc + 1], func=AF.Ln,
                                                  bias=epsb[:, 0:1]),
                reads=[K("st", stc), K("epsb")], writes=[K("st", stc + 2)])
        sch.add(ACT, lambda: nc.scalar.activation(out=st[:, stc + 3:stc + 4], in_=st[:, stc + 2:stc + 3], func=AF.Exp,
                                                  scale=-0.5),
                reads=[K("st", stc + 2)], writes=[K("st", stc + 3)])
        hi_ = tt % 2
        sch.add(DVE, lambda: nc.vector.scalar_tensor_tensor(out=hbuf[:, hi_, :], in0=src_ap,
                                                            scalar=st[:, stc + 3:stc + 4],
                                                            in1=gb_t[:], op0=ALU.mult, op1=ALU.mult),
                reads=list(src_keys) + [K("st", stc + 3), gkey], writes=[K("hbuf", hi_)])

    def norm_B(tt):
        hi_ = tt % 2
        ti = rT.nxt()
        for kc in range(8):
            tp(Tb[ti][:, kc * 128:(kc + 1) * 128], hbuf[:, hi_, kc * 128:(kc + 1) * 128], [K("hbuf", hi_)],
               [K("T", ti)], kc == 7)
        sch.add(DVE, lambda: nc.vector.tensor_copy(out=hT[:, :, tt * 128:(tt + 1) * 128],
                                                   in_=Tb[ti][:, :].rearrange("p (kc t) -> p kc t", t=128)),
                reads=[K("T", ti)], writes=[K("hT", tt)])

    def norm_tile(src_ap, src_keys, gb_t, gkey, tt, stc):
        norm_A(src_ap, src_keys, gb_t, gkey, tt, stc)
        norm_B(tt)

    def x_norm_A(s_, c_, tt):
        xi = rxt.nxt()
        xs = 12 + 2 * xi
        xap = fS[:, xs:xs + 2, :].rearrange("p a b -> p (a b)")
        xk = [K("fS", xs), K("fS", xs + 1)]
        r0 = c_ * CH + tt * 128
        sch.add(XQ[xi], lambda: nc.sync.dma_start(out=xap, in_=x[s_, r0:r0 + 128, :]), writes=xk)
        norm_A(xap, xk, g1b, K("g1b"), tt, 4 * tt)

    prefetched = set()

    HTK = [K("hT", t) for t in range(4)]

    def chunk(s, c):
        gc = s * NCH + c
        first = (gc == 0)
        for tt in range(4):
            if (gc, tt) in prefetched:
                continue
            x_norm_A(s, c, tt)
            norm_B(tt)
        if first:
            dump("hT", hT[:], HTK, [128, 8, CH], BF16)
        wq_, kq_ = wblk(gc, B_HQ)
        wf_, kf_ = wblk(gc, B_HF, 1)
        if c == 0:
            sch.add(DVE, lambda: nc.vector.memset(Sst[:], 0.0), writes=[K("Sst")])
        KQT = [K("bB", i) for i in range(8)]

        def h_stageA(tt):
            r = tt % 2
            qs, ks, ls = 0 + r, 2 + r, 4 + r
            pq = rP6.nxt()
            for kc in range(8):
                mm(Pf[pq][:, :], hT[:, kc, tt * 128:(tt + 1) * 128], wq_[:, kc, :], kc == 0, kc == 7,
                   [K("hT", tt), kq_], [K("P", pq)], kc == 7)
            sch.add(ACT, lambda: nc.scalar.activation(out=fS[:, qs, :], in_=Pf[pq][:, :], func=AF.Sigmoid),
                    reads=[K("P", pq)], writes=[K("fS", qs)])
            sch.add(DVE, lambda: nc.vector.tensor_tensor(out=fS[:, qs, :], in0=fS[:, qs, :], in1=Pf[pq][:, :],
                                                         op=ALU.mult),
                    reads=[K("fS", qs), K("P", pq)], writes=[K("fS", qs)])
            pf = rP6.nxt()
            for kc in range(8):
                mm(Pf[pf][:, :], hT[:, kc, tt * 128:(tt + 1) * 128], wf_[:, kc, :], kc == 0, kc == 7,
                   [K("hT", tt), kf_], [K("P", pf)], kc == 7)
            sch.add(ACT, lambda: nc.scalar.activation(out=fS[:, ks, :], in_=Pf[pf][:, :], func=AF.Sigmoid, scale=-1.0),
                    reads=[K("P", pf)], writes=[K("fS", ks)])
            sch.add(DVE, lambda: nc.vector.tensor_tensor(out=fS[:, ks, :], in0=fS[:, ks, :], in1=omlb[:], op=ALU.mult),
                    reads=[K("fS", ks), K("omlb")], writes=[K("fS", ks)])
            sch.add(ACT, lambda: nc.scalar.activation(out=fS[:, ls, :], in_=fS[:, ks, :], func=AF.Ln, scale=-1.0,
                                                      bias=1.0),
                    reads=[K("fS", ks)], writes=[K("fS", ls)])

        def h_stageB(tt):
            r = tt % 2
            qs, ks, ls, es_, ns = 0 + r, 2 + r, 4 + r, 6 + r, 8 + r
            pa = rP6.nxt()
            mm(Pf[pa][:, :], Um[:], fS[:, ls, :], True, True, [K("Um"), K("fS", ls)], [K("P", pa)], True)
            sch.add(ACT, lambda: nc.scalar.activation(out=fS[:, es_, :], in_=Pf[pa][:, :], func=AF.Exp),
                    reads=[K("P", pa)], writes=[K("fS", es_)])
            sch.add(ACT, lambda: nc.scalar.activation(out=fS[:, ns, :], in_=Pf[pa][:, :], func=AF.Exp, scale=-1.0),
                    reads=[K("P", pa)], writes=[K("fS", ns)])
            pl = rP6.nxt()
            for h in range(4):
                mm(Pf[pl][:, h * 2:h * 2 + 2], fS[:, ls, h * 128:(h + 1) * 128], cind[:], True, True,
                   [K("fS", ls), K("cind")], [K("P", pl)], h == 3)
            sch.add(ACT, lambda: nc.scalar.activation(
                out=elast[:, tt, :, :].rearrange("p h j -> p (h j)"), in_=Pf[pl][:, 0:8], func=AF.Exp),
                reads=[K("P", pl)], writes=[K("elast", tt)])
            sch.add(DVE, lambda: nc.vector.tensor_tensor(out=bB[:, 12 + tt, :], in0=fS[:, ks, :], in1=fS[:, es_, :],
                                                         op=ALU.mult),
                    reads=[K("fS", ks), K("fS", es_)], writes=[K("bB", 12 + tt)])
            sch.add(DVE, lambda: nc.vector.tensor_tensor(out=qtil[:, r, :], in0=fS[:, qs, :], in1=fS[:, ns, :],
                                                         op=ALU.mult),
                    reads=[K("fS", qs), K("fS", ns)], writes=[K("qtil", r)])
            if first and tt == 0:
                dump("khat0", bB[:, 12, :], [K("bB", 12)], [128, 512], BF16)
                dump("qtil0", qtil[:, 0, :], [K("qtil", 0)], [128, 512], BF16)

        def h_stageB2(tt):
            r = tt % 2
            ti = rT.nxt()
            for h in range(4):
                tp(Tb[ti][:, h * 128:(h + 1) * 128], bB[:, 12 + tt, h * 128:(h + 1) * 128], [K("bB", 12 + tt)],
                   [K("T", ti)], False)
            for h in range(4):
                tp(Tb[ti][:, (4 + h) * 128:(5 + h) * 128], qtil[:, r, h * 128:(h + 1) * 128], [K("qtil", r)],
                   [K("T", ti)], h == 3)
            sch.add(ACT, lambda: nc.scalar.copy(out=bB[:, 0:8, tt * 128:(tt + 1) * 128],
                                                in_=Tb[ti][:, :].rearrange("p (a t) -> p a t", t=128)),
                    reads=[K("T", ti)], writes=[K("kqT", tt)] + KQT)

        hi_state = {}

        def h_hi(tt):
            if "w" not in hi_state:
                hi_state["w"] = wblk(gc, B_HI)
            wi_, ki_ = hi_state["w"]
            pv = rP6.nxt()
            for kc in range(8):
                mm(Pf[pv][:, :], hT[:, kc, tt * 128:(tt + 1) * 128], wi_[:, kc, :], kc == 0, kc == 7,
                   [K("hT", tt), ki_], [K("P", pv)], kc == 7)
            sch.add(ACT, lambda: nc.scalar.copy(out=bB[:, 8 + tt, :], in_=Pf[pv][:, :]),
                    reads=[K("P", pv)], writes=[K("bB", 8 + tt)])

        h_stageA(0); h_stageA(1); h_stageB(0); h_stageA(2); h_stageB(1); h_stageB2(0); h_stageA(3); h_stageB(2)
        h_stageB2(1); h_hi(0); h_stageB(3); h_hi(1); h_stageB2(2); h_hi(2); h_hi(3); h_stageB2(3)
        wg_, kg_ = wblk(gc, B_HG)
        for tt in range(4):
            ph = rP6.nxt()
            for kc in range(8):
                mm(Pf[ph][:, :], hT[:, kc, tt * 128:(tt + 1) * 128], wg_[:, kc, :], kc == 0, kc == 7,
                   [K("hT", tt), kg_], [K("P", ph)], kc == 7)
            sch.add(ACT, lambda ph=ph, tt=tt: nc.scalar.activation(out=fS[:, tt, :], in_=Pf[ph][:, :], func=AF.Silu),
                    reads=[K("P", ph)], writes=[K("fS", tt)])
            sch.add(POOL, lambda tt=tt: nc.gpsimd.tensor_tensor(out=fS[:, tt, :], in0=fS[:, tt, :], in1=gob[:],
                                                                op=ALU.mult),
                    reads=[K("fS", tt), K("gob")], writes=[K("fS", tt)])

        def h_post_tr(tt):
            r = tt % 2
            ti = rT.nxt()
            for kc in range(4):
                tp(Tb[ti][:, kc * 128:(kc + 1) * 128], oabf[:, r, kc * 128:(kc + 1) * 128], [K("oabf", r)],
                   [K("T", ti)], kc == 3)
            sch.add(ACT, lambda: nc.scalar.copy(out=oaT[:, :, tt * 128:(tt + 1) * 128],
                                                in_=Tb[ti][:, 0:512].rearrange("p (a t) -> p a t", t=128)),
                    reads=[K("T", ti)], writes=[K("oaT", tt)])

        def h_attn(tt):
            r = tt % 2
            po = 4 + r
            pat = rP3.nxt()
            for h in range(4):
                mm(Pf[pat][:, h * 128:(h + 1) * 128], bB[:, h, tt * 128:(tt + 1) * 128],
                   bB[:, 4 + h, tt * 128:(tt + 1) * 128], True, True, KQT, [K("P", pat)], h == 3)
            sch.add(DVE, lambda: nc.vector.scalar_tensor_tensor(
                out=attn[:, r, :].rearrange("p (h t) -> p h t", h=4),
                in0=Pf[pat][:, :].rearrange("p (h t) -> p h t", h=4), scalar=1e30,
                in1=bd[:].unsqueeze(1).broadcast_to([128, 4, 128]), op0=ALU.min, op1=ALU.mult),
                reads=[K("P", pat), K("bd")], writes=[K("attn", r)])
            for h in range(4):
                mm(Pf[po][:, h * 128:(h + 1) * 128], attn[:, r, h * 128:(h + 1) * 128],
                   bB[:, 8 + tt, h * 128:(h + 1) * 128], h == 0, False, [K("attn", r), K("bB", 8 + tt)],
                   [K("P", po)], False, skip_group_check=True)

        def h_rec(tt):
            r = tt % 2
            po = 4 + r
            for j in range(2):
                n = 2 * tt + j
                sd = n % 2
                sch.add(DVE, lambda j=j: nc.vector.tensor_tensor(
                    out=fS[:, 16, :].rearrange("p (h v) -> p h v", h=4),
                    in0=Sst[:].rearrange("p (h v) -> p h v", h=4),
                    in1=elast[:, tt, :, j:j + 1].broadcast_to([128, 4, 128]), op=ALU.mult),
                    reads=[K("Sst"), K("elast", tt)], writes=[K("fS", 16)])
                sch.add(ACT, lambda sd=sd: nc.scalar.copy(out=sdb[:, sd, :], in_=fS[:, 16, :]),
                        reads=[K("fS", 16)], writes=[K("sdb", sd)])
                for h in range(4):
                    mm(Pf[3][:, h * 128:(h + 1) * 128], bB[j * 64:(j + 1) * 64, 12 + tt, h * 128:(h + 1) * 128],
                       bB[j * 64:(j + 1) * 64, 8 + tt, h * 128:(h + 1) * 128], True, True,
                       [K("bB", 12 + tt), K("bB", 8 + tt)], [K("P", 3)], h == 3)
                for h in range(4):
                    t0 = tt * 128 + j * 64
                    mm(Pf[po][j * 64:(j + 1) * 64, h * 128:(h + 1) * 128], bB[:, 4 + h, t0:t0 + 64],
                       sdb[:, sd, h * 128:(h + 1) * 128], False, (j == 1 and h == 3), KQT + [K("sdb", sd)],
                       [K("P", po)], (j == 1 and h == 3), skip_group_check=True)
                sch.add(DVE, lambda: nc.vector.tensor_tensor(out=Sst[:], in0=fS[:, 16, :], in1=Pf[3][:, :], op=ALU.add),
                        reads=[K("fS", 16), K("P", 3)], writes=[K("Sst")])

        def h_post(tt):
            r = tt % 2
            po = 4 + r
            so = 16 + 16 * r
            for h in range(4):
                sch.add(ACT, lambda h=h: nc.scalar.activation(
                    out=bB[:, 22, h * 128:(h + 1) * 128], in_=Pf[po][:, h * 128:(h + 1) * 128], func=AF.Square,
                    scale=float(1.0 / np.sqrt(128.0)), accum_out=st[:, so + h:so + h + 1]),
                    reads=[K("P", po)], writes=[K("bB", 22), K("st", so + h)])
            sch.add(ACT, lambda: nc.scalar.activation(out=st[:, so + 8:so + 12], in_=st[:, so:so + 4], func=AF.Ln,
                                                      bias=epsb[:, 0:1]),
                    reads=[K("st", so + h) for h in range(4)] + [K("epsb")], writes=[K("st", so + 8)])
            sch.add(ACT, lambda: nc.scalar.activation(out=st[:, so + 12:so + 16], in_=st[:, so + 8:so + 12],
                                                      func=AF.Exp, scale=-0.5),
                    reads=[K("st", so + 8)], writes=[K("st", so + 12)])
            for h in range(4):
                sch.add(DVE, lambda h=h: nc.vector.scalar_tensor_tensor(
                    out=oabf[:, r, h * 128:(h + 1) * 128], in0=Pf[po][:, h * 128:(h + 1) * 128],
                    scalar=st[:, so + 12 + h:so + 13 + h], in1=fS[:, tt, h * 128:(h + 1) * 128], op0=ALU.mult,
                    op1=ALU.mult),
                    reads=[K("P", po), K("st", so + 12), K("fS", tt)], writes=[K("oabf", r)])
            if first and tt == 0:
                dump("oa0", oabf[:, 0, :], [K("oabf", 0)], [128, 512], BF16)

        h_attn(0)
        for tt in range(4):
            h_rec(tt)
            if tt < 3:
                h_attn(tt + 1)
            h_post(tt)
            if tt >= 1:
                h_post_tr(tt - 1)
        QTE = [K("bB", 16 + i) for i in range(8)]
        qTe = bB[:, 16:24, :]
        wmk, kmk = wblk(gc, B_MK)
        wstore = {"k": (wmk, kmk)}

        def m_stageA(kind, tt, idx):
            is_q = kind == "q"
            if is_q and "q" not in wstore:
                wstore["q"] = wblk(gc, B_MQ)
            wv, wk = wstore[kind]
            par = idx % 3
            sq, mn = [4, 5, 10][par], [6, 7, 11][par]
            sc = 64 + 16 * par
            pm = rP.nxt()
            for kc in range(8):
                mm(Pf[pm][:, :], hT[:, kc, tt * 128:(tt + 1) * 128], wv[:, kc, :], kc == 0, kc == 7,
                   [K("hT", tt), wk], [K("P", pm)], kc == 7)
            sch.add(ACT, lambda: nc.scalar.activation(out=fS[:, sq, :], in_=Pf[pm][:, :], func=AF.Square, scale=0.125),
                    reads=[K("P", pm)], writes=[K("fS", sq)])
            sch.add(DVE, lambda: nc.vector.tensor_reduce(out=st[:, sc:sc + 8],
                                                         in_=fS[:, sq, :].rearrange("p (h d) -> p h d", h=8),
                                                         axis=AX.X, op=ALU.add),
                    reads=[K("fS", sq)], writes=[K("st", sc)])
            sch.add(ACT, lambda: nc.scalar.activation(out=st[:, sc + 8:sc + 16], in_=st[:, sc:sc + 8], func=AF.Ln,
                                                      bias=epsb[:, 0:1]),
                    reads=[K("st", sc), K("epsb")], writes=[K("st", sc + 8)])
            sch.add(ACT, lambda: nc.scalar.activation(out=st[:, sc:sc + 8], in_=st[:, sc + 8:sc + 16], func=AF.Exp,
                                                      scale=-0.5),
                    reads=[K("st", sc + 8)], writes=[K("st", sc)])
            sch.add(DVE, lambda: nc.vector.tensor_tensor(
                out=fS[:, mn, :].rearrange("p (h d) -> p h d", h=8),
                in0=Pf[pm][:, :].rearrange("p (h d) -> p h d", h=8),
                in1=st[:, sc:sc + 8].unsqueeze(2).broadcast_to([128, 8, 64]), op=ALU.mult),
                reads=[K("P", pm), K("st", sc)], writes=[K("fS", mn)])
            gvec, gkey = (gq8, K("gq8")) if is_q else (gkb, K("gkb"))
            m3 = fS[:, mn, :].rearrange("p (h d) -> p h d", h=8)
            sch.add(DVE, lambda: nc.vector.tensor_tensor(out=m3, in0=m3,
                                                         in1=gvec[:].unsqueeze(1).broadcast_to([128, 8, 64]),
                                                         op=ALU.mult),
                    reads=[K("fS", mn), gkey], writes=[K("fS", mn)])

        def m_stageB(kind, tt, idx):
            is_q = kind == "q"
            par = idx % 3
            mn, tB = [6, 7, 11][par], [8, 9, 17][par]
            tile_i = c * 4 + tt
            m4 = fS[:, mn, :].rearrange("p (h a d) -> p h a d", h=8, a=2)
            tB4 = fS[:, tB, :].rearrange("p (h a d) -> p h a d", h=8, a=2)
            cosb = cs[:, tile_i, 0, :]
            sinb3 = cs[:, tile_i, 1, :].unsqueeze(1).broadcast_to([128, 8, 32])
            sch.add(POOL, lambda: nc.gpsimd.tensor_tensor(out=tB4[:, :, 0, :], in0=m4[:, :, 1, :], in1=sinb3,
                                                          op=ALU.mult),
                    reads=[K("fS", mn), K("cs")], writes=[K("fS", tB)])
            sch.add(POOL, lambda: nc.gpsimd.tensor_tensor(out=tB4[:, :, 1, :], in0=m4[:, :, 0, :], in1=sinb3,
                                                          op=ALU.mult),
                    reads=[K("fS", mn), K("cs")], writes=[K("fS", tB)])
            sch.add(DVE, lambda: nc.vector.tensor_tensor(
                out=m4, in0=m4, in1=cosb.unsqueeze(1).unsqueeze(1).broadcast_to([128, 8, 2, 32]), op=ALU.mult),
                reads=[K("fS", mn), K("cs"), K("fS", tB)], writes=[K("fS", mn)])
            ro = idx % 2
            ro4 = ropeo[:, ro, :].rearrange("p (h a d) -> p h a d", h=8, a=2)
            sch.add(POOL, lambda: nc.gpsimd.tensor_tensor(out=ro4[:, :, 0, :], in0=m4[:, :, 0, :], in1=tB4[:, :, 0, :],
                                                          op=ALU.subtract),
                    reads=[K("fS", mn), K("fS", tB)], writes=[K("ropeo", ro)])
            sch.add(POOL, lambda: nc.gpsimd.tensor_tensor(out=ro4[:, :, 1, :], in0=m4[:, :, 1, :], in1=tB4[:, :, 1, :],
                                                          op=ALU.add),
                    reads=[K("fS", mn), K("fS", tB)], writes=[K("ropeo", ro)])

        def m_stageB2(kind, tt, idx):
            is_q = kind == "q"
            ro = idx % 2
            tile_i = c * 4 + tt
            ti = rT.nxt()
            for h in range(8):
                tp(Tb[ti][0:64, h * 128:(h + 1) * 128], ropeo[:, ro, h * 64:(h + 1) * 64], [K("ropeo", ro)],
                   [K("T", ti)], h == 7)
            src = Tb[ti][0:64, :].rearrange("p (h t) -> p h t", h=8)
            if is_q:
                sch.add(DVE, lambda: nc.vector.tensor_copy(out=qTe[0:64, :, tt * 128:(tt + 1) * 128], in_=src),
                        reads=[K("T", ti)], writes=QTE + [K("qTe", tt)])
            else:
                p0 = c * CH + tt * 128
                sch.add(DVE, lambda: nc.vector.tensor_copy(out=kTe[0:64, :, p0:p0 + 128], in_=src),
                        reads=[K("T", ti)], writes=[K("kTe", tile_i)])
                if tt % 2 == 1:
                    blk = 2 * c + tt // 2
                    sch.add(DVE, lambda: nc.vector.tensor_reduce(out=kmf[0:64, :],
                                                                 in_=kTe[0:64, :, blk * 256:(blk + 1) * 256],
                                                                 axis=AX.X, op=ALU.add),
                            reads=[K("kTe", 2 * blk), K("kTe", 2 * blk + 1)], writes=[K("kmf")])
                    sch.add(DVE, lambda: nc.vector.tensor_scalar(out=kmT[0:64, :, blk:blk + 1],
                                                                 in0=kmf[0:64, :].unsqueeze(2), scalar1=1.0 / 256.0,
                                                                 scalar2=None, op0=ALU.mult),
                            reads=[K("kmf")], writes=[K("kmT")])

        def m_stageC(tt):
            qb = 2 * c + tt // 2
            bi = tt % 2
            if qb >= 4:
                pg = rP.nxt()
                for h in range(8):
                    mm(Pf[pg][:, h * 8:h * 8 + qb], qTe[0:64, h, tt * 128:(tt + 1) * 128], kmT[0:64, h, 0:qb], True, True,
                       [K("qTe", tt), K("kmT")], [K("P", pg)], h == 7)
                sch.add(ACT, lambda: nc.scalar.copy(
                    out=gsm[:].rearrange("p (h j) -> p h j", h=8)[:, :, 0:qb],
                    in_=Pf[pg][:, 0:64].rearrange("p (h j) -> p h j", h=8)[:, :, 0:qb]),
                        reads=[K("P", pg)], writes=[K("gsm")])
                g3 = gsm[:].rearrange("p (h j) -> p h j", h=8)[:, :, 0:qb]
                c4 = cmpb[:, 0:8 * qb * qb].rearrange("p (h j k) -> p h j k", h=8, j=qb)
                sch.add(DVE, lambda: nc.vector.tensor_tensor(
                    out=c4, in0=g3.unsqueeze(2).broadcast_to([128, 8, qb, qb]),
                    in1=g3.unsqueeze(3).broadcast_to([128, 8, qb, qb]), op=ALU.is_gt),
                    reads=[K("gsm")], writes=[K("cmpb")])
                cn3 = cnt[:, 0:8 * qb].rearrange("p (h j) -> p h j", h=8)
                sch.add(DVE, lambda: nc.vector.tensor_reduce(out=cn3, in_=c4, axis=AX.X, op=ALU.add),
                        reads=[K("cmpb")], writes=[K("cnt")])
                sch.add(DVE, lambda: nc.vector.tensor_scalar(
                    out=bias[:, bi, :, 64:64 + qb], in0=cn3, scalar1=2.5, scalar2=-BIG, op0=ALU.is_gt, op1=ALU.mult),
                    reads=[K("cnt")], writes=[K("bias", bi)])
                sch.add(POOL, lambda: nc.gpsimd.memset(bias[:, bi, :, 64 + qb:65 + qb], 0.0), writes=[K("bias", bi)])
            else:
                sch.add(POOL, lambda: nc.gpsimd.memset(bias[:, bi, :, 64:65 + qb], 0.0), writes=[K("bias", bi)])
            if qb < 7:
                sch.add(POOL, lambda: nc.gpsimd.memset(bias[:, bi, :, 65 + qb:72], -BIG), writes=[K("bias", bi)])

        def m_stageC2(tt):
            bi = tt % 2
            for hh in range(2):
                pb = rP.nxt()
                for h4 in range(4):
                    h = hh * 4 + h4
                    mm(Pf[pb][0:72, h4 * 128:(h4 + 1) * 128], bias[:, bi, h, :], ident[:], True, True,
                       [K("bias", bi), K("ident")], [K("P", pb)], h4 == 3)
                sch.add(ACT, lambda pb=pb, hh=hh: nc.scalar.copy(
                    out=qTe[64:72, hh * 4:hh * 4 + 4, tt * 128:(tt + 1) * 128],
                    in_=Pf[pb][64:72, :].rearrange("p (h t) -> p h t", h=4)),
                    reads=[K("P", pb)], writes=QTE + [K("qTeb", tt)])

        def m_v(tt):
            if "v" not in wstore:
                wstore["v"] = wblk(gc, B_MV)
            wmv, kmv = wstore["v"]
            tile_i = c * 4 + tt
            pv = rP.nxt()
            for kc in range(8):
                mm(Pf[pv][:, :], hT[:, kc, tt * 128:(tt + 1) * 128], wmv[:, kc, :], kc == 0, kc == 7,
                   [K("hT", tt), kmv], [K("P", pv)], kc == 7)
            sch.add(ACT, lambda: nc.scalar.copy(
                out=vext[:, tile_i, :, 0:64], in_=Pf[pv][:, :].rearrange("p (h d) -> p h d", h=8)),
                reads=[K("P", pv)], writes=[K("vext", tile_i)])

        items2 = [("k", t) for t in range(4)] + [("q", t) for t in range(4)]
        m_stageA("k", 0, 0)
        h_post_tr(3)
        m_stageA("k", 1, 1)
        m_stageB("k", 0, 0)
        for i in range(2, 8):
            m_stageB(items2[i - 1][0], items2[i - 1][1], i - 1)
            m_stageA(items2[i][0], items2[i][1], i)
            m_stageB2(items2[i - 2][0], items2[i - 2][1], i - 2)
        m_stageB("q", 3, 7)
        m_v(0)
        m_stageB2("q", 2, 6)
        m_stageC(0)
        m_v(1)
        m_stageB2("q", 3, 7)
        m_stageC(1)
        m_stageC2(0)
        m_v(2)
        m_stageC2(1)
        m_stageC(2)
        m_v(3)
        m_stageC(3)
        m_stageC2(2)
        m_stageC2(3)
        if first:
            dump("kTe", kTe[0:72, :, 0:512], [K("kTe", i) for i in range(4)] + [K("kTe_ind")], [72, 8, 512], BF16)
            dump("qTe", qTe[0:72, :, :], QTE, [72, 8, 512], BF16)
        nkt = 4 * c + 4
        OB = [K("bB", 8 + i) for i in range(4)]
        obv = bB[:, 8:12, :].rearrange("p t (h d) -> p t h d", h=8)
        items = [(h, kt) for h in range(8) for kt in range(nkt)]
        pend = None

        def att_qk(h, kt):
            n0 = max(0, kt * 128 - c * CH)
            pst = rP.nxt()
            pi = rpt.nxt()
            mm(Pf[pst][:, n0:512], kTe[0:72, h, kt * 128:(kt + 1) * 128], qTe[0:72, h, n0:512], True, True,
               [K("kTe", kt), K("kTe_ind")] + QTE, [K("P", pst)], True)
            sch.add(ACT, lambda pst=pst, pi=pi, n0=n0: nc.scalar.activation(out=pt[:, pi, n0:512],
                                                                          in_=Pf[pst][:, n0:512], func=AF.Exp),
                    reads=[K("P", pst)], writes=[K("pt", pi)])
            if kt * 128 >= c * CH:
                sch.add(DVE, lambda pi=pi, n0=n0: nc.vector.tensor_tensor(out=pt[:, pi, n0:n0 + 128],
                                                                          in0=pt[:, pi, n0:n0 + 128], in1=tri[:],
                                                                          op=ALU.mult),
                        reads=[K("pt", pi), K("tri")], writes=[K("pt", pi)])
            return (h, kt, n0, pi)

        def att_pv(h, kt, n0, pi):
            pob = 4 + (h % 2)
            for sub in range(n0 // 128, 4):
                last = (kt == nkt - 1 and sub == 3)
                mm(Pf[pob][:, sub * 65:(sub + 1) * 65], pt[:, pi, sub * 128:(sub + 1) * 128], vext[:, kt, h, :],
                   (kt == 0 and sub == 0), last, [K("pt", pi), K("vext", kt), K("vext_one")], [K("P", pob)],
                   sub == 3, skip_group_check=True)
            if kt == nkt - 1:
                rd = h % 2
                po3 = Pf[pob][:, 0:260].rearrange("p (s e) -> p s e", e=65)
                sch.add(DVE, lambda po3=po3, rd=rd: nc.vector.reciprocal(out=rden[:, rd, :].unsqueeze(2),
                                                                         in_=po3[:, :, 64:65]),
                        reads=[K("P", pob)], writes=[K("rden", rd)])
                sch.add(DVE, lambda po3=po3, rd=rd, h=h: nc.vector.tensor_tensor(
                    out=obv[:, :, h, :], in0=po3[:, :, 0:64],
                    in1=rden[:, rd, :].unsqueeze(2).broadcast_to([128, 4, 64]), op=ALU.mult),
                    reads=[K("P", pob), K("rden", rd)], writes=OB)

        pendq = []
        for (h, kt) in items:
            pendq.append(att_qk(h, kt))
            if len(pendq) > 2:
                att_pv(*pendq.pop(0))
        while pendq:
            att_pv(*pendq.pop(0))
        if first:
            dump("ob", bB[:, 8:12, :], OB, [128, 4, 512], BF16)
        for tt in range(4):
            ti = rT.nxt()
            for kc in range(4):
                tp(Tb[ti][:, kc * 128:(kc + 1) * 128], bB[:, 8 + tt, kc * 128:(kc + 1) * 128], [K("bB", 8 + tt)],
                   [K("T", ti)], kc == 3)
            sch.add(ACT, lambda ti=ti, tt=tt: nc.scalar.copy(out=obT[:, :, tt * 128:(tt + 1) * 128],
                                                             in_=Tb[ti][:, 0:512].rearrange("p (a t) -> p a t", t=128)),
                    reads=[K("T", ti)], writes=[K("obT", tt)])
        OAT = [K("oaT", t) for t in range(4)]
        OBT = [K("obT", t) for t in range(4)]
        MIX = [K("bB", i) for i in range(8)]
        gab_ids = [B_GAB0, B_GAB1, B_GAB2, B_GAB3]
        for qd in range(4):
            wg2, kg2 = wblk(gc, gab_ids[qd], 1)
            wab, kab = wblk(gc, B_WAB0 if qd < 2 else B_WAB1, 1)
            for e in range(2):
                i = 2 * qd + e
                col = (i % 4) * 128
                res = []
                for br in range(2):
                    pgt = rP.nxt()
                    for kc in range(8):
                        mm(Pf[pgt][:, :], wg2[:, kc, br * 256 + e * 128: br * 256 + (e + 1) * 128], hT[:, kc, :],
                           kc == 0, kc == 7, HTK + [kg2], [K("P", pgt)], kc == 7)
                    pab = rP.nxt()
                    src = oaT if br == 0 else obT
                    srk = OAT if br == 0 else OBT
                    for kc in range(4):
                        mm(Pf[pab][:, :], wab[:, br * 4 + kc, col:col + 128], src[:, kc, :], kc == 0, kc == 3,
                           srk + [kab], [K("P", pab)], kc == 3)
                    ss_, ms_ = 0 + br, 2 + br
                    sch.add(ACT, lambda pgt=pgt, ss_=ss_: nc.scalar.activation(out=fS[:, ss_, :], in_=Pf[pgt][:, :],
                                                                             func=AF.Sigmoid),
                            reads=[K("P", pgt)], writes=[K("fS", ss_)])
                    sch.add(DVE, lambda pab=pab, ss_=ss_, ms_=ms_: nc.vector.tensor_tensor(
                        out=fS[:, ms_, :], in0=fS[:, ss_, :], in1=Pf[pab][:, :], op=ALU.mult),
                        reads=[K("fS", ss_), K("P", pab)], writes=[K("fS", ms_)])
                sch.add(POOL, lambda i=i: nc.gpsimd.tensor_tensor(out=bB[:, i, :], in0=fS[:, 2, :], in1=fS[:, 3, :],
                                                                  op=ALU.add),
                        reads=[K("fS", 2), K("fS", 3)], writes=[K("bB", i)])
        if first:
            dump("mixT", bB[:, 0:8, :], MIX, [128, 8, 512], BF16)
        wo0, ko0 = wblk(gc, B_WO0)
        wo1, ko1 = wblk(gc, B_WO1, 1)

        def w_out_tile(tt):
            xi = rxt.nxt()
            xs = 12 + 2 * xi
            xap = fS[:, xs:xs + 2, :].rearrange("p a b -> p (a b)")
            xk = [K("fS", xs), K("fS", xs + 1)]
            r0 = c * CH + tt * 128
            sch.add(XQ[xi], lambda: nc.sync.dma_start(out=xap, in_=x[s, r0:r0 + 128, :]), writes=xk)
            for nh, (wo, ko) in enumerate([(wo0, ko0), (wo1, ko1)]):
                pw = rP.nxt()
                for kc in range(8):
                    mm(Pf[pw][:, :], bB[:, kc, tt * 128:(tt + 1) * 128], wo[:, kc, :], kc == 0, kc == 7,
                       MIX + [ko], [K("P", pw)], kc == 7)
                sch.add(DVE, lambda pw=pw, nh=nh: nc.vector.tensor_tensor(
                    out=xmid[:, tt, nh * 512:(nh + 1) * 512], in0=xap[:, nh * 512:(nh + 1) * 512], in1=Pf[pw][:, :],
                    op=ALU.add), reads=xk + [K("P", pw)], writes=[K("xmid", tt, nh)])

        def n2A(tt):
            norm_A(xmid[:, tt, :], [K("xmid", tt, 0), K("xmid", tt, 1)], g2b, K("g2b"), tt, 4 * tt)

        w_out_tile(0); w_out_tile(1); n2A(0); w_out_tile(2); n2A(1); norm_B(0); w_out_tile(3)
        if first:
            dump("xmid", xmid[:], [K("xmid", t, n) for t in range(4) for n in range(2)], [128, 4, D])
        n2A(2); norm_B(1); n2A(3); norm_B(2); norm_B(3)
        par = c % 2
        if c == 0:
            sch.add(POOL, lambda: nc.gpsimd.memset(halo[:, 0, :, :], 0.0), writes=[K("halo", 0)])
        GT = [K("bB", i) for i in range(NKF)]
        ngc = gc + 1
        has_next = ngc < NSEQ * NCH
        for u in range(11):
            wu, ku = wblk(gc, B_UP0 + u)
            if has_next and u == 8:
                x_norm_A(ngc // NCH, ngc % NCH, 0)
                x_norm_A(ngc // NCH, ngc % NCH, 1)
            for e in range(2):
                i = 2 * u + e
                rb = i % 2
                ub = 0 + 2 * rb
                ac = 4 + rb
                gl = 6 + rb
                ubuf = fS[:, ub:ub + 2, :].rearrange("p a b -> p (a b)")
                UBK = [K("fS", ub), K("fS", ub + 1)]
                pu = rP.nxt()
                for kc in range(8):
                    mm(Pf[pu][:, :], wu[:, kc, e * 128:(e + 1) * 128], hT[:, kc, :], kc == 0, kc == 7, HTK + [ku],
                       [K("P", pu)], kc == 7)
                pv2 = rP.nxt()
                for kc in range(8):
                    mm(Pf[pv2][:, :], wu[:, kc, 256 + e * 128:256 + (e + 1) * 128], hT[:, kc, :], kc == 0, kc == 7,
                       HTK + [ku], [K("P", pv2)], kc == 7)
                sch.add(ACT, lambda pu=pu, ubuf=ubuf: nc.scalar.copy(out=ubuf[:, 2:514], in_=Pf[pu][:, :]),
                        reads=[K("P", pu)], writes=UBK)
                sch.add(POOL, lambda ubuf=ubuf, i=i: nc.gpsimd.tensor_copy(out=ubuf[:, 0:2], in_=halo[:, par, i, :]),
                        reads=[K("halo", par)], writes=UBK)
                sch.add(POOL, lambda ubuf=ubuf, i=i: nc.gpsimd.tensor_copy(out=halo[:, 1 - par, i, :],
                                                                           in_=ubuf[:, 512:514]),
                        reads=UBK, writes=[K("halo", 1 - par)])
                sch.add(DVE, lambda ubuf=ubuf, i=i, ac=ac: nc.vector.tensor_scalar(
                    out=fS[:, ac, :], in0=ubuf[:, 2:514], scalar1=cw[:, 2, i:i + 1], scalar2=cb[:, i:i + 1],
                    op0=ALU.mult, op1=ALU.add), reads=UBK + [K("cw"), K("cb")], writes=[K("fS", ac)])
                sch.add(DVE, lambda ubuf=ubuf, i=i, ac=ac: nc.vector.scalar_tensor_tensor(
                    out=fS[:, ac, :], in0=ubuf[:, 1:513], scalar=cw[:, 1, i:i + 1], in1=fS[:, ac, :], op0=ALU.mult,
                    op1=ALU.add), reads=UBK + [K("cw"), K("fS", ac)], writes=[K("fS", ac)])
                sch.add(DVE, lambda ubuf=ubuf, i=i, ac=ac: nc.vector.scalar_tensor_tensor(
                    out=fS[:, ac, :], in0=ubuf[:, 0:512], scalar=cw[:, 0, i:i + 1], in1=fS[:, ac, :], op0=ALU.mult,
                    op1=ALU.add), reads=UBK + [K("cw"), K("fS", ac)], writes=[K("fS", ac)])
                sch.add(ACT, lambda ac=ac, gl=gl: nc.scalar.activation(out=fS[:, gl, :], in_=fS[:, ac, :], func=AF.Gelu),
                        reads=[K("fS", ac)], writes=[K("fS", gl)])
                sch.add(DVE, lambda gl=gl, pv2=pv2, i=i: nc.vector.tensor_tensor(out=bB[:, i, :], in0=fS[:, gl, :],
                                                                                 in1=Pf[pv2][:, :], op=ALU.mult),
                        reads=[K("fS", gl), K("P", pv2)], writes=[K("bB", i)])
        if first:
            dump("gT", bB[:, 0:NKF, :], GT, [128, NKF, 512], BF16)
        if has_next:
            for ptt in (0, 1):
                norm_B(ptt)
                prefetched.add((ngc, ptt))
            for ptt in (2, 3):
                x_norm_A(ngc // NCH, ngc % NCH, ptt)
        for nh in range(2):
            banks = [0, 1, 2, 3] if nh == 0 else [4, 5, 0, 1]
            for pi_, (k0, nk) in enumerate(DN_P):
                wd, kd = wblk(gc, B_DN0 + nh * 3 + pi_)
                for tt in range(4):
                    for kk in range(nk):
                        kc = k0 + kk
                        mm(Pf[banks[tt]][:, :], bB[:, kc, tt * 128:(tt + 1) * 128], wd[:, kk, :], kc == 0,
                           kc == NKF - 1, [K("bB", kc), kd], [K("P", banks[tt])], kk == nk - 1)
            for tt in range(4):
                sch.add(DVE, lambda nh=nh, tt=tt, b=banks[tt]: nc.vector.tensor_tensor(
                    out=xmid[:, tt, nh * 512:(nh + 1) * 512], in0=xmid[:, tt, nh * 512:(nh + 1) * 512],
                    in1=Pf[b][:, :], op=ALU.add),
                    reads=[K("xmid", tt, nh), K("P", banks[tt])], writes=[K("xmid", tt, nh)])
            if nh == 0 and has_next:
                for ptt in (2, 3):
                    norm_B(ptt)
                    prefetched.add((ngc, ptt))
        sch.add(OQ, lambda: nc.gpsimd.dma_start(
            out=out[s, c * CH:(c + 1) * CH, :].rearrange("(t p) d -> p t d", p=128), in_=xmid[:]),
            reads=[K("xmid", t, n) for t in range(4) for n in range(2)])

    for s in range(NSEQ):
        for c in range(NCH):
            chunk(s, c)
    stats = sch.finalize(nc.sync)
    es.close()
    return nc, dumps, stats


_CACHE = {}


def kernel(**inputs):
    nseq = 32 // NCORES
    if "nc" not in _CACHE:
        _CACHE["nc"] = build(nseq)[0]
    nc = _CACHE["nc"]
    consts = host_consts()
    x = np.ascontiguousarray(np.asarray(inputs["x"], dtype=np.float32))
    shared = {
        "norm1_g": np.asarray(inputs["norm1_g"], np.float32).reshape(1, D),
        "norm2_g": np.asarray(inputs["norm2_g"], np.float32).reshape(1, D),
        "w_in": np.asarray(inputs["w_in"], np.float32).reshape(D, 5632),
        "hg_lb_logits": np.asarray(inputs["hg_lb_logits"], np.float32).reshape(2, 512),
        "hg_onorm_g": np.asarray(inputs["hg_onorm_g"], np.float32).reshape(1, 512),
        "q_norm_g": np.asarray(inputs["q_norm_g"], np.float32).reshape(1, 64),
        "k_norm_g": np.asarray(inputs["k_norm_g"], np.float32).reshape(1, 64),
        "w_a": np.asarray(inputs["w_a"], np.float32).reshape(512, D),
        "w_b": np.asarray(inputs["w_b"], np.float32).reshape(512, D),
        "w_out": np.asarray(inputs["w_out"], np.float32).reshape(D, D),
        "w_up": np.asarray(inputs["w_up"], np.float32).reshape(D, 2 * DFF),
        "conv_w": np.asarray(inputs["conv_w"], np.float32).reshape(3, DFF),
        "conv_b": np.asarray(inputs["conv_b"], np.float32).reshape(1, DFF),
        "w_down": np.asarray(inputs["w_down"], np.float32).reshape(DFF, D),
    }
    shared.update(consts)
    in_maps = []
    for i in range(NCORES):
        m = dict(shared)
        m["x"] = x[i * nseq:(i + 1) * nseq]
        in_maps.append(m)
    res = run_bass_kernel_spmd(nc, in_maps, core_ids=list(range(NCORES)))
    return np.concatenate([np.asarray(r["out"]) for r in res.results], axis=0).astype(np.float32)
```
